# Optimizing a Trainium2 kernel written in Bass

```python
import math
import jax, jax.numpy as jnp
from jax import lax
import numpy as np

D_MODEL = 1024
BATCH = 8
SEQ = 2048
DEPTH = 4
DEC_BATCH = 32
DEC_SEQ = 4
PAST_LEN = 8192
PAGE_SIZE = 128

N_GDN_LAYERS = (DEPTH + 1) // 2
N_NSA_LAYERS = DEPTH // 2
GDN_HEADS = 8
GDN_DK = D_MODEL // GDN_HEADS
GDN_DV = GDN_DK
GDN_WIDTH = GDN_HEADS * GDN_DK
GDN_CONV = 4
GDN_CHUNK = 64
NSA_HEADS = 16
NSA_KV_HEADS = 4
NSA_GROUP = NSA_HEADS // NSA_KV_HEADS
HEAD_DIM = D_MODEL // NSA_HEADS
Q_WIDTH = NSA_HEADS * HEAD_DIM
KV_WIDTH = NSA_KV_HEADS * HEAD_DIM
NSA_BLOCK = 64
N_SEL = 16
WINDOW = 512
Q_BLOCK = 128
ROPE_THETA = 10000.0
ATTN_SCALE = HEAD_DIM ** -0.5
D_FF = 2816
FFN_CONV = 3
RMS_EPS = 1e-6
NEG = -1e30

kernel_name = 'gdn_nsa_convffn_hybrid_step'

f32 = jnp.float32


def rmsnorm(x, w):
    xf = x.astype(f32)
    y = xf * lax.rsqrt(jnp.mean(xf * xf, axis=-1, keepdims=True) + RMS_EPS)
    return (y * w.astype(f32)).astype(x.dtype)


def l2norm(x):
    xf = x.astype(f32)
    return xf * lax.rsqrt(jnp.sum(xf * xf, axis=-1, keepdims=True) + RMS_EPS)


def causal_conv(ext, w, L):
    out = ext[:, 0:L] * w[0]
    for j in range(1, w.shape[0]):
        out = out + ext[:, j:j + L] * w[j]
    return out


def rope(x, pos):
    half = HEAD_DIM // 2
    inv = ROPE_THETA ** (-jnp.arange(half, dtype=f32) / half)
    ang = pos.astype(f32)[:, None] * inv[None, :]
    cos, sin = jnp.cos(ang)[:, None, :], jnp.sin(ang)[:, None, :]
    xf = x.astype(f32)
    x1, x2 = xf[..., :half], xf[..., half:]
    return jnp.concatenate([x1 * cos - x2 * sin, x2 * cos + x1 * sin], axis=-1).astype(x.dtype)


def gated_delta_chunked(q, k, v, g, beta, S0):
    B, L, H, _ = q.shape
    DVh = v.shape[-1]
    C = min(GDN_CHUNK, L)
    N = -(-L // C)
    pad = N * C - L

    def prep(t):
        t = t.astype(f32)
        if pad:
            t = jnp.pad(t, [(0, 0), (0, pad)] + [(0, 0)] * (t.ndim - 2))
        t = t.reshape((B, N, C) + t.shape[2:])
        return jnp.moveaxis(t, 3, 1)

    q, k, v, g, beta = prep(q), prep(k), prep(v), prep(g), prep(beta)
    G = jnp.cumsum(g, axis=-1)
    ii = jnp.arange(C)
    tril = ii[:, None] >= ii[None, :]
    strict = ii[:, None] > ii[None, :]
    decay = jnp.exp(jnp.where(tril, G[..., :, None] - G[..., None, :], NEG))
    kk = jnp.einsum('bhncd,bhnsd->bhncs', k, k)
    A = jnp.where(strict, beta[..., :, None] * kk * decay, 0.0)
    eye = jnp.eye(C, dtype=f32)
    T = lax.linalg.triangular_solve(eye + A, jnp.broadcast_to(eye, A.shape), left_side=True, lower=True, unit_diagonal=True)
    u_v = T @ (v * beta[..., None])
    w_k = T @ (k * (beta * jnp.exp(G))[..., None])
    qk = jnp.einsum('bhncd,bhnsd->bhncs', q, k) * decay
    gl = G[..., -1]
    k_dec = k * jnp.exp(gl[..., None] - G)[..., None]

    def step(S, xs):
        qc, qkc, uvc, wkc, kdc, Gc, glc = xs
        u = uvc - wkc @ S
        o = (qc * jnp.exp(Gc)[..., None]) @ S + qkc @ u
        S = S * jnp.exp(glc)[..., None, None] + jnp.einsum('bhcd,bhce->bhde', kdc, u)
        return S, o

    xs = (jnp.moveaxis(q, 2, 0), jnp.moveaxis(qk, 2, 0), jnp.moveaxis(u_v, 2, 0), jnp.moveaxis(w_k, 2, 0),
          jnp.moveaxis(k_dec, 2, 0), jnp.moveaxis(G, 2, 0), jnp.moveaxis(gl, 2, 0))
    S, o = lax.scan(step, S0.astype(f32), xs)
    o = jnp.moveaxis(jnp.moveaxis(o, 0, 2), 1, 3).reshape(B, N * C, H, DVh)[:, :L]
    return o, S


def gdn_mixer(h, S0, conv_buf, w_in, conv_w, a_log, dt_bias, norm_w, w_out):
    B, L, _ = h.shape
    proj = h @ w_in
    qkv = proj[..., :3 * GDN_WIDTH]
    z = proj[..., 3 * GDN_WIDTH:4 * GDN_WIDTH].reshape(B, L, GDN_HEADS, GDN_DV)
    a = proj[..., 4 * GDN_WIDTH:4 * GDN_WIDTH + GDN_HEADS].astype(f32)
    b = proj[..., 4 * GDN_WIDTH + GDN_HEADS:].astype(f32)
    ext = jnp.concatenate([conv_buf.astype(qkv.dtype), qkv], axis=1)
    new_buf = ext[:, -(GDN_CONV - 1):]
    c = jax.nn.silu(causal_conv(ext, conv_w, L))
    q = l2norm(c[..., :GDN_WIDTH].reshape(B, L, GDN_HEADS, GDN_DK)) * (GDN_DK ** -0.5)
    k = l2norm(c[..., GDN_WIDTH:2 * GDN_WIDTH].reshape(B, L, GDN_HEADS, GDN_DK))
    v = c[..., 2 * GDN_WIDTH:].reshape(B, L, GDN_HEADS, GDN_DV)
    beta = jax.nn.sigmoid(b)
    g = -jnp.exp(a_log.astype(f32)) * jax.nn.softplus(a + dt_bias.astype(f32))
    o, S = gated_delta_chunked(q, k, v, g, beta, S0)
    o = rmsnorm(o, norm_w) * jax.nn.silu(z.astype(f32))
    return o.astype(h.dtype).reshape(B, L, GDN_WIDTH) @ w_out, S.astype(S0.dtype), new_buf


def nsa_project(h, w_in, pos):
    B, L, _ = h.shape
    proj = h @ w_in
    q = proj[..., :Q_WIDTH].reshape(B, L, NSA_HEADS, HEAD_DIM)
    kv = proj[..., Q_WIDTH:Q_WIDTH + 6 * KV_WIDTH].reshape(B, L, 3, 2, NSA_KV_HEADS, HEAD_DIM)
    gates = jax.nn.sigmoid(proj[..., Q_WIDTH + 6 * KV_WIDTH:].astype(f32)).reshape(B, L, NSA_KV_HEADS, NSA_GROUP, 3)
    q_raw = q.reshape(B, L, NSA_KV_HEADS, NSA_GROUP, HEAD_DIM)
    q_rot = rope(q, pos).reshape(B, L, NSA_KV_HEADS, NSA_GROUP, HEAD_DIM)
    kv_cmp = kv[:, :, 0]
    kv_sel = jnp.stack([rope(kv[:, :, 1, 0], pos), kv[:, :, 1, 1]], axis=2)
    kv_win = jnp.stack([rope(kv[:, :, 2, 0], pos), kv[:, :, 2, 1]], axis=2)
    return q_raw, q_rot, kv_cmp, kv_sel, kv_win, gates


def compress(rows, pe, w):
    B, T = rows.shape[:2]
    nb = T // NSA_BLOCK
    blk = rows.reshape(B, nb, NSA_BLOCK, 2, NSA_KV_HEADS, HEAD_DIM) + pe[:, :, None, :]
    return jnp.einsum('bnlcgd,lcde->bncge', blk, w)


def cmp_attend_select(q, pos, ckv, n_top):
    NB = ckv.shape[1]
    s = jnp.einsum('bqgnd,bkgd->bqgnk', q, ckv[:, :, 0]).astype(f32) * ATTN_SCALE
    blk_end = (jnp.arange(NB, dtype=jnp.int32) + 1) * NSA_BLOCK - 1
    m = (blk_end[None, :] <= pos[:, None])[None, :, None, None, :]
    p = jax.nn.softmax(jnp.where(m, s, NEG), axis=-1) * m
    o = jnp.einsum('bqgnk,bkgd->bqgnd', p, ckv[:, :, 1].astype(f32))
    imp = jnp.sum(p, axis=3)
    cur = pos // NSA_BLOCK
    cand = (jnp.arange(NB, dtype=jnp.int32)[None, :] < cur[:, None])[None, :, None, :]
    vals, top = lax.top_k(jnp.where(cand, imp, -1.0), n_top)
    cur_b = jnp.broadcast_to(cur[None, :, None, None], top.shape[:-1] + (1,)).astype(jnp.int32)
    idx = jnp.concatenate([top.astype(jnp.int32), cur_b], axis=-1)
    valid = jnp.concatenate([vals > -0.5, jnp.ones(cur_b.shape, dtype=bool)], axis=-1)
    return o, idx, valid


def sel_attend(q, pos, idx, valid, k, v):
    B, Lq, nk, ng, _ = q.shape
    KS = idx.shape[-1]
    s = jnp.einsum('bqgnd,bqgkld->bqgnkl', q, k).astype(f32) * ATTN_SCALE
    kpos = idx[..., None] * NSA_BLOCK + jnp.arange(NSA_BLOCK, dtype=jnp.int32)
    m = (kpos <= pos[None, :, None, None, None]) & valid[..., None]
    s = jnp.where(m[:, :, :, None], s, NEG).reshape(B, Lq, nk, ng, KS * NSA_BLOCK)
    p = jax.nn.softmax(s, axis=-1).reshape(B, Lq, nk, ng, KS, NSA_BLOCK)
    return jnp.einsum('bqgnkl,bqgkld->bqgnd', p, v.astype(f32))


def window_attend(q, pos, k, v, kpos):
    s = jnp.einsum('bqgnd,bkgd->bqgnk', q, k).astype(f32) * ATTN_SCALE
    dist = pos[:, None] - kpos[None, :]
    m = (dist >= 0) & (dist < WINDOW) & (kpos[None, :] >= 0)
    p = jax.nn.softmax(jnp.where(m[None, :, None, None, :], s, NEG), axis=-1)
    return jnp.einsum('bqgnk,bkgd->bqgnd', p, v.astype(f32))


def nsa_prompt_attend(q_raw, q_rot, kv_cmp, kv_sel, kv_win, pos, cmp_pe, cmp_w):
    B, L = q_raw.shape[:2]
    ckv = compress(kv_cmp, cmp_pe, cmp_w)
    o_c, idx, valid = cmp_attend_select(q_raw, pos, ckv, min(N_SEL - 1, ckv.shape[1]))
    qb = min(Q_BLOCK, L)
    nq = L // qb
    kvb = kv_sel.reshape(B, L // NSA_BLOCK, NSA_BLOCK, 2, NSA_KV_HEADS, HEAD_DIM)
    kw_pad = jnp.pad(kv_win, ((0, 0), (WINDOW, 0), (0, 0), (0, 0), (0, 0)))
    bi = jnp.arange(B)[:, None, None, None]
    gi = jnp.arange(NSA_KV_HEADS)[None, None, :, None]

    def blocks(t):
        return jnp.swapaxes(t.reshape((B, nq, qb) + t.shape[2:]), 0, 1)

    def body(xs):
        jb, qblk, iblk, vblk = xs
        pblk = jb * qb + jnp.arange(qb, dtype=jnp.int32)
        gth = kvb[bi, iblk, :, :, gi]
        o_s = sel_attend(qblk, pblk, iblk, vblk, gth[..., 0, :], gth[..., 1, :])
        span = lax.dynamic_slice_in_dim(kw_pad, jb * qb, WINDOW + qb, axis=1)
        kpos = jb * qb - WINDOW + jnp.arange(WINDOW + qb, dtype=jnp.int32)
        o_w = window_attend(qblk, pblk, span[:, :, 0], span[:, :, 1], kpos)
        return o_s, o_w

    o_s, o_w = lax.map(body, (jnp.arange(nq, dtype=jnp.int32), blocks(q_rot), blocks(idx), blocks(valid)))
    o_s = jnp.swapaxes(o_s, 0, 1).reshape((B, L) + o_s.shape[3:])
    o_w = jnp.swapaxes(o_w, 0, 1).reshape((B, L) + o_w.shape[3:])
    return o_c, o_s, o_w


def nsa_sample_attend(q_raw, q_rot, kv_cmp, kv_sel, kv_win, pos, pool_cmp, pool_sel, win_buf, page_table, cmp_pe, cmp_w):
    DB, DS = q_raw.shape[:2]
    past = page_table.shape[1] * PAGE_SIZE
    past_cmp = pool_cmp[page_table].reshape(DB, past, 2, NSA_KV_HEADS, HEAD_DIM)
    n_new_full = (DS // NSA_BLOCK) * NSA_BLOCK
    ckv = jnp.concatenate([compress(past_cmp, cmp_pe, cmp_w),
                           compress(kv_cmp[:, :n_new_full], cmp_pe, cmp_w).astype(past_cmp.dtype)], axis=1)
    o_c, idx, valid = cmp_attend_select(q_raw, pos, ckv, min(N_SEL - 1, ckv.shape[1]))
    ppb = PAGE_SIZE // NSA_BLOCK
    npb = past // NSA_BLOCK
    pool_b = pool_sel.reshape(-1, NSA_BLOCK, 2, NSA_KV_HEADS, HEAD_DIM)
    si = jnp.arange(DB)[:, None, None, None]
    gi = jnp.arange(NSA_KV_HEADS)[None, None, :, None]
    b_past = jnp.minimum(idx, npb - 1)
    phys = page_table[si, b_past // ppb] * ppb + b_past % ppb
    past_blk = pool_b[phys, :, :, gi]
    nbn = -(-DS // NSA_BLOCK)
    new_b = jnp.pad(kv_sel, ((0, 0), (0, nbn * NSA_BLOCK - DS), (0, 0), (0, 0), (0, 0)))
    new_b = new_b.reshape(DB, nbn, NSA_BLOCK, 2, NSA_KV_HEADS, HEAD_DIM)
    new_blk = new_b[si, jnp.clip(idx - npb, 0, nbn - 1), :, :, gi]
    blk = jnp.where((idx >= npb)[..., None, None, None], new_blk.astype(past_blk.dtype), past_blk)
    o_s = sel_attend(q_rot, pos, idx, valid, blk[..., 0, :], blk[..., 1, :])
    wbuf = win_buf.shape[1]
    wk = jnp.concatenate([win_buf.astype(kv_win.dtype), kv_win], axis=1)
    kpos = past - wbuf + jnp.arange(wbuf + DS, dtype=jnp.int32)
    o_w = window_attend(q_rot, pos, wk[:, :, 0], wk[:, :, 1], kpos)
    return o_c, o_s, o_w, wk[:, -wbuf:]


def nsa_merge(o_c, o_s, o_w, gates, w_out, dtype):
    o = gates[..., 0:1] * o_c + gates[..., 1:2] * o_s + gates[..., 2:3] * o_w
    B, L = o.shape[:2]
    return o.astype(dtype).reshape(B, L, Q_WIDTH) @ w_out


def conv_ffn(h, buf, w_up, conv_w, conv_b, w_down):
    L = h.shape[1]
    u = h @ w_up
    ext = jnp.concatenate([buf.astype(u.dtype), u], axis=1)
    c = causal_conv(ext, conv_w, L) + conv_b
    out = (jax.nn.silu(c[..., :D_FF]) * c[..., D_FF:]) @ w_down
    return out, ext[:, -(FFN_CONV - 1):]


def setup_inputs(seed: int = 0) -> dict:
    key = jax.random.key(seed)
    ks = iter(jax.random.split(key, 40))

    def nrm(shape, scale):
        return jax.random.normal(next(ks), shape, f32) * scale

    n_pages = PAST_LEN // PAGE_SIZE
    n_used = DEC_BATCH * n_pages
    n_phys = n_used + (n_used + 3) // 4
    wbuf = min(WINDOW, PAST_LEN)
    gdn_in_w = 4 * GDN_WIDTH + 2 * GDN_HEADS
    nsa_in_w = Q_WIDTH + 6 * KV_WIDTH + 3 * NSA_HEADS

    x_prompt = nrm((BATCH, SEQ, D_MODEL), 1.0)
    x_sample = nrm((DEC_BATCH, DEC_SEQ, D_MODEL), 1.0)
    state_gdn = nrm((N_GDN_LAYERS, DEC_BATCH, GDN_HEADS, GDN_DK, GDN_DV), 0.1)
    state_gdn_conv = nrm((N_GDN_LAYERS, DEC_BATCH, GDN_CONV - 1, 3 * GDN_WIDTH), 1.0)
    cache_cmp = nrm((N_NSA_LAYERS, n_phys, PAGE_SIZE, 2, NSA_KV_HEADS, HEAD_DIM), 1.0)
    cache_sel = nrm((N_NSA_LAYERS, n_phys, PAGE_SIZE, 2, NSA_KV_HEADS, HEAD_DIM), 1.0)
    state_win = nrm((N_NSA_LAYERS, DEC_BATCH, wbuf, 2, NSA_KV_HEADS, HEAD_DIM), 1.0)
    state_ffn_conv = nrm((DEPTH, DEC_BATCH, FFN_CONV - 1, 2 * D_FF), 1.0)
    page_table = jax.random.permutation(next(ks), n_phys)[:n_used].reshape(DEC_BATCH, n_pages).astype(jnp.int32)

    norm_mix = 1.0 + nrm((DEPTH, D_MODEL), 0.01)
    norm_ffn = 1.0 + nrm((DEPTH, D_MODEL), 0.01)
    norm_final = 1.0 + nrm((D_MODEL,), 0.01)
    gdn_w_in = nrm((N_GDN_LAYERS, D_MODEL, gdn_in_w), D_MODEL ** -0.5)
    gdn_conv_w = nrm((N_GDN_LAYERS, GDN_CONV, 3 * GDN_WIDTH), GDN_CONV ** -0.5)
    gdn_a_log = jnp.log(jax.random.uniform(next(ks), (N_GDN_LAYERS, GDN_HEADS), f32, 1.0, 16.0))
    dt = jnp.exp(jax.random.uniform(next(ks), (N_GDN_LAYERS, GDN_HEADS), f32, math.log(1e-3), math.log(1e-1)))
    gdn_dt_bias = dt + jnp.log(-jnp.expm1(-dt))
    gdn_norm_w = 1.0 + nrm((N_GDN_LAYERS, GDN_DV), 0.01)
    gdn_w_out = nrm((N_GDN_LAYERS, GDN_WIDTH, D_MODEL), GDN_WIDTH ** -0.5)
    nsa_w_in = nrm((N_NSA_LAYERS, D_MODEL, nsa_in_w), D_MODEL ** -0.5)
    nsa_cmp_pe = nrm((N_NSA_LAYERS, NSA_BLOCK, 2, HEAD_DIM), 0.1)
    nsa_cmp_w = nrm((N_NSA_LAYERS, NSA_BLOCK, 2, HEAD_DIM, HEAD_DIM), (NSA_BLOCK * HEAD_DIM) ** -0.5)
    nsa_w_out = nrm((N_NSA_LAYERS, Q_WIDTH, D_MODEL), Q_WIDTH ** -0.5)
    ffn_w_up = nrm((DEPTH, D_MODEL, 2 * D_FF), D_MODEL ** -0.5)
    ffn_conv_w = nrm((DEPTH, FFN_CONV, 2 * D_FF), FFN_CONV ** -0.5)
    ffn_conv_b = nrm((DEPTH, 2 * D_FF), 0.01)
    ffn_w_down = nrm((DEPTH, D_FF, D_MODEL), D_FF ** -0.5)
    return {'x_prompt': x_prompt, 'x_sample': x_sample, 'state_gdn': state_gdn, 'state_gdn_conv': state_gdn_conv,
            'cache_cmp': cache_cmp, 'cache_sel': cache_sel, 'state_win': state_win, 'state_ffn_conv': state_ffn_conv,
            'page_table': page_table, 'norm_mix': norm_mix, 'norm_ffn': norm_ffn, 'norm_final': norm_final,
            'gdn_w_in': gdn_w_in, 'gdn_conv_w': gdn_conv_w, 'gdn_a_log': gdn_a_log, 'gdn_dt_bias': gdn_dt_bias,
            'gdn_norm_w': gdn_norm_w, 'gdn_w_out': gdn_w_out, 'nsa_w_in': nsa_w_in, 'nsa_cmp_pe': nsa_cmp_pe,
            'nsa_cmp_w': nsa_cmp_w, 'nsa_w_out': nsa_w_out, 'ffn_w_up': ffn_w_up, 'ffn_conv_w': ffn_conv_w,
            'ffn_conv_b': ffn_conv_b, 'ffn_w_down': ffn_w_down}


def reference(x_prompt, x_sample, state_gdn, state_gdn_conv, cache_cmp, cache_sel, state_win, state_ffn_conv,
              page_table, norm_mix, norm_ffn, norm_final, gdn_w_in, gdn_conv_w, gdn_a_log, gdn_dt_bias, gdn_norm_w,
              gdn_w_out, nsa_w_in, nsa_cmp_pe, nsa_cmp_w, nsa_w_out, ffn_w_up, ffn_conv_w, ffn_conv_b, ffn_w_down):
    B, L, _ = x_prompt.shape
    DB, DS, _ = x_sample.shape
    past = page_table.shape[1] * PAGE_SIZE
    pos_p = jnp.arange(L, dtype=jnp.int32)
    pos_s = past + jnp.arange(DS, dtype=jnp.int32)
    dt = x_prompt.dtype
    yp, ys = x_prompt, x_sample
    gdn_p, gdn_s, gconv_p, gconv_s = [], [], [], []
    cmp_p, cmp_s, sel_p, sel_s, win_p, win_s = [], [], [], [], [], []
    ffn_p, ffn_s = [], []
    for i in range(DEPTH):
        j = i // 2
        hp = rmsnorm(yp, norm_mix[i])
        hs = rmsnorm(ys, norm_mix[i])
        if i % 2 == 0:
            S0p = jnp.zeros((B, GDN_HEADS, GDN_DK, GDN_DV), dt)
            c0p = jnp.zeros((B, GDN_CONV - 1, 3 * GDN_WIDTH), dt)
            op, Sp, cp = gdn_mixer(hp, S0p, c0p, gdn_w_in[j], gdn_conv_w[j], gdn_a_log[j], gdn_dt_bias[j],
                                   gdn_norm_w[j], gdn_w_out[j])
            os_, Ss, cs = gdn_mixer(hs, state_gdn[j], state_gdn_conv[j], gdn_w_in[j], gdn_conv_w[j], gdn_a_log[j],
                                    gdn_dt_bias[j], gdn_norm_w[j], gdn_w_out[j])
            gdn_p.append(Sp)
            gdn_s.append(Ss)
            gconv_p.append(cp)
            gconv_s.append(cs)
        else:
            qr, qo, kc, ksl, kw, gt = nsa_project(hp, nsa_w_in[j], pos_p)
            o_c, o_s, o_w = nsa_prompt_attend(qr, qo, kc, ksl, kw, pos_p, nsa_cmp_pe[j], nsa_cmp_w[j])
            op = nsa_merge(o_c, o_s, o_w, gt, nsa_w_out[j], dt)
            cmp_p.append(kc)
            sel_p.append(ksl)
            win_p.append(kw[:, -min(WINDOW, L):])
            qr2, qo2, kc2, ksl2, kw2, gt2 = nsa_project(hs, nsa_w_in[j], pos_s)
            o_c2, o_s2, o_w2, nwin = nsa_sample_attend(qr2, qo2, kc2, ksl2, kw2, pos_s, cache_cmp[j], cache_sel[j],
                                                       state_win[j], page_table, nsa_cmp_pe[j], nsa_cmp_w[j])
            os_ = nsa_merge(o_c2, o_s2, o_w2, gt2, nsa_w_out[j], dt)
            cmp_s.append(kc2)
            sel_s.append(ksl2)
            win_s.append(nwin)
        yp = yp + op
        ys = ys + os_
        fp, bp = conv_ffn(rmsnorm(yp, norm_ffn[i]), jnp.zeros((B, FFN_CONV - 1, 2 * D_FF), dt),
                          ffn_w_up[i], ffn_conv_w[i], ffn_conv_b[i], ffn_w_down[i])
        fs, bs = conv_ffn(rmsnorm(ys, norm_ffn[i]), state_ffn_conv[i],
                          ffn_w_up[i], ffn_conv_w[i], ffn_conv_b[i], ffn_w_down[i])
        yp = yp + fp
        ys = ys + fs
        ffn_p.append(bp)
        ffn_s.append(bs)
    y_prompt = rmsnorm(yp, norm_final)
    y_sample = rmsnorm(ys, norm_final)
    new_gdn_p = jnp.stack(gdn_p)
    new_gdn_s = jnp.stack(gdn_s)
    new_gdn_conv_p = jnp.stack(gconv_p)
    new_gdn_conv_s = jnp.stack(gconv_s)
    new_cmp_p = jnp.stack(cmp_p)
    new_cmp_s = jnp.stack(cmp_s)
    new_sel_p = jnp.stack(sel_p)
    new_sel_s = jnp.stack(sel_s)
    new_win_p = jnp.stack(win_p)
    new_win_s = jnp.stack(win_s)
    new_ffn_conv_p = jnp.stack(ffn_p)
    new_ffn_conv_s = jnp.stack(ffn_s)
    return (y_prompt, y_sample, new_gdn_p, new_gdn_s, new_gdn_conv_p, new_gdn_conv_s, new_cmp_p, new_cmp_s,
            new_sel_p, new_sel_s, new_win_p, new_win_s, new_ffn_conv_p, new_ffn_conv_s)
```

```python
import numpy as np
import concourse.bass as bass
import concourse.mybir as mybir
from concourse.bass_utils import run_bass_kernel_spmd

F32 = mybir.dt.float32
BF16 = mybir.dt.bfloat16
I32 = mybir.dt.int32
U32 = mybir.dt.uint32
AF = mybir.ActivationFunctionType
ALU = mybir.AluOpType
AX = mybir.AxisListType

D = 1024
LP = 2048
NS = 4
DS = 4
NT = LP + NS * DS
DFF = 2816
NPG = 64
EPS = 1e-6
GW = 4112
NW = 2608


class Buf:
    __slots__ = ("t", "name", "lw", "rd", "psum")

    def __init__(self, t, name, psum=False):
        self.t = t
        self.name = name
        self.lw = None
        self.rd = []
        self.psum = psum

    def __getitem__(self, idx):
        return self.t[idx]

    def ap(self):
        return self.t.ap()


class Prog:
    ENGS = ("pe", "act", "dve", "pool", "sp")

    def __init__(self, nc, ndma=8):
        self.nc = nc
        self.stack = []
        self.ops = {e: [] for e in self.ENGS}
        self.sems = {}
        self.cnt = {}
        self.waited = {e: {} for e in self.ENGS}
        self.semguards = []
        self.ekey = {}
        self.epoch = {}
        self.LIMIT = 3000
        for e in self.ENGS:
            self._mksem("E_" + e)
            self.ekey[e] = "E_" + e
            self.epoch[e] = 0
        self.dkey = {}
        self.ndma = ndma
        self.dma_rr = {"sp": 0, "pool": 0, "act": 0}
        for q in ("sp", "pool", "act"):
            for i in range(ndma):
                self._mksem("D_%s%d" % (q, i))
                self.dkey[(q, i)] = "D_%s%d" % (q, i)
        self.nrot = 0
        self.nbuf = 0
        self.nops = 0

    def _mksem(self, key):
        g = self.nc.semaphore(key)
        s = g.__enter__()
        self.semguards.append(g)
        self.sems[key] = s
        self.cnt[key] = 0

    def sb(self, shape, dt=F32, name=None):
        self.nbuf += 1
        name = (name or "sb") + "_%d" % self.nbuf
        g = self.nc.sbuf_tensor(name, list(shape), dt)
        t = g.__enter__()
        self.stack.append(g)
        return Buf(t, name)

    def ps(self, shape, dt=F32, name=None):
        self.nbuf += 1
        name = (name or "ps") + "_%d" % self.nbuf
        g = self.nc.psum_tensor(name, list(shape), dt)
        t = g.__enter__()
        self.stack.append(g)
        return Buf(t, name, psum=True)

    def dram(self, name, shape, dt=F32, kind="Internal"):
        t = self.nc.dram_tensor(name, list(shape), dt, kind=kind)
        return Buf(t, name)

    def mark(self):
        return len(self.stack)

    def release(self, mark):
        self.barrier()
        while len(self.stack) > mark:
            g = self.stack.pop()
            g.__exit__(None, None, None)

    def _need(self, eng, ev, waits):
        if ev is None:
            return
        key, val = ev
        if eng == "pe" and key.startswith("E_pe"):
            return
        if self.waited[eng].get(key, 0) >= val:
            return
        self.waited[eng][key] = val
        waits.append((key, val))

    def _deps(self, eng, reads, writes):
        waits = []
        for b in reads:
            self._need(eng, b.lw, waits)
            if b.psum:
                for ev in b.rd:
                    if not ev[0].startswith("E_" + eng):
                        self._need(eng, ev, waits)
        for b in writes:
            self._need(eng, b.lw, waits)
            for ev in b.rd:
                self._need(eng, ev, waits)
        return waits

    def _commit(self, ev, reads, writes):
        for b in reads:
            b.rd.append(ev)
            if len(b.rd) > 48:
                m = {}
                for k, v in b.rd:
                    if m.get(k, 0) < v:
                        m[k] = v
                b.rd = list(m.items())
        for b in writes:
            b.lw = ev
            b.rd = []

    def op(self, eng, fn, reads=(), writes=()):
        waits = self._deps(eng, reads, writes)
        key = self.ekey[eng]
        self.cnt[key] += 1
        ev = (key, self.cnt[key])
        self.ops[eng].append((waits, fn, (key, 1)))
        self._commit(ev, reads, writes)
        self.nops += 1
        if self.cnt[key] >= self.LIMIT:
            self.epoch[eng] += 1
            self.ekey[eng] = self._rotate(key, "E_%s#%d" % (eng, self.epoch[eng]))
        return ev

    def _rotate(self, old_key, new_key):
        final = self.cnt[old_key]
        for e in self.ENGS:
            waits = []
            self._need(e, (old_key, final), waits)
            if waits:
                self.ops[e].append((waits, None, None))
        self._mksem(new_key)
        return new_key

    def i(self, eng, name, *args, reads=(), writes=(), **kw):
        def fn(e, name=name, args=args, kw=kw, eng=eng):
            try:
                return getattr(e, name)(*args, **kw)
            except Exception as ex:
                raise RuntimeError("instr %s.%s failed: %s | args=%s kw=%s" % (eng, name, ex, [str(a)[:160] for a in args], kw)) from ex
        return self.op(eng, fn, reads, writes)

    def dma(self, q, out_ap, in_ap, reads=(), writes=(), indirect=None, **kw):
        i = self.dma_rr[q]
        self.dma_rr[q] = (i + 1) % self.ndma
        key = self.dkey[(q, i)]
        if self.cnt[key] >= self.LIMIT:
            self.nrot += 1
            key = self._rotate(key, "D_%s%d#%d" % (q, i, self.nrot))
            self.dkey[(q, i)] = key
        waits = self._deps(q, reads, writes)
        if self.cnt[key] > 0:
            self._need(q, (key, self.cnt[key]), waits)
        self.cnt[key] += 16
        ev = (key, self.cnt[key])
        if indirect is None:
            def fn(e, out_ap=out_ap, in_ap=in_ap, kw=kw):
                return e.dma_start(out=out_ap, in_=in_ap, **kw)
        else:
            def fn(e, out_ap=out_ap, in_ap=in_ap, idx=indirect):
                return e.indirect_dma_start(out=out_ap, out_offset=None, in_=in_ap,
                                            in_offset=bass.IndirectOffsetOnAxis(ap=idx, axis=0))
        self.ops[q].append((waits, fn, (key, 16)))
        self._commit(ev, reads, writes)
        self.nops += 1
        return ev

    def barrier(self):
        for e in self.ENGS:
            waits = []
            for key, c in self.cnt.items():
                if c > 0:
                    self._need(e, (key, c), waits)
            if waits:
                self.ops[e].append((waits, None, None))

    def finish(self):
        self.barrier()
        nc = self.nc
        hmap = {"pe": "tensor", "act": "scalar", "dve": "vector", "pool": "gpsimd", "sp": "sync"}
        with nc.Block() as block:
            for e in self.ENGS:
                oplist = self.ops[e]

                def body(h, oplist=oplist):
                    for waits, fn, inc in oplist:
                        for key, val in waits:
                            h.wait_ge(self.sems[key], val)
                        if fn is not None:
                            ins = fn(h)
                            ins.then_inc(self.sems[inc[0]], inc[1])
                getattr(block, hmap[e])(body)
        while self.stack:
            self.stack.pop().__exit__(None, None, None)
        for g in reversed(self.semguards):
            g.__exit__(None, None, None)


def chunks(t0, t1, sz=512):
    out = []
    t = t0
    while t < t1:
        n = min(sz, t1 - t)
        out.append((t, n))
        t += n
    return out


class K:
    pass


class StopNSA(Exception):
    pass


STOP = [None]
SKIP_PROMPT = [False]


def ck(name):
    if STOP[0] == name:
        raise StopNSA(name)


def build(n_phys, n_layers=4, mixers=True):
    nc = bass.Bass("TRN2", target_bir_lowering=False)
    P = Prog(nc)
    k = K()
    k.P = P
    k.nc = nc
    k.n_phys = n_phys
    EI, EO = "ExternalInput", "ExternalOutput"
    d = {}
    k.d = d
    d["xp"] = P.dram("xp", [LP, D], F32, EI)
    d["xs"] = P.dram("xs", [NS * DS, D], F32, EI)
    d["state_gdn"] = P.dram("state_gdn", [2, NS, 8, 128, 128], F32, EI)
    d["state_gdn_conv"] = P.dram("state_gdn_conv", [2, NS, 3, 3072], F32, EI)
    d["cache_cmp"] = P.dram("cache_cmp", [2, n_phys * 128, 512], F32, EI)
    d["cache_sel"] = P.dram("cache_sel", [2, n_phys * 128, 512], F32, EI)
    d["state_win"] = P.dram("state_win", [2, NS, 512, 512], F32, EI)
    d["state_ffn_conv"] = P.dram("state_ffn_conv", [4, NS, 2, 2 * DFF], F32, EI)
    d["page_table"] = P.dram("page_table", [NS, NPG], I32, EI)
    d["norm_mix"] = P.dram("norm_mix", [4, D], F32, EI)
    d["norm_ffn"] = P.dram("norm_ffn", [4, D], F32, EI)
    d["norm_final"] = P.dram("norm_final", [D], F32, EI)
    d["gdn_w_in"] = P.dram("gdn_w_in", [2, D, GW], F32, EI)
    d["gdn_conv_w"] = P.dram("gdn_conv_w", [2, 4, 3072], F32, EI)
    d["gdn_a_log"] = P.dram("gdn_a_log", [2, 8], F32, EI)
    d["gdn_dt_bias"] = P.dram("gdn_dt_bias", [2, 8], F32, EI)
    d["gdn_norm_w"] = P.dram("gdn_norm_w", [2, 128], F32, EI)
    d["gdn_w_out"] = P.dram("gdn_w_out", [2, D, D], F32, EI)
    d["nsa_w_in"] = P.dram("nsa_w_in", [2, D, NW], F32, EI)
    d["nsa_cmp_pe"] = P.dram("nsa_cmp_pe", [2, 64, 2, 64], F32, EI)
    d["nsa_cmp_w"] = P.dram("nsa_cmp_w", [2, 64, 2, 64, 64], F32, EI)
    d["nsa_w_out"] = P.dram("nsa_w_out", [2, D, D], F32, EI)
    d["ffn_w_up"] = P.dram("ffn_w_up", [4, D, 2 * DFF], F32, EI)
    d["ffn_conv_w"] = P.dram("ffn_conv_w", [4, 3, 2 * DFF], F32, EI)
    d["ffn_conv_b"] = P.dram("ffn_conv_b", [4, 2 * DFF], F32, EI)
    d["ffn_w_down"] = P.dram("ffn_w_down", [4, DFF, D], F32, EI)
    d["rope_tab"] = P.dram("rope_tab", [17, 128, 64], F32, EI)
    d["y_p"] = P.dram("y_p", [LP, D], F32, EO)
    d["y_s"] = P.dram("y_s", [NS * DS, D], F32, EO)
    d["gdn_p"] = P.dram("gdn_p", [2, 8, 128, 128], F32, EO)
    d["gdn_s"] = P.dram("gdn_s", [2, NS, 8, 128, 128], F32, EO)
    d["gconv_p"] = P.dram("gconv_p", [2, 3, 3072], F32, EO)
    d["gconv_s"] = P.dram("gconv_s", [2, NS, 3, 3072], F32, EO)
    d["cmp_p"] = P.dram("cmp_p", [2, LP, 512], F32, EO)
    d["cmp_s"] = P.dram("cmp_s", [2, NS * DS, 512], F32, EO)
    d["sel_p"] = P.dram("sel_p", [2, LP, 512], F32, EO)
    d["sel_s"] = P.dram("sel_s", [2, NS * DS, 512], F32, EO)
    d["win_p"] = P.dram("win_p", [2, 512, 512], F32, EO)
    d["win_s"] = P.dram("win_s", [2, NS, 512, 512], F32, EO)
    d["ffn_p"] = P.dram("ffn_p", [4, 2, 2 * DFF], F32, EO)
    d["ffn_s"] = P.dram("ffn_s", [4, NS, 2, 2 * DFF], F32, EO)

    k.R = [P.sb([128, NT], F32, "R%d" % i) for i in range(8)]
    k.ident = P.sb([128, 128], F32, "ident")
    k.identb = P.sb([128, 128], BF16, "identb")
    k.ones = P.sb([128, 128], F32, "ones")
    k.ncol = P.sb([128, 72], F32, "ncol")
    k.PS = [P.ps([128, 512], F32, "psb%d" % i) for i in range(8)]
    k.stg = P.sb([128, 128], F32, "stg")

    P.i("pool", "memset", k.ident[:], 0.0, writes=[k.ident])
    P.i("pool", "affine_select", out=k.ident[:], in_=k.ident[:], pattern=[[-1, 128]],
                                            compare_op=ALU.not_equal, fill=1.0, base=0, channel_multiplier=1,
         reads=[k.ident], writes=[k.ident])
    P.i("dve", "tensor_copy", k.identb[:], k.ident[:], reads=[k.ident], writes=[k.identb])
    P.i("pool", "memset", k.ones[:], 1.0, writes=[k.ones])


    k.ltri = P.sb([64, 64], F32, "ltri")
    k.msl = P.sb([64, 64], F32, "msl")
    k.e63 = P.sb([64, 128], F32, "e63")
    k.pm4 = P.sb([64, 1], F32, "pm4")
    P.i("pool", "memset", k.ltri[:], 1.0, writes=[k.ltri])
    P.i("pool", "affine_select", out=k.ltri[:], in_=k.ltri[:], pattern=[[1, 64]], compare_op=ALU.is_ge, fill=0.0, base=0,
        channel_multiplier=-1, reads=[k.ltri], writes=[k.ltri])
    P.i("pool", "memset", k.msl[:], 1.0, writes=[k.msl])
    P.i("pool", "affine_select", out=k.msl[:], in_=k.msl[:], pattern=[[-1, 64]], compare_op=ALU.is_ge, fill=0.0, base=-1,
        channel_multiplier=1, reads=[k.msl], writes=[k.msl])
    P.i("pool", "memset", k.e63[:], 0.0, writes=[k.e63])
    P.i("pool", "affine_select", out=k.e63[:], in_=k.e63[:], pattern=[[0, 128]], compare_op=ALU.not_equal, fill=1.0, base=-63,
        channel_multiplier=1, reads=[k.e63], writes=[k.e63])
    P.i("pool", "memset", k.pm4[:], 1.0, writes=[k.pm4])
    P.i("pool", "affine_select", out=k.pm4[:], in_=k.pm4[:], pattern=[[0, 1]], compare_op=ALU.is_ge, fill=0.0, base=3,
        channel_multiplier=-1, reads=[k.pm4], writes=[k.pm4])


    k.caus = P.sb([128, 128], BF16, "caus")
    k.wmask = P.sb([128, 128], BF16, "wmask")
    P.i("pool", "memset", k.caus[:], 1.0, writes=[k.caus])
    P.i("pool", "affine_select", out=k.caus[:], in_=k.caus[:], pattern=[[1, 128]], compare_op=ALU.is_ge, fill=0.0, base=0,
        channel_multiplier=-1, reads=[k.caus], writes=[k.caus])
    P.i("pool", "memset", k.wmask[:], 1.0, writes=[k.wmask])
    P.i("pool", "affine_select", out=k.wmask[:], in_=k.wmask[:], pattern=[[-1, 128]], compare_op=ALU.is_gt, fill=0.0, base=0,
        channel_multiplier=1, reads=[k.wmask], writes=[k.wmask])

    load_cols(k, d["norm_mix"], d["norm_mix"].ap().rearrange("l (t p) -> (l t) p", p=128), 32, k.ncol, 0)
    load_cols(k, d["norm_ffn"], d["norm_ffn"].ap().rearrange("l (t p) -> (l t) p", p=128), 32, k.ncol, 32)
    load_cols(k, d["norm_final"], d["norm_final"].ap().rearrange("(t p) -> t p", p=128), 8, k.ncol, 64)

    load_x(k)
    for l in range(n_layers):
        if mixers:
            if l % 2 == 0:
                gdn_layer(k, l)
            else:
                nsa_layer(k, l)
        ffn_layer(k, l)
    final_out(k)
    P.finish()
    return nc


def load_cols(k, src_buf, src2d, nrows, dst, col0, ps=None):
    P = k.P
    ps = ps or k.PS[6]
    r0 = 0
    while r0 < nrows:
        n = min(128, nrows - r0)
        P.dma("sp", k.stg[0:n, :], src2d[r0:r0 + n, :], reads=[src_buf], writes=[k.stg])
        P.i("pe", "transpose", ps[:, 0:n], k.stg[0:n, :], k.ident[0:n, 0:n],
             reads=[k.stg, k.ident], writes=[ps])
        P.i("dve", "tensor_copy", dst[:, col0 + r0:col0 + r0 + n], ps[:, 0:n], reads=[ps], writes=[dst])
        r0 += n


def load_x(k):
    P = k.P
    d = k.d
    m = P.mark()
    xt = [P.sb([128, 4, D], F32, "xt%d" % i) for i in range(2)]
    for g in range(4):
        b = xt[g % 2]
        P.dma("sp", b[:], d["xp"].ap()[g * 512:(g + 1) * 512, :].rearrange("(a p) c -> p a c", p=128),
              reads=[d["xp"]], writes=[b])
        for dt in range(8):
            ps = k.PS[dt % 4]
            for a in range(4):
                P.i("pe", "transpose", ps[:, a * 128:(a + 1) * 128], b[:, a, dt * 128:(dt + 1) * 128], k.ident[:],
                     reads=[b, k.ident], writes=[ps])
            eng = "dve" if dt % 2 == 0 else "act"
            if eng == "dve":
                P.i("dve", "tensor_copy", k.R[dt][:, g * 512:(g + 1) * 512], ps[:], reads=[ps], writes=[k.R[dt]])
            else:
                P.i("act", "copy", k.R[dt][:, g * 512:(g + 1) * 512], ps[:], reads=[ps], writes=[k.R[dt]])
    b = xt[0]
    P.dma("sp", b[0:16, 0, :], d["xs"].ap(), reads=[d["xs"]], writes=[b])
    ps = k.PS[0]
    for dt in range(8):
        P.i("pe", "transpose", ps[:, dt * 16:(dt + 1) * 16], b[0:16, 0, dt * 128:(dt + 1) * 128], k.ident[0:16, 0:16],
             reads=[b, k.ident], writes=[ps])
    for dt in range(8):
        P.i("dve", "tensor_copy", k.R[dt][:, LP:NT], ps[:, dt * 16:(dt + 1) * 16], reads=[ps], writes=[k.R[dt]])
    P.release(m)


def rmsnorm(k, wcol0, t0, n, out_tiles, o0, scratch):
    P = k.P
    for (c0, cn) in chunks(t0, t0 + n):
        ps = k.PS[6]
        for dt in range(8):
            sq = scratch["sq"][dt % 2]
            P.i("act", "activation", sq[:, 0:cn], k.R[dt][:, c0:c0 + cn], AF.Square,
                 reads=[k.R[dt]], writes=[sq])
            P.i("pe", "matmul", ps[:, 0:cn], k.ones[:], sq[:, 0:cn], start=(dt == 0), stop=(dt == 7),
                 reads=[sq, k.ones], writes=[ps])
        rstd = scratch["rstd"]
        P.i("dve", "tensor_scalar", rstd[:, 0:cn], ps[:, 0:cn], 1.0 / D, EPS, ALU.mult, ALU.add, reads=[ps], writes=[rstd])
        P.i("act", "activation", rstd[:, 0:cn], rstd[:, 0:cn], AF.Sqrt, reads=[rstd], writes=[rstd])
        P.i("dve", "reciprocal", rstd[:, 0:cn], rstd[:, 0:cn], reads=[rstd], writes=[rstd])
        for dt in range(8):
            eng = "dve"
            P.i(eng, "scalar_tensor_tensor",
                out_tiles[dt][:, o0 + c0 - t0:o0 + c0 - t0 + cn], k.R[dt][:, c0:c0 + cn],
                k.ncol[:, wcol0 + dt:wcol0 + dt + 1], rstd[:, 0:cn], ALU.mult, ALU.mult,
                reads=[k.R[dt], k.ncol, rstd], writes=[out_tiles[dt]])


def ffn_layer(k, l):
    P = k.P
    d = k.d
    m = P.mark()
    NH = 1042
    hT = [P.sb([128, NH], BF16, "fh%d" % i) for i in range(8)]
    act = [P.sb([128, 1040], BF16, "fa%d" % i) for i in range(22)]
    wup = [P.sb([128, 8, 512], BF16, "wup%d" % i) for i in range(2)]
    wdn = [P.sb([128, 22, 128], BF16, "wdn%d" % i) for i in range(2)]
    ub = [[P.sb([128, 1056], F32, "ub%d%d" % (i, j)) for j in range(2)] for i in range(2)]
    cb = [P.sb([128, 1040], F32, "cb%d" % i) for i in range(2)]
    scratch = {"sq": [P.sb([128, 512], F32, "sq%d" % i) for i in range(2)], "rstd": P.sb([128, 512], F32, "rstd")}
    fpar = P.sb([128, 176], F32, "fpar")
    hist = P.sb([128, 352], F32, "hist")
    hsel = P.sb([128, 8, 16], BF16, "hsel")
    halo = P.sb([128, 88], F32, "halo")
    strow = P.sb([16, 2 * DFF], F32, "strow") if False else None
    st_sb = P.sb([16, 512], F32, "stsb")

    for j in range(3):
        load_cols(k, d["ffn_conv_w"], d["ffn_conv_w"].ap()[l, j].rearrange("(t p) -> t p", p=128), 44, fpar, j * 44)
    load_cols(k, d["ffn_conv_b"], d["ffn_conv_b"].ap()[l].rearrange("(t p) -> t p", p=128), 44, fpar, 132)
    load_cols(k, d["state_ffn_conv"], d["state_ffn_conv"].ap()[l].rearrange("s j (t p) -> (s j t) p", p=128), 352, hist, 0)

    wdma = [0]

    def load_wup(jb, buf):
        for which in range(2):
            c0 = which * DFF + jb * 256
            P.dma("pool", buf[:, :, which * 256:(which + 1) * 256],
                  d["ffn_w_up"].ap()[l, :, c0:c0 + 256].rearrange("(kt p) c -> p kt c", p=128),
                  reads=[d["ffn_w_up"]], writes=[buf])

    def load_wdn(dt, buf):
        P.dma("pool", buf[:], d["ffn_w_down"].ap()[l, :, dt * 128:(dt + 1) * 128].rearrange("(j p) c -> p j c", p=128),
              reads=[d["ffn_w_down"]], writes=[buf])

    psi = [0]
    for half in range(2):
        if half == 0:
            t0, n = 0, 1024
            npr = 1024
            ucol0 = 2
        else:
            t0, n = 1024, 1040
            npr = 1024
            ucol0 = 2
        rmsnorm(k, 32 + l * 8, t0, n, hT, 0, scratch)
        nprm = n - (16 if half == 1 else 0)
        if half == 1:
            for kt in range(8):
                P.i("pool", "tensor_copy", hsel[:, kt, 0:2], hT[kt][:, 1022:1024], reads=[hT[kt]], writes=[hsel])
                P.i("pool", "tensor_copy",
                    hsel[:, kt, 2:10].rearrange("p (s c) -> p s c", c=2),
                    hT[kt][:, 1024:1040].rearrange("p (s c) -> p s c", c=4)[:, :, 2:4], reads=[hT[kt]], writes=[hsel])
        for jb in range(11):
            wb = wup[jb % 2]
            load_wup(jb, wb)
            if half == 1:
                pst = k.PS[7]
                for which in range(2):
                    for kt in range(8):
                        P.i("pe", "matmul",
                            pst[0:10, which * 256:(which + 1) * 256], hsel[:, kt, 0:10], wb[:, kt, which * 256:(which + 1) * 256],
                            start=(kt == 0), stop=(kt == 7), reads=[hsel, wb], writes=[pst])
                P.i("act", "copy", st_sb[0:10, :], pst[0:10, :], reads=[pst], writes=[st_sb])
                for which in range(2):
                    c0 = which * DFF + jb * 256
                    P.dma("sp", d["ffn_p"].ap()[l, :, c0:c0 + 256], st_sb[0:2, which * 256:(which + 1) * 256], reads=[st_sb], writes=[d["ffn_p"]])
                    P.dma("sp", d["ffn_s"].ap()[l, :, :, c0:c0 + 256].rearrange("s j c -> (s j) c"), st_sb[2:10, which * 256:(which + 1) * 256], reads=[st_sb], writes=[d["ffn_s"]])
            for jj in range(2):
                j = jb * 2 + jj
                par = j % 2
                for which in range(2):
                    u = ub[par][which]
                    tix = j + 22 * which
                    if half == 0:
                        P.i("pool", "memset", u[:, 0:2], 0.0, writes=[u])
                    else:
                        P.i("pool", "tensor_copy", u[:, 0:2], halo[:, tix * 2:tix * 2 + 2], reads=[halo], writes=[u])
                        P.i("pool", "tensor_copy",
                            u[:, 1026:1050].rearrange("p (s c) -> p s c", c=6)[:, :, 0:2],
                            hist[:, :].rearrange("p (s j t) -> p s j t", j=2, t=44)[:, :, :, tix], reads=[hist], writes=[u])
                    for (c0, cn) in chunks(0, nprm):
                        ps = k.PS[psi[0] % 4]
                        psi[0] += 1
                        for kt in range(8):
                            P.i("pe", "matmul",
                                ps[:, 0:cn], wb[:, kt, which * 256 + jj * 128:which * 256 + (jj + 1) * 128], hT[kt][:, c0:c0 + cn],
                                start=(kt == 0), stop=(kt == 7), reads=[wb, hT[kt]], writes=[ps])
                        P.i("act", "copy", u[:, ucol0 + c0:ucol0 + c0 + cn], ps[:, 0:cn], reads=[ps], writes=[u])
                    if half == 1:
                        ps = k.PS[psi[0] % 4]
                        psi[0] += 1
                        for kt in range(8):
                            P.i("pe", "matmul",
                                ps[:, 0:16], wb[:, kt, which * 256 + jj * 128:which * 256 + (jj + 1) * 128], hT[kt][:, 1024:1040],
                                start=(kt == 0), stop=(kt == 7), reads=[wb, hT[kt]], writes=[ps])
                        P.i("act", "copy",
                            u[:, 1026:1050].rearrange("p (s c) -> p s c", c=6)[:, :, 2:6],
                            ps[:, 0:16].rearrange("p (s c) -> p s c", c=4), reads=[ps], writes=[u])
                    if half == 0:
                        P.i("pool", "tensor_copy", halo[:, tix * 2:tix * 2 + 2], u[:, 1024:1026], reads=[u], writes=[halo])
                    c = cb[which]
                    w0 = fpar[:, tix:tix + 1]
                    w1 = fpar[:, 44 + tix:44 + tix + 1]
                    w2 = fpar[:, 88 + tix:88 + tix + 1]
                    bb = fpar[:, 132 + tix:132 + tix + 1]
                    eng = "dve"
                    regions = [(lambda a, off: a[:, off:off + npr], lambda a: a[:, 0:npr])]
                    if half == 1:
                        regions.append((lambda a, off: a[:, 1026:1050].rearrange("p (s c) -> p s c", c=6)[:, :, off:off + 4],
                                        lambda a: a[:, 1024:1040].rearrange("p (s c) -> p s c", c=4)))
                    for (uin, cout) in regions:
                        P.i(eng, "tensor_scalar", cout(c), uin(u, 2), w2, bb, ALU.mult, ALU.add,
                             reads=[u, fpar], writes=[c])
                        P.i(eng, "scalar_tensor_tensor", cout(c), uin(u, 1), w1, cout(c), ALU.mult, ALU.add,
                             reads=[u, fpar, c], writes=[c])
                        P.i(eng, "scalar_tensor_tensor", cout(c), uin(u, 0), w0, cout(c), ALU.mult, ALU.add,
                             reads=[u, fpar, c], writes=[c])
                ntok = npr + (16 if half == 1 else 0)
                P.i("act", "activation", cb[0][:, 0:ntok], cb[0][:, 0:ntok], AF.Silu, reads=[cb[0]], writes=[cb[0]])
                P.i("dve", "tensor_tensor", act[j][:, 0:ntok], cb[0][:, 0:ntok], cb[1][:, 0:ntok], ALU.mult,
                     reads=[cb[0], cb[1]], writes=[act[j]])
        ntok = npr + (16 if half == 1 else 0)
        tok0 = 0 if half == 0 else 1024
        for dt in range(8):
            wd = wdn[dt % 2]
            load_wdn(dt, wd)
            for (c0, cn) in chunks(0, ntok):
                ps = k.PS[4 + (psi[0] % 2)]
                psi[0] += 1
                for j in range(22):
                    P.i("pe", "matmul", ps[:, 0:cn], wd[:, j, :], act[j][:, c0:c0 + cn], start=(j == 0), stop=(j == 21),
                         reads=[wd, act[j]], writes=[ps])
                P.i("dve", "tensor_tensor",
                    k.R[dt][:, tok0 + c0:tok0 + c0 + cn], ps[:, 0:cn], k.R[dt][:, tok0 + c0:tok0 + c0 + cn], ALU.add,
                    reads=[ps, k.R[dt]], writes=[k.R[dt]])
    P.release(m)


def final_out(k):
    P = k.P
    d = k.d
    m = P.mark()
    yt = [P.sb([128, 4, D], F32, "yt%d" % i) for i in range(2)]
    hn = [P.sb([128, 528], F32, "hn%d" % i) for i in range(8)]
    scratch = {"sq": [P.sb([128, 512], F32, "sq%d" % i) for i in range(2)], "rstd": P.sb([128, 512], F32, "rstd")}
    for g, (c0, cn) in enumerate(chunks(0, NT)):
        rmsnorm(k, 64, c0, cn, hn, 0, scratch)
        b = yt[g % 2]
        na = (cn + 127) // 128
        for a in range(na):
            tn = min(128, cn - a * 128)
            for dt in range(8):
                ps = k.PS[(a * 8 + dt) // 4 % 4]
                q = dt % 4
                P.i("pe", "transpose", ps[0:tn, q * 128:(q + 1) * 128], hn[dt][:, a * 128:a * 128 + tn], k.ident[:],
                     reads=[hn[dt], k.ident], writes=[ps])
                if q == 3:
                    h4 = dt // 4
                    eng = "dve" if h4 == 0 else "act"
                    if eng == "dve":
                        P.i("dve", "tensor_copy", b[0:tn, a, h4 * 512:(h4 + 1) * 512], ps[0:tn, :], reads=[ps], writes=[b])
                    else:
                        P.i("act", "copy", b[0:tn, a, h4 * 512:(h4 + 1) * 512], ps[0:tn, :], reads=[ps], writes=[b])
        if c0 < LP:
            P.dma("sp", d["y_p"].ap()[c0:c0 + cn, :].rearrange("(a p) c -> p a c", p=128), b[:], reads=[b], writes=[d["y_p"]])
        else:
            P.dma("sp", d["y_s"].ap(), b[0:16, 0, :], reads=[b], writes=[d["y_s"]])
    P.release(m)


def gdn_layer(k, l):
    P = k.P
    d = k.d
    j = l // 2
    m = P.mark()
    TG = 2304
    hT = [P.sb([128, NT], BF16, "gh%d" % i) for i in range(8)]
    scratch = {"sq": [P.sb([128, 512], F32, "sq%d" % i) for i in range(2)], "rstd": P.sb([128, 512], F32, "rstd")}
    sq, rstd = scratch["sq"], scratch["rstd"]
    rmsnorm(k, l * 8, 0, NT, hT, 0, scratch)
    PS = k.PS
    gcw = P.sb([128, 96], F32, "gcw")
    load_cols(k, d["gdn_conv_w"], d["gdn_conv_w"].ap()[j].rearrange("r (t p) -> (r t) p", p=128), 96, gcw, 0)
    gnw = P.sb([128, 1], F32, "gnw")
    load_cols(k, d["gdn_norm_w"], d["gdn_norm_w"].ap()[j:j + 1, :], 1, gnw, 0)
    wab = P.sb([128, 8, 16], BF16, "wab")
    P.dma("pool", wab[:], d["gdn_w_in"].ap()[j, :, 4096:4112].rearrange("(kt p) c -> p kt c", p=128), reads=[d["gdn_w_in"]], writes=[wab])
    hsp = P.sb([128, 8, 4, 64], BF16, "hsp")
    P.i("pool", "memset", hsp[:], 0.0, writes=[hsp])
    hsel = P.sb([128, 8, 16], BF16, "ghsel")
    for kt in range(8):
        P.i("pool", "tensor_copy", hsp[:, kt, :, 0:4], hT[kt][:, LP:NT].rearrange("p (s c) -> p s c", c=4), reads=[hT[kt]], writes=[hsp])
        P.i("dve", "tensor_copy", hsel[:, kt, 0:3], hT[kt][:, LP - 3:LP], reads=[hT[kt]], writes=[hsel])
        P.i("dve", "tensor_copy", hsel[:, kt, 3:15].rearrange("p (s c) -> p s c", c=3),
            hT[kt][:, LP:NT].rearrange("p (s c) -> p s c", c=4)[:, :, 1:4], reads=[hT[kt]], writes=[hsel])
    ab_all = P.sb([64, 36, 16], F32, "ab_all")
    for n in range(36):
        ps = PS[0] if n < 32 else PS[1]
        c0 = (n % 32) * 16
        for kt in range(8):
            lhsT = hT[kt][:, n * 64:(n + 1) * 64] if n < 32 else hsp[:, kt, n - 32, :]
            P.i("pe", "matmul", ps[0:64, c0:c0 + 16], lhsT, wab[:, kt, :], start=(kt == 0), stop=(kt == 7),
                reads=[hT[kt], hsp, wab], writes=[ps])
    P.i("dve", "tensor_copy", ab_all[:, 0:32, :], PS[0][0:64, 0:512].rearrange("p (n c) -> p n c", c=16), reads=[PS[0]], writes=[ab_all])
    P.i("dve", "tensor_copy", ab_all[:, 32:36, :], PS[1][0:64, 0:64].rearrange("p (n c) -> p n c", c=16), reads=[PS[1]], writes=[ab_all])
    alog = P.sb([64, 8], F32, "alog")
    dtb = P.sb([64, 8], F32, "dtb")
    P.dma("sp", alog[:], d["gdn_a_log"].ap()[j].partition_broadcast(64), reads=[d["gdn_a_log"]], writes=[alog])
    P.dma("sp", dtb[:], d["gdn_dt_bias"].ap()[j].partition_broadcast(64), reads=[d["gdn_dt_bias"]], writes=[dtb])
    P.i("act", "activation", alog[:], alog[:], AF.Exp, reads=[alog], writes=[alog])
    P.i("dve", "tensor_scalar_mul", alog[:], alog[:], -1.0, reads=[alog], writes=[alog])
    g_all = P.sb([64, 36, 8], F32, "g_all")
    beta = P.sb([64, 36, 8], F32, "beta")
    bc8 = lambda t: t[:, :].unsqueeze(1).to_broadcast([64, 36, 8])
    P.i("dve", "tensor_tensor", g_all[:], ab_all[:, :, 0:8], bc8(dtb), ALU.add, reads=[ab_all, dtb], writes=[g_all])
    P.i("act", "activation", g_all[:], g_all[:], AF.Exp, reads=[g_all], writes=[g_all])
    P.i("dve", "tensor_scalar_add", g_all[:], g_all[:], 1.0, reads=[g_all], writes=[g_all])
    P.i("act", "activation", g_all[:], g_all[:], AF.Ln, reads=[g_all], writes=[g_all])
    P.i("dve", "tensor_tensor", g_all[:], g_all[:], bc8(alog), ALU.mult, reads=[g_all, alog], writes=[g_all])
    P.i("act", "activation", beta[:], ab_all[:, :, 8:16], AF.Sigmoid, reads=[ab_all], writes=[beta])
    P.i("dve", "tensor_scalar_mul", g_all[:, 32:36, :], g_all[:, 32:36, :], k.pm4[:, 0:1], reads=[g_all, k.pm4], writes=[g_all])
    P.i("dve", "tensor_scalar_mul", beta[:, 32:36, :], beta[:, 32:36, :], k.pm4[:, 0:1], reads=[beta, k.pm4], writes=[beta])
    fl = lambda t: t[:].rearrange("p n h -> p (n h)")
    Gc = P.sb([64, 36, 8], F32, "Gc")
    P.i("pe", "matmul", PS[0][0:64, 0:288], k.ltri[:], fl(g_all), start=True, stop=True, reads=[k.ltri, g_all], writes=[PS[0]])
    P.i("dve", "tensor_copy", fl(Gc), PS[0][0:64, 0:288], reads=[PS[0]], writes=[Gc])
    P.i("pe", "matmul", PS[1][:, 0:288], k.e63[:], fl(Gc), start=True, stop=True, reads=[k.e63, Gc], writes=[PS[1]])
    egl = P.sb([128, 288], F32, "egl")
    P.i("act", "activation", egl[:], PS[1][:, 0:288], AF.Exp, reads=[PS[1]], writes=[egl])
    edec = P.sb([64, 36, 8], F32, "edec")
    P.i("dve", "tensor_tensor", fl(edec), PS[1][0:64, 0:288], fl(Gc), ALU.subtract, reads=[PS[1], Gc], writes=[edec])
    P.i("act", "activation", edec[:], edec[:], AF.Exp, reads=[edec], writes=[edec])
    ebg = P.sb([64, 36, 8], F32, "ebg")
    P.i("act", "activation", ebg[:], Gc[:], AF.Exp, reads=[Gc], writes=[ebg])
    P.i("dve", "tensor_tensor", ebg[:], ebg[:], beta[:], ALU.mult, reads=[ebg, beta], writes=[ebg])
    nbeta = P.sb([64, 36, 8], F32, "nbeta")
    P.i("dve", "tensor_scalar_mul", nbeta[:], beta[:], -1.0, reads=[beta], writes=[nbeta])

    wqkvz = [P.sb([128, 8, 128], BF16, "gw%d" % i) for i in range(4)]
    wout = P.sb([128, D], BF16, "gwout")
    xb = P.sb([128, 2080], F32, "gxb")
    X = [P.sb([128, TG], BF16, "gX%d" % i) for i in range(3)]
    for t in X:
        P.i("pool", "memset", t[:, LP:TG], 0.0, writes=[t])
    qgb = P.sb([128, TG], BF16, "qgb")
    zsb = P.sb([128, NT], BF16, "zsb")
    ogb = P.sb([128, NT], BF16, "ogb")
    cf = P.sb([128, 512], F32, "gcf")
    hist12 = P.sb([128, 12], F32, "hist12")
    st_sb = P.sb([16, 128], F32, "gst")
    G = {}
    for nm in ["negD", "decL", "decU", "tmp", "Q0", "Q1", "R0", "R1", "Tt", "egrow"]:
        G[nm] = P.sb([64 if nm != "egrow" else 128, 512], F32, "g" + nm)
    DG = P.sb([64, 512], F32, "gDG")
    Ttb = P.sb([64, 512], BF16, "gTtb")
    kb = P.sb([64, 8, 128], BF16, "gkb")
    kdec = P.sb([64, 8, 128], BF16, "gkdec")
    vb = P.sb([64, 8, 128], BF16, "gvb")
    nwk = P.sb([128, 512], BF16, "gnwk")
    qkm = P.sb([64, 512], BF16, "gqkm")
    S = P.sb([128, 128], F32, "gS")
    Sb = P.sb([128, 128], BF16, "gSb")
    ub = P.sb([64, 128], BF16, "gub")
    on = P.sb([128, 512], F32, "gon")
    i64b = k.ident[0:64, 0:64].unsqueeze(1).to_broadcast([64, 8, 64])
    v3 = lambda ap: ap.rearrange("p (n c) -> p n c", c=64)

    for h in range(8):
        for part in range(4):
            c0 = part * 1024 + h * 128
            P.dma("pool", wqkvz[part][:], d["gdn_w_in"].ap()[j, :, c0:c0 + 128].rearrange("(kt p) c -> p kt c", p=128),
                  reads=[d["gdn_w_in"]], writes=[wqkvz[part]])
        P.dma("pool", wout[:], d["gdn_w_out"].ap()[j, h * 128:(h + 1) * 128, :], reads=[d["gdn_w_out"]], writes=[wout])
        for part in range(4):
            w = wqkvz[part]
            col0 = part * 1024 + h * 128
            tix = part * 8 + h
            if part < 3:
                pst = PS[7]
                for kt in range(8):
                    P.i("pe", "matmul", pst[0:15, 256:384], hsel[:, kt, 0:15], w[:, kt, :], start=(kt == 0), stop=(kt == 7),
                        reads=[hsel, w], writes=[pst])
                P.i("act", "copy", st_sb[0:15, :], pst[0:15, 256:384], reads=[pst], writes=[st_sb])
                P.dma("sp", d["gconv_p"].ap()[j, :, col0:col0 + 128], st_sb[0:3, :], reads=[st_sb], writes=[d["gconv_p"]])
                P.dma("sp", d["gconv_s"].ap()[j, :, :, col0:col0 + 128].rearrange("s r c -> (s r) c"), st_sb[3:15, :],
                      reads=[st_sb], writes=[d["gconv_s"]])
                load_cols(k, d["state_gdn_conv"], d["state_gdn_conv"].ap()[j, :, :, col0:col0 + 128].rearrange("s r c -> (s r) c"),
                          12, hist12, 0)
                P.i("pool", "memset", xb[:, 0:3], 0.0, writes=[xb])
                P.i("pool", "tensor_copy", xb[:, 2051:2079].rearrange("p (s c) -> p s c", c=7)[:, :, 0:3],
                    hist12[:, :].rearrange("p (s c) -> p s c", c=3), reads=[hist12], writes=[xb])
            for ci, (c0, cn) in enumerate(chunks(0, NT)):
                ps = PS[ci % 2]
                for kt in range(8):
                    P.i("pe", "matmul", ps[:, 0:cn], w[:, kt, :], hT[kt][:, c0:c0 + cn], start=(kt == 0), stop=(kt == 7),
                        reads=[w, hT[kt]], writes=[ps])
                if part == 3:
                    P.i("act", "activation", zsb[:, c0:c0 + cn], ps[:, 0:cn], AF.Silu, reads=[ps], writes=[zsb])
                elif c0 < LP:
                    P.i("act", "copy", xb[:, 3 + c0:3 + c0 + cn], ps[:, 0:cn], reads=[ps], writes=[xb])
                else:
                    P.i("act", "copy", xb[:, 2051:2079].rearrange("p (s c) -> p s c", c=7)[:, :, 3:7],
                        ps[:, 0:16].rearrange("p (s c) -> p s c", c=4), reads=[ps], writes=[xb])
            if part == 3:
                continue
            wc = [gcw[:, r * 24 + tix:r * 24 + tix + 1] for r in range(4)]
            for (c0, cn) in chunks(0, NT):
                if c0 < LP:
                    src = lambda off: xb[:, c0 + off:c0 + off + cn]
                    cfv = cf[:, 0:cn]
                    dst = X[part][:, c0:c0 + cn]
                else:
                    src = lambda off: xb[:, 2051:2079].rearrange("p (s c) -> p s c", c=7)[:, :, off:off + 4]
                    cfv = cf[:, 0:16].rearrange("p (s c) -> p s c", c=4)
                    dst = X[part][:, LP:TG].rearrange("p (s c) -> p s c", c=64)[:, :, 0:4]
                P.i("dve", "tensor_scalar_mul", cfv, src(3), wc[3], reads=[xb, gcw], writes=[cf])
                for r in range(3):
                    P.i("dve", "scalar_tensor_tensor", cfv, src(r), wc[r], cfv, ALU.mult, ALU.add, reads=[xb, gcw, cf], writes=[cf])
                P.i("act", "activation", cf[:, 0:cn], cf[:, 0:cn], AF.Silu, reads=[cf], writes=[cf])
                if part == 2:
                    P.i("act", "copy", dst, cfv, reads=[cf], writes=[X[part]])
                else:
                    sqb = sq[0]
                    P.i("act", "activation", sqb[:, 0:cn], cf[:, 0:cn], AF.Square, reads=[cf], writes=[sqb])
                    P.i("pe", "matmul", PS[6][:, 0:cn], k.ones[:], sqb[:, 0:cn], start=True, stop=True, reads=[sqb, k.ones], writes=[PS[6]])
                    P.i("dve", "tensor_scalar_add", rstd[:, 0:cn], PS[6][:, 0:cn], EPS, reads=[PS[6]], writes=[rstd])
                    P.i("act", "activation", rstd[:, 0:cn], rstd[:, 0:cn], AF.Sqrt, reads=[rstd], writes=[rstd])
                    P.i("dve", "reciprocal", rstd[:, 0:cn], rstd[:, 0:cn], reads=[rstd], writes=[rstd])
                    rv = rstd[:, 0:cn] if c0 < LP else rstd[:, 0:16].rearrange("p (s c) -> p s c", c=4)
                    P.i("dve", "scalar_tensor_tensor", dst, cfv, (128.0 ** -0.5) if part == 0 else 1.0, rv, ALU.mult, ALU.mult,
                        reads=[cf, rstd], writes=[X[part]])
        qn, kn, vn = X
        for g in range(5):
            nch = 8 if g < 4 else 4
            W = nch * 64
            n0 = g * 8
            gcol = lambda t: t[:, n0:n0 + nch, h].unsqueeze(2)
            P.i("dve", "tensor_tensor", v3(DG[:, 0:W]), i64b[:, 0:nch, :], gcol(Gc).to_broadcast([64, nch, 64]), ALU.mult,
                reads=[k.ident, Gc], writes=[DG])
            P.i("pe", "matmul", PS[6][:, 0:W], k.ones[0:64, :], DG[:, 0:W], start=True, stop=True, reads=[k.ones, DG], writes=[PS[6]])
            P.i("act", "activation", G["egrow"][:, 0:W], PS[6][:, 0:W], AF.Exp, reads=[PS[6]], writes=[G["egrow"]])
            P.i("dve", "tensor_tensor", qgb[:, g * 512:g * 512 + W], qn[:, g * 512:g * 512 + W], G["egrow"][:, 0:W], ALU.mult,
                reads=[qn, G["egrow"]], writes=[qgb])
            P.i("dve", "tensor_tensor", v3(G["negD"][:, 0:W]), v3(PS[6][0:64, 0:W]), gcol(Gc).to_broadcast([64, nch, 64]), ALU.subtract,
                reads=[PS[6], Gc], writes=[G["negD"]])
            P.i("dve", "tensor_scalar_max", G["decL"][:, 0:W], G["negD"][:, 0:W], 0.0, reads=[G["negD"]], writes=[G["decL"]])
            P.i("act", "activation", G["decL"][:, 0:W], G["decL"][:, 0:W], AF.Exp, scale=-1.0, reads=[G["decL"]], writes=[G["decL"]])
            P.i("pool", "tensor_tensor", v3(G["decL"][:, 0:W]), v3(G["decL"][:, 0:W]), k.msl[:, :].unsqueeze(1).to_broadcast([64, nch, 64]), ALU.mult,
                reads=[G["decL"], k.msl], writes=[G["decL"]])
            P.i("dve", "tensor_scalar_min", G["decU"][:, 0:W], G["negD"][:, 0:W], 0.0, reads=[G["negD"]], writes=[G["decU"]])
            P.i("act", "activation", G["decU"][:, 0:W], G["decU"][:, 0:W], AF.Exp, reads=[G["decU"]], writes=[G["decU"]])
            P.i("pool", "tensor_tensor", v3(G["decU"][:, 0:W]), v3(G["decU"][:, 0:W]), k.ltri[:, :].unsqueeze(1).to_broadcast([64, nch, 64]), ALU.mult,
                reads=[G["decU"], k.ltri], writes=[G["decU"]])
            cs = lambda c: slice(g * 512 + c * 64, g * 512 + (c + 1) * 64)
            for c in range(nch):
                P.i("pe", "matmul", PS[0][0:64, c * 64:(c + 1) * 64], kn[:, cs(c)], kn[:, cs(c)], start=True, stop=True, reads=[kn], writes=[PS[0]])
            P.i("dve", "tensor_tensor", G["tmp"][:, 0:W], PS[0][0:64, 0:W], G["decL"][:, 0:W], ALU.mult, reads=[PS[0], G["decL"]], writes=[G["tmp"]])
            P.i("pool", "tensor_tensor", v3(G["R0"][:, 0:W]), v3(G["tmp"][:, 0:W]), gcol(nbeta).to_broadcast([64, nch, 64]), ALU.mult,
                reads=[G["tmp"], nbeta], writes=[G["R0"]])
            for c in range(nch):
                P.i("pe", "transpose", PS[1][0:64, c * 64:(c + 1) * 64], G["R0"][:, c * 64:(c + 1) * 64], k.ident[0:64, 0:64],
                    reads=[G["R0"], k.ident], writes=[PS[1]])
            P.i("act", "copy", G["Q0"][:, 0:W], PS[1][0:64, 0:W], reads=[PS[1]], writes=[G["Q0"]])
            P.i("dve", "tensor_tensor", v3(G["Tt"][:, 0:W]), v3(PS[1][0:64, 0:W]), i64b[:, 0:nch, :], ALU.add, reads=[PS[1], k.ident], writes=[G["Tt"]])
            for step in range(1, 6):
                cur, nxt = str((step - 1) % 2), str(step % 2)
                Qc, Rc, Qn, Rn = G["Q" + cur], G["R" + cur], G["Q" + nxt], G["R" + nxt]
                for c in range(nch):
                    sl = slice(c * 64, (c + 1) * 64)
                    if step < 5:
                        P.i("pe", "matmul", PS[1][0:64, sl], Rc[:, sl], Qc[:, sl], start=True, stop=True, reads=[Rc, Qc], writes=[PS[1]])
                    P.i("pe", "matmul", PS[2][0:64, sl], Qc[:, sl], Rc[:, sl], start=True, stop=True, reads=[Rc, Qc], writes=[PS[2]])
                if step < 5:
                    P.i("act", "copy", Qn[:, 0:W], PS[1][0:64, 0:W], reads=[PS[1]], writes=[Qn])
                P.i("dve", "tensor_copy", Rn[:, 0:W], PS[2][0:64, 0:W], reads=[PS[2]], writes=[Rn])
                for c in range(nch):
                    sl = slice(c * 64, (c + 1) * 64)
                    P.i("pe", "matmul", PS[0][0:64, sl], Rn[:, sl], G["Tt"][:, sl], start=True, stop=True, reads=[Rn, G["Tt"]], writes=[PS[0]])
                P.i("dve", "tensor_tensor", G["Tt"][:, 0:W], G["Tt"][:, 0:W], PS[0][0:64, 0:W], ALU.add, reads=[PS[0], G["Tt"]], writes=[G["Tt"]])
            P.i("act", "copy", Ttb[:, 0:W], G["Tt"][:, 0:W], reads=[G["Tt"]], writes=[Ttb])
            pk = PS[3][0:64, :].bitcast(BF16)
            pv = PS[4][0:64, :].bitcast(BF16)
            for c in range(nch):
                P.i("pe", "transpose", pk[:, c * 128:(c + 1) * 128], kn[:, cs(c)], k.identb[:], reads=[kn, k.identb], writes=[PS[3]])
                P.i("pe", "transpose", pv[:, c * 128:(c + 1) * 128], vn[:, cs(c)], k.identb[:], reads=[vn, k.identb], writes=[PS[4]])
            p3 = lambda ap: ap.rearrange("p (n c) -> p n c", c=128)
            bc = lambda t: gcol(t).to_broadcast([64, nch, 128])
            P.i("dve", "tensor_tensor", kb[:, 0:nch, :], p3(pk[:, 0:nch * 128]), bc(ebg), ALU.mult, reads=[PS[3], ebg], writes=[kb])
            P.i("dve", "tensor_tensor", kdec[:, 0:nch, :], p3(pk[:, 0:nch * 128]), bc(edec), ALU.mult, reads=[PS[3], edec], writes=[kdec])
            P.i("dve", "tensor_tensor", vb[:, 0:nch, :], p3(pv[:, 0:nch * 128]), bc(beta), ALU.mult, reads=[PS[4], beta], writes=[vb])
            for c in range(nch):
                sl = slice(c * 64, (c + 1) * 64)
                P.i("pe", "matmul", PS[2][:, sl], kb[:, c, :], Ttb[:, sl], start=True, stop=True, reads=[kb, Ttb], writes=[PS[2]])
                P.i("pe", "matmul", PS[1][0:64, sl], kn[:, cs(c)], qn[:, cs(c)], start=True, stop=True, reads=[kn, qn], writes=[PS[1]])
            P.i("act", "activation", nwk[:, 0:W], PS[2][:, 0:W], AF.Copy, scale=-1.0, reads=[PS[2]], writes=[nwk])
            P.i("dve", "tensor_tensor", qkm[:, 0:W], PS[1][0:64, 0:W], G["decU"][:, 0:W], ALU.mult, reads=[PS[1], G["decU"]], writes=[qkm])
            for c in range(nch):
                n = n0 + c
                sl = slice(c * 64, (c + 1) * 64)
                if n == 0:
                    P.i("pool", "memset", S[:], 0.0, writes=[S])
                    P.i("pool", "memset", Sb[:], 0.0, writes=[Sb])
                elif n >= 32:
                    P.dma("sp", S[:], d["state_gdn"].ap()[j, n - 32, h], reads=[d["state_gdn"]], writes=[S])
                    P.i("act", "copy", Sb[:], S[:], reads=[S], writes=[Sb])
                P.i("pe", "matmul", PS[7][0:64, 0:128], Ttb[:, sl], vb[:, c, :], start=True, stop=False, reads=[Ttb, vb], writes=[PS[7]])
                P.i("pe", "matmul", PS[7][0:64, 0:128], nwk[:, sl], Sb[:], start=False, stop=True, reads=[nwk, Sb], writes=[PS[7]])
                P.i("act", "copy", ub[:], PS[7][0:64, 0:128], reads=[PS[7]], writes=[ub])
                P.i("pe", "matmul", PS[5][:, sl], Sb[:], qgb[:, cs(c)], start=True, stop=False, reads=[Sb, qgb], writes=[PS[5]])
                P.i("pe", "matmul", PS[5][:, sl], ub[:], qkm[:, sl], start=False, stop=True, reads=[ub, qkm], writes=[PS[5]])
                P.i("pe", "matmul", PS[4][:, 0:128], kdec[:, c, :], ub[:], start=True, stop=True, reads=[kdec, ub], writes=[PS[4]])
                P.i("dve", "scalar_tensor_tensor", S[:], S[:], egl[:, n * 8 + h:n * 8 + h + 1], PS[4][:, 0:128], ALU.mult, ALU.add,
                    reads=[S, egl, PS[4]], writes=[S])
                P.i("act", "copy", Sb[:], S[:], reads=[S], writes=[Sb])
                if n == 31:
                    P.dma("sp", d["gdn_p"].ap()[j, h], S[:], reads=[S], writes=[d["gdn_p"]])
                elif n >= 32:
                    P.dma("sp", d["gdn_s"].ap()[j, n - 32, h], S[:], reads=[S], writes=[d["gdn_s"]])
            P.i("act", "activation", sq[1][:, 0:W], PS[5][:, 0:W], AF.Square, reads=[PS[5]], writes=[sq[1]])
            P.i("pe", "matmul", PS[6][:, 0:W], k.ones[:], sq[1][:, 0:W], start=True, stop=True, reads=[sq[1], k.ones], writes=[PS[6]])
            P.i("dve", "tensor_scalar", rstd[:, 0:W], PS[6][:, 0:W], 1.0 / 128, EPS, ALU.mult, ALU.add, reads=[PS[6]], writes=[rstd])
            P.i("act", "activation", rstd[:, 0:W], rstd[:, 0:W], AF.Sqrt, reads=[rstd], writes=[rstd])
            P.i("dve", "reciprocal", rstd[:, 0:W], rstd[:, 0:W], reads=[rstd], writes=[rstd])
            P.i("dve", "scalar_tensor_tensor", on[:, 0:W], PS[5][:, 0:W], gnw[:, 0:1], rstd[:, 0:W], ALU.mult, ALU.mult,
                reads=[PS[5], gnw, rstd], writes=[on])
            if g < 4:
                P.i("dve", "tensor_tensor", ogb[:, g * 512:(g + 1) * 512], on[:, 0:512], zsb[:, g * 512:(g + 1) * 512], ALU.mult,
                    reads=[on, zsb], writes=[ogb])
            else:
                P.i("dve", "tensor_tensor", ogb[:, LP:NT].rearrange("p (s c) -> p s c", c=4), v3(on[:, 0:256])[:, :, 0:4],
                    zsb[:, LP:NT].rearrange("p (s c) -> p s c", c=4), ALU.mult, reads=[on, zsb], writes=[ogb])
        for dt in range(8):
            for ci, (c0, cn) in enumerate(chunks(0, NT)):
                ps = PS[(dt * 5 + ci) % 2]
                P.i("pe", "matmul", ps[:, 0:cn], wout[:, dt * 128:(dt + 1) * 128], ogb[:, c0:c0 + cn], start=True, stop=True,
                    reads=[wout, ogb], writes=[ps])
                P.i("dve", "tensor_tensor", k.R[dt][:, c0:c0 + cn], ps[:, 0:cn], k.R[dt][:, c0:c0 + cn], ALU.add,
                    reads=[ps, k.R[dt]], writes=[k.R[dt]])
    P.release(m)


SCALE = 64.0 ** -0.5


def rope_apply(P, out, x, tab, nh, np_, tmps, tabbuf):
    t1, t2 = tmps
    cosb = tab[:, 0:32].unsqueeze(1).to_broadcast([np_, nh, 32])
    sinb = tab[:, 32:64].unsqueeze(1).to_broadcast([np_, nh, 32])
    x1, x2 = x[:, :, 0:32], x[:, :, 32:64]
    a, b = t1[0:np_, 0:nh, :], t2[0:np_, 0:nh, :]
    P.i("dve", "tensor_tensor", a, x1, cosb, ALU.mult, reads=x.bufs + [tabbuf], writes=[t1])
    P.i("dve", "tensor_tensor", b, x2, sinb, ALU.mult, reads=x.bufs, writes=[t2])
    P.i("dve", "tensor_tensor", out[:, :, 0:32], a, b, ALU.subtract, reads=[t1, t2], writes=out.bufs)
    P.i("dve", "tensor_tensor", a, x2, cosb, ALU.mult, reads=x.bufs + [t2], writes=[t1])
    P.i("dve", "tensor_tensor", b, x1, sinb, ALU.mult, reads=x.bufs + [t1], writes=[t2])
    P.i("dve", "tensor_tensor", out[:, :, 32:64], a, b, ALU.add, reads=[t1, t2], writes=out.bufs)


class V:
    def __init__(self, ap, bufs):
        self.ap = ap
        self.bufs = bufs

    def __getitem__(self, idx):
        return self.ap[idx]


def nsa_layer(k, l):
    mm = k.P.mark()
    try:
        _nsa_layer(k, l)
    except StopNSA:
        k.P.release(mm)


def _nsa_layer(k, l):
    P = k.P
    d = k.d
    j = l // 2
    PS = k.PS
    m0 = P.mark()
    hTs = P.sb([128, 8, 16], BF16, "nhTs")
    m1 = P.mark()
    hT = [P.sb([128, NT], BF16, "nh%d" % i) for i in range(8)]
    m2 = P.mark()
    scratch = {"sq": [P.sb([128, 512], F32, "sq%d" % i) for i in range(2)], "rstd": P.sb([128, 512], F32, "rstd")}
    rmsnorm(k, l * 8, 0, NT, hT, 0, scratch)
    P.release(m2)
    for kt in range(8):
        P.i("pool", "tensor_copy", hTs[:, kt, :], hT[kt][:, LP:NT], reads=[hT[kt]], writes=[hTs])
    ropeT = P.sb([128, 16, 64], F32, "ropeT")
    P.dma("sp", ropeT[:], d["rope_tab"].ap()[0:16].rearrange("t p c -> p t c"), reads=[d["rope_tab"]], writes=[ropeT])
    eexp = P.sb([32, LP], BF16, "eexp")
    P.i("pool", "memset", eexp[:], 1.0, writes=[eexp])
    P.i("pool", "affine_select", out=eexp[:], in_=eexp[:], pattern=[[1, LP]], compare_op=ALU.is_ge, fill=0.0, base=0,
        channel_multiplier=-64, reads=[eexp], writes=[eexp])
    P.i("pool", "affine_select", out=eexp[:], in_=eexp[:], pattern=[[-1, LP]], compare_op=ALU.is_ge, fill=0.0, base=63,
        channel_multiplier=64, reads=[eexp], writes=[eexp])
    niota = P.sb([128, 32], F32, "niota")
    curcol = P.sb([128, 16], F32, "curcol")
    hfcol = P.sb([128, 1], F32, "hfcol")
    tiota = P.sb([32, 128], F32, "tiota")
    thr = P.sb([32, 16], F32, "thr")
    P.i("pool", "iota", niota[:], pattern=[[1, 32]], base=0, channel_multiplier=0, allow_small_or_imprecise_dtypes=True, writes=[niota])
    P.i("pool", "iota", curcol[:], pattern=[[2, 16]], base=0, channel_multiplier=0, allow_small_or_imprecise_dtypes=True, writes=[curcol])
    P.i("pool", "memset", hfcol[0:64, :], 0.0, writes=[hfcol])
    P.i("pool", "memset", hfcol[64:128, :], 1.0, writes=[hfcol])
    P.i("dve", "tensor_scalar_add", curcol[:], curcol[:], hfcol[:, 0:1], reads=[curcol, hfcol], writes=[curcol])
    P.i("pool", "iota", tiota[:], pattern=[[1, 128]], base=0, channel_multiplier=0, allow_small_or_imprecise_dtypes=True, writes=[tiota])
    P.i("pool", "iota", thr[:], pattern=[[-128, 16]], base=63, channel_multiplier=64, allow_small_or_imprecise_dtypes=True, writes=[thr])
    cm_ = P.sb([32, 128], F32, "cm_")
    cand = P.sb([128, 32], F32, "cand")
    wc = P.sb([64, 64, 2, 64], BF16, "wc")
    for lh in range(4):
        P.dma("pool", wc[:, lh * 16:(lh + 1) * 16], d["nsa_cmp_w"].ap()[j, lh * 16:(lh + 1) * 16].rearrange("l c d e -> d l c e"),
              reads=[d["nsa_cmp_w"]], writes=[wc])
    peT = P.sb([64, 2, 64], F32, "peT")
    stg64 = P.sb([128, 64], F32, "stg64")
    P.dma("sp", stg64[:], d["nsa_cmp_pe"].ap()[j].rearrange("l c d -> (l c) d"), reads=[d["nsa_cmp_pe"]], writes=[stg64])
    P.i("pe", "transpose", PS[6][0:64, 0:128], stg64[:], k.ident[:], reads=[stg64, k.ident], writes=[PS[6]])
    P.i("dve", "tensor_copy", peT[:], PS[6][0:64, 0:128].rearrange("p (l c) -> p c l", c=2), reads=[PS[6]], writes=[peT])

    wg = P.sb([128, 8, 652], BF16, "wg")
    wog = P.sb([128, 2, D], BF16, "wog")
    tm = P.sb([128, 652], F32, "tm")
    tr = P.sb([128, 384], F32, "trr")
    tmb = P.sb([128, 640], BF16, "tmb")
    trb = P.sb([128, 384], BF16, "trb")
    rt = [P.sb([128, 6, 32], F32, "rt%d" % i) for i in range(2)]
    qT = P.sb([64, 8, 128], BF16, "qT")
    KselT = P.sb([64, LP], BF16, "KselT")
    KwinT = P.sb([64, LP], BF16, "KwinT")
    XkT = P.sb([64, LP], BF16, "XkT")
    XvT = P.sb([64, LP], BF16, "XvT")
    Vsel = P.sb([128, 16, 65], BF16, "Vsel")
    Vwin = P.sb([128, 16, 65], BF16, "Vwin")
    P.i("pool", "memset", Vsel[:], 1.0, writes=[Vsel])
    P.i("pool", "memset", Vwin[:], 1.0, writes=[Vwin])
    gt = P.sb([128, 12], F32, "gt")
    CkT = P.sb([64, 32], BF16, "CkT")
    CvA = P.sb([32, 64], BF16, "CvA")
    Ec = P.sb([32, 512], F32, "Ec")
    rs = P.sb([32, 512], F32, "rs")
    pb = P.sb([32, 512], BF16, "pb")
    oc = P.sb([128, 256], F32, "oc")
    impT = P.sb([32, 128], F32, "impT")
    impc = P.sb([128, 32], F32, "impc")
    tmp32 = P.sb([128, 32], F32, "tmp32")
    sel = P.sb([128, 32], F32, "sel")
    m8 = P.sb([128, 16], F32, "m8")
    smT = P.sb([32, 128], BF16, "smT")
    Eb = [P.sb([128, 512], BF16, "Eb%d" % i) for i in range(2)]
    Mb = [P.sb([128, 128], BF16, "Mb%d" % i) for i in range(2)]
    cf = P.sb([128, 8], F32, "ncf")
    om = P.sb([128, 256], F32, "om")
    om2 = P.sb([128, 256], F32, "om2")
    omb = P.sb([128, 256], BF16, "omb")
    omT = P.sb([128, 2, 128], BF16, "omT")
    cnt = [0]
    ck("n_setup")

    for g in range(0 if not SKIP_PROMPT[0] else 4, 4):
        srcs = [(g * 256, 256, 0), (1024 + 2 * 256 + g * 64, 64, 256), (1024 + 4 * 256 + g * 64, 64, 320),
                (1024 + 0 * 256 + g * 64, 64, 384), (1024 + 1 * 256 + g * 64, 64, 448), (1024 + 3 * 256 + g * 64, 64, 512),
                (1024 + 5 * 256 + g * 64, 64, 576), (2560 + g * 12, 12, 640)]
        for (c0, w_, o0) in srcs:
            P.dma("pool", wg[:, :, o0:o0 + w_], d["nsa_w_in"].ap()[j, :, c0:c0 + w_].rearrange("(kt p) c -> p kt c", p=128),
                  reads=[d["nsa_w_in"]], writes=[wg])
        P.dma("pool", wog[:], d["nsa_w_out"].ap()[j, g * 256:(g + 1) * 256, :].rearrange("(a p) c -> p a c", p=128),
              reads=[d["nsa_w_out"]], writes=[wog])
        for i in range(16):
            tok = slice(i * 128, (i + 1) * 128)
            ps = PS[i % 2]
            for kt in range(8):
                P.i("pe", "matmul", ps[:, 0:128], hT[kt][:, tok], wg[:, kt, 384:512], start=(kt == 0), stop=(kt == 7),
                    reads=[hT[kt], wg], writes=[ps])
            P.i("act", "copy", tm[:, 384:512], ps[:, 0:128], reads=[ps], writes=[tm])
            P.i("dve", "tensor_copy", tmb[:, 384:512], ps[:, 0:128], reads=[ps], writes=[tmb])
            for c in range(2):
                P.dma("sp", d["cmp_p"].ap()[j, tok, c * 256 + g * 64:c * 256 + (g + 1) * 64], tm[:, 384 + c * 64:448 + c * 64],
                      reads=[tm], writes=[d["cmp_p"]])
            pt = PS[2][0:64, :].bitcast(BF16)
            for c in range(2):
                P.i("pe", "transpose", pt[:, c * 128:(c + 1) * 128], tmb[:, 384 + c * 64:448 + c * 64], k.identb[:],
                    reads=[tmb, k.identb], writes=[PS[2]])
            for c, XT in enumerate((XkT, XvT)):
                P.i("dve", "tensor_tensor", XT[:, tok].rearrange("p (n l) -> p n l", l=64),
                    pt[:, c * 128:(c + 1) * 128].rearrange("p (n l) -> p n l", l=64),
                    peT[:, c, :].unsqueeze(1).to_broadcast([64, 2, 64]), ALU.add, reads=[PS[2], peT], writes=[XT])
        for ll in range(64):
            P.i("pe", "matmul", PS[3][0:64, 0:32], wc[:, ll, 0, :], XkT[:, :].rearrange("p (n l) -> p n l", l=64)[:, :, ll],
                start=(ll == 0), stop=(ll == 63), reads=[wc, XkT], writes=[PS[3]])
        P.i("act", "copy", CkT[:], PS[3][0:64, 0:32], reads=[PS[3]], writes=[CkT])
        for ll in range(64):
            P.i("pe", "matmul", PS[3][0:32, 64:128], XvT[:, :].rearrange("p (n l) -> p n l", l=64)[:, :, ll], wc[:, ll, 1, :],
                start=(ll == 0), stop=(ll == 63), reads=[wc, XvT], writes=[PS[3]])
        P.i("act", "copy", CvA[:], PS[3][0:32, 64:128], reads=[PS[3]], writes=[CvA])
        ck("n_pre%d" % g)

        for i in range(16):
            tok = slice(i * 128, (i + 1) * 128)
            for (c0, cn, ps) in ((0, 384, PS[0]), (512, 140, PS[1])):
                for kt in range(8):
                    P.i("pe", "matmul", ps[:, 0:cn], hT[kt][:, tok], wg[:, kt, c0:c0 + cn], start=(kt == 0), stop=(kt == 7),
                        reads=[hT[kt], wg], writes=[ps])
                P.i("act", "copy", tm[:, c0:c0 + cn], ps[:, 0:cn], reads=[ps], writes=[tm])
            rope_apply(P, V(tr[:, :].rearrange("p (h c) -> p h c", c=64), [tr]), V(tm[:, 0:384].rearrange("p (h c) -> p h c", c=64), [tm]),
                       ropeT[:, i, :], 6, 128, rt, ropeT)
            P.i("act", "copy", tmb[:, 0:256], tm[:, 0:256], reads=[tm], writes=[tmb])
            P.i("act", "copy", trb[:], tr[:], reads=[tr], writes=[trb])
            P.i("act", "activation", gt[:], tm[:, 640:652], AF.Sigmoid, reads=[tm], writes=[gt])
            P.i("pool", "tensor_copy", Vsel[:, i, 0:64], tm[:, 512:576], reads=[tm], writes=[Vsel])
            P.i("pool", "tensor_copy", Vwin[:, i, 0:64], tm[:, 576:640], reads=[tm], writes=[Vwin])
            P.dma("sp", d["sel_p"].ap()[j, tok, g * 64:(g + 1) * 64], tr[:, 256:320], reads=[tr], writes=[d["sel_p"]])
            P.dma("sp", d["sel_p"].ap()[j, tok, 256 + g * 64:256 + (g + 1) * 64], tm[:, 512:576], reads=[tm], writes=[d["sel_p"]])
            if i >= 12:
                wt = slice((i - 12) * 128, (i - 11) * 128)
                P.dma("sp", d["win_p"].ap()[j, wt, g * 64:(g + 1) * 64], tr[:, 320:384], reads=[tr], writes=[d["win_p"]])
                P.dma("sp", d["win_p"].ap()[j, wt, 256 + g * 64:256 + (g + 1) * 64], tm[:, 576:640], reads=[tm], writes=[d["win_p"]])
            pq = PS[2][0:64, :].bitcast(BF16)
            pk = PS[3][0:64, :].bitcast(BF16)
            for h in range(4):
                P.i("pe", "transpose", pq[:, h * 128:(h + 1) * 128], tmb[:, h * 64:(h + 1) * 64], k.identb[:], reads=[tmb, k.identb], writes=[PS[2]])
                P.i("pe", "transpose", pq[:, (4 + h) * 128:(5 + h) * 128], trb[:, h * 64:(h + 1) * 64], k.identb[:], reads=[trb, k.identb], writes=[PS[2]])
            P.i("pe", "transpose", pk[:, 0:128], trb[:, 256:320], k.identb[:], reads=[trb, k.identb], writes=[PS[3]])
            P.i("pe", "transpose", pk[:, 128:256], trb[:, 320:384], k.identb[:], reads=[trb, k.identb], writes=[PS[3]])
            P.i("dve", "tensor_copy", qT[:].rearrange("p a t -> p (a t)"), pq[:, 0:1024], reads=[PS[2]], writes=[qT])
            P.i("act", "copy", KselT[:, tok], pk[:, 0:128], reads=[PS[3]], writes=[KselT])
            P.i("act", "copy", KwinT[:, tok], pk[:, 128:256], reads=[PS[3]], writes=[KwinT])
            qraw = qT[:, 0:4, :]
            qrot = qT[:, 4:8, :]
            P.i("pe", "matmul", PS[5][0:32, :], CkT[:], qraw, start=True, stop=True, reads=[CkT, qT], writes=[PS[5]])
            P.i("act", "activation", Ec[:], PS[5][0:32, :], AF.Exp, scale=SCALE, reads=[PS[5]], writes=[Ec])
            P.i("dve", "tensor_scalar", cm_[:], tiota[:], thr[:, i:i + 1], None, ALU.is_ge, reads=[tiota, thr], writes=[cm_])
            P.i("dve", "tensor_tensor", Ec[:].rearrange("p (a t) -> p a t", a=4), Ec[:].rearrange("p (a t) -> p a t", a=4),
                cm_[:, :].unsqueeze(1).to_broadcast([32, 4, 128]), ALU.mult, reads=[Ec, cm_], writes=[Ec])
            P.i("pe", "matmul", PS[5][0:32, :], k.ones[0:32, 0:32], Ec[:], start=True, stop=True, reads=[k.ones, Ec], writes=[PS[5]])
            P.i("dve", "tensor_scalar_max", rs[:], PS[5][0:32, :], 1e-30, reads=[PS[5]], writes=[rs])
            P.i("dve", "reciprocal", rs[:], rs[:], reads=[rs], writes=[rs])
            P.i("dve", "tensor_tensor", Ec[:], Ec[:], rs[:], ALU.mult, reads=[Ec, rs], writes=[Ec])
            P.i("act", "copy", pb[:], Ec[:], reads=[Ec], writes=[pb])
            for h in range(4):
                P.i("pe", "matmul", PS[5][:, h * 64:(h + 1) * 64], pb[:, h * 128:(h + 1) * 128], CvA[:], start=True, stop=True,
                    reads=[pb, CvA], writes=[PS[5]])
            P.i("act", "copy", oc[:], PS[5][:, 0:256], reads=[PS[5]], writes=[oc])
            P.i("dve", "tensor_reduce", impT[:], Ec[:].rearrange("p (a t) -> p t a", a=4), AX.X, ALU.add, reads=[Ec], writes=[impT])
            P.i("pe", "transpose", PS[4][:, 0:32], impT[:], k.ident[0:32, 0:32], reads=[impT, k.ident], writes=[PS[4]])
            P.i("dve", "tensor_scalar", cand[:], niota[:], curcol[:, i:i + 1], None, ALU.is_lt, reads=[niota, curcol], writes=[cand])
            P.i("dve", "tensor_scalar_add", impc[:], PS[4][:, 0:32], 1.0, reads=[PS[4]], writes=[impc])
            P.i("dve", "tensor_tensor", impc[:], impc[:], cand[:], ALU.mult, reads=[impc, cand], writes=[impc])
            P.i("dve", "tensor_scalar_add", impc[:], impc[:], -1.0, reads=[impc], writes=[impc])
            P.i("dve", "max", m8[:, 0:8], impc[:], reads=[impc], writes=[m8])
            P.i("dve", "match_replace", tmp32[:], m8[:, 0:8], impc[:], -2.0, reads=[m8, impc], writes=[tmp32])
            P.i("dve", "max", m8[:, 8:16], tmp32[:], reads=[tmp32], writes=[m8])
            P.i("dve", "tensor_scalar", sel[:], impc[:], m8[:, 14:15], None, ALU.is_ge, reads=[impc, m8], writes=[sel])
            P.i("dve", "tensor_single_scalar", tmp32[:], impc[:], -0.5, ALU.is_gt, reads=[impc], writes=[tmp32])
            P.i("dve", "tensor_tensor", sel[:], sel[:], tmp32[:], ALU.mult, reads=[sel, tmp32], writes=[sel])
            P.i("dve", "tensor_scalar", cand[:], niota[:], curcol[:, i:i + 1], None, ALU.is_equal, reads=[niota, curcol], writes=[cand])
            P.i("dve", "tensor_tensor", sel[:], sel[:], cand[:], ALU.max, reads=[sel, cand], writes=[sel])
            P.i("pe", "transpose", PS[4][0:32, 128:256], sel[:], k.ident[:], reads=[sel, k.ident], writes=[PS[4]])
            P.i("act", "copy", smT[:], PS[4][0:32, 128:256], reads=[PS[4]], writes=[smT])
            for br in range(2):
                KT, Vv = (KselT, Vsel) if br == 0 else (KwinT, Vwin)
                acc = PS[6 + br]
                j0 = 0 if br == 0 else max(0, i - 4)
                for jt in range(j0, i + 1):
                    kk = slice(jt * 128, (jt + 1) * 128)
                    E = Eb[cnt[0] % 2]
                    M = Mb[cnt[0] % 2]
                    cnt[0] += 1
                    msk = None
                    if br == 0:
                        P.i("pe", "matmul", PS[4][:, 256:384], eexp[:, kk], smT[:], start=True, stop=True, reads=[eexp, smT], writes=[PS[4]])
                        if jt == i:
                            P.i("dve", "tensor_tensor", M[:], PS[4][:, 256:384], k.caus[:], ALU.mult, reads=[PS[4], k.caus], writes=[M])
                        else:
                            P.i("dve", "tensor_copy", M[:], PS[4][:, 256:384], reads=[PS[4]], writes=[M])
                        msk = M
                    elif jt == i:
                        msk = k.caus
                    elif jt == i - 4:
                        msk = k.wmask
                    P.i("pe", "matmul", PS[5][:, :], KT[:, kk], qrot, start=True, stop=True, reads=[KT, qT], writes=[PS[5]])
                    P.i("act", "activation", E[:], PS[5][:, :], AF.Exp, scale=SCALE, reads=[PS[5]], writes=[E])
                    if msk is not None:
                        P.i("dve", "tensor_tensor", E[:].rearrange("p (a t) -> p a t", a=4), E[:].rearrange("p (a t) -> p a t", a=4),
                            msk[:, :].unsqueeze(1).to_broadcast([128, 4, 128]), ALU.mult, reads=[E, msk], writes=[E])
                    for h in range(4):
                        P.i("pe", "matmul", acc[:, h * 65:(h + 1) * 65], E[:, h * 128:(h + 1) * 128], Vv[:, jt, :],
                            start=(jt == j0 and h == 0), stop=(jt == i), skip_group_check=True, reads=[E, Vv], writes=[acc])
            g3 = gt[:, :].rearrange("p (h c) -> p h c", c=3)
            a3 = lambda ps_: ps_[:, 0:260].rearrange("p (h c) -> p h c", c=65)
            P.i("dve", "reciprocal", cf[:, 0:4], a3(PS[6])[:, :, 64], reads=[PS[6]], writes=[cf])
            P.i("dve", "reciprocal", cf[:, 4:8], a3(PS[7])[:, :, 64], reads=[PS[7]], writes=[cf])
            P.i("dve", "tensor_tensor", cf[:, 0:4], cf[:, 0:4], g3[:, :, 1], ALU.mult, reads=[cf, gt], writes=[cf])
            P.i("dve", "tensor_tensor", cf[:, 4:8], cf[:, 4:8], g3[:, :, 2], ALU.mult, reads=[cf, gt], writes=[cf])
            o3 = lambda t: t[:, :].rearrange("p (h c) -> p h c", c=64)
            bc = lambda ap: ap.unsqueeze(2).to_broadcast([128, 4, 64])
            P.i("dve", "tensor_tensor", o3(om), o3(oc), bc(g3[:, :, 0]), ALU.mult, reads=[oc, gt], writes=[om])
            P.i("dve", "tensor_tensor", o3(om2), a3(PS[6])[:, :, 0:64], bc(cf[:, 0:4]), ALU.mult, reads=[PS[6], cf], writes=[om2])
            P.i("pool", "tensor_tensor", om[:], om[:], om2[:], ALU.add, reads=[om, om2], writes=[om])
            P.i("dve", "tensor_tensor", o3(om2), a3(PS[7])[:, :, 0:64], bc(cf[:, 4:8]), ALU.mult, reads=[PS[7], cf], writes=[om2])
            P.i("pool", "tensor_tensor", omb[:], om[:], om2[:], ALU.add, reads=[om, om2], writes=[omb])
            po = PS[2][:, :].bitcast(BF16)
            for a in range(2):
                P.i("pe", "transpose", po[:, a * 128:(a + 1) * 128], omb[:, a * 128:(a + 1) * 128], k.identb[:], reads=[omb, k.identb], writes=[PS[2]])
            P.i("act", "copy", omT[:].rearrange("p a t -> p (a t)"), po[:, 0:256], reads=[PS[2]], writes=[omT])
            for hb in range(2):
                ps = PS[hb]
                for q in range(4):
                    dt = hb * 4 + q
                    for a in range(2):
                        P.i("pe", "matmul", ps[:, q * 128:(q + 1) * 128], wog[:, a, dt * 128:(dt + 1) * 128], omT[:, a, :],
                            start=(a == 0), stop=(a == 1), reads=[wog, omT], writes=[ps])
                for q in range(4):
                    dt = hb * 4 + q
                    P.i("dve", "tensor_tensor", k.R[dt][:, tok], ps[:, q * 128:(q + 1) * 128], k.R[dt][:, tok], ALU.add,
                        reads=[ps, k.R[dt]], writes=[k.R[dt]])
            ck("n_main%d_%d" % (g, i))
        ck("n_main%d" % g)
    ck("n_prompt")
    P.release(m1)
    nsa_sample(k, l, hTs)
    P.release(m0)

def nsa_sample(k, l, hTs):
    P = k.P
    d = k.d
    j = l // 2
    PS = k.PS
    m = P.mark()
    SQ2 = P.sb([128, 4, 4, 8, 4], BF16, "SQ2")
    SKs = P.sb([64, 4, 4, 4], BF16, "SKs")
    SKw = P.sb([64, 4, 4, 4], BF16, "SKw")
    SVs = P.sb([4, 4, 4, 65], BF16, "SVs")
    SVw = P.sb([4, 4, 4, 65], BF16, "SVw")
    SG = P.sb([4, 4, 4, 12], F32, "SG")
    oTs = P.sb([64, 4, 16, 4], BF16, "oTs")
    ropeS = P.sb([4, 64], F32, "ropeS")
    P.dma("sp", ropeS[:], d["rope_tab"].ap()[16, 0:4, :], reads=[d["rope_tab"]], writes=[ropeS])
    P.i("pool", "memset", SVs[:], 1.0, writes=[SVs])
    P.i("pool", "memset", SVw[:], 1.0, writes=[SVw])
    i4 = P.sb([4, 4], F32, "i4")
    P.i("dve", "tensor_copy", i4[:], k.ident[0:4, 0:4], reads=[k.ident], writes=[i4])
    pti = P.sb([128, NS * NPG], I32, "pti")
    ptf = P.sb([128, NS * NPG], F32, "ptf")
    iot = P.sb([128, 1], F32, "iot")
    idx = P.sb([128, NS * NPG], I32, "idx")
    P.dma("sp", pti[:], d["page_table"].ap().rearrange("s g -> (s g)").partition_broadcast(128), reads=[d["page_table"]], writes=[pti])
    P.i("dve", "tensor_copy", ptf[:], pti[:], reads=[pti], writes=[ptf])
    P.i("pool", "iota", iot[:], pattern=[[0, 1]], base=j * k.n_phys * 128, channel_multiplier=1, allow_small_or_imprecise_dtypes=True, writes=[iot])
    P.i("dve", "tensor_scalar", ptf[:], ptf[:], 128.0, iot[:, 0:1], ALU.mult, ALU.add, reads=[ptf, iot], writes=[ptf])
    P.i("dve", "tensor_copy", idx[:], ptf[:], reads=[ptf], writes=[idx])
    ck("s_setup")

    m0 = P.mark()
    wg = P.sb([128, 8, 652], BF16, "swg")
    tmS = P.sb([4, 652], F32, "tmS")
    trS = P.sb([4, 384], F32, "trS")
    qd = P.sb([4, 8, 2, 64], BF16, "qd")
    kb2 = P.sb([4, 128], BF16, "kb2")
    rt = [P.sb([4, 6, 32], F32, "srt%d" % i) for i in range(2)]
    for g in range(4):
        srcs = [(g * 256, 256, 0), (1024 + 2 * 256 + g * 64, 64, 256), (1024 + 4 * 256 + g * 64, 64, 320),
                (1024 + 0 * 256 + g * 64, 64, 384), (1024 + 1 * 256 + g * 64, 64, 448), (1024 + 3 * 256 + g * 64, 64, 512),
                (1024 + 5 * 256 + g * 64, 64, 576), (2560 + g * 12, 12, 640)]
        for (c0, w_, o0) in srcs:
            P.dma("pool", wg[:, :, o0:o0 + w_], d["nsa_w_in"].ap()[j, :, c0:c0 + w_].rearrange("(kt p) c -> p kt c", p=128),
                  reads=[d["nsa_w_in"]], writes=[wg])
        for s_ in range(4):
            rows = slice(s_ * 4, (s_ + 1) * 4)
            for (c0, cn, ps) in ((0, 512, PS[0]), (512, 140, PS[1])):
                for kt in range(8):
                    P.i("pe", "matmul", ps[0:4, 0:cn], hTs[:, kt, rows], wg[:, kt, c0:c0 + cn], start=(kt == 0), stop=(kt == 7),
                        reads=[hTs, wg], writes=[ps])
                P.i("act", "copy", tmS[:, c0:c0 + cn], ps[0:4, 0:cn], reads=[ps], writes=[tmS])
            rope_apply(P, V(trS[:, :].rearrange("p (h c) -> p h c", c=64), [trS]), V(tmS[:, 0:384].rearrange("p (h c) -> p h c", c=64), [tmS]),
                       ropeS[:, :], 6, 4, rt, ropeS)
            for c in range(2):
                P.dma("sp", d["cmp_s"].ap()[j, rows, c * 256 + g * 64:c * 256 + (g + 1) * 64], tmS[:, 384 + c * 64:448 + c * 64],
                      reads=[tmS], writes=[d["cmp_s"]])
            P.dma("sp", d["sel_s"].ap()[j, rows, g * 64:(g + 1) * 64], trS[:, 256:320], reads=[trS], writes=[d["sel_s"]])
            P.dma("sp", d["sel_s"].ap()[j, rows, 256 + g * 64:256 + (g + 1) * 64], tmS[:, 512:576], reads=[tmS], writes=[d["sel_s"]])
            P.dma("sp", d["win_s"].ap()[j, s_, 508:512, g * 64:(g + 1) * 64], trS[:, 320:384], reads=[trS], writes=[d["win_s"]])
            P.dma("sp", d["win_s"].ap()[j, s_, 508:512, 256 + g * 64:256 + (g + 1) * 64], tmS[:, 576:640], reads=[tmS], writes=[d["win_s"]])
            for r in range(2):
                P.i("dve", "tensor_copy", qd[:, 0:4, r, :], tmS[:, 0:256].rearrange("p (h c) -> p h c", c=64), reads=[tmS], writes=[qd])
                P.i("dve", "tensor_copy", qd[:, 4:8, r, :], trS[:, 0:256].rearrange("p (h c) -> p h c", c=64), reads=[trS], writes=[qd])
            P.i("dve", "tensor_copy", kb2[:], trS[:, 256:384], reads=[trS], writes=[kb2])
            P.i("act", "copy", SVs[:, s_, g, 0:64], tmS[:, 512:576], reads=[tmS], writes=[SVs])
            P.i("act", "copy", SVw[:, s_, g, 0:64], tmS[:, 576:640], reads=[tmS], writes=[SVw])
            P.i("act", "activation", SG[:, s_, g, :], tmS[:, 640:652], AF.Sigmoid, reads=[tmS], writes=[SG])
            pq = PS[2][:, :].bitcast(BF16)
            for h in range(8):
                P.i("pe", "transpose", pq[:, h * 4:(h + 1) * 4], qd[:, h, :, :].rearrange("p r c -> p (r c)"), k.identb[0:4, 0:4],
                    reads=[qd, k.identb], writes=[PS[2]])
            P.i("dve", "tensor_copy", SQ2[:, s_, g, :, :].rearrange("p h t -> p (h t)"), pq[:, 0:32], reads=[PS[2]], writes=[SQ2])
            pk = PS[3][0:64, :].bitcast(BF16)
            P.i("pe", "transpose", pk[:, 0:4], kb2[:, 0:64], k.identb[0:4, 0:4], reads=[kb2, k.identb], writes=[PS[3]])
            P.i("pe", "transpose", pk[:, 4:8], kb2[:, 64:128], k.identb[0:4, 0:4], reads=[kb2, k.identb], writes=[PS[3]])
            P.i("act", "copy", SKs[:, s_, g, :], pk[:, 0:4], reads=[PS[3]], writes=[SKs])
            P.i("act", "copy", SKw[:, s_, g, :], pk[:, 4:8], reads=[PS[3]], writes=[SKw])
    P.release(m0)
    ck("s_S0")

    m1 = P.mark()
    wc2 = P.sb([128, 64, 2, 64], BF16, "wc2")
    for hf in range(2):
        for lh in range(4):
            P.dma("pool", wc2[hf * 64:(hf + 1) * 64, lh * 16:(lh + 1) * 16], d["nsa_cmp_w"].ap()[j, lh * 16:(lh + 1) * 16].rearrange("l c d e -> d l c e"),
                  reads=[d["nsa_cmp_w"]], writes=[wc2])
    pe4 = P.sb([128, 4, 64], F32, "pe4")
    stg64 = P.sb([128, 128], F32, "sstg")
    for r in range(2):
        P.dma("sp", stg64[:, r * 64:(r + 1) * 64], d["nsa_cmp_pe"].ap()[j].rearrange("l c d -> (l c) d"), reads=[d["nsa_cmp_pe"]], writes=[stg64])
    P.i("pe", "transpose", PS[6][:, 0:128], stg64[:], k.ident[:], reads=[stg64, k.ident], writes=[PS[6]])
    for b in range(4):
        P.i("dve", "tensor_copy", pe4[:, b, :], PS[6][:, 0:128].rearrange("p (l c) -> p c l", c=2)[:, b // 2, :], reads=[PS[6]], writes=[pe4])
    XT = P.sb([128, 4, NPG * 128], BF16, "XT")
    pgf = [P.sb([128, 512], F32, "pgf%d" % i) for i in range(2)]
    pgb = [P.sb([128, 512], BF16, "pgb%d" % i) for i in range(2)]
    CkS = P.sb([64, 4, 128], BF16, "CkS")
    CvS = P.sb([128, 4, 64], BF16, "CvS")
    Ecs = P.sb([128, 64], F32, "Ecs")
    rss = P.sb([128, 64], F32, "rss")
    pbs = P.sb([128, 64], BF16, "pbs")
    impTs = P.sb([128, 16], F32, "impTs")
    imp16 = P.sb([16, 128], F32, "imp16")
    tmp16 = P.sb([16, 128], F32, "tmp16")
    sel16 = P.sb([16, 128], BF16, "sel16")
    m8s = P.sb([16, 16], F32, "m8s")
    selX = [P.sb([16, 128], BF16, "selX%d" % i) for i in range(2)]
    Mbs = [P.sb([128, 16], BF16, "Mbs%d" % i) for i in range(2)]
    KT4 = [P.sb([64, 4, 128], BF16, "KT4%d" % i) for i in range(2)]
    onesb = P.sb([128, 64], BF16, "onesb")
    P.i("pool", "memset", onesb[:], 1.0, writes=[onesb])
    Es = [P.sb([128, 64], BF16, "Es%d" % i) for i in range(2)]
    occ = P.sb([64, 64], F32, "occ")
    bcs = P.sb([64, 64], F32, "bcs")
    gd = P.sb([4, 16, 4], F32, "gd")
    om_ = P.sb([64, 64], F32, "som")
    om2_ = P.sb([64, 64], F32, "som2")
    cnt = [0]

    def gather(cache, s_, pg, buf):
        c = s_ * NPG + pg
        P.dma("pool", buf[:], d[cache].ap().rearrange("l r c -> (l r) c"), indirect=idx[:, c:c + 1].bitcast(U32),
              reads=[d[cache], idx], writes=[buf])

    def attend_tile(kt4, pb_, mask_fn, first, qsl):
        E = Es[cnt[0] % 2]
        cnt[0] += 1
        for g in range(4):
            P.i("pe", "matmul", PS[5][:, g * 16:(g + 1) * 16], kt4[:, g, :], SQ2[0:64, qsl[0], g, 4:8, :], start=True, stop=True,
                reads=[kt4, SQ2], writes=[PS[5]])
        P.i("act", "activation", E[:], PS[5][:, 0:64], AF.Exp, scale=SCALE, reads=[PS[5]], writes=[E])
        if mask_fn is not None:
            mask_fn(E)
        for g in range(4):
            P.i("pe", "matmul", PS[7][0:64, g * 16:(g + 1) * 16], pb_[:, 256 + g * 64:256 + (g + 1) * 64], E[:, g * 16:(g + 1) * 16],
                start=(first and g == 0), stop=False, skip_group_check=True, reads=[pb_, E], writes=[PS[7]])
        P.i("pe", "matmul", PS[6][0:64, 0:64], onesb[:, :], E[:, :], start=first, stop=False, skip_group_check=True,
            reads=[onesb, E], writes=[PS[6]])

    def finish_branch(dst):
        P.i("dve", "reciprocal", bcs[:], PS[6][0:64, 0:64], reads=[PS[6]], writes=[bcs])
        P.i("dve", "tensor_tensor", dst[:], PS[7][0:64, 0:64], bcs[:], ALU.mult, reads=[PS[7], bcs], writes=[dst])

    def gate_bc(s_, which):
        P.i("dve", "tensor_tensor", gd[:], SG[:, s_, :, :].rearrange("p g (h c) -> p (g h) c", c=3)[:, :, which].unsqueeze(2).to_broadcast([4, 16, 4]),
            i4[:, :].unsqueeze(1).to_broadcast([4, 16, 4]), ALU.mult, reads=[SG, i4], writes=[gd])
        P.i("pe", "matmul", PS[5][0:64, 128:192], k.ones[0:4, 0:64], gd[:].rearrange("p a t -> p (a t)"), start=True, stop=True,
            reads=[k.ones, gd], writes=[PS[5]])

    for s_ in range(4):
        qs = (s_,)
        for pg in range(NPG):
            pf, pb_ = pgf[pg % 2], pgb[pg % 2]
            gather("cache_cmp", s_, pg, pf)
            if pg % 2 == 0:
                P.i("act", "copy", pb_[:], pf[:], reads=[pf], writes=[pb_])
            else:
                P.i("dve", "tensor_copy", pb_[:], pf[:], reads=[pf], writes=[pb_])
            pt = PS[pg % 2][:, :].bitcast(BF16)
            for b in range(4):
                P.i("pe", "transpose", pt[:, b * 128:(b + 1) * 128], pb_[:, b * 128:(b + 1) * 128], k.identb[:], reads=[pb_, k.identb], writes=[PS[pg % 2]])
            P.i("dve", "tensor_tensor", XT[:, :, pg * 128:(pg + 1) * 128].rearrange("p b (n l) -> p b n l", l=64),
                pt[:, 0:512].rearrange("p (b n l) -> p b n l", b=4, l=64), pe4[:, :, :].unsqueeze(2).to_broadcast([128, 4, 2, 64]), ALU.add,
                reads=[PS[pg % 2], pe4], writes=[XT])
        ck("s_a%d" % s_)
        for g in range(4):
            gp, gl = g // 2, g % 2
            pr = slice(gl * 64, (gl + 1) * 64)
            xk = XT[pr, 0 * 2 + gp, :].rearrange("p (n l) -> p n l", l=64)
            xv = XT[pr, 1 * 2 + gp, :].rearrange("p (n l) -> p n l", l=64)
            for ll in range(64):
                P.i("pe", "matmul", PS[2][0:64, 0:128], wc2[pr, ll, 0, :], xk[:, :, ll], start=(ll == 0), stop=(ll == 63), reads=[wc2, XT], writes=[PS[2]])
            P.i("act", "copy", CkS[:, g, :], PS[2][0:64, 0:128], reads=[PS[2]], writes=[CkS])
            for ll in range(64):
                P.i("pe", "matmul", PS[3][:, 0:64], xv[:, :, ll], wc2[pr, ll, 1, :], start=(ll == 0), stop=(ll == 63), reads=[wc2, XT], writes=[PS[3]])
            P.i("act", "copy", CvS[:, g, :], PS[3][:, 0:64], reads=[PS[3]], writes=[CvS])
        ck("s_b%d" % s_)
        for g in range(4):
            P.i("pe", "matmul", PS[5][:, g * 16:(g + 1) * 16], CkS[:, g, :], SQ2[0:64, s_, g, 0:4, :], start=True, stop=True,
                reads=[CkS, SQ2], writes=[PS[5]])
        P.i("act", "activation", Ecs[:], PS[5][:, 0:64], AF.Exp, scale=SCALE, reads=[PS[5]], writes=[Ecs])
        P.i("pe", "matmul", PS[5][:, 0:64], k.ones[:], Ecs[:], start=True, stop=True, reads=[k.ones, Ecs], writes=[PS[5]])
        P.i("dve", "reciprocal", rss[:], PS[5][:, 0:64], reads=[PS[5]], writes=[rss])
        P.i("dve", "tensor_tensor", Ecs[:], Ecs[:], rss[:], ALU.mult, reads=[Ecs, rss], writes=[Ecs])
        P.i("act", "copy", pbs[:], Ecs[:], reads=[Ecs], writes=[pbs])
        for g in range(4):
            P.i("pe", "matmul", PS[6][0:64, g * 16:(g + 1) * 16], CvS[:, g, :], pbs[:, g * 16:(g + 1) * 16], start=True, stop=True,
                reads=[CvS, pbs], writes=[PS[6]])
        P.i("act", "copy", occ[:], PS[6][0:64, 0:64], reads=[PS[6]], writes=[occ])
        P.i("dve", "tensor_reduce", impTs[:].rearrange("p (g t) -> p g t", g=4), Ecs[:].rearrange("p (g a t) -> p g t a", g=4, a=4), AX.X, ALU.add,
            reads=[Ecs], writes=[impTs])
        P.i("pe", "transpose", PS[4][0:16, 0:128], impTs[:], k.ident[:], reads=[impTs, k.ident], writes=[PS[4]])
        P.i("dve", "tensor_copy", imp16[:], PS[4][0:16, 0:128], reads=[PS[4]], writes=[imp16])
        P.i("dve", "max", m8s[:, 0:8], imp16[:], reads=[imp16], writes=[m8s])
        P.i("dve", "match_replace", tmp16[:], m8s[:, 0:8], imp16[:], -2.0, reads=[m8s, imp16], writes=[tmp16])
        P.i("dve", "max", m8s[:, 8:16], tmp16[:], reads=[tmp16], writes=[m8s])
        P.i("dve", "tensor_scalar", sel16[:], imp16[:], m8s[:, 14:15], None, ALU.is_ge, reads=[imp16, m8s], writes=[sel16])
        ck("s_c%d" % s_)
        for pg in range(NPG):
            pf, pb_ = pgf[pg % 2], pgb[pg % 2]
            gather("cache_sel", s_, pg, pf)
            if pg % 2 == 0:
                P.i("act", "copy", pb_[:], pf[:], reads=[pf], writes=[pb_])
            else:
                P.i("dve", "tensor_copy", pb_[:], pf[:], reads=[pf], writes=[pb_])
            kt4, sx, mb = KT4[pg % 2], selX[pg % 2], Mbs[pg % 2]
            pt = PS[pg % 2][0:64, :].bitcast(BF16)
            for g in range(4):
                P.i("pe", "transpose", pt[:, g * 128:(g + 1) * 128], pb_[:, g * 64:(g + 1) * 64], k.identb[:], reads=[pb_, k.identb], writes=[PS[pg % 2]])
            P.i("act", "copy", kt4[:].rearrange("p a t -> p (a t)"), pt[:, 0:512], reads=[PS[pg % 2]], writes=[kt4])
            P.i("dve", "tensor_copy", sx[:].rearrange("p (n l) -> p n l", l=64), sel16[:, 2 * pg:2 * pg + 2].unsqueeze(2).to_broadcast([16, 2, 64]),
                reads=[sel16], writes=[sx])
            P.i("pe", "matmul", PS[4][:, 128:144], sx[:], k.identb[0:16, 0:16], start=True, stop=True, reads=[sx, k.identb], writes=[PS[4]])
            P.i("dve", "tensor_copy", mb[:], PS[4][:, 128:144], reads=[PS[4]], writes=[mb])

            def mfn(E, mb=mb):
                P.i("dve", "tensor_tensor", E[:].rearrange("p (g a t) -> p g a t", g=4, a=4), E[:].rearrange("p (g a t) -> p g a t", g=4, a=4),
                    mb[:].rearrange("p (g t) -> p g t", g=4).unsqueeze(2).to_broadcast([128, 4, 4, 4]), ALU.mult, reads=[E, mb], writes=[E])
            attend_tile(kt4, pb_, mfn, pg == 0, qs)
            if pg == 0:
                ck("s_dpage%d" % s_)
        ck("s_dpre%d" % s_)
        new_tile(k, P, PS, SKs, SVs, SQ2, s_, Es, cnt, onesb)
        ck("s_dnew%d" % s_)
        finish_branch(om2_)
        ck("s_dfin%d" % s_)
        gate_bc(s_, 1)
        P.i("dve", "tensor_tensor", om_[:], om2_[:], PS[5][0:64, 128:192], ALU.mult, reads=[om2_, PS[5]], writes=[om_])
        gate_bc(s_, 0)
        P.i("dve", "tensor_tensor", om2_[:], occ[:], PS[5][0:64, 128:192], ALU.mult, reads=[occ, PS[5]], writes=[om2_])
        P.i("pool", "tensor_tensor", om_[:], om_[:], om2_[:], ALU.add, reads=[om_, om2_], writes=[om_])
        ck("s_d%d" % s_)
        for a in range(4):
            pf, pb_ = pgf[a % 2], pgb[a % 2]
            P.dma("sp", pf[:], d["state_win"].ap()[j, s_, a * 128:(a + 1) * 128, :], reads=[d["state_win"]], writes=[pf])
            if a == 0:
                P.dma("sp", d["win_s"].ap()[j, s_, 0:124, :], pf[4:128, :], reads=[pf], writes=[d["win_s"]])
            else:
                P.dma("sp", d["win_s"].ap()[j, s_, a * 128 - 4:a * 128 + 124, :], pf[:, :], reads=[pf], writes=[d["win_s"]])
            P.i("act", "copy", pb_[:], pf[:], reads=[pf], writes=[pb_])
            kt4 = KT4[a % 2]
            pt = PS[a % 2][0:64, :].bitcast(BF16)
            for g in range(4):
                P.i("pe", "transpose", pt[:, g * 128:(g + 1) * 128], pb_[:, g * 64:(g + 1) * 64], k.identb[:], reads=[pb_, k.identb], writes=[PS[a % 2]])
            P.i("act", "copy", kt4[:].rearrange("p a t -> p (a t)"), pt[:, 0:512], reads=[PS[a % 2]], writes=[kt4])
            mfn = None
            if a == 0:
                def mfn(E):
                    P.i("dve", "tensor_tensor", E[:].rearrange("p (a t) -> p a t", t=4), E[:].rearrange("p (a t) -> p a t", t=4),
                        k.wmask[:, 0:4].unsqueeze(1).to_broadcast([128, 16, 4]), ALU.mult, reads=[E, k.wmask], writes=[E])
            attend_tile(kt4, pb_, mfn, a == 0, qs)
        new_tile(k, P, PS, SKw, SVw, SQ2, s_, Es, cnt, onesb)
        finish_branch(om2_)
        gate_bc(s_, 2)
        P.i("dve", "tensor_tensor", om2_[:], om2_[:], PS[5][0:64, 128:192], ALU.mult, reads=[om2_, PS[5]], writes=[om2_])
        P.i("pool", "tensor_tensor", oTs[:, s_, :, :].rearrange("p a t -> p (a t)"), om_[:], om2_[:], ALU.add, reads=[om_, om2_], writes=[oTs])
        ck("s_e%d" % s_)
    P.release(m1)

    m2 = P.mark()
    wo = P.sb([64, 16, D], BF16, "swo")
    for q4 in range(4):
        P.dma("pool", wo[:, q4 * 4:(q4 + 1) * 4, :], d["nsa_w_out"].ap()[j, q4 * 256:(q4 + 1) * 256, :].rearrange("(h p) c -> p h c", p=64),
              reads=[d["nsa_w_out"]], writes=[wo])
    for dt in range(8):
        ps = PS[dt % 2]
        for h in range(16):
            P.i("pe", "matmul", ps[:, 0:16], wo[:, h, dt * 128:(dt + 1) * 128], oTs[:, :, h, :], start=(h == 0), stop=(h == 15),
                reads=[wo, oTs], writes=[ps])
        P.i("dve", "tensor_tensor", k.R[dt][:, LP:NT], ps[:, 0:16], k.R[dt][:, LP:NT], ALU.add, reads=[ps, k.R[dt]], writes=[k.R[dt]])
    P.release(m2)
    P.release(m)


def new_tile(k, P, PS, SK, SV, SQ2, s_, Es, cnt, onesb):
    E = Es[cnt[0] % 2]
    cnt[0] += 1
    for g in range(4):
        P.i("pe", "matmul", PS[5][0:4, g * 16:(g + 1) * 16], SK[:, s_, g, :], SQ2[0:64, s_, g, 4:8, :], start=True, stop=True,
            reads=[SK, SQ2], writes=[PS[5]])
    P.i("act", "activation", E[0:4, :], PS[5][0:4, 0:64], AF.Exp, scale=SCALE, reads=[PS[5]], writes=[E])
    P.i("dve", "tensor_tensor", E[0:4, :].rearrange("p (a t) -> p a t", t=4), E[0:4, :].rearrange("p (a t) -> p a t", t=4),
        k.caus[0:4, 0:4].unsqueeze(1).to_broadcast([4, 16, 4]), ALU.mult, reads=[E, k.caus], writes=[E])
    for g in range(4):
        P.i("pe", "matmul", PS[7][0:64, g * 16:(g + 1) * 16], SV[:, s_, g, 0:64], E[0:4, g * 16:(g + 1) * 16],
            start=False, stop=(g == 3), skip_group_check=True, reads=[SV, E], writes=[PS[7]])
    P.i("pe", "matmul", PS[6][0:64, 0:64], onesb[0:4, :], E[0:4, :], start=False, stop=True, skip_group_check=True,
        reads=[onesb, E], writes=[PS[6]])


_OUT_ORDER = ["y_p", "y_s", "gdn_p", "gdn_s", "gconv_p", "gconv_s", "cmp_p", "cmp_s", "sel_p", "sel_s",
              "win_p", "win_s", "ffn_p", "ffn_s"]


def _rope_table():
    inv = (np.float32(10000.0) ** (-np.arange(32, dtype=np.float32) / np.float32(32))).astype(np.float32)
    pos = np.zeros((17, 128), np.float32)
    pos[:16] = np.arange(LP, dtype=np.float32).reshape(16, 128)
    pos[16] = 8192 + (np.arange(128) % 4)
    ang = (pos[:, :, None] * inv[None, None, :]).astype(np.float32)
    return np.concatenate([np.cos(ang), np.sin(ang)], axis=-1).astype(np.float32)


_ROPE_TAB = _rope_table()


def make_in_maps(inp, compact=False):
    maps = []
    nphys = inp["cache_cmp"].shape[1]
    cc = np.ascontiguousarray(inp["cache_cmp"]).reshape(2, nphys * 128, 512)
    cs = np.ascontiguousarray(inp["cache_sel"]).reshape(2, nphys * 128, 512)
    for c in range(8):
        s0, s1 = c * NS, (c + 1) * NS
        m = {}
        m["xp"] = np.ascontiguousarray(inp["x_prompt"][c])
        m["xs"] = np.ascontiguousarray(inp["x_sample"][s0:s1]).reshape(NS * DS, D)
        m["state_gdn"] = np.ascontiguousarray(inp["state_gdn"][:, s0:s1])
        m["state_gdn_conv"] = np.ascontiguousarray(inp["state_gdn_conv"][:, s0:s1])
        m["state_win"] = np.ascontiguousarray(inp["state_win"][:, s0:s1]).reshape(2, NS, 512, 512)
        m["state_ffn_conv"] = np.ascontiguousarray(inp["state_ffn_conv"][:, s0:s1])
        pt = np.ascontiguousarray(inp["page_table"][s0:s1]).astype(np.int32)
        if compact:
            flat = pt.reshape(-1)
            m["cache_cmp"] = np.ascontiguousarray(inp["cache_cmp"][:, flat]).reshape(2, flat.size * 128, 512)
            m["cache_sel"] = np.ascontiguousarray(inp["cache_sel"][:, flat]).reshape(2, flat.size * 128, 512)
            pt = np.arange(flat.size, dtype=np.int32).reshape(NS, NPG)
        else:
            m["cache_cmp"] = cc
            m["cache_sel"] = cs
        m["page_table"] = pt
        m["rope_tab"] = _ROPE_TAB
        for nm in ["norm_mix", "norm_ffn", "norm_final", "gdn_w_in", "gdn_conv_w", "gdn_a_log", "gdn_dt_bias", "gdn_norm_w",
                   "gdn_w_out", "nsa_w_in", "nsa_cmp_pe", "nsa_cmp_w", "nsa_w_out", "ffn_w_up", "ffn_conv_w", "ffn_conv_b", "ffn_w_down"]:
            m[nm] = np.ascontiguousarray(inp[nm])
        maps.append(m)
    return maps


def assemble(results):
    def cat(nm, axis):
        return np.concatenate([r[nm] for r in results], axis=axis)

    def stack(nm, axis):
        return np.stack([r[nm] for r in results], axis=axis)
    y_p = stack("y_p", 0)
    y_s = cat("y_s", 0).reshape(32, DS, D)
    gdn_p = stack("gdn_p", 1)
    gdn_s = cat("gdn_s", 1)
    gconv_p = stack("gconv_p", 1)
    gconv_s = cat("gconv_s", 1)
    cmp_p = stack("cmp_p", 1).reshape(2, 8, LP, 2, 4, 64)
    cmp_s = cat("cmp_s", 1).reshape(2, 32, DS, 2, 4, 64)
    sel_p = stack("sel_p", 1).reshape(2, 8, LP, 2, 4, 64)
    sel_s = cat("sel_s", 1).reshape(2, 32, DS, 2, 4, 64)
    win_p = stack("win_p", 1).reshape(2, 8, 512, 2, 4, 64)
    win_s = cat("win_s", 1).reshape(2, 32, 512, 2, 4, 64)
    ffn_p = stack("ffn_p", 1)
    ffn_s = cat("ffn_s", 1)
    return (y_p, y_s, gdn_p, gdn_s, gconv_p, gconv_s, cmp_p, cmp_s, sel_p, sel_s, win_p, win_s, ffn_p, ffn_s)


def kernel(**inputs):
    inp = {k_: np.asarray(v) for k_, v in inputs.items()}
    n_phys = inp["cache_cmp"].shape[1]
    nc = build(n_phys)
    maps = make_in_maps(inp)
    res = run_bass_kernel_spmd(nc, maps, core_ids=list(range(8)))
    return assemble(res.results)
```

```python
import numpy as np
import concourse.bass as bass
import concourse.mybir as mybir
from concourse.bass_utils import run_bass_kernel_spmd

F32 = mybir.dt.float32
BF16 = mybir.dt.bfloat16
I32 = mybir.dt.int32
U32 = mybir.dt.uint32
AF = mybir.ActivationFunctionType
ALU = mybir.AluOpType
AX = mybir.AxisListType

D = 1024
LP = 2048
NS = 4
DS = 4
NT = LP + NS * DS
DFF = 2816
NPG = 64
EPS = 1e-6
GW = 4112
NW = 2608


class Buf:
    __slots__ = ("t", "name", "lw", "rd", "psum")

    def __init__(self, t, name, psum=False):
        self.t = t
        self.name = name
        self.lw = None
        self.rd = []
        self.psum = psum

    def __getitem__(self, idx):
        return self.t[idx]

    def ap(self):
        return self.t.ap()


class Prog:
    ENGS = ("pe", "act", "dve", "pool", "sp")

    def __init__(self, nc, ndma=8):
        self.nc = nc
        self.stack = []
        self.ops = {e: [] for e in self.ENGS}
        self.sems = {}
        self.cnt = {}
        self.waited = {e: {} for e in self.ENGS}
        self.semguards = []
        self.ekey = {}
        self.epoch = {}
        self.LIMIT = 3000
        for e in self.ENGS:
            self._mksem("E_" + e)
            self.ekey[e] = "E_" + e
            self.epoch[e] = 0
        self.dkey = {}
        self.ndma = ndma
        self.dma_rr = {"sp": 0, "pool": 0, "act": 0}
        for q in ("sp", "pool", "act"):
            for i in range(ndma):
                self._mksem("D_%s%d" % (q, i))
                self.dkey[(q, i)] = "D_%s%d" % (q, i)
        self.nrot = 0
        self.nbuf = 0
        self.nops = 0

    def _mksem(self, key):
        g = self.nc.semaphore(key)
        s = g.__enter__()
        self.semguards.append(g)
        self.sems[key] = s
        self.cnt[key] = 0

    def sb(self, shape, dt=F32, name=None):
        self.nbuf += 1
        name = (name or "sb") + "_%d" % self.nbuf
        g = self.nc.sbuf_tensor(name, list(shape), dt)
        t = g.__enter__()
        self.stack.append(g)
        return Buf(t, name)

    def ps(self, shape, dt=F32, name=None):
        self.nbuf += 1
        name = (name or "ps") + "_%d" % self.nbuf
        g = self.nc.psum_tensor(name, list(shape), dt)
        t = g.__enter__()
        self.stack.append(g)
        return Buf(t, name, psum=True)

    def dram(self, name, shape, dt=F32, kind="Internal"):
        t = self.nc.dram_tensor(name, list(shape), dt, kind=kind)
        return Buf(t, name)

    def mark(self):
        return len(self.stack)

    def release(self, mark):
        self.barrier()
        while len(self.stack) > mark:
            g = self.stack.pop()
            g.__exit__(None, None, None)

    def _need(self, eng, ev, waits):
        if ev is None:
            return
        key, val = ev
        if eng == "pe" and key.startswith("E_pe"):
            return
        if self.waited[eng].get(key, 0) >= val:
            return
        self.waited[eng][key] = val
        waits.append((key, val))

    def _deps(self, eng, reads, writes):
        waits = []
        for b in reads:
            self._need(eng, b.lw, waits)
            if b.psum:
                for ev in b.rd:
                    if not ev[0].startswith("E_" + eng):
                        self._need(eng, ev, waits)
        for b in writes:
            self._need(eng, b.lw, waits)
            for ev in b.rd:
                self._need(eng, ev, waits)
        return waits

    def _commit(self, ev, reads, writes):
        for b in reads:
            b.rd.append(ev)
            if len(b.rd) > 48:
                m = {}
                for k, v in b.rd:
                    if m.get(k, 0) < v:
                        m[k] = v
                b.rd = list(m.items())
        for b in writes:
            b.lw = ev
            b.rd = []

    def op(self, eng, fn, reads=(), writes=()):
        waits = self._deps(eng, reads, writes)
        key = self.ekey[eng]
        self.cnt[key] += 1
        ev = (key, self.cnt[key])
        self.ops[eng].append((waits, fn, (key, 1)))
        self._commit(ev, reads, writes)
        self.nops += 1
        if self.cnt[key] >= self.LIMIT:
            self.epoch[eng] += 1
            self.ekey[eng] = self._rotate(key, "E_%s#%d" % (eng, self.epoch[eng]))
        return ev

    def _rotate(self, old_key, new_key):
        final = self.cnt[old_key]
        for e in self.ENGS:
            waits = []
            self._need(e, (old_key, final), waits)
            if waits:
                self.ops[e].append((waits, None, None))
        self._mksem(new_key)
        return new_key

    def i(self, eng, name, *args, reads=(), writes=(), **kw):
        def fn(e, name=name, args=args, kw=kw, eng=eng):
            try:
                return getattr(e, name)(*args, **kw)
            except Exception as ex:
                raise RuntimeError("instr %s.%s failed: %s | args=%s kw=%s" % (eng, name, ex, [str(a)[:160] for a in args], kw)) from ex
        return self.op(eng, fn, reads, writes)

    def dma(self, q, out_ap, in_ap, reads=(), writes=(), indirect=None, **kw):
        i = self.dma_rr[q]
        self.dma_rr[q] = (i + 1) % self.ndma
        key = self.dkey[(q, i)]
        if self.cnt[key] >= self.LIMIT:
            self.nrot += 1
            key = self._rotate(key, "D_%s%d#%d" % (q, i, self.nrot))
            self.dkey[(q, i)] = key
        waits = self._deps(q, reads, writes)
        if self.cnt[key] > 0:
            self._need(q, (key, self.cnt[key]), waits)
        self.cnt[key] += 16
        ev = (key, self.cnt[key])
        if indirect is None:
            def fn(e, out_ap=out_ap, in_ap=in_ap, kw=kw):
                return e.dma_start(out=out_ap, in_=in_ap, **kw)
        else:
            def fn(e, out_ap=out_ap, in_ap=in_ap, idx=indirect):
                return e.indirect_dma_start(out=out_ap, out_offset=None, in_=in_ap,
                                            in_offset=bass.IndirectOffsetOnAxis(ap=idx, axis=0))
        self.ops[q].append((waits, fn, (key, 16)))
        self._commit(ev, reads, writes)
        self.nops += 1
        return ev

    def barrier(self):
        for e in self.ENGS:
            waits = []
            for key, c in self.cnt.items():
                if c > 0:
                    self._need(e, (key, c), waits)
            if waits:
                self.ops[e].append((waits, None, None))

    def finish(self):
        self.barrier()
        nc = self.nc
        hmap = {"pe": "tensor", "act": "scalar", "dve": "vector", "pool": "gpsimd", "sp": "sync"}
        with nc.Block() as block:
            for e in self.ENGS:
                oplist = self.ops[e]

                def body(h, oplist=oplist):
                    for waits, fn, inc in oplist:
                        for key, val in waits:
                            h.wait_ge(self.sems[key], val)
                        if fn is not None:
                            ins = fn(h)
                            ins.then_inc(self.sems[inc[0]], inc[1])
                getattr(block, hmap[e])(body)
        while self.stack:
            self.stack.pop().__exit__(None, None, None)
        for g in reversed(self.semguards):
            g.__exit__(None, None, None)


def run_il(gens):
    gens = [g for g in gens if g is not None]
    while gens:
        for g in list(gens):
            try:
                next(g)
            except StopIteration:
                gens.remove(g)


def chunks(t0, t1, sz=512):
    out = []
    t = t0
    while t < t1:
        n = min(sz, t1 - t)
        out.append((t, n))
        t += n
    return out


class K:
    pass


class StopNSA(Exception):
    pass


STOP = [None]
SKIP_PROMPT = [False]


def ck(name):
    if STOP[0] == name:
        raise StopNSA(name)


def build(n_phys, n_layers=4, mixers=True):
    nc = bass.Bass("TRN2", target_bir_lowering=False)
    P = Prog(nc)
    k = K()
    k.P = P
    k.nc = nc
    k.n_phys = n_phys
    EI, EO = "ExternalInput", "ExternalOutput"
    d = {}
    k.d = d
    d["xp"] = P.dram("xp", [LP, D], F32, EI)
    d["xs"] = P.dram("xs", [NS * DS, D], F32, EI)
    d["state_gdn"] = P.dram("state_gdn", [2, NS, 8, 128, 128], F32, EI)
    d["state_gdn_conv"] = P.dram("state_gdn_conv", [2, NS, 3, 3072], F32, EI)
    d["cache_cmp"] = P.dram("cache_cmp", [2, n_phys * 128, 512], F32, EI)
    d["cache_sel"] = P.dram("cache_sel", [2, n_phys * 128, 512], F32, EI)
    d["state_win"] = P.dram("state_win", [2, NS, 512, 512], F32, EI)
    d["state_ffn_conv"] = P.dram("state_ffn_conv", [4, NS, 2, 2 * DFF], F32, EI)
    d["page_table"] = P.dram("page_table", [NS, NPG], I32, EI)
    d["norm_mix"] = P.dram("norm_mix", [4, D], F32, EI)
    d["norm_ffn"] = P.dram("norm_ffn", [4, D], F32, EI)
    d["norm_final"] = P.dram("norm_final", [D], F32, EI)
    d["gdn_w_in"] = P.dram("gdn_w_in", [2, D, GW], F32, EI)
    d["gdn_conv_w"] = P.dram("gdn_conv_w", [2, 4, 3072], F32, EI)
    d["gdn_a_log"] = P.dram("gdn_a_log", [2, 8], F32, EI)
    d["gdn_dt_bias"] = P.dram("gdn_dt_bias", [2, 8], F32, EI)
    d["gdn_norm_w"] = P.dram("gdn_norm_w", [2, 128], F32, EI)
    d["gdn_w_out"] = P.dram("gdn_w_out", [2, D, D], F32, EI)
    d["nsa_w_in"] = P.dram("nsa_w_in", [2, D, NW], F32, EI)
    d["nsa_cmp_pe"] = P.dram("nsa_cmp_pe", [2, 64, 2, 64], F32, EI)
    d["nsa_cmp_w"] = P.dram("nsa_cmp_w", [2, 64, 2, 64, 64], F32, EI)
    d["nsa_w_out"] = P.dram("nsa_w_out", [2, D, D], F32, EI)
    d["ffn_w_up"] = P.dram("ffn_w_up", [4, D, 2 * DFF], F32, EI)
    d["ffn_conv_w"] = P.dram("ffn_conv_w", [4, 3, 2 * DFF], F32, EI)
    d["ffn_conv_b"] = P.dram("ffn_conv_b", [4, 2 * DFF], F32, EI)
    d["ffn_w_down"] = P.dram("ffn_w_down", [4, DFF, D], F32, EI)
    d["rope_tab"] = P.dram("rope_tab", [17, 128, 64], F32, EI)
    d["y_p"] = P.dram("y_p", [LP, D], F32, EO)
    d["y_s"] = P.dram("y_s", [NS * DS, D], F32, EO)
    d["gdn_p"] = P.dram("gdn_p", [2, 8, 128, 128], F32, EO)
    d["gdn_s"] = P.dram("gdn_s", [2, NS, 8, 128, 128], F32, EO)
    d["gconv_p"] = P.dram("gconv_p", [2, 3, 3072], F32, EO)
    d["gconv_s"] = P.dram("gconv_s", [2, NS, 3, 3072], F32, EO)
    d["cmp_p"] = P.dram("cmp_p", [2, LP, 512], F32, EO)
    d["cmp_s"] = P.dram("cmp_s", [2, NS * DS, 512], F32, EO)
    d["sel_p"] = P.dram("sel_p", [2, LP, 512], F32, EO)
    d["sel_s"] = P.dram("sel_s", [2, NS * DS, 512], F32, EO)
    d["win_p"] = P.dram("win_p", [2, 512, 512], F32, EO)
    d["win_s"] = P.dram("win_s", [2, NS, 512, 512], F32, EO)
    d["ffn_p"] = P.dram("ffn_p", [4, 2, 2 * DFF], F32, EO)
    d["ffn_s"] = P.dram("ffn_s", [4, NS, 2, 2 * DFF], F32, EO)

    k.R = [P.sb([128, NT], F32, "R%d" % i) for i in range(8)]
    k.ident = P.sb([128, 128], F32, "ident")
    k.identb = P.sb([128, 128], BF16, "identb")
    k.ones = P.sb([128, 128], F32, "ones")
    k.ncol = P.sb([128, 72], F32, "ncol")
    k.PS = [P.ps([128, 512], F32, "psb%d" % i) for i in range(8)]
    k.stg = P.sb([128, 128], F32, "stg")

    P.i("pool", "memset", k.ident[:], 0.0, writes=[k.ident])
    P.i("pool", "affine_select", out=k.ident[:], in_=k.ident[:], pattern=[[-1, 128]],
                                            compare_op=ALU.not_equal, fill=1.0, base=0, channel_multiplier=1,
         reads=[k.ident], writes=[k.ident])
    P.i("dve", "tensor_copy", k.identb[:], k.ident[:], reads=[k.ident], writes=[k.identb])
    P.i("pool", "memset", k.ones[:], 1.0, writes=[k.ones])


    k.ltri = P.sb([64, 64], F32, "ltri")
    k.msl = P.sb([64, 64], F32, "msl")
    k.e63 = P.sb([64, 128], F32, "e63")
    k.pm4 = P.sb([64, 1], F32, "pm4")
    P.i("pool", "memset", k.ltri[:], 1.0, writes=[k.ltri])
    P.i("pool", "affine_select", out=k.ltri[:], in_=k.ltri[:], pattern=[[1, 64]], compare_op=ALU.is_ge, fill=0.0, base=0,
        channel_multiplier=-1, reads=[k.ltri], writes=[k.ltri])
    P.i("pool", "memset", k.msl[:], 1.0, writes=[k.msl])
    P.i("pool", "affine_select", out=k.msl[:], in_=k.msl[:], pattern=[[-1, 64]], compare_op=ALU.is_ge, fill=0.0, base=-1,
        channel_multiplier=1, reads=[k.msl], writes=[k.msl])
    P.i("pool", "memset", k.e63[:], 0.0, writes=[k.e63])
    P.i("pool", "affine_select", out=k.e63[:], in_=k.e63[:], pattern=[[0, 128]], compare_op=ALU.not_equal, fill=1.0, base=-63,
        channel_multiplier=1, reads=[k.e63], writes=[k.e63])
    P.i("pool", "memset", k.pm4[:], 1.0, writes=[k.pm4])
    P.i("pool", "affine_select", out=k.pm4[:], in_=k.pm4[:], pattern=[[0, 1]], compare_op=ALU.is_ge, fill=0.0, base=3,
        channel_multiplier=-1, reads=[k.pm4], writes=[k.pm4])


    k.caus = P.sb([128, 128], BF16, "caus")
    k.wmask = P.sb([128, 128], BF16, "wmask")
    P.i("pool", "memset", k.caus[:], 1.0, writes=[k.caus])
    P.i("pool", "affine_select", out=k.caus[:], in_=k.caus[:], pattern=[[1, 128]], compare_op=ALU.is_ge, fill=0.0, base=0,
        channel_multiplier=-1, reads=[k.caus], writes=[k.caus])
    P.i("pool", "memset", k.wmask[:], 1.0, writes=[k.wmask])
    P.i("pool", "affine_select", out=k.wmask[:], in_=k.wmask[:], pattern=[[-1, 128]], compare_op=ALU.is_gt, fill=0.0, base=0,
        channel_multiplier=1, reads=[k.wmask], writes=[k.wmask])

    load_cols(k, d["norm_mix"], d["norm_mix"].ap().rearrange("l (t p) -> (l t) p", p=128), 32, k.ncol, 0)
    load_cols(k, d["norm_ffn"], d["norm_ffn"].ap().rearrange("l (t p) -> (l t) p", p=128), 32, k.ncol, 32)
    load_cols(k, d["norm_final"], d["norm_final"].ap().rearrange("(t p) -> t p", p=128), 8, k.ncol, 64)

    load_x(k)
    for l in range(n_layers):
        if mixers:
            if l % 2 == 0:
                gdn_layer(k, l)
            else:
                nsa_layer(k, l)
        ffn_layer(k, l)
    final_out(k)
    P.finish()
    return nc


def load_cols(k, src_buf, src2d, nrows, dst, col0, ps=None):
    P = k.P
    ps = ps or k.PS[6]
    r0 = 0
    while r0 < nrows:
        n = min(128, nrows - r0)
        P.dma("sp", k.stg[0:n, :], src2d[r0:r0 + n, :], reads=[src_buf], writes=[k.stg])
        P.i("pe", "transpose", ps[:, 0:n], k.stg[0:n, :], k.ident[0:n, 0:n],
             reads=[k.stg, k.ident], writes=[ps])
        P.i("dve", "tensor_copy", dst[:, col0 + r0:col0 + r0 + n], ps[:, 0:n], reads=[ps], writes=[dst])
        r0 += n


def load_x(k):
    P = k.P
    d = k.d
    m = P.mark()
    xt = [P.sb([128, 4, D], F32, "xt%d" % i) for i in range(2)]
    for g in range(4):
        b = xt[g % 2]
        P.dma("sp", b[:], d["xp"].ap()[g * 512:(g + 1) * 512, :].rearrange("(a p) c -> p a c", p=128),
              reads=[d["xp"]], writes=[b])
        for dt in range(8):
            ps = k.PS[dt % 4]
            for a in range(4):
                P.i("pe", "transpose", ps[:, a * 128:(a + 1) * 128], b[:, a, dt * 128:(dt + 1) * 128], k.ident[:],
                     reads=[b, k.ident], writes=[ps])
            eng = "dve" if dt % 2 == 0 else "act"
            if eng == "dve":
                P.i("dve", "tensor_copy", k.R[dt][:, g * 512:(g + 1) * 512], ps[:], reads=[ps], writes=[k.R[dt]])
            else:
                P.i("act", "copy", k.R[dt][:, g * 512:(g + 1) * 512], ps[:], reads=[ps], writes=[k.R[dt]])
    b = xt[0]
    P.dma("sp", b[0:16, 0, :], d["xs"].ap(), reads=[d["xs"]], writes=[b])
    ps = k.PS[0]
    for dt in range(8):
        P.i("pe", "transpose", ps[:, dt * 16:(dt + 1) * 16], b[0:16, 0, dt * 128:(dt + 1) * 128], k.ident[0:16, 0:16],
             reads=[b, k.ident], writes=[ps])
    for dt in range(8):
        P.i("dve", "tensor_copy", k.R[dt][:, LP:NT], ps[:, dt * 16:(dt + 1) * 16], reads=[ps], writes=[k.R[dt]])
    P.release(m)


def rmsnorm(k, wcol0, t0, n, out_tiles, o0, scratch):
    P = k.P
    for (c0, cn) in chunks(t0, t0 + n):
        ps = k.PS[6]
        for dt in range(8):
            sq = scratch["sq"][dt % 2]
            P.i("act", "activation", sq[:, 0:cn], k.R[dt][:, c0:c0 + cn], AF.Square,
                 reads=[k.R[dt]], writes=[sq])
            P.i("pe", "matmul", ps[:, 0:cn], k.ones[:], sq[:, 0:cn], start=(dt == 0), stop=(dt == 7),
                 reads=[sq, k.ones], writes=[ps])
        rstd = scratch["rstd"]
        P.i("dve", "tensor_scalar", rstd[:, 0:cn], ps[:, 0:cn], 1.0 / D, EPS, ALU.mult, ALU.add, reads=[ps], writes=[rstd])
        P.i("act", "activation", rstd[:, 0:cn], rstd[:, 0:cn], AF.Sqrt, reads=[rstd], writes=[rstd])
        P.i("dve", "reciprocal", rstd[:, 0:cn], rstd[:, 0:cn], reads=[rstd], writes=[rstd])
        for dt in range(8):
            eng = "dve"
            P.i(eng, "scalar_tensor_tensor",
                out_tiles[dt][:, o0 + c0 - t0:o0 + c0 - t0 + cn], k.R[dt][:, c0:c0 + cn],
                k.ncol[:, wcol0 + dt:wcol0 + dt + 1], rstd[:, 0:cn], ALU.mult, ALU.mult,
                reads=[k.R[dt], k.ncol, rstd], writes=[out_tiles[dt]])


def ffn_layer(k, l):
    P = k.P
    d = k.d
    m = P.mark()
    NH = 1042
    hT = [P.sb([128, NH], BF16, "fh%d" % i) for i in range(8)]
    act = [P.sb([128, 1040], BF16, "fa%d" % i) for i in range(22)]
    wup = [P.sb([128, 8, 512], BF16, "wup%d" % i) for i in range(2)]
    wdn = [P.sb([128, 22, 128], BF16, "wdn%d" % i) for i in range(2)]
    ub = [[P.sb([128, 1056], F32, "ub%d%d" % (i, j)) for j in range(2)] for i in range(2)]
    cb = [P.sb([128, 1040], F32, "cb%d" % i) for i in range(2)]
    scratch = {"sq": [P.sb([128, 512], F32, "sq%d" % i) for i in range(2)], "rstd": P.sb([128, 512], F32, "rstd")}
    fpar = P.sb([128, 176], F32, "fpar")
    hist = P.sb([128, 352], F32, "hist")
    hsel = P.sb([128, 8, 16], BF16, "hsel")
    halo = P.sb([128, 88], F32, "halo")
    strow = P.sb([16, 2 * DFF], F32, "strow") if False else None
    st_sb = P.sb([16, 512], F32, "stsb")

    for j in range(3):
        load_cols(k, d["ffn_conv_w"], d["ffn_conv_w"].ap()[l, j].rearrange("(t p) -> t p", p=128), 44, fpar, j * 44)
    load_cols(k, d["ffn_conv_b"], d["ffn_conv_b"].ap()[l].rearrange("(t p) -> t p", p=128), 44, fpar, 132)
    load_cols(k, d["state_ffn_conv"], d["state_ffn_conv"].ap()[l].rearrange("s j (t p) -> (s j t) p", p=128), 352, hist, 0)

    wdma = [0]

    def load_wup(jb, buf):
        for which in range(2):
            c0 = which * DFF + jb * 256
            P.dma("pool", buf[:, :, which * 256:(which + 1) * 256],
                  d["ffn_w_up"].ap()[l, :, c0:c0 + 256].rearrange("(kt p) c -> p kt c", p=128),
                  reads=[d["ffn_w_up"]], writes=[buf])

    def load_wdn(dt, buf):
        P.dma("pool", buf[:], d["ffn_w_down"].ap()[l, :, dt * 128:(dt + 1) * 128].rearrange("(j p) c -> p j c", p=128),
              reads=[d["ffn_w_down"]], writes=[buf])

    psi = [0]
    for half in range(2):
        if half == 0:
            t0, n = 0, 1024
            npr = 1024
            ucol0 = 2
        else:
            t0, n = 1024, 1040
            npr = 1024
            ucol0 = 2
        rmsnorm(k, 32 + l * 8, t0, n, hT, 0, scratch)
        nprm = n - (16 if half == 1 else 0)
        if half == 1:
            for kt in range(8):
                P.i("pool", "tensor_copy", hsel[:, kt, 0:2], hT[kt][:, 1022:1024], reads=[hT[kt]], writes=[hsel])
                P.i("pool", "tensor_copy",
                    hsel[:, kt, 2:10].rearrange("p (s c) -> p s c", c=2),
                    hT[kt][:, 1024:1040].rearrange("p (s c) -> p s c", c=4)[:, :, 2:4], reads=[hT[kt]], writes=[hsel])
        for jb in range(11):
            wb = wup[jb % 2]
            load_wup(jb, wb)
            if half == 1:
                pst = k.PS[7]
                for which in range(2):
                    for kt in range(8):
                        P.i("pe", "matmul",
                            pst[0:10, which * 256:(which + 1) * 256], hsel[:, kt, 0:10], wb[:, kt, which * 256:(which + 1) * 256],
                            start=(kt == 0), stop=(kt == 7), reads=[hsel, wb], writes=[pst])
                P.i("act", "copy", st_sb[0:10, :], pst[0:10, :], reads=[pst], writes=[st_sb])
                for which in range(2):
                    c0 = which * DFF + jb * 256
                    P.dma("sp", d["ffn_p"].ap()[l, :, c0:c0 + 256], st_sb[0:2, which * 256:(which + 1) * 256], reads=[st_sb], writes=[d["ffn_p"]])
                    P.dma("sp", d["ffn_s"].ap()[l, :, :, c0:c0 + 256].rearrange("s j c -> (s j) c"), st_sb[2:10, which * 256:(which + 1) * 256], reads=[st_sb], writes=[d["ffn_s"]])
            for jj in range(2):
                j = jb * 2 + jj
                par = j % 2
                for which in range(2):
                    u = ub[par][which]
                    tix = j + 22 * which
                    if half == 0:
                        P.i("pool", "memset", u[:, 0:2], 0.0, writes=[u])
                    else:
                        P.i("pool", "tensor_copy", u[:, 0:2], halo[:, tix * 2:tix * 2 + 2], reads=[halo], writes=[u])
                        P.i("pool", "tensor_copy",
                            u[:, 1026:1050].rearrange("p (s c) -> p s c", c=6)[:, :, 0:2],
                            hist[:, :].rearrange("p (s j t) -> p s j t", j=2, t=44)[:, :, :, tix], reads=[hist], writes=[u])
                    for (c0, cn) in chunks(0, nprm):
                        ps = k.PS[psi[0] % 4]
                        psi[0] += 1
                        for kt in range(8):
                            P.i("pe", "matmul",
                                ps[:, 0:cn], wb[:, kt, which * 256 + jj * 128:which * 256 + (jj + 1) * 128], hT[kt][:, c0:c0 + cn],
                                start=(kt == 0), stop=(kt == 7), reads=[wb, hT[kt]], writes=[ps])
                        P.i("act", "copy", u[:, ucol0 + c0:ucol0 + c0 + cn], ps[:, 0:cn], reads=[ps], writes=[u])
                    if half == 1:
                        ps = k.PS[psi[0] % 4]
                        psi[0] += 1
                        for kt in range(8):
                            P.i("pe", "matmul",
                                ps[:, 0:16], wb[:, kt, which * 256 + jj * 128:which * 256 + (jj + 1) * 128], hT[kt][:, 1024:1040],
                                start=(kt == 0), stop=(kt == 7), reads=[wb, hT[kt]], writes=[ps])
                        P.i("act", "copy",
                            u[:, 1026:1050].rearrange("p (s c) -> p s c", c=6)[:, :, 2:6],
                            ps[:, 0:16].rearrange("p (s c) -> p s c", c=4), reads=[ps], writes=[u])
                    if half == 0:
                        P.i("pool", "tensor_copy", halo[:, tix * 2:tix * 2 + 2], u[:, 1024:1026], reads=[u], writes=[halo])
                    c = cb[which]
                    w0 = fpar[:, tix:tix + 1]
                    w1 = fpar[:, 44 + tix:44 + tix + 1]
                    w2 = fpar[:, 88 + tix:88 + tix + 1]
                    bb = fpar[:, 132 + tix:132 + tix + 1]
                    eng = "dve"
                    regions = [(lambda a, off: a[:, off:off + npr], lambda a: a[:, 0:npr])]
                    if half == 1:
                        regions.append((lambda a, off: a[:, 1026:1050].rearrange("p (s c) -> p s c", c=6)[:, :, off:off + 4],
                                        lambda a: a[:, 1024:1040].rearrange("p (s c) -> p s c", c=4)))
                    for (uin, cout) in regions:
                        P.i(eng, "tensor_scalar", cout(c), uin(u, 2), w2, bb, ALU.mult, ALU.add,
                             reads=[u, fpar], writes=[c])
                        P.i(eng, "scalar_tensor_tensor", cout(c), uin(u, 1), w1, cout(c), ALU.mult, ALU.add,
                             reads=[u, fpar, c], writes=[c])
                        P.i(eng, "scalar_tensor_tensor", cout(c), uin(u, 0), w0, cout(c), ALU.mult, ALU.add,
                             reads=[u, fpar, c], writes=[c])
                ntok = npr + (16 if half == 1 else 0)
                P.i("act", "activation", cb[0][:, 0:ntok], cb[0][:, 0:ntok], AF.Silu, reads=[cb[0]], writes=[cb[0]])
                P.i("dve", "tensor_tensor", act[j][:, 0:ntok], cb[0][:, 0:ntok], cb[1][:, 0:ntok], ALU.mult,
                     reads=[cb[0], cb[1]], writes=[act[j]])
        ntok = npr + (16 if half == 1 else 0)
        tok0 = 0 if half == 0 else 1024
        for dt in range(8):
            wd = wdn[dt % 2]
            load_wdn(dt, wd)
            for (c0, cn) in chunks(0, ntok):
                ps = k.PS[4 + (psi[0] % 2)]
                psi[0] += 1
                for j in range(22):
                    P.i("pe", "matmul", ps[:, 0:cn], wd[:, j, :], act[j][:, c0:c0 + cn], start=(j == 0), stop=(j == 21),
                         reads=[wd, act[j]], writes=[ps])
                P.i("dve", "tensor_tensor",
                    k.R[dt][:, tok0 + c0:tok0 + c0 + cn], ps[:, 0:cn], k.R[dt][:, tok0 + c0:tok0 + c0 + cn], ALU.add,
                    reads=[ps, k.R[dt]], writes=[k.R[dt]])
    P.release(m)


def final_out(k):
    P = k.P
    d = k.d
    m = P.mark()
    yt = [P.sb([128, 4, D], F32, "yt%d" % i) for i in range(2)]
    hn = [P.sb([128, 528], F32, "hn%d" % i) for i in range(8)]
    scratch = {"sq": [P.sb([128, 512], F32, "sq%d" % i) for i in range(2)], "rstd": P.sb([128, 512], F32, "rstd")}
    for g, (c0, cn) in enumerate(chunks(0, NT)):
        rmsnorm(k, 64, c0, cn, hn, 0, scratch)
        b = yt[g % 2]
        na = (cn + 127) // 128
        for a in range(na):
            tn = min(128, cn - a * 128)
            for dt in range(8):
                ps = k.PS[(a * 8 + dt) // 4 % 4]
                q = dt % 4
                P.i("pe", "transpose", ps[0:tn, q * 128:(q + 1) * 128], hn[dt][:, a * 128:a * 128 + tn], k.ident[:],
                     reads=[hn[dt], k.ident], writes=[ps])
                if q == 3:
                    h4 = dt // 4
                    eng = "dve" if h4 == 0 else "act"
                    if eng == "dve":
                        P.i("dve", "tensor_copy", b[0:tn, a, h4 * 512:(h4 + 1) * 512], ps[0:tn, :], reads=[ps], writes=[b])
                    else:
                        P.i("act", "copy", b[0:tn, a, h4 * 512:(h4 + 1) * 512], ps[0:tn, :], reads=[ps], writes=[b])
        if c0 < LP:
            P.dma("sp", d["y_p"].ap()[c0:c0 + cn, :].rearrange("(a p) c -> p a c", p=128), b[:], reads=[b], writes=[d["y_p"]])
        else:
            P.dma("sp", d["y_s"].ap(), b[0:16, 0, :], reads=[b], writes=[d["y_s"]])
    P.release(m)


def gdn_layer(k, l):
    P = k.P
    d = k.d
    j = l // 2
    m = P.mark()
    TG = 2304
    hT = [P.sb([128, NT], BF16, "gh%d" % i) for i in range(8)]
    scratch = {"sq": [P.sb([128, 512], F32, "sq%d" % i) for i in range(2)], "rstd": P.sb([128, 512], F32, "rstd")}
    sq, rstd = scratch["sq"], scratch["rstd"]
    rmsnorm(k, l * 8, 0, NT, hT, 0, scratch)
    PS = k.PS
    gcw = P.sb([128, 96], F32, "gcw")
    load_cols(k, d["gdn_conv_w"], d["gdn_conv_w"].ap()[j].rearrange("r (t p) -> (r t) p", p=128), 96, gcw, 0)
    gnw = P.sb([128, 1], F32, "gnw")
    load_cols(k, d["gdn_norm_w"], d["gdn_norm_w"].ap()[j:j + 1, :], 1, gnw, 0)
    hsel = P.sb([128, 8, 16], BF16, "ghsel")
    beta = P.sb([64, 36, 8], F32, "beta")
    Gc = P.sb([64, 36, 8], F32, "Gc")
    egl = P.sb([128, 288], F32, "egl")
    edec = P.sb([64, 36, 8], F32, "edec")
    ebg = P.sb([64, 36, 8], F32, "ebg")
    nbeta = P.sb([64, 36, 8], F32, "nbeta")
    mpre = P.mark()
    wab = P.sb([128, 8, 16], BF16, "wab")
    P.dma("pool", wab[:], d["gdn_w_in"].ap()[j, :, 4096:4112].rearrange("(kt p) c -> p kt c", p=128), reads=[d["gdn_w_in"]], writes=[wab])
    hsp = P.sb([128, 8, 4, 64], BF16, "hsp")
    P.i("pool", "memset", hsp[:], 0.0, writes=[hsp])
    for kt in range(8):
        P.i("pool", "tensor_copy", hsp[:, kt, :, 0:4], hT[kt][:, LP:NT].rearrange("p (s c) -> p s c", c=4), reads=[hT[kt]], writes=[hsp])
        P.i("dve", "tensor_copy", hsel[:, kt, 0:3], hT[kt][:, LP - 3:LP], reads=[hT[kt]], writes=[hsel])
        P.i("dve", "tensor_copy", hsel[:, kt, 3:15].rearrange("p (s c) -> p s c", c=3),
            hT[kt][:, LP:NT].rearrange("p (s c) -> p s c", c=4)[:, :, 1:4], reads=[hT[kt]], writes=[hsel])
    ab_all = P.sb([64, 36, 16], F32, "ab_all")
    for n in range(36):
        ps = PS[0] if n < 32 else PS[1]
        c0 = (n % 32) * 16
        for kt in range(8):
            lhsT = hT[kt][:, n * 64:(n + 1) * 64] if n < 32 else hsp[:, kt, n - 32, :]
            P.i("pe", "matmul", ps[0:64, c0:c0 + 16], lhsT, wab[:, kt, :], start=(kt == 0), stop=(kt == 7),
                reads=[hT[kt], hsp, wab], writes=[ps])
    P.i("dve", "tensor_copy", ab_all[:, 0:32, :], PS[0][0:64, 0:512].rearrange("p (n c) -> p n c", c=16), reads=[PS[0]], writes=[ab_all])
    P.i("dve", "tensor_copy", ab_all[:, 32:36, :], PS[1][0:64, 0:64].rearrange("p (n c) -> p n c", c=16), reads=[PS[1]], writes=[ab_all])
    alog = P.sb([64, 8], F32, "alog")
    dtb = P.sb([64, 8], F32, "dtb")
    P.dma("sp", alog[:], d["gdn_a_log"].ap()[j].partition_broadcast(64), reads=[d["gdn_a_log"]], writes=[alog])
    P.dma("sp", dtb[:], d["gdn_dt_bias"].ap()[j].partition_broadcast(64), reads=[d["gdn_dt_bias"]], writes=[dtb])
    P.i("act", "activation", alog[:], alog[:], AF.Exp, reads=[alog], writes=[alog])
    P.i("dve", "tensor_scalar_mul", alog[:], alog[:], -1.0, reads=[alog], writes=[alog])
    g_all = P.sb([64, 36, 8], F32, "g_all")
    bc8 = lambda t: t[:, :].unsqueeze(1).to_broadcast([64, 36, 8])
    P.i("dve", "tensor_tensor", g_all[:], ab_all[:, :, 0:8], bc8(dtb), ALU.add, reads=[ab_all, dtb], writes=[g_all])
    P.i("act", "activation", g_all[:], g_all[:], AF.Exp, reads=[g_all], writes=[g_all])
    P.i("dve", "tensor_scalar_add", g_all[:], g_all[:], 1.0, reads=[g_all], writes=[g_all])
    P.i("act", "activation", g_all[:], g_all[:], AF.Ln, reads=[g_all], writes=[g_all])
    P.i("dve", "tensor_tensor", g_all[:], g_all[:], bc8(alog), ALU.mult, reads=[g_all, alog], writes=[g_all])
    P.i("act", "activation", beta[:], ab_all[:, :, 8:16], AF.Sigmoid, reads=[ab_all], writes=[beta])
    P.i("dve", "tensor_scalar_mul", g_all[:, 32:36, :], g_all[:, 32:36, :], k.pm4[:, 0:1], reads=[g_all, k.pm4], writes=[g_all])
    P.i("dve", "tensor_scalar_mul", beta[:, 32:36, :], beta[:, 32:36, :], k.pm4[:, 0:1], reads=[beta, k.pm4], writes=[beta])
    fl = lambda t: t[:].rearrange("p n h -> p (n h)")
    P.i("pe", "matmul", PS[0][0:64, 0:288], k.ltri[:], fl(g_all), start=True, stop=True, reads=[k.ltri, g_all], writes=[PS[0]])
    P.i("dve", "tensor_copy", fl(Gc), PS[0][0:64, 0:288], reads=[PS[0]], writes=[Gc])
    P.i("pe", "matmul", PS[1][:, 0:288], k.e63[:], fl(Gc), start=True, stop=True, reads=[k.e63, Gc], writes=[PS[1]])
    P.i("act", "activation", egl[:], PS[1][:, 0:288], AF.Exp, reads=[PS[1]], writes=[egl])
    P.i("dve", "tensor_tensor", fl(edec), PS[1][0:64, 0:288], fl(Gc), ALU.subtract, reads=[PS[1], Gc], writes=[edec])
    P.i("act", "activation", edec[:], edec[:], AF.Exp, reads=[edec], writes=[edec])
    P.i("act", "activation", ebg[:], Gc[:], AF.Exp, reads=[Gc], writes=[ebg])
    P.i("dve", "tensor_tensor", ebg[:], ebg[:], beta[:], ALU.mult, reads=[ebg, beta], writes=[ebg])
    P.i("dve", "tensor_scalar_mul", nbeta[:], beta[:], -1.0, reads=[beta], writes=[nbeta])
    P.release(mpre)

    wqkvz = [P.sb([128, 8, 128], BF16, "gw%d" % i) for i in range(4)]
    wout = P.sb([128, D], BF16, "gwout")
    xb = P.sb([128, 2080], F32, "gxb")
    X = [P.sb([128, TG], BF16, "gX%d" % i) for i in range(3)]
    for t in X:
        P.i("pool", "memset", t[:, LP:TG], 0.0, writes=[t])
    qgb = [P.sb([128, 512], BF16, "qgb%d" % i) for i in range(5)]
    zsb = P.sb([128, NT], BF16, "zsb")
    ogb = P.sb([128, NT], BF16, "ogb")
    cf = P.sb([128, 512], F32, "gcf")
    hist12 = P.sb([128, 12], F32, "hist12")
    st_sb = P.sb([16, 128], F32, "gst")
    G = {}
    for nm in ["negD", "decL", "decU", "tmp", "Q0", "Q1", "R0", "R1", "Tt", "egrow"]:
        G[nm] = P.sb([64 if nm != "egrow" else 128, 512], F32, "g" + nm)
    DG = P.sb([64, 512], F32, "gDG")
    Ttb2 = [P.sb([64, 512], BF16, "gTtb%d" % i) for i in range(2)]
    kb = P.sb([64, 8, 128], BF16, "gkb")
    kdec2 = [P.sb([64, 8, 128], BF16, "gkdec%d" % i) for i in range(2)]
    vb2 = [P.sb([64, 8, 128], BF16, "gvb%d" % i) for i in range(2)]
    nwk2 = [P.sb([128, 512], BF16, "gnwk%d" % i) for i in range(2)]
    qkm2 = [P.sb([64, 512], BF16, "gqkm%d" % i) for i in range(2)]
    S = P.sb([128, 128], F32, "gS")
    Sb = P.sb([128, 128], BF16, "gSb")
    ub = P.sb([64, 128], BF16, "gub")
    on = P.sb([128, 512], F32, "gon")
    i64b = k.ident[0:64, 0:64].unsqueeze(1).to_broadcast([64, 8, 64])
    v3 = lambda ap: ap.rearrange("p (n c) -> p n c", c=64)

    for h in range(8):
        for part in range(4):
            c0 = part * 1024 + h * 128
            P.dma("pool", wqkvz[part][:], d["gdn_w_in"].ap()[j, :, c0:c0 + 128].rearrange("(kt p) c -> p kt c", p=128),
                  reads=[d["gdn_w_in"]], writes=[wqkvz[part]])
        P.dma("pool", wout[:], d["gdn_w_out"].ap()[j, h * 128:(h + 1) * 128, :], reads=[d["gdn_w_out"]], writes=[wout])
        for part in range(4):
            w = wqkvz[part]
            col0 = part * 1024 + h * 128
            tix = part * 8 + h
            if part < 3:
                pst = PS[7]
                for kt in range(8):
                    P.i("pe", "matmul", pst[0:15, 256:384], hsel[:, kt, 0:15], w[:, kt, :], start=(kt == 0), stop=(kt == 7),
                        reads=[hsel, w], writes=[pst])
                P.i("act", "copy", st_sb[0:15, :], pst[0:15, 256:384], reads=[pst], writes=[st_sb])
                P.dma("sp", d["gconv_p"].ap()[j, :, col0:col0 + 128], st_sb[0:3, :], reads=[st_sb], writes=[d["gconv_p"]])
                P.dma("sp", d["gconv_s"].ap()[j, :, :, col0:col0 + 128].rearrange("s r c -> (s r) c"), st_sb[3:15, :],
                      reads=[st_sb], writes=[d["gconv_s"]])
                load_cols(k, d["state_gdn_conv"], d["state_gdn_conv"].ap()[j, :, :, col0:col0 + 128].rearrange("s r c -> (s r) c"),
                          12, hist12, 0)
                P.i("pool", "memset", xb[:, 0:3], 0.0, writes=[xb])
                P.i("pool", "tensor_copy", xb[:, 2051:2079].rearrange("p (s c) -> p s c", c=7)[:, :, 0:3],
                    hist12[:, :].rearrange("p (s c) -> p s c", c=3), reads=[hist12], writes=[xb])
            for ci, (c0, cn) in enumerate(chunks(0, NT)):
                ps = PS[ci % 2]
                for kt in range(8):
                    P.i("pe", "matmul", ps[:, 0:cn], w[:, kt, :], hT[kt][:, c0:c0 + cn], start=(kt == 0), stop=(kt == 7),
                        reads=[w, hT[kt]], writes=[ps])
                if part == 3:
                    P.i("act", "activation", zsb[:, c0:c0 + cn], ps[:, 0:cn], AF.Silu, reads=[ps], writes=[zsb])
                elif c0 < LP:
                    P.i("act", "copy", xb[:, 3 + c0:3 + c0 + cn], ps[:, 0:cn], reads=[ps], writes=[xb])
                else:
                    P.i("act", "copy", xb[:, 2051:2079].rearrange("p (s c) -> p s c", c=7)[:, :, 3:7],
                        ps[:, 0:16].rearrange("p (s c) -> p s c", c=4), reads=[ps], writes=[xb])
            if part == 3:
                continue
            wc = [gcw[:, r * 24 + tix:r * 24 + tix + 1] for r in range(4)]
            for (c0, cn) in chunks(0, NT):
                if c0 < LP:
                    src = lambda off: xb[:, c0 + off:c0 + off + cn]
                    cfv = cf[:, 0:cn]
                    dst = X[part][:, c0:c0 + cn]
                else:
                    src = lambda off: xb[:, 2051:2079].rearrange("p (s c) -> p s c", c=7)[:, :, off:off + 4]
                    cfv = cf[:, 0:16].rearrange("p (s c) -> p s c", c=4)
                    dst = X[part][:, LP:TG].rearrange("p (s c) -> p s c", c=64)[:, :, 0:4]
                P.i("dve", "tensor_scalar_mul", cfv, src(3), wc[3], reads=[xb, gcw], writes=[cf])
                for r in range(3):
                    P.i("dve", "scalar_tensor_tensor", cfv, src(r), wc[r], cfv, ALU.mult, ALU.add, reads=[xb, gcw, cf], writes=[cf])
                P.i("act", "activation", cf[:, 0:cn], cf[:, 0:cn], AF.Silu, reads=[cf], writes=[cf])
                if part == 2:
                    P.i("act", "copy", dst, cfv, reads=[cf], writes=[X[part]])
                else:
                    sqb = sq[0]
                    P.i("act", "activation", sqb[:, 0:cn], cf[:, 0:cn], AF.Square, reads=[cf], writes=[sqb])
                    P.i("pe", "matmul", PS[6][:, 0:cn], k.ones[:], sqb[:, 0:cn], start=True, stop=True, reads=[sqb, k.ones], writes=[PS[6]])
                    P.i("dve", "tensor_scalar_add", rstd[:, 0:cn], PS[6][:, 0:cn], EPS, reads=[PS[6]], writes=[rstd])
                    P.i("act", "activation", rstd[:, 0:cn], rstd[:, 0:cn], AF.Sqrt, reads=[rstd], writes=[rstd])
                    P.i("dve", "reciprocal", rstd[:, 0:cn], rstd[:, 0:cn], reads=[rstd], writes=[rstd])
                    rv = rstd[:, 0:cn] if c0 < LP else rstd[:, 0:16].rearrange("p (s c) -> p s c", c=4)
                    P.i("dve", "scalar_tensor_tensor", dst, cfv, (128.0 ** -0.5) if part == 0 else 1.0, rv, ALU.mult, ALU.mult,
                        reads=[cf, rstd], writes=[X[part]])
        qn, kn, vn = X
        def pre(g):
            nch = 8 if g < 4 else 4
            W = nch * 64
            n0 = g * 8
            Ttb, kdec, vb, nwk, qkm = Ttb2[g % 2], kdec2[g % 2], vb2[g % 2], nwk2[g % 2], qkm2[g % 2]
            gcol = lambda t: t[:, n0:n0 + nch, h].unsqueeze(2)
            cs = lambda c: slice(g * 512 + c * 64, g * 512 + (c + 1) * 64)
            P.i("dve", "tensor_tensor", v3(DG[:, 0:W]), i64b[:, 0:nch, :], gcol(Gc).to_broadcast([64, nch, 64]), ALU.mult,
                reads=[k.ident, Gc], writes=[DG])
            P.i("pe", "matmul", PS[6][:, 0:W], k.ones[0:64, :], DG[:, 0:W], start=True, stop=True, reads=[k.ones, DG], writes=[PS[6]])
            P.i("act", "activation", G["egrow"][:, 0:W], PS[6][:, 0:W], AF.Exp, reads=[PS[6]], writes=[G["egrow"]])
            yield
            P.i("dve", "tensor_tensor", qgb[g][:, 0:W], qn[:, g * 512:g * 512 + W], G["egrow"][:, 0:W], ALU.mult,
                reads=[qn, G["egrow"]], writes=[qgb[g]])
            P.i("dve", "tensor_tensor", v3(G["negD"][:, 0:W]), v3(PS[6][0:64, 0:W]), gcol(Gc).to_broadcast([64, nch, 64]), ALU.subtract,
                reads=[PS[6], Gc], writes=[G["negD"]])
            P.i("dve", "tensor_scalar_max", G["decL"][:, 0:W], G["negD"][:, 0:W], 0.0, reads=[G["negD"]], writes=[G["decL"]])
            P.i("act", "activation", G["decL"][:, 0:W], G["decL"][:, 0:W], AF.Exp, scale=-1.0, reads=[G["decL"]], writes=[G["decL"]])
            P.i("pool", "tensor_tensor", v3(G["decL"][:, 0:W]), v3(G["decL"][:, 0:W]), k.msl[:, :].unsqueeze(1).to_broadcast([64, nch, 64]), ALU.mult,
                reads=[G["decL"], k.msl], writes=[G["decL"]])
            P.i("dve", "tensor_scalar_min", G["decU"][:, 0:W], G["negD"][:, 0:W], 0.0, reads=[G["negD"]], writes=[G["decU"]])
            P.i("act", "activation", G["decU"][:, 0:W], G["decU"][:, 0:W], AF.Exp, reads=[G["decU"]], writes=[G["decU"]])
            P.i("pool", "tensor_tensor", v3(G["decU"][:, 0:W]), v3(G["decU"][:, 0:W]), k.ltri[:, :].unsqueeze(1).to_broadcast([64, nch, 64]), ALU.mult,
                reads=[G["decU"], k.ltri], writes=[G["decU"]])
            yield
            for c in range(nch):
                P.i("pe", "matmul", PS[0][0:64, c * 64:(c + 1) * 64], kn[:, cs(c)], kn[:, cs(c)], start=True, stop=True, reads=[kn], writes=[PS[0]])
            P.i("dve", "tensor_tensor", G["tmp"][:, 0:W], PS[0][0:64, 0:W], G["decL"][:, 0:W], ALU.mult, reads=[PS[0], G["decL"]], writes=[G["tmp"]])
            P.i("pool", "tensor_tensor", v3(G["R0"][:, 0:W]), v3(G["tmp"][:, 0:W]), gcol(nbeta).to_broadcast([64, nch, 64]), ALU.mult,
                reads=[G["tmp"], nbeta], writes=[G["R0"]])
            yield
            for c in range(nch):
                P.i("pe", "transpose", PS[1][0:64, c * 64:(c + 1) * 64], G["R0"][:, c * 64:(c + 1) * 64], k.ident[0:64, 0:64],
                    reads=[G["R0"], k.ident], writes=[PS[1]])
            P.i("act", "copy", G["Q0"][:, 0:W], PS[1][0:64, 0:W], reads=[PS[1]], writes=[G["Q0"]])
            P.i("dve", "tensor_tensor", v3(G["Tt"][:, 0:W]), v3(PS[1][0:64, 0:W]), i64b[:, 0:nch, :], ALU.add, reads=[PS[1], k.ident], writes=[G["Tt"]])
            yield
            for step in range(1, 6):
                cur, nxt = str((step - 1) % 2), str(step % 2)
                Qc, Rc, Qn, Rn = G["Q" + cur], G["R" + cur], G["Q" + nxt], G["R" + nxt]
                for c in range(nch):
                    sl = slice(c * 64, (c + 1) * 64)
                    if step < 5:
                        P.i("pe", "matmul", PS[1][0:64, sl], Rc[:, sl], Qc[:, sl], start=True, stop=True, reads=[Rc, Qc], writes=[PS[1]])
                    P.i("pe", "matmul", PS[2][0:64, sl], Qc[:, sl], Rc[:, sl], start=True, stop=True, reads=[Rc, Qc], writes=[PS[2]])
                if step < 5:
                    P.i("act", "copy", Qn[:, 0:W], PS[1][0:64, 0:W], reads=[PS[1]], writes=[Qn])
                P.i("dve", "tensor_copy", Rn[:, 0:W], PS[2][0:64, 0:W], reads=[PS[2]], writes=[Rn])
                yield
                for c in range(nch):
                    sl = slice(c * 64, (c + 1) * 64)
                    P.i("pe", "matmul", PS[0][0:64, sl], Rn[:, sl], G["Tt"][:, sl], start=True, stop=True, reads=[Rn, G["Tt"]], writes=[PS[0]])
                P.i("dve", "tensor_tensor", G["Tt"][:, 0:W], G["Tt"][:, 0:W], PS[0][0:64, 0:W], ALU.add, reads=[PS[0], G["Tt"]], writes=[G["Tt"]])
                yield
            P.i("act", "copy", Ttb[:, 0:W], G["Tt"][:, 0:W], reads=[G["Tt"]], writes=[Ttb])
            yield
            pk = PS[3][0:64, :].bitcast(BF16)
            pv = PS[0][0:64, :].bitcast(BF16)
            for c in range(nch):
                P.i("pe", "transpose", pk[:, c * 128:(c + 1) * 128], kn[:, cs(c)], k.identb[:], reads=[kn, k.identb], writes=[PS[3]])
                P.i("pe", "transpose", pv[:, c * 128:(c + 1) * 128], vn[:, cs(c)], k.identb[:], reads=[vn, k.identb], writes=[PS[0]])
            p3 = lambda ap: ap.rearrange("p (n c) -> p n c", c=128)
            bc = lambda t: gcol(t).to_broadcast([64, nch, 128])
            P.i("dve", "tensor_tensor", kb[:, 0:nch, :], p3(pk[:, 0:nch * 128]), bc(ebg), ALU.mult, reads=[PS[3], ebg], writes=[kb])
            P.i("dve", "tensor_tensor", kdec[:, 0:nch, :], p3(pk[:, 0:nch * 128]), bc(edec), ALU.mult, reads=[PS[3], edec], writes=[kdec])
            P.i("dve", "tensor_tensor", vb[:, 0:nch, :], p3(pv[:, 0:nch * 128]), bc(beta), ALU.mult, reads=[PS[0], beta], writes=[vb])
            yield
            for c in range(nch):
                sl = slice(c * 64, (c + 1) * 64)
                P.i("pe", "matmul", PS[2][:, sl], kb[:, c, :], Ttb[:, sl], start=True, stop=True, reads=[kb, Ttb], writes=[PS[2]])
                P.i("pe", "matmul", PS[1][0:64, sl], kn[:, cs(c)], qn[:, cs(c)], start=True, stop=True, reads=[kn, qn], writes=[PS[1]])
            P.i("act", "activation", nwk[:, 0:W], PS[2][:, 0:W], AF.Copy, scale=-1.0, reads=[PS[2]], writes=[nwk])
            P.i("dve", "tensor_tensor", qkm[:, 0:W], PS[1][0:64, 0:W], G["decU"][:, 0:W], ALU.mult, reads=[PS[1], G["decU"]], writes=[qkm])
            yield

        def rec(g):
            nch = 8 if g < 4 else 4
            W = nch * 64
            n0 = g * 8
            Ttb, kdec, vb, nwk, qkm = Ttb2[g % 2], kdec2[g % 2], vb2[g % 2], nwk2[g % 2], qkm2[g % 2]
            gcol = lambda t: t[:, n0:n0 + nch, h].unsqueeze(2)
            cs = lambda c: slice(g * 512 + c * 64, g * 512 + (c + 1) * 64)
            for c in range(nch):
                n = n0 + c
                sl = slice(c * 64, (c + 1) * 64)
                if n == 0:
                    P.i("pool", "memset", S[:], 0.0, writes=[S])
                    P.i("pool", "memset", Sb[:], 0.0, writes=[Sb])
                elif n >= 32:
                    P.dma("sp", S[:], d["state_gdn"].ap()[j, n - 32, h], reads=[d["state_gdn"]], writes=[S])
                    P.i("act", "copy", Sb[:], S[:], reads=[S], writes=[Sb])
                P.i("pe", "matmul", PS[7][0:64, 0:128], Ttb[:, sl], vb[:, c, :], start=True, stop=False, reads=[Ttb, vb], writes=[PS[7]])
                P.i("pe", "matmul", PS[7][0:64, 0:128], nwk[:, sl], Sb[:], start=False, stop=True, reads=[nwk, Sb], writes=[PS[7]])
                P.i("act", "copy", ub[:], PS[7][0:64, 0:128], reads=[PS[7]], writes=[ub])
                yield
                P.i("pe", "matmul", PS[5][:, sl], Sb[:], qgb[g][:, sl], start=True, stop=False, reads=[Sb, qgb[g]], writes=[PS[5]])
                P.i("pe", "matmul", PS[5][:, sl], ub[:], qkm[:, sl], start=False, stop=True, reads=[ub, qkm], writes=[PS[5]])
                P.i("pe", "matmul", PS[4][:, 0:128], kdec[:, c, :], ub[:], start=True, stop=True, reads=[kdec, ub], writes=[PS[4]])
                P.i("dve", "scalar_tensor_tensor", S[:], S[:], egl[:, n * 8 + h:n * 8 + h + 1], PS[4][:, 0:128], ALU.mult, ALU.add,
                    reads=[S, egl, PS[4]], writes=[S])
                P.i("act", "copy", Sb[:], S[:], reads=[S], writes=[Sb])
                if n == 31:
                    P.dma("sp", d["gdn_p"].ap()[j, h], S[:], reads=[S], writes=[d["gdn_p"]])
                elif n >= 32:
                    P.dma("sp", d["gdn_s"].ap()[j, n - 32, h], S[:], reads=[S], writes=[d["gdn_s"]])
                yield
            P.i("act", "activation", sq[1][:, 0:W], PS[5][:, 0:W], AF.Square, reads=[PS[5]], writes=[sq[1]])
            P.i("pe", "matmul", PS[6][:, 0:W], k.ones[:], sq[1][:, 0:W], start=True, stop=True, reads=[sq[1], k.ones], writes=[PS[6]])
            P.i("dve", "tensor_scalar", rstd[:, 0:W], PS[6][:, 0:W], 1.0 / 128, EPS, ALU.mult, ALU.add, reads=[PS[6]], writes=[rstd])
            P.i("act", "activation", rstd[:, 0:W], rstd[:, 0:W], AF.Sqrt, reads=[rstd], writes=[rstd])
            yield
            P.i("dve", "reciprocal", rstd[:, 0:W], rstd[:, 0:W], reads=[rstd], writes=[rstd])
            P.i("dve", "scalar_tensor_tensor", on[:, 0:W], PS[5][:, 0:W], gnw[:, 0:1], rstd[:, 0:W], ALU.mult, ALU.mult,
                reads=[PS[5], gnw, rstd], writes=[on])
            if g < 4:
                P.i("dve", "tensor_tensor", ogb[:, g * 512:(g + 1) * 512], on[:, 0:512], zsb[:, g * 512:(g + 1) * 512], ALU.mult,
                    reads=[on, zsb], writes=[ogb])
            else:
                P.i("dve", "tensor_tensor", ogb[:, LP:NT].rearrange("p (s c) -> p s c", c=4), v3(on[:, 0:256])[:, :, 0:4],
                    zsb[:, LP:NT].rearrange("p (s c) -> p s c", c=4), ALU.mult, reads=[on, zsb], writes=[ogb])
            yield

        run_il([pre(0)])
        for g in range(5):
            run_il([rec(g), pre(g + 1) if g < 4 else None])
        for dt in range(8):
            for ci, (c0, cn) in enumerate(chunks(0, NT)):
                ps = PS[(dt * 5 + ci) % 2]
                P.i("pe", "matmul", ps[:, 0:cn], wout[:, dt * 128:(dt + 1) * 128], ogb[:, c0:c0 + cn], start=True, stop=True,
                    reads=[wout, ogb], writes=[ps])
                P.i("dve", "tensor_tensor", k.R[dt][:, c0:c0 + cn], ps[:, 0:cn], k.R[dt][:, c0:c0 + cn], ALU.add,
                    reads=[ps, k.R[dt]], writes=[k.R[dt]])
    P.release(m)


SCALE = 64.0 ** -0.5


def rope_apply(P, out, x, tab, nh, np_, tmps, tabbuf):
    t1, t2 = tmps
    cosb = tab[:, 0:32].unsqueeze(1).to_broadcast([np_, nh, 32])
    sinb = tab[:, 32:64].unsqueeze(1).to_broadcast([np_, nh, 32])
    x1, x2 = x[:, :, 0:32], x[:, :, 32:64]
    a, b = t1[0:np_, 0:nh, :], t2[0:np_, 0:nh, :]
    P.i("dve", "tensor_tensor", a, x1, cosb, ALU.mult, reads=x.bufs + [tabbuf], writes=[t1])
    P.i("dve", "tensor_tensor", b, x2, sinb, ALU.mult, reads=x.bufs, writes=[t2])
    P.i("dve", "tensor_tensor", out[:, :, 0:32], a, b, ALU.subtract, reads=[t1, t2], writes=out.bufs)
    P.i("dve", "tensor_tensor", a, x2, cosb, ALU.mult, reads=x.bufs + [t2], writes=[t1])
    P.i("dve", "tensor_tensor", b, x1, sinb, ALU.mult, reads=x.bufs + [t1], writes=[t2])
    P.i("dve", "tensor_tensor", out[:, :, 32:64], a, b, ALU.add, reads=[t1, t2], writes=out.bufs)


class V:
    def __init__(self, ap, bufs):
        self.ap = ap
        self.bufs = bufs

    def __getitem__(self, idx):
        return self.ap[idx]


def nsa_layer(k, l):
    mm = k.P.mark()
    try:
        _nsa_layer(k, l)
    except StopNSA:
        k.P.release(mm)


def _nsa_layer(k, l):
    P = k.P
    d = k.d
    j = l // 2
    PS = k.PS
    m0 = P.mark()
    hTs = P.sb([128, 8, 16], BF16, "nhTs")
    m1 = P.mark()
    hT = [P.sb([128, NT], BF16, "nh%d" % i) for i in range(8)]
    m2 = P.mark()
    scratch = {"sq": [P.sb([128, 512], F32, "sq%d" % i) for i in range(2)], "rstd": P.sb([128, 512], F32, "rstd")}
    rmsnorm(k, l * 8, 0, NT, hT, 0, scratch)
    P.release(m2)
    for kt in range(8):
        P.i("pool", "tensor_copy", hTs[:, kt, :], hT[kt][:, LP:NT], reads=[hT[kt]], writes=[hTs])
    ropeT = P.sb([128, 16, 64], F32, "ropeT")
    P.dma("sp", ropeT[:], d["rope_tab"].ap()[0:16].rearrange("t p c -> p t c"), reads=[d["rope_tab"]], writes=[ropeT])
    eexp = P.sb([32, LP], BF16, "eexp")
    P.i("pool", "memset", eexp[:], 1.0, writes=[eexp])
    P.i("pool", "affine_select", out=eexp[:], in_=eexp[:], pattern=[[1, LP]], compare_op=ALU.is_ge, fill=0.0, base=0,
        channel_multiplier=-64, reads=[eexp], writes=[eexp])
    P.i("pool", "affine_select", out=eexp[:], in_=eexp[:], pattern=[[-1, LP]], compare_op=ALU.is_ge, fill=0.0, base=63,
        channel_multiplier=64, reads=[eexp], writes=[eexp])
    niota = P.sb([128, 32], F32, "niota")
    curcol = P.sb([128, 16], F32, "curcol")
    hfcol = P.sb([128, 1], F32, "hfcol")
    tiota = P.sb([32, 128], F32, "tiota")
    thr = P.sb([32, 16], F32, "thr")
    P.i("pool", "iota", niota[:], pattern=[[1, 32]], base=0, channel_multiplier=0, allow_small_or_imprecise_dtypes=True, writes=[niota])
    P.i("pool", "iota", curcol[:], pattern=[[2, 16]], base=0, channel_multiplier=0, allow_small_or_imprecise_dtypes=True, writes=[curcol])
    P.i("pool", "memset", hfcol[0:64, :], 0.0, writes=[hfcol])
    P.i("pool", "memset", hfcol[64:128, :], 1.0, writes=[hfcol])
    P.i("dve", "tensor_scalar_add", curcol[:], curcol[:], hfcol[:, 0:1], reads=[curcol, hfcol], writes=[curcol])
    P.i("pool", "iota", tiota[:], pattern=[[1, 128]], base=0, channel_multiplier=0, allow_small_or_imprecise_dtypes=True, writes=[tiota])
    P.i("pool", "iota", thr[:], pattern=[[-128, 16]], base=63, channel_multiplier=64, allow_small_or_imprecise_dtypes=True, writes=[thr])
    cm_ = P.sb([32, 128], F32, "cm_")
    cand = P.sb([128, 32], F32, "cand")
    wc = P.sb([64, 64, 2, 64], BF16, "wc")
    for lh in range(4):
        P.dma("pool", wc[:, lh * 16:(lh + 1) * 16], d["nsa_cmp_w"].ap()[j, lh * 16:(lh + 1) * 16].rearrange("l c d e -> d l c e"),
              reads=[d["nsa_cmp_w"]], writes=[wc])
    peT = P.sb([64, 2, 64], F32, "peT")
    stg64 = P.sb([128, 64], F32, "stg64")
    P.dma("sp", stg64[:], d["nsa_cmp_pe"].ap()[j].rearrange("l c d -> (l c) d"), reads=[d["nsa_cmp_pe"]], writes=[stg64])
    P.i("pe", "transpose", PS[6][0:64, 0:128], stg64[:], k.ident[:], reads=[stg64, k.ident], writes=[PS[6]])
    P.i("dve", "tensor_copy", peT[:], PS[6][0:64, 0:128].rearrange("p (l c) -> p c l", c=2), reads=[PS[6]], writes=[peT])

    wg = P.sb([128, 8, 652], BF16, "wg")
    wog = P.sb([128, 2, D], BF16, "wog")
    tm = P.sb([128, 652], F32, "tm")
    tr = P.sb([128, 384], F32, "trr")
    tmb = P.sb([128, 640], BF16, "tmb")
    trb = P.sb([128, 384], BF16, "trb")
    rt = [P.sb([128, 6, 32], F32, "rt%d" % i) for i in range(2)]
    qT2 = [P.sb([64, 8, 128], BF16, "qT%d" % i) for i in range(2)]
    KselT = [P.sb([64, 128], BF16, "KselT%d" % i) for i in range(16)]
    KwinT = [P.sb([64, 128], BF16, "KwinT%d" % i) for i in range(16)]
    XkT = P.sb([64, LP], BF16, "XkT")
    XvT = P.sb([64, LP], BF16, "XvT")
    Vsel = [P.sb([128, 66], BF16, "Vsel%d" % i) for i in range(16)]
    Vwin = [P.sb([128, 66], BF16, "Vwin%d" % i) for i in range(16)]
    for t in Vsel + Vwin:
        P.i("pool", "memset", t[:], 1.0, writes=[t])
    gt2 = [P.sb([128, 12], F32, "gt%d" % i) for i in range(2)]
    CkT = P.sb([64, 32], BF16, "CkT")
    CvA = P.sb([32, 64], BF16, "CvA")
    Ec = P.sb([32, 512], F32, "Ec")
    rs = P.sb([32, 512], F32, "rs")
    pb = P.sb([32, 512], BF16, "pb")
    oc2 = [P.sb([128, 256], F32, "oc%d" % i) for i in range(2)]
    impT = P.sb([32, 128], F32, "impT")
    impc = P.sb([128, 32], F32, "impc")
    tmp32 = P.sb([128, 32], F32, "tmp32")
    sel = P.sb([128, 32], F32, "sel")
    m8 = P.sb([128, 16], F32, "m8")
    smT2 = [P.sb([32, 128], BF16, "smT%d" % i) for i in range(2)]
    Eb = [P.sb([128, 512], BF16, "Eb%d" % i) for i in range(2)]
    Mb = [P.sb([128, 128], BF16, "Mb%d" % i) for i in range(2)]
    cf = P.sb([128, 8], F32, "ncf")
    om = P.sb([128, 256], F32, "om")
    om2 = P.sb([128, 256], F32, "om2")
    omb = P.sb([128, 256], BF16, "omb")
    omT = P.sb([128, 2, 128], BF16, "omT")
    cnt = [0]
    ck("n_setup")

    for g in range(0 if not SKIP_PROMPT[0] else 4, 4):
        srcs = [(g * 256, 256, 0), (1024 + 2 * 256 + g * 64, 64, 256), (1024 + 4 * 256 + g * 64, 64, 320),
                (1024 + 0 * 256 + g * 64, 64, 384), (1024 + 1 * 256 + g * 64, 64, 448), (1024 + 3 * 256 + g * 64, 64, 512),
                (1024 + 5 * 256 + g * 64, 64, 576), (2560 + g * 12, 12, 640)]
        for (c0, w_, o0) in srcs:
            P.dma("pool", wg[:, :, o0:o0 + w_], d["nsa_w_in"].ap()[j, :, c0:c0 + w_].rearrange("(kt p) c -> p kt c", p=128),
                  reads=[d["nsa_w_in"]], writes=[wg])
        P.dma("pool", wog[:], d["nsa_w_out"].ap()[j, g * 256:(g + 1) * 256, :].rearrange("(a p) c -> p a c", p=128),
              reads=[d["nsa_w_out"]], writes=[wog])
        for i in range(16):
            tok = slice(i * 128, (i + 1) * 128)
            ps = PS[i % 2]
            for kt in range(8):
                P.i("pe", "matmul", ps[:, 0:128], hT[kt][:, tok], wg[:, kt, 384:512], start=(kt == 0), stop=(kt == 7),
                    reads=[hT[kt], wg], writes=[ps])
            P.i("act", "copy", tm[:, 384:512], ps[:, 0:128], reads=[ps], writes=[tm])
            P.i("dve", "tensor_copy", tmb[:, 384:512], ps[:, 0:128], reads=[ps], writes=[tmb])
            for c in range(2):
                P.dma("sp", d["cmp_p"].ap()[j, tok, c * 256 + g * 64:c * 256 + (g + 1) * 64], tm[:, 384 + c * 64:448 + c * 64],
                      reads=[tm], writes=[d["cmp_p"]])
            pt = PS[2][0:64, :].bitcast(BF16)
            for c in range(2):
                P.i("pe", "transpose", pt[:, c * 128:(c + 1) * 128], tmb[:, 384 + c * 64:448 + c * 64], k.identb[:],
                    reads=[tmb, k.identb], writes=[PS[2]])
            for c, XT in enumerate((XkT, XvT)):
                P.i("dve", "tensor_tensor", XT[:, tok].rearrange("p (n l) -> p n l", l=64),
                    pt[:, c * 128:(c + 1) * 128].rearrange("p (n l) -> p n l", l=64),
                    peT[:, c, :].unsqueeze(1).to_broadcast([64, 2, 64]), ALU.add, reads=[PS[2], peT], writes=[XT])
        for ll in range(64):
            P.i("pe", "matmul", PS[3][0:64, 0:32], wc[:, ll, 0, :], XkT[:, :].rearrange("p (n l) -> p n l", l=64)[:, :, ll],
                start=(ll == 0), stop=(ll == 63), reads=[wc, XkT], writes=[PS[3]])
        P.i("act", "copy", CkT[:], PS[3][0:64, 0:32], reads=[PS[3]], writes=[CkT])
        for ll in range(64):
            P.i("pe", "matmul", PS[3][0:32, 64:128], XvT[:, :].rearrange("p (n l) -> p n l", l=64)[:, :, ll], wc[:, ll, 1, :],
                start=(ll == 0), stop=(ll == 63), reads=[wc, XvT], writes=[PS[3]])
        P.i("act", "copy", CvA[:], PS[3][0:32, 64:128], reads=[PS[3]], writes=[CvA])
        ck("n_pre%d" % g)

        def chain(i):
            slot = i % 2
            qT, gt, oc, smT = qT2[slot], gt2[slot], oc2[slot], smT2[slot]
            tok = slice(i * 128, (i + 1) * 128)
            for (c0, cn, ps) in ((0, 384, PS[0]), (512, 140, PS[1])):
                for kt in range(8):
                    P.i("pe", "matmul", ps[:, 0:cn], hT[kt][:, tok], wg[:, kt, c0:c0 + cn], start=(kt == 0), stop=(kt == 7),
                        reads=[hT[kt], wg], writes=[ps])
                P.i("act", "copy", tm[:, c0:c0 + cn], ps[:, 0:cn], reads=[ps], writes=[tm])
            yield
            rope_apply(P, V(tr[:, :].rearrange("p (h c) -> p h c", c=64), [tr]), V(tm[:, 0:384].rearrange("p (h c) -> p h c", c=64), [tm]),
                       ropeT[:, i, :], 6, 128, rt, ropeT)
            P.i("act", "copy", tmb[:, 0:256], tm[:, 0:256], reads=[tm], writes=[tmb])
            P.i("act", "copy", trb[:], tr[:], reads=[tr], writes=[trb])
            P.i("act", "activation", gt[:], tm[:, 640:652], AF.Sigmoid, reads=[tm], writes=[gt])
            P.i("pool", "tensor_copy", Vsel[i][:, 0:64], tm[:, 512:576], reads=[tm], writes=[Vsel[i]])
            P.i("pool", "tensor_copy", Vwin[i][:, 0:64], tm[:, 576:640], reads=[tm], writes=[Vwin[i]])
            P.dma("sp", d["sel_p"].ap()[j, tok, g * 64:(g + 1) * 64], tr[:, 256:320], reads=[tr], writes=[d["sel_p"]])
            P.dma("sp", d["sel_p"].ap()[j, tok, 256 + g * 64:256 + (g + 1) * 64], tm[:, 512:576], reads=[tm], writes=[d["sel_p"]])
            if i >= 12:
                wt = slice((i - 12) * 128, (i - 11) * 128)
                P.dma("sp", d["win_p"].ap()[j, wt, g * 64:(g + 1) * 64], tr[:, 320:384], reads=[tr], writes=[d["win_p"]])
                P.dma("sp", d["win_p"].ap()[j, wt, 256 + g * 64:256 + (g + 1) * 64], tm[:, 576:640], reads=[tm], writes=[d["win_p"]])
            yield
            pq = PS[2][0:64, :].bitcast(BF16)
            pk = PS[1][0:64, :].bitcast(BF16)[:, 512:768]
            for h in range(4):
                P.i("pe", "transpose", pq[:, h * 128:(h + 1) * 128], tmb[:, h * 64:(h + 1) * 64], k.identb[:], reads=[tmb, k.identb], writes=[PS[2]])
                P.i("pe", "transpose", pq[:, (4 + h) * 128:(5 + h) * 128], trb[:, h * 64:(h + 1) * 64], k.identb[:], reads=[trb, k.identb], writes=[PS[2]])
            P.i("pe", "transpose", pk[:, 0:128], trb[:, 256:320], k.identb[:], reads=[trb, k.identb], writes=[PS[1]])
            P.i("pe", "transpose", pk[:, 128:256], trb[:, 320:384], k.identb[:], reads=[trb, k.identb], writes=[PS[1]])
            P.i("dve", "tensor_copy", qT[:].rearrange("p a t -> p (a t)"), pq[:, 0:1024], reads=[PS[2]], writes=[qT])
            P.i("act", "copy", KselT[i][:], pk[:, 0:128], reads=[PS[1]], writes=[KselT[i]])
            P.i("act", "copy", KwinT[i][:], pk[:, 128:256], reads=[PS[1]], writes=[KwinT[i]])
            yield
            qraw = qT[:, 0:4, :]
            P.i("pe", "matmul", PS[5][0:32, :], CkT[:], qraw, start=True, stop=True, reads=[CkT, qT], writes=[PS[5]])
            P.i("act", "activation", Ec[:], PS[5][0:32, :], AF.Exp, scale=SCALE, reads=[PS[5]], writes=[Ec])
            P.i("dve", "tensor_scalar", cm_[:], tiota[:], thr[:, i:i + 1], None, ALU.is_ge, reads=[tiota, thr], writes=[cm_])
            P.i("dve", "tensor_tensor", Ec[:].rearrange("p (a t) -> p a t", a=4), Ec[:].rearrange("p (a t) -> p a t", a=4),
                cm_[:, :].unsqueeze(1).to_broadcast([32, 4, 128]), ALU.mult, reads=[Ec, cm_], writes=[Ec])
            yield
            P.i("pe", "matmul", PS[5][0:32, :], k.ones[0:32, 0:32], Ec[:], start=True, stop=True, reads=[k.ones, Ec], writes=[PS[5]])
            P.i("dve", "tensor_scalar_max", rs[:], PS[5][0:32, :], 1e-30, reads=[PS[5]], writes=[rs])
            P.i("dve", "reciprocal", rs[:], rs[:], reads=[rs], writes=[rs])
            P.i("dve", "tensor_tensor", Ec[:], Ec[:], rs[:], ALU.mult, reads=[Ec, rs], writes=[Ec])
            P.i("act", "copy", pb[:], Ec[:], reads=[Ec], writes=[pb])
            yield
            for h in range(4):
                P.i("pe", "matmul", PS[5][:, h * 64:(h + 1) * 64], pb[:, h * 128:(h + 1) * 128], CvA[:], start=True, stop=True,
                    reads=[pb, CvA], writes=[PS[5]])
            P.i("act", "copy", oc[:], PS[5][:, 0:256], reads=[PS[5]], writes=[oc])
            P.i("dve", "tensor_reduce", impT[:], Ec[:].rearrange("p (a t) -> p t a", a=4), AX.X, ALU.add, reads=[Ec], writes=[impT])
            yield
            P.i("pe", "transpose", PS[5][:, 256:288], impT[:], k.ident[0:32, 0:32], reads=[impT, k.ident], writes=[PS[5]])
            P.i("dve", "tensor_scalar", cand[:], niota[:], curcol[:, i:i + 1], None, ALU.is_lt, reads=[niota, curcol], writes=[cand])
            P.i("dve", "tensor_scalar_add", impc[:], PS[5][:, 256:288], 1.0, reads=[PS[5]], writes=[impc])
            P.i("dve", "tensor_tensor", impc[:], impc[:], cand[:], ALU.mult, reads=[impc, cand], writes=[impc])
            P.i("dve", "tensor_scalar_add", impc[:], impc[:], -1.0, reads=[impc], writes=[impc])
            P.i("dve", "max", m8[:, 0:8], impc[:], reads=[impc], writes=[m8])
            P.i("dve", "match_replace", tmp32[:], m8[:, 0:8], impc[:], -2.0, reads=[m8, impc], writes=[tmp32])
            P.i("dve", "max", m8[:, 8:16], tmp32[:], reads=[tmp32], writes=[m8])
            yield
            P.i("dve", "tensor_scalar", sel[:], impc[:], m8[:, 14:15], None, ALU.is_ge, reads=[impc, m8], writes=[sel])
            P.i("dve", "tensor_single_scalar", tmp32[:], impc[:], -0.5, ALU.is_gt, reads=[impc], writes=[tmp32])
            P.i("dve", "tensor_tensor", sel[:], sel[:], tmp32[:], ALU.mult, reads=[sel, tmp32], writes=[sel])
            P.i("dve", "tensor_scalar", cand[:], niota[:], curcol[:, i:i + 1], None, ALU.is_equal, reads=[niota, curcol], writes=[cand])
            P.i("dve", "tensor_tensor", sel[:], sel[:], cand[:], ALU.max, reads=[sel, cand], writes=[sel])
            P.i("pe", "transpose", PS[5][0:32, 384:512], sel[:], k.ident[:], reads=[sel, k.ident], writes=[PS[5]])
            P.i("act", "copy", smT[:], PS[5][0:32, 384:512], reads=[PS[5]], writes=[smT])
            yield

        def attn(i):
            slot = i % 2
            qT, gt, oc, smT = qT2[slot], gt2[slot], oc2[slot], smT2[slot]
            tok = slice(i * 128, (i + 1) * 128)
            qrot = qT[:, 4:8, :]
            for br in range(2):
                KT, Vv = (KselT, Vsel) if br == 0 else (KwinT, Vwin)
                acc = PS[6 + br]
                j0 = 0 if br == 0 else max(0, i - 4)
                for jt in range(j0, i + 1):
                    kk = slice(jt * 128, (jt + 1) * 128)
                    E = Eb[cnt[0] % 2]
                    M = Mb[cnt[0] % 2]
                    cnt[0] += 1
                    msk = None
                    if br == 0:
                        P.i("pe", "matmul", PS[4][:, 0:128], eexp[:, kk], smT[:], start=True, stop=True, reads=[eexp, smT], writes=[PS[4]])
                        if jt == i:
                            P.i("dve", "tensor_tensor", M[:], PS[4][:, 0:128], k.caus[:], ALU.mult, reads=[PS[4], k.caus], writes=[M])
                        else:
                            P.i("dve", "tensor_copy", M[:], PS[4][:, 0:128], reads=[PS[4]], writes=[M])
                        msk = M
                    elif jt == i:
                        msk = k.caus
                    elif jt == i - 4:
                        msk = k.wmask
                    P.i("pe", "matmul", PS[3][:, :], KT[jt][:], qrot, start=True, stop=True, reads=[KT[jt], qT], writes=[PS[3]])
                    P.i("act", "activation", E[:], PS[3][:, :], AF.Exp, scale=SCALE, reads=[PS[3]], writes=[E])
                    if msk is not None:
                        P.i("dve", "tensor_tensor", E[:].rearrange("p (a t) -> p a t", a=4), E[:].rearrange("p (a t) -> p a t", a=4),
                            msk[:, :].unsqueeze(1).to_broadcast([128, 4, 128]), ALU.mult, reads=[E, msk], writes=[E])
                    for h in range(4):
                        P.i("pe", "matmul", acc[:, h * 65:(h + 1) * 65], E[:, h * 128:(h + 1) * 128], Vv[jt][:, 0:65],
                            start=(jt == j0 and h == 0), stop=(jt == i), skip_group_check=True, reads=[E, Vv[jt]], writes=[acc])
                    yield
            g3 = gt[:, :].rearrange("p (h c) -> p h c", c=3)
            a3 = lambda ps_: ps_[:, 0:260].rearrange("p (h c) -> p h c", c=65)
            P.i("dve", "reciprocal", cf[:, 0:4], a3(PS[6])[:, :, 64], reads=[PS[6]], writes=[cf])
            P.i("dve", "reciprocal", cf[:, 4:8], a3(PS[7])[:, :, 64], reads=[PS[7]], writes=[cf])
            P.i("dve", "tensor_tensor", cf[:, 0:4], cf[:, 0:4], g3[:, :, 1], ALU.mult, reads=[cf, gt], writes=[cf])
            P.i("dve", "tensor_tensor", cf[:, 4:8], cf[:, 4:8], g3[:, :, 2], ALU.mult, reads=[cf, gt], writes=[cf])
            o3 = lambda t: t[:, :].rearrange("p (h c) -> p h c", c=64)
            bc = lambda ap: ap.unsqueeze(2).to_broadcast([128, 4, 64])
            P.i("dve", "tensor_tensor", o3(om), o3(oc), bc(g3[:, :, 0]), ALU.mult, reads=[oc, gt], writes=[om])
            P.i("dve", "tensor_tensor", o3(om2), a3(PS[6])[:, :, 0:64], bc(cf[:, 0:4]), ALU.mult, reads=[PS[6], cf], writes=[om2])
            P.i("pool", "tensor_tensor", om[:], om[:], om2[:], ALU.add, reads=[om, om2], writes=[om])
            P.i("dve", "tensor_tensor", o3(om2), a3(PS[7])[:, :, 0:64], bc(cf[:, 4:8]), ALU.mult, reads=[PS[7], cf], writes=[om2])
            P.i("pool", "tensor_tensor", omb[:], om[:], om2[:], ALU.add, reads=[om, om2], writes=[omb])
            yield
            po = PS[4][:, :].bitcast(BF16)[:, 256:512]
            for a in range(2):
                P.i("pe", "transpose", po[:, a * 128:(a + 1) * 128], omb[:, a * 128:(a + 1) * 128], k.identb[:], reads=[omb, k.identb], writes=[PS[4]])
            P.i("act", "copy", omT[:].rearrange("p a t -> p (a t)"), po[:, 0:256], reads=[PS[4]], writes=[omT])
            yield
            for hb in range(2):
                ps = PS[6 + hb]
                for q in range(4):
                    dt = hb * 4 + q
                    for a in range(2):
                        P.i("pe", "matmul", ps[:, q * 128:(q + 1) * 128], wog[:, a, dt * 128:(dt + 1) * 128], omT[:, a, :],
                            start=(a == 0), stop=(a == 1), reads=[wog, omT], writes=[ps])
                for q in range(4):
                    dt = hb * 4 + q
                    P.i("dve", "tensor_tensor", k.R[dt][:, tok], ps[:, q * 128:(q + 1) * 128], k.R[dt][:, tok], ALU.add,
                        reads=[ps, k.R[dt]], writes=[k.R[dt]])
                yield
            ck("n_main%d_%d" % (g, i))

        run_il([chain(0)])
        for i in range(16):
            run_il([attn(i), chain(i + 1) if i < 15 else None])
        ck("n_main%d" % g)
    ck("n_prompt")
    P.release(m1)
    nsa_sample(k, l, hTs)
    P.release(m0)

def nsa_sample(k, l, hTs):
    P = k.P
    d = k.d
    j = l // 2
    PS = k.PS
    m = P.mark()
    SQ2 = P.sb([128, 4, 4, 8, 4], BF16, "SQ2")
    SKs = P.sb([64, 4, 4, 4], BF16, "SKs")
    SKw = P.sb([64, 4, 4, 4], BF16, "SKw")
    SVs = P.sb([4, 4, 4, 65], BF16, "SVs")
    SVw = P.sb([4, 4, 4, 65], BF16, "SVw")
    SG = P.sb([4, 4, 4, 12], F32, "SG")
    oTs = P.sb([64, 4, 16, 4], BF16, "oTs")
    ropeS = P.sb([4, 64], F32, "ropeS")
    P.dma("sp", ropeS[:], d["rope_tab"].ap()[16, 0:4, :], reads=[d["rope_tab"]], writes=[ropeS])
    P.i("pool", "memset", SVs[:], 1.0, writes=[SVs])
    P.i("pool", "memset", SVw[:], 1.0, writes=[SVw])
    i4 = P.sb([4, 4], F32, "i4")
    P.i("dve", "tensor_copy", i4[:], k.ident[0:4, 0:4], reads=[k.ident], writes=[i4])
    pti = P.sb([128, NS * NPG], I32, "pti")
    ptf = P.sb([128, NS * NPG], F32, "ptf")
    iot = P.sb([128, 1], F32, "iot")
    idx = P.sb([128, NS * NPG], I32, "idx")
    P.dma("sp", pti[:], d["page_table"].ap().rearrange("s g -> (s g)").partition_broadcast(128), reads=[d["page_table"]], writes=[pti])
    P.i("dve", "tensor_copy", ptf[:], pti[:], reads=[pti], writes=[ptf])
    P.i("pool", "iota", iot[:], pattern=[[0, 1]], base=j * k.n_phys * 128, channel_multiplier=1, allow_small_or_imprecise_dtypes=True, writes=[iot])
    P.i("dve", "tensor_scalar", ptf[:], ptf[:], 128.0, iot[:, 0:1], ALU.mult, ALU.add, reads=[ptf, iot], writes=[ptf])
    P.i("dve", "tensor_copy", idx[:], ptf[:], reads=[ptf], writes=[idx])
    ck("s_setup")

    m0 = P.mark()
    wg = P.sb([128, 8, 652], BF16, "swg")
    tmS = P.sb([4, 652], F32, "tmS")
    trS = P.sb([4, 384], F32, "trS")
    qd = P.sb([4, 8, 2, 64], BF16, "qd")
    kb2 = P.sb([4, 128], BF16, "kb2")
    rt = [P.sb([4, 6, 32], F32, "srt%d" % i) for i in range(2)]
    for g in range(4):
        srcs = [(g * 256, 256, 0), (1024 + 2 * 256 + g * 64, 64, 256), (1024 + 4 * 256 + g * 64, 64, 320),
                (1024 + 0 * 256 + g * 64, 64, 384), (1024 + 1 * 256 + g * 64, 64, 448), (1024 + 3 * 256 + g * 64, 64, 512),
                (1024 + 5 * 256 + g * 64, 64, 576), (2560 + g * 12, 12, 640)]
        for (c0, w_, o0) in srcs:
            P.dma("pool", wg[:, :, o0:o0 + w_], d["nsa_w_in"].ap()[j, :, c0:c0 + w_].rearrange("(kt p) c -> p kt c", p=128),
                  reads=[d["nsa_w_in"]], writes=[wg])
        for s_ in range(4):
            rows = slice(s_ * 4, (s_ + 1) * 4)
            for (c0, cn, ps) in ((0, 512, PS[0]), (512, 140, PS[1])):
                for kt in range(8):
                    P.i("pe", "matmul", ps[0:4, 0:cn], hTs[:, kt, rows], wg[:, kt, c0:c0 + cn], start=(kt == 0), stop=(kt == 7),
                        reads=[hTs, wg], writes=[ps])
                P.i("act", "copy", tmS[:, c0:c0 + cn], ps[0:4, 0:cn], reads=[ps], writes=[tmS])
            rope_apply(P, V(trS[:, :].rearrange("p (h c) -> p h c", c=64), [trS]), V(tmS[:, 0:384].rearrange("p (h c) -> p h c", c=64), [tmS]),
                       ropeS[:, :], 6, 4, rt, ropeS)
            for c in range(2):
                P.dma("sp", d["cmp_s"].ap()[j, rows, c * 256 + g * 64:c * 256 + (g + 1) * 64], tmS[:, 384 + c * 64:448 + c * 64],
                      reads=[tmS], writes=[d["cmp_s"]])
            P.dma("sp", d["sel_s"].ap()[j, rows, g * 64:(g + 1) * 64], trS[:, 256:320], reads=[trS], writes=[d["sel_s"]])
            P.dma("sp", d["sel_s"].ap()[j, rows, 256 + g * 64:256 + (g + 1) * 64], tmS[:, 512:576], reads=[tmS], writes=[d["sel_s"]])
            P.dma("sp", d["win_s"].ap()[j, s_, 508:512, g * 64:(g + 1) * 64], trS[:, 320:384], reads=[trS], writes=[d["win_s"]])
            P.dma("sp", d["win_s"].ap()[j, s_, 508:512, 256 + g * 64:256 + (g + 1) * 64], tmS[:, 576:640], reads=[tmS], writes=[d["win_s"]])
            for r in range(2):
                P.i("dve", "tensor_copy", qd[:, 0:4, r, :], tmS[:, 0:256].rearrange("p (h c) -> p h c", c=64), reads=[tmS], writes=[qd])
                P.i("dve", "tensor_copy", qd[:, 4:8, r, :], trS[:, 0:256].rearrange("p (h c) -> p h c", c=64), reads=[trS], writes=[qd])
            P.i("dve", "tensor_copy", kb2[:], trS[:, 256:384], reads=[trS], writes=[kb2])
            P.i("act", "copy", SVs[:, s_, g, 0:64], tmS[:, 512:576], reads=[tmS], writes=[SVs])
            P.i("act", "copy", SVw[:, s_, g, 0:64], tmS[:, 576:640], reads=[tmS], writes=[SVw])
            P.i("act", "activation", SG[:, s_, g, :], tmS[:, 640:652], AF.Sigmoid, reads=[tmS], writes=[SG])
            pq = PS[2][:, :].bitcast(BF16)
            for h in range(8):
                P.i("pe", "transpose", pq[:, h * 4:(h + 1) * 4], qd[:, h, :, :].rearrange("p r c -> p (r c)"), k.identb[0:4, 0:4],
                    reads=[qd, k.identb], writes=[PS[2]])
            P.i("dve", "tensor_copy", SQ2[:, s_, g, :, :].rearrange("p h t -> p (h t)"), pq[:, 0:32], reads=[PS[2]], writes=[SQ2])
            pk = PS[3][0:64, :].bitcast(BF16)
            P.i("pe", "transpose", pk[:, 0:4], kb2[:, 0:64], k.identb[0:4, 0:4], reads=[kb2, k.identb], writes=[PS[3]])
            P.i("pe", "transpose", pk[:, 4:8], kb2[:, 64:128], k.identb[0:4, 0:4], reads=[kb2, k.identb], writes=[PS[3]])
            P.i("act", "copy", SKs[:, s_, g, :], pk[:, 0:4], reads=[PS[3]], writes=[SKs])
            P.i("act", "copy", SKw[:, s_, g, :], pk[:, 4:8], reads=[PS[3]], writes=[SKw])
    P.release(m0)
    ck("s_S0")

    m1 = P.mark()
    wc2 = P.sb([128, 64, 2, 64], BF16, "wc2")
    for hf in range(2):
        for lh in range(4):
            P.dma("pool", wc2[hf * 64:(hf + 1) * 64, lh * 16:(lh + 1) * 16], d["nsa_cmp_w"].ap()[j, lh * 16:(lh + 1) * 16].rearrange("l c d e -> d l c e"),
                  reads=[d["nsa_cmp_w"]], writes=[wc2])
    pe4 = P.sb([128, 4, 64], F32, "pe4")
    stg64 = P.sb([128, 128], F32, "sstg")
    for r in range(2):
        P.dma("sp", stg64[:, r * 64:(r + 1) * 64], d["nsa_cmp_pe"].ap()[j].rearrange("l c d -> (l c) d"), reads=[d["nsa_cmp_pe"]], writes=[stg64])
    P.i("pe", "transpose", PS[6][:, 0:128], stg64[:], k.ident[:], reads=[stg64, k.ident], writes=[PS[6]])
    for b in range(4):
        P.i("dve", "tensor_copy", pe4[:, b, :], PS[6][:, 0:128].rearrange("p (l c) -> p c l", c=2)[:, b // 2, :], reads=[PS[6]], writes=[pe4])
    XT = P.sb([128, 4, NPG * 128], BF16, "XT")
    pgf = [P.sb([128, 512], F32, "pgf%d" % i) for i in range(2)]
    pgb = [P.sb([128, 512], BF16, "pgb%d" % i) for i in range(2)]
    CkS = P.sb([64, 4, 128], BF16, "CkS")
    CvS = P.sb([128, 4, 64], BF16, "CvS")
    Ecs = P.sb([128, 64], F32, "Ecs")
    rss = P.sb([128, 64], F32, "rss")
    pbs = P.sb([128, 64], BF16, "pbs")
    impTs = P.sb([128, 16], F32, "impTs")
    imp16 = P.sb([16, 128], F32, "imp16")
    tmp16 = P.sb([16, 128], F32, "tmp16")
    sel16 = P.sb([16, 128], BF16, "sel16")
    m8s = P.sb([16, 16], F32, "m8s")
    selX = [P.sb([16, 128], BF16, "selX%d" % i) for i in range(2)]
    Mbs = [P.sb([128, 16], BF16, "Mbs%d" % i) for i in range(2)]
    KT4 = [P.sb([64, 4, 128], BF16, "KT4%d" % i) for i in range(2)]
    onesb = P.sb([128, 64], BF16, "onesb")
    P.i("pool", "memset", onesb[:], 1.0, writes=[onesb])
    Es = [P.sb([128, 64], BF16, "Es%d" % i) for i in range(2)]
    occ = P.sb([64, 64], F32, "occ")
    bcs = P.sb([64, 64], F32, "bcs")
    gd = P.sb([4, 16, 4], F32, "gd")
    om_ = P.sb([64, 64], F32, "som")
    om2_ = P.sb([64, 64], F32, "som2")
    cnt = [0]

    def gather(cache, s_, pg, buf):
        c = s_ * NPG + pg
        P.dma("pool", buf[:], d[cache].ap().rearrange("l r c -> (l r) c"), indirect=idx[:, c:c + 1].bitcast(U32),
              reads=[d[cache], idx], writes=[buf])

    def attend_tile(kt4, pb_, mask_fn, first, qsl):
        E = Es[cnt[0] % 2]
        cnt[0] += 1
        for g in range(4):
            P.i("pe", "matmul", PS[5][:, g * 16:(g + 1) * 16], kt4[:, g, :], SQ2[0:64, qsl[0], g, 4:8, :], start=True, stop=True,
                reads=[kt4, SQ2], writes=[PS[5]])
        P.i("act", "activation", E[:], PS[5][:, 0:64], AF.Exp, scale=SCALE, reads=[PS[5]], writes=[E])
        if mask_fn is not None:
            mask_fn(E)
        for g in range(4):
            P.i("pe", "matmul", PS[7][0:64, g * 16:(g + 1) * 16], pb_[:, 256 + g * 64:256 + (g + 1) * 64], E[:, g * 16:(g + 1) * 16],
                start=(first and g == 0), stop=False, skip_group_check=True, reads=[pb_, E], writes=[PS[7]])
        P.i("pe", "matmul", PS[6][0:64, 0:64], onesb[:, :], E[:, :], start=first, stop=False, skip_group_check=True,
            reads=[onesb, E], writes=[PS[6]])

    def finish_branch(dst):
        P.i("dve", "reciprocal", bcs[:], PS[6][0:64, 0:64], reads=[PS[6]], writes=[bcs])
        P.i("dve", "tensor_tensor", dst[:], PS[7][0:64, 0:64], bcs[:], ALU.mult, reads=[PS[7], bcs], writes=[dst])

    def gate_bc(s_, which):
        P.i("dve", "tensor_tensor", gd[:], SG[:, s_, :, :].rearrange("p g (h c) -> p (g h) c", c=3)[:, :, which].unsqueeze(2).to_broadcast([4, 16, 4]),
            i4[:, :].unsqueeze(1).to_broadcast([4, 16, 4]), ALU.mult, reads=[SG, i4], writes=[gd])
        P.i("pe", "matmul", PS[5][0:64, 128:192], k.ones[0:4, 0:64], gd[:].rearrange("p a t -> p (a t)"), start=True, stop=True,
            reads=[k.ones, gd], writes=[PS[5]])

    for s_ in range(4):
        qs = (s_,)
        for pg in range(NPG):
            pf, pb_ = pgf[pg % 2], pgb[pg % 2]
            gather("cache_cmp", s_, pg, pf)
            if pg % 2 == 0:
                P.i("act", "copy", pb_[:], pf[:], reads=[pf], writes=[pb_])
            else:
                P.i("dve", "tensor_copy", pb_[:], pf[:], reads=[pf], writes=[pb_])
            pt = PS[pg % 2][:, :].bitcast(BF16)
            for b in range(4):
                P.i("pe", "transpose", pt[:, b * 128:(b + 1) * 128], pb_[:, b * 128:(b + 1) * 128], k.identb[:], reads=[pb_, k.identb], writes=[PS[pg % 2]])
            P.i("dve", "tensor_tensor", XT[:, :, pg * 128:(pg + 1) * 128].rearrange("p b (n l) -> p b n l", l=64),
                pt[:, 0:512].rearrange("p (b n l) -> p b n l", b=4, l=64), pe4[:, :, :].unsqueeze(2).to_broadcast([128, 4, 2, 64]), ALU.add,
                reads=[PS[pg % 2], pe4], writes=[XT])
        ck("s_a%d" % s_)
        for g in range(4):
            gp, gl = g // 2, g % 2
            pr = slice(gl * 64, (gl + 1) * 64)
            xk = XT[pr, 0 * 2 + gp, :].rearrange("p (n l) -> p n l", l=64)
            xv = XT[pr, 1 * 2 + gp, :].rearrange("p (n l) -> p n l", l=64)
            for ll in range(64):
                P.i("pe", "matmul", PS[2][0:64, 0:128], wc2[pr, ll, 0, :], xk[:, :, ll], start=(ll == 0), stop=(ll == 63), reads=[wc2, XT], writes=[PS[2]])
            P.i("act", "copy", CkS[:, g, :], PS[2][0:64, 0:128], reads=[PS[2]], writes=[CkS])
            for ll in range(64):
                P.i("pe", "matmul", PS[3][:, 0:64], xv[:, :, ll], wc2[pr, ll, 1, :], start=(ll == 0), stop=(ll == 63), reads=[wc2, XT], writes=[PS[3]])
            P.i("act", "copy", CvS[:, g, :], PS[3][:, 0:64], reads=[PS[3]], writes=[CvS])
        ck("s_b%d" % s_)
        for g in range(4):
            P.i("pe", "matmul", PS[5][:, g * 16:(g + 1) * 16], CkS[:, g, :], SQ2[0:64, s_, g, 0:4, :], start=True, stop=True,
                reads=[CkS, SQ2], writes=[PS[5]])
        P.i("act", "activation", Ecs[:], PS[5][:, 0:64], AF.Exp, scale=SCALE, reads=[PS[5]], writes=[Ecs])
        P.i("pe", "matmul", PS[5][:, 0:64], k.ones[:], Ecs[:], start=True, stop=True, reads=[k.ones, Ecs], writes=[PS[5]])
        P.i("dve", "reciprocal", rss[:], PS[5][:, 0:64], reads=[PS[5]], writes=[rss])
        P.i("dve", "tensor_tensor", Ecs[:], Ecs[:], rss[:], ALU.mult, reads=[Ecs, rss], writes=[Ecs])
        P.i("act", "copy", pbs[:], Ecs[:], reads=[Ecs], writes=[pbs])
        for g in range(4):
            P.i("pe", "matmul", PS[6][0:64, g * 16:(g + 1) * 16], CvS[:, g, :], pbs[:, g * 16:(g + 1) * 16], start=True, stop=True,
                reads=[CvS, pbs], writes=[PS[6]])
        P.i("act", "copy", occ[:], PS[6][0:64, 0:64], reads=[PS[6]], writes=[occ])
        P.i("dve", "tensor_reduce", impTs[:].rearrange("p (g t) -> p g t", g=4), Ecs[:].rearrange("p (g a t) -> p g t a", g=4, a=4), AX.X, ALU.add,
            reads=[Ecs], writes=[impTs])
        P.i("pe", "transpose", PS[4][0:16, 0:128], impTs[:], k.ident[:], reads=[impTs, k.ident], writes=[PS[4]])
        P.i("dve", "tensor_copy", imp16[:], PS[4][0:16, 0:128], reads=[PS[4]], writes=[imp16])
        P.i("dve", "max", m8s[:, 0:8], imp16[:], reads=[imp16], writes=[m8s])
        P.i("dve", "match_replace", tmp16[:], m8s[:, 0:8], imp16[:], -2.0, reads=[m8s, imp16], writes=[tmp16])
        P.i("dve", "max", m8s[:, 8:16], tmp16[:], reads=[tmp16], writes=[m8s])
        P.i("dve", "tensor_scalar", sel16[:], imp16[:], m8s[:, 14:15], None, ALU.is_ge, reads=[imp16, m8s], writes=[sel16])
        ck("s_c%d" % s_)
        for pg in range(NPG):
            pf, pb_ = pgf[pg % 2], pgb[pg % 2]
            gather("cache_sel", s_, pg, pf)
            if pg % 2 == 0:
                P.i("act", "copy", pb_[:], pf[:], reads=[pf], writes=[pb_])
            else:
                P.i("dve", "tensor_copy", pb_[:], pf[:], reads=[pf], writes=[pb_])
            kt4, sx, mb = KT4[pg % 2], selX[pg % 2], Mbs[pg % 2]
            pt = PS[pg % 2][0:64, :].bitcast(BF16)
            for g in range(4):
                P.i("pe", "transpose", pt[:, g * 128:(g + 1) * 128], pb_[:, g * 64:(g + 1) * 64], k.identb[:], reads=[pb_, k.identb], writes=[PS[pg % 2]])
            P.i("act", "copy", kt4[:].rearrange("p a t -> p (a t)"), pt[:, 0:512], reads=[PS[pg % 2]], writes=[kt4])
            P.i("dve", "tensor_copy", sx[:].rearrange("p (n l) -> p n l", l=64), sel16[:, 2 * pg:2 * pg + 2].unsqueeze(2).to_broadcast([16, 2, 64]),
                reads=[sel16], writes=[sx])
            P.i("pe", "matmul", PS[4][:, 128:144], sx[:], k.identb[0:16, 0:16], start=True, stop=True, reads=[sx, k.identb], writes=[PS[4]])
            P.i("dve", "tensor_copy", mb[:], PS[4][:, 128:144], reads=[PS[4]], writes=[mb])

            def mfn(E, mb=mb):
                P.i("dve", "tensor_tensor", E[:].rearrange("p (g a t) -> p g a t", g=4, a=4), E[:].rearrange("p (g a t) -> p g a t", g=4, a=4),
                    mb[:].rearrange("p (g t) -> p g t", g=4).unsqueeze(2).to_broadcast([128, 4, 4, 4]), ALU.mult, reads=[E, mb], writes=[E])
            attend_tile(kt4, pb_, mfn, pg == 0, qs)
            if pg == 0:
                ck("s_dpage%d" % s_)
        ck("s_dpre%d" % s_)
        new_tile(k, P, PS, SKs, SVs, SQ2, s_, Es, cnt, onesb)
        ck("s_dnew%d" % s_)
        finish_branch(om2_)
        ck("s_dfin%d" % s_)
        gate_bc(s_, 1)
        P.i("dve", "tensor_tensor", om_[:], om2_[:], PS[5][0:64, 128:192], ALU.mult, reads=[om2_, PS[5]], writes=[om_])
        gate_bc(s_, 0)
        P.i("dve", "tensor_tensor", om2_[:], occ[:], PS[5][0:64, 128:192], ALU.mult, reads=[occ, PS[5]], writes=[om2_])
        P.i("pool", "tensor_tensor", om_[:], om_[:], om2_[:], ALU.add, reads=[om_, om2_], writes=[om_])
        ck("s_d%d" % s_)
        for a in range(4):
            pf, pb_ = pgf[a % 2], pgb[a % 2]
            P.dma("sp", pf[:], d["state_win"].ap()[j, s_, a * 128:(a + 1) * 128, :], reads=[d["state_win"]], writes=[pf])
            if a == 0:
                P.dma("sp", d["win_s"].ap()[j, s_, 0:124, :], pf[4:128, :], reads=[pf], writes=[d["win_s"]])
            else:
                P.dma("sp", d["win_s"].ap()[j, s_, a * 128 - 4:a * 128 + 124, :], pf[:, :], reads=[pf], writes=[d["win_s"]])
            P.i("act", "copy", pb_[:], pf[:], reads=[pf], writes=[pb_])
            kt4 = KT4[a % 2]
            pt = PS[a % 2][0:64, :].bitcast(BF16)
            for g in range(4):
                P.i("pe", "transpose", pt[:, g * 128:(g + 1) * 128], pb_[:, g * 64:(g + 1) * 64], k.identb[:], reads=[pb_, k.identb], writes=[PS[a % 2]])
            P.i("act", "copy", kt4[:].rearrange("p a t -> p (a t)"), pt[:, 0:512], reads=[PS[a % 2]], writes=[kt4])
            mfn = None
            if a == 0:
                def mfn(E):
                    P.i("dve", "tensor_tensor", E[:].rearrange("p (a t) -> p a t", t=4), E[:].rearrange("p (a t) -> p a t", t=4),
                        k.wmask[:, 0:4].unsqueeze(1).to_broadcast([128, 16, 4]), ALU.mult, reads=[E, k.wmask], writes=[E])
            attend_tile(kt4, pb_, mfn, a == 0, qs)
        new_tile(k, P, PS, SKw, SVw, SQ2, s_, Es, cnt, onesb)
        finish_branch(om2_)
        gate_bc(s_, 2)
        P.i("dve", "tensor_tensor", om2_[:], om2_[:], PS[5][0:64, 128:192], ALU.mult, reads=[om2_, PS[5]], writes=[om2_])
        P.i("pool", "tensor_tensor", oTs[:, s_, :, :].rearrange("p a t -> p (a t)"), om_[:], om2_[:], ALU.add, reads=[om_, om2_], writes=[oTs])
        ck("s_e%d" % s_)
    P.release(m1)

    m2 = P.mark()
    wo = P.sb([64, 16, D], BF16, "swo")
    for q4 in range(4):
        P.dma("pool", wo[:, q4 * 4:(q4 + 1) * 4, :], d["nsa_w_out"].ap()[j, q4 * 256:(q4 + 1) * 256, :].rearrange("(h p) c -> p h c", p=64),
              reads=[d["nsa_w_out"]], writes=[wo])
    for dt in range(8):
        ps = PS[dt % 2]
        for h in range(16):
            P.i("pe", "matmul", ps[:, 0:16], wo[:, h, dt * 128:(dt + 1) * 128], oTs[:, :, h, :], start=(h == 0), stop=(h == 15),
                reads=[wo, oTs], writes=[ps])
        P.i("dve", "tensor_tensor", k.R[dt][:, LP:NT], ps[:, 0:16], k.R[dt][:, LP:NT], ALU.add, reads=[ps, k.R[dt]], writes=[k.R[dt]])
    P.release(m2)
    P.release(m)


def new_tile(k, P, PS, SK, SV, SQ2, s_, Es, cnt, onesb):
    E = Es[cnt[0] % 2]
    cnt[0] += 1
    for g in range(4):
        P.i("pe", "matmul", PS[5][0:4, g * 16:(g + 1) * 16], SK[:, s_, g, :], SQ2[0:64, s_, g, 4:8, :], start=True, stop=True,
            reads=[SK, SQ2], writes=[PS[5]])
    P.i("act", "activation", E[0:4, :], PS[5][0:4, 0:64], AF.Exp, scale=SCALE, reads=[PS[5]], writes=[E])
    P.i("dve", "tensor_tensor", E[0:4, :].rearrange("p (a t) -> p a t", t=4), E[0:4, :].rearrange("p (a t) -> p a t", t=4),
        k.caus[0:4, 0:4].unsqueeze(1).to_broadcast([4, 16, 4]), ALU.mult, reads=[E, k.caus], writes=[E])
    for g in range(4):
        P.i("pe", "matmul", PS[7][0:64, g * 16:(g + 1) * 16], SV[:, s_, g, 0:64], E[0:4, g * 16:(g + 1) * 16],
            start=False, stop=(g == 3), skip_group_check=True, reads=[SV, E], writes=[PS[7]])
    P.i("pe", "matmul", PS[6][0:64, 0:64], onesb[0:4, :], E[0:4, :], start=False, stop=True, skip_group_check=True,
        reads=[onesb, E], writes=[PS[6]])


_OUT_ORDER = ["y_p", "y_s", "gdn_p", "gdn_s", "gconv_p", "gconv_s", "cmp_p", "cmp_s", "sel_p", "sel_s",
              "win_p", "win_s", "ffn_p", "ffn_s"]


def _rope_table():
    inv = (np.float32(10000.0) ** (-np.arange(32, dtype=np.float32) / np.float32(32))).astype(np.float32)
    pos = np.zeros((17, 128), np.float32)
    pos[:16] = np.arange(LP, dtype=np.float32).reshape(16, 128)
    pos[16] = 8192 + (np.arange(128) % 4)
    ang = (pos[:, :, None] * inv[None, None, :]).astype(np.float32)
    return np.concatenate([np.cos(ang), np.sin(ang)], axis=-1).astype(np.float32)


_ROPE_TAB = _rope_table()


def make_in_maps(inp, compact=False):
    maps = []
    nphys = inp["cache_cmp"].shape[1]
    cc = np.ascontiguousarray(inp["cache_cmp"]).reshape(2, nphys * 128, 512)
    cs = np.ascontiguousarray(inp["cache_sel"]).reshape(2, nphys * 128, 512)
    for c in range(8):
        s0, s1 = c * NS, (c + 1) * NS
        m = {}
        m["xp"] = np.ascontiguousarray(inp["x_prompt"][c])
        m["xs"] = np.ascontiguousarray(inp["x_sample"][s0:s1]).reshape(NS * DS, D)
        m["state_gdn"] = np.ascontiguousarray(inp["state_gdn"][:, s0:s1])
        m["state_gdn_conv"] = np.ascontiguousarray(inp["state_gdn_conv"][:, s0:s1])
        m["state_win"] = np.ascontiguousarray(inp["state_win"][:, s0:s1]).reshape(2, NS, 512, 512)
        m["state_ffn_conv"] = np.ascontiguousarray(inp["state_ffn_conv"][:, s0:s1])
        pt = np.ascontiguousarray(inp["page_table"][s0:s1]).astype(np.int32)
        if compact:
            flat = pt.reshape(-1)
            m["cache_cmp"] = np.ascontiguousarray(inp["cache_cmp"][:, flat]).reshape(2, flat.size * 128, 512)
            m["cache_sel"] = np.ascontiguousarray(inp["cache_sel"][:, flat]).reshape(2, flat.size * 128, 512)
            pt = np.arange(flat.size, dtype=np.int32).reshape(NS, NPG)
        else:
            m["cache_cmp"] = cc
            m["cache_sel"] = cs
        m["page_table"] = pt
        m["rope_tab"] = _ROPE_TAB
        for nm in ["norm_mix", "norm_ffn", "norm_final", "gdn_w_in", "gdn_conv_w", "gdn_a_log", "gdn_dt_bias", "gdn_norm_w",
                   "gdn_w_out", "nsa_w_in", "nsa_cmp_pe", "nsa_cmp_w", "nsa_w_out", "ffn_w_up", "ffn_conv_w", "ffn_conv_b", "ffn_w_down"]:
            m[nm] = np.ascontiguousarray(inp[nm])
        maps.append(m)
    return maps


def assemble(results):
    def cat(nm, axis):
        return np.concatenate([r[nm] for r in results], axis=axis)

    def stack(nm, axis):
        return np.stack([r[nm] for r in results], axis=axis)
    y_p = stack("y_p", 0)
    y_s = cat("y_s", 0).reshape(32, DS, D)
    gdn_p = stack("gdn_p", 1)
    gdn_s = cat("gdn_s", 1)
    gconv_p = stack("gconv_p", 1)
    gconv_s = cat("gconv_s", 1)
    cmp_p = stack("cmp_p", 1).reshape(2, 8, LP, 2, 4, 64)
    cmp_s = cat("cmp_s", 1).reshape(2, 32, DS, 2, 4, 64)
    sel_p = stack("sel_p", 1).reshape(2, 8, LP, 2, 4, 64)
    sel_s = cat("sel_s", 1).reshape(2, 32, DS, 2, 4, 64)
    win_p = stack("win_p", 1).reshape(2, 8, 512, 2, 4, 64)
    win_s = cat("win_s", 1).reshape(2, 32, 512, 2, 4, 64)
    ffn_p = stack("ffn_p", 1)
    ffn_s = cat("ffn_s", 1)
    return (y_p, y_s, gdn_p, gdn_s, gconv_p, gconv_s, cmp_p, cmp_s, sel_p, sel_s, win_p, win_s, ffn_p, ffn_s)


def kernel(**inputs):
    inp = {k_: np.asarray(v) for k_, v in inputs.items()}
    n_phys = inp["cache_cmp"].shape[1]
    nc = build(n_phys)
    maps = make_in_maps(inp)
    res = run_bass_kernel_spmd(nc, maps, core_ids=list(range(8)))
    return assemble(res.results)
```

```python
import numpy as np
import concourse.bass as bass
import concourse.mybir as mybir
from concourse.bass_utils import run_bass_kernel_spmd

F32 = mybir.dt.float32
BF16 = mybir.dt.bfloat16
I32 = mybir.dt.int32
U32 = mybir.dt.uint32
AF = mybir.ActivationFunctionType
ALU = mybir.AluOpType
AX = mybir.AxisListType

D = 1024
LP = 2048
NS = 4
DS = 4
NT = LP + NS * DS
DFF = 2816
NPG = 64
EPS = 1e-6
GW = 4112
NW = 2608


class Buf:
    __slots__ = ("t", "name", "lw", "rd", "psum")

    def __init__(self, t, name, psum=False):
        self.t = t
        self.name = name
        self.lw = None
        self.rd = []
        self.psum = psum

    def __getitem__(self, idx):
        return self.t[idx]

    def ap(self):
        return self.t.ap()


class Prog:
    ENGS = ("pe", "act", "dve", "pool", "sp")

    def __init__(self, nc, ndma=8):
        self.nc = nc
        self.stack = []
        self.ops = {e: [] for e in self.ENGS}
        self.sems = {}
        self.cnt = {}
        self.waited = {e: {} for e in self.ENGS}
        self.semguards = []
        self.ekey = {}
        self.epoch = {}
        self.LIMIT = 3000
        for e in self.ENGS:
            self._mksem("E_" + e)
            self.ekey[e] = "E_" + e
            self.epoch[e] = 0
        self.dkey = {}
        self.ndma = ndma
        self.dma_rr = {"sp": 0, "pool": 0, "act": 0}
        for q in ("sp", "pool", "act"):
            for i in range(ndma):
                self._mksem("D_%s%d" % (q, i))
                self.dkey[(q, i)] = "D_%s%d" % (q, i)
        self.nrot = 0
        self.nbuf = 0
        self.nops = 0

    def _mksem(self, key):
        g = self.nc.semaphore(key)
        s = g.__enter__()
        self.semguards.append(g)
        self.sems[key] = s
        self.cnt[key] = 0

    def sb(self, shape, dt=F32, name=None):
        self.nbuf += 1
        name = (name or "sb") + "_%d" % self.nbuf
        g = self.nc.sbuf_tensor(name, list(shape), dt)
        t = g.__enter__()
        self.stack.append(g)
        return Buf(t, name)

    def ps(self, shape, dt=F32, name=None):
        self.nbuf += 1
        name = (name or "ps") + "_%d" % self.nbuf
        g = self.nc.psum_tensor(name, list(shape), dt)
        t = g.__enter__()
        self.stack.append(g)
        return Buf(t, name, psum=True)

    def dram(self, name, shape, dt=F32, kind="Internal"):
        t = self.nc.dram_tensor(name, list(shape), dt, kind=kind)
        return Buf(t, name)

    def mark(self):
        return len(self.stack)

    def release(self, mark):
        self.barrier()
        while len(self.stack) > mark:
            g = self.stack.pop()
            g.__exit__(None, None, None)

    def _need(self, eng, ev, waits):
        if ev is None:
            return
        key, val = ev
        if eng == "pe" and key.startswith("E_pe"):
            return
        if self.waited[eng].get(key, 0) >= val:
            return
        self.waited[eng][key] = val
        waits.append((key, val))

    def _deps(self, eng, reads, writes):
        waits = []
        for b in reads:
            self._need(eng, b.lw, waits)
            if b.psum:
                for ev in b.rd:
                    if not ev[0].startswith("E_" + eng):
                        self._need(eng, ev, waits)
        for b in writes:
            self._need(eng, b.lw, waits)
            for ev in b.rd:
                self._need(eng, ev, waits)
        return waits

    def _commit(self, ev, reads, writes):
        for b in reads:
            b.rd.append(ev)
            if len(b.rd) > 48:
                m = {}
                for k, v in b.rd:
                    if m.get(k, 0) < v:
                        m[k] = v
                b.rd = list(m.items())
        for b in writes:
            b.lw = ev
            b.rd = []

    def op(self, eng, fn, reads=(), writes=()):
        waits = self._deps(eng, reads, writes)
        key = self.ekey[eng]
        self.cnt[key] += 1
        ev = (key, self.cnt[key])
        self.ops[eng].append((waits, fn, (key, 1)))
        self._commit(ev, reads, writes)
        self.nops += 1
        if self.cnt[key] >= self.LIMIT:
            self.epoch[eng] += 1
            self.ekey[eng] = self._rotate(key, "E_%s#%d" % (eng, self.epoch[eng]))
        return ev

    def _rotate(self, old_key, new_key):
        final = self.cnt[old_key]
        for e in self.ENGS:
            waits = []
            self._need(e, (old_key, final), waits)
            if waits:
                self.ops[e].append((waits, None, None))
        self._mksem(new_key)
        return new_key

    def i(self, eng, name, *args, reads=(), writes=(), **kw):
        def fn(e, name=name, args=args, kw=kw, eng=eng):
            try:
                return getattr(e, name)(*args, **kw)
            except Exception as ex:
                raise RuntimeError("instr %s.%s failed: %s | args=%s kw=%s" % (eng, name, ex, [str(a)[:160] for a in args], kw)) from ex
        return self.op(eng, fn, reads, writes)

    def dma(self, q, out_ap, in_ap, reads=(), writes=(), indirect=None, **kw):
        i = self.dma_rr[q]
        self.dma_rr[q] = (i + 1) % self.ndma
        key = self.dkey[(q, i)]
        if self.cnt[key] >= self.LIMIT:
            self.nrot += 1
            key = self._rotate(key, "D_%s%d#%d" % (q, i, self.nrot))
            self.dkey[(q, i)] = key
        waits = self._deps(q, reads, writes)
        if self.cnt[key] > 0:
            self._need(q, (key, self.cnt[key]), waits)
        self.cnt[key] += 16
        ev = (key, self.cnt[key])
        if indirect is None:
            def fn(e, out_ap=out_ap, in_ap=in_ap, kw=kw):
                return e.dma_start(out=out_ap, in_=in_ap, **kw)
        else:
            def fn(e, out_ap=out_ap, in_ap=in_ap, idx=indirect):
                return e.indirect_dma_start(out=out_ap, out_offset=None, in_=in_ap,
                                            in_offset=bass.IndirectOffsetOnAxis(ap=idx, axis=0))
        self.ops[q].append((waits, fn, (key, 16)))
        self._commit(ev, reads, writes)
        self.nops += 1
        return ev

    def barrier(self):
        for e in self.ENGS:
            waits = []
            for key, c in self.cnt.items():
                if c > 0:
                    self._need(e, (key, c), waits)
            if waits:
                self.ops[e].append((waits, None, None))

    def finish(self):
        self.barrier()
        nc = self.nc
        hmap = {"pe": "tensor", "act": "scalar", "dve": "vector", "pool": "gpsimd", "sp": "sync"}
        with nc.Block() as block:
            for e in self.ENGS:
                oplist = self.ops[e]

                def body(h, oplist=oplist):
                    for waits, fn, inc in oplist:
                        for key, val in waits:
                            h.wait_ge(self.sems[key], val)
                        if fn is not None:
                            ins = fn(h)
                            ins.then_inc(self.sems[inc[0]], inc[1])
                getattr(block, hmap[e])(body)
        while self.stack:
            self.stack.pop().__exit__(None, None, None)
        for g in reversed(self.semguards):
            g.__exit__(None, None, None)


def run_il(gens):
    gens = [g for g in gens if g is not None]
    while gens:
        for g in list(gens):
            try:
                next(g)
            except StopIteration:
                gens.remove(g)


def chunks(t0, t1, sz=512):
    out = []
    t = t0
    while t < t1:
        n = min(sz, t1 - t)
        out.append((t, n))
        t += n
    return out


class K:
    pass


class StopNSA(Exception):
    pass


STOP = [None]
SKIP_PROMPT = [False]


def ck(name):
    if STOP[0] == name:
        raise StopNSA(name)


def build(n_phys, n_layers=4, mixers=True):
    nc = bass.Bass("TRN2", target_bir_lowering=False)
    P = Prog(nc)
    k = K()
    k.P = P
    k.nc = nc
    k.n_phys = n_phys
    EI, EO = "ExternalInput", "ExternalOutput"
    d = {}
    k.d = d
    d["xp"] = P.dram("xp", [LP, D], F32, EI)
    d["xs"] = P.dram("xs", [NS * DS, D], F32, EI)
    d["state_gdn"] = P.dram("state_gdn", [2, NS, 8, 128, 128], F32, EI)
    d["state_gdn_conv"] = P.dram("state_gdn_conv", [2, NS, 3, 3072], F32, EI)
    d["cache_cmp"] = P.dram("cache_cmp", [2, n_phys * 128, 512], F32, EI)
    d["cache_sel"] = P.dram("cache_sel", [2, n_phys * 128, 512], F32, EI)
    d["state_win"] = P.dram("state_win", [2, NS, 512, 512], F32, EI)
    d["state_ffn_conv"] = P.dram("state_ffn_conv", [4, NS, 2, 2 * DFF], F32, EI)
    d["page_table"] = P.dram("page_table", [NS, NPG], I32, EI)
    d["norm_mix"] = P.dram("norm_mix", [4, D], F32, EI)
    d["norm_ffn"] = P.dram("norm_ffn", [4, D], F32, EI)
    d["norm_final"] = P.dram("norm_final", [D], F32, EI)
    d["gdn_w_in"] = P.dram("gdn_w_in", [2, D, GW], F32, EI)
    d["gdn_conv_w"] = P.dram("gdn_conv_w", [2, 4, 3072], F32, EI)
    d["gdn_a_log"] = P.dram("gdn_a_log", [2, 8], F32, EI)
    d["gdn_dt_bias"] = P.dram("gdn_dt_bias", [2, 8], F32, EI)
    d["gdn_norm_w"] = P.dram("gdn_norm_w", [2, 128], F32, EI)
    d["gdn_w_out"] = P.dram("gdn_w_out", [2, D, D], F32, EI)
    d["nsa_w_in"] = P.dram("nsa_w_in", [2, D, NW], F32, EI)
    d["nsa_cmp_pe"] = P.dram("nsa_cmp_pe", [2, 64, 2, 64], F32, EI)
    d["nsa_cmp_w"] = P.dram("nsa_cmp_w", [2, 64, 2, 64, 64], F32, EI)
    d["nsa_w_out"] = P.dram("nsa_w_out", [2, D, D], F32, EI)
    d["ffn_w_up"] = P.dram("ffn_w_up", [4, D, 2 * DFF], F32, EI)
    d["ffn_conv_w"] = P.dram("ffn_conv_w", [4, 3, 2 * DFF], F32, EI)
    d["ffn_conv_b"] = P.dram("ffn_conv_b", [4, 2 * DFF], F32, EI)
    d["ffn_w_down"] = P.dram("ffn_w_down", [4, DFF, D], F32, EI)
    d["rope_tab"] = P.dram("rope_tab", [17, 128, 64], F32, EI)
    d["y_p"] = P.dram("y_p", [LP, D], F32, EO)
    d["y_s"] = P.dram("y_s", [NS * DS, D], F32, EO)
    d["gdn_p"] = P.dram("gdn_p", [2, 8, 128, 128], F32, EO)
    d["gdn_s"] = P.dram("gdn_s", [2, NS, 8, 128, 128], F32, EO)
    d["gconv_p"] = P.dram("gconv_p", [2, 3, 3072], F32, EO)
    d["gconv_s"] = P.dram("gconv_s", [2, NS, 3, 3072], F32, EO)
    d["cmp_p"] = P.dram("cmp_p", [2, LP, 512], F32, EO)
    d["cmp_s"] = P.dram("cmp_s", [2, NS * DS, 512], F32, EO)
    d["sel_p"] = P.dram("sel_p", [2, LP, 512], F32, EO)
    d["sel_s"] = P.dram("sel_s", [2, NS * DS, 512], F32, EO)
    d["win_p"] = P.dram("win_p", [2, 512, 512], F32, EO)
    d["win_s"] = P.dram("win_s", [2, NS, 512, 512], F32, EO)
    d["ffn_p"] = P.dram("ffn_p", [4, 2, 2 * DFF], F32, EO)
    d["ffn_s"] = P.dram("ffn_s", [4, NS, 2, 2 * DFF], F32, EO)

    k.R = [P.sb([128, NT], F32, "R%d" % i) for i in range(8)]
    k.ident = P.sb([128, 128], F32, "ident")
    k.identb = P.sb([128, 128], BF16, "identb")
    k.ones = P.sb([128, 128], F32, "ones")
    k.ncol = P.sb([128, 72], F32, "ncol")
    k.PS = [P.ps([128, 512], F32, "psb%d" % i) for i in range(8)]
    k.stg = P.sb([128, 128], F32, "stg")

    P.i("pool", "memset", k.ident[:], 0.0, writes=[k.ident])
    P.i("pool", "affine_select", out=k.ident[:], in_=k.ident[:], pattern=[[-1, 128]],
                                            compare_op=ALU.not_equal, fill=1.0, base=0, channel_multiplier=1,
         reads=[k.ident], writes=[k.ident])
    P.i("dve", "tensor_copy", k.identb[:], k.ident[:], reads=[k.ident], writes=[k.identb])
    P.i("pool", "memset", k.ones[:], 1.0, writes=[k.ones])


    k.ltri = P.sb([64, 64], F32, "ltri")
    k.msl = P.sb([64, 64], F32, "msl")
    k.e63 = P.sb([64, 128], F32, "e63")
    k.pm4 = P.sb([64, 1], F32, "pm4")
    P.i("pool", "memset", k.ltri[:], 1.0, writes=[k.ltri])
    P.i("pool", "affine_select", out=k.ltri[:], in_=k.ltri[:], pattern=[[1, 64]], compare_op=ALU.is_ge, fill=0.0, base=0,
        channel_multiplier=-1, reads=[k.ltri], writes=[k.ltri])
    P.i("pool", "memset", k.msl[:], 1.0, writes=[k.msl])
    P.i("pool", "affine_select", out=k.msl[:], in_=k.msl[:], pattern=[[-1, 64]], compare_op=ALU.is_ge, fill=0.0, base=-1,
        channel_multiplier=1, reads=[k.msl], writes=[k.msl])
    P.i("pool", "memset", k.e63[:], 0.0, writes=[k.e63])
    P.i("pool", "affine_select", out=k.e63[:], in_=k.e63[:], pattern=[[0, 128]], compare_op=ALU.not_equal, fill=1.0, base=-63,
        channel_multiplier=1, reads=[k.e63], writes=[k.e63])
    P.i("pool", "memset", k.pm4[:], 1.0, writes=[k.pm4])
    P.i("pool", "affine_select", out=k.pm4[:], in_=k.pm4[:], pattern=[[0, 1]], compare_op=ALU.is_ge, fill=0.0, base=3,
        channel_multiplier=-1, reads=[k.pm4], writes=[k.pm4])


    k.caus = P.sb([128, 128], BF16, "caus")
    k.wmask = P.sb([128, 128], BF16, "wmask")
    P.i("pool", "memset", k.caus[:], 1.0, writes=[k.caus])
    P.i("pool", "affine_select", out=k.caus[:], in_=k.caus[:], pattern=[[1, 128]], compare_op=ALU.is_ge, fill=0.0, base=0,
        channel_multiplier=-1, reads=[k.caus], writes=[k.caus])
    P.i("pool", "memset", k.wmask[:], 1.0, writes=[k.wmask])
    P.i("pool", "affine_select", out=k.wmask[:], in_=k.wmask[:], pattern=[[-1, 128]], compare_op=ALU.is_gt, fill=0.0, base=0,
        channel_multiplier=1, reads=[k.wmask], writes=[k.wmask])

    load_cols(k, d["norm_mix"], d["norm_mix"].ap().rearrange("l (t p) -> (l t) p", p=128), 32, k.ncol, 0)
    load_cols(k, d["norm_ffn"], d["norm_ffn"].ap().rearrange("l (t p) -> (l t) p", p=128), 32, k.ncol, 32)
    load_cols(k, d["norm_final"], d["norm_final"].ap().rearrange("(t p) -> t p", p=128), 8, k.ncol, 64)

    load_x(k)
    for l in range(n_layers):
        if mixers:
            if l % 2 == 0:
                gdn_layer(k, l)
            else:
                nsa_layer(k, l)
        ffn_layer(k, l)
    final_out(k)
    P.finish()
    return nc


def load_cols(k, src_buf, src2d, nrows, dst, col0, ps=None):
    P = k.P
    ps = ps or k.PS[6]
    r0 = 0
    while r0 < nrows:
        n = min(128, nrows - r0)
        P.dma("sp", k.stg[0:n, :], src2d[r0:r0 + n, :], reads=[src_buf], writes=[k.stg])
        P.i("pe", "transpose", ps[:, 0:n], k.stg[0:n, :], k.ident[0:n, 0:n],
             reads=[k.stg, k.ident], writes=[ps])
        P.i("dve", "tensor_copy", dst[:, col0 + r0:col0 + r0 + n], ps[:, 0:n], reads=[ps], writes=[dst])
        r0 += n


def load_x(k):
    P = k.P
    d = k.d
    m = P.mark()
    xt = [P.sb([128, 4, D], F32, "xt%d" % i) for i in range(2)]
    for g in range(4):
        b = xt[g % 2]
        P.dma("sp", b[:], d["xp"].ap()[g * 512:(g + 1) * 512, :].rearrange("(a p) c -> p a c", p=128),
              reads=[d["xp"]], writes=[b])
        for dt in range(8):
            ps = k.PS[dt % 4]
            for a in range(4):
                P.i("pe", "transpose", ps[:, a * 128:(a + 1) * 128], b[:, a, dt * 128:(dt + 1) * 128], k.ident[:],
                     reads=[b, k.ident], writes=[ps])
            eng = "dve" if dt % 2 == 0 else "act"
            if eng == "dve":
                P.i("dve", "tensor_copy", k.R[dt][:, g * 512:(g + 1) * 512], ps[:], reads=[ps], writes=[k.R[dt]])
            else:
                P.i("act", "copy", k.R[dt][:, g * 512:(g + 1) * 512], ps[:], reads=[ps], writes=[k.R[dt]])
    b = xt[0]
    P.dma("sp", b[0:16, 0, :], d["xs"].ap(), reads=[d["xs"]], writes=[b])
    ps = k.PS[0]
    for dt in range(8):
        P.i("pe", "transpose", ps[:, dt * 16:(dt + 1) * 16], b[0:16, 0, dt * 128:(dt + 1) * 128], k.ident[0:16, 0:16],
             reads=[b, k.ident], writes=[ps])
    for dt in range(8):
        P.i("dve", "tensor_copy", k.R[dt][:, LP:NT], ps[:, dt * 16:(dt + 1) * 16], reads=[ps], writes=[k.R[dt]])
    P.release(m)


def rmsnorm(k, wcol0, t0, n, out_tiles, o0, scratch):
    P = k.P
    for (c0, cn) in chunks(t0, t0 + n):
        ps = k.PS[6]
        for dt in range(8):
            sq = scratch["sq"][dt % 2]
            P.i("act", "activation", sq[:, 0:cn], k.R[dt][:, c0:c0 + cn], AF.Square,
                 reads=[k.R[dt]], writes=[sq])
            P.i("pe", "matmul", ps[:, 0:cn], k.ones[:], sq[:, 0:cn], start=(dt == 0), stop=(dt == 7),
                 reads=[sq, k.ones], writes=[ps])
        rstd = scratch["rstd"]
        P.i("dve", "tensor_scalar", rstd[:, 0:cn], ps[:, 0:cn], 1.0 / D, EPS, ALU.mult, ALU.add, reads=[ps], writes=[rstd])
        P.i("act", "activation", rstd[:, 0:cn], rstd[:, 0:cn], AF.Ln, reads=[rstd], writes=[rstd])
        P.i("act", "activation", rstd[:, 0:cn], rstd[:, 0:cn], AF.Exp, scale=-0.5, reads=[rstd], writes=[rstd])
        for dt in range(8):
            eng = "dve"
            P.i(eng, "scalar_tensor_tensor",
                out_tiles[dt][:, o0 + c0 - t0:o0 + c0 - t0 + cn], k.R[dt][:, c0:c0 + cn],
                k.ncol[:, wcol0 + dt:wcol0 + dt + 1], rstd[:, 0:cn], ALU.mult, ALU.mult,
                reads=[k.R[dt], k.ncol, rstd], writes=[out_tiles[dt]])


def ffn_layer(k, l):
    P = k.P
    d = k.d
    m = P.mark()
    NH = 1042
    hT = [P.sb([128, NH], BF16, "fh%d" % i) for i in range(8)]
    act = [P.sb([128, 1040], BF16, "fa%d" % i) for i in range(22)]
    wup = [P.sb([128, 8, 512], BF16, "wup%d" % i) for i in range(2)]
    wdn = [P.sb([128, 22, 128], BF16, "wdn%d" % i) for i in range(2)]
    ub = [[P.sb([128, 1056], F32, "ub%d%d" % (i, j)) for j in range(2)] for i in range(2)]
    cb2 = [[P.sb([128, 1040], F32, "cb%d%d" % (p_, i)) for i in range(2)] for p_ in range(2)]
    scratch = {"sq": [P.sb([128, 512], F32, "sq%d" % i) for i in range(2)], "rstd": P.sb([128, 512], F32, "rstd")}
    fpar = P.sb([128, 176], F32, "fpar")
    hist = P.sb([128, 352], F32, "hist")
    hsel = P.sb([128, 8, 16], BF16, "hsel")
    halo = P.sb([128, 88], F32, "halo")
    strow = P.sb([16, 2 * DFF], F32, "strow") if False else None
    st_sb = P.sb([16, 512], F32, "stsb")

    for j in range(3):
        load_cols(k, d["ffn_conv_w"], d["ffn_conv_w"].ap()[l, j].rearrange("(t p) -> t p", p=128), 44, fpar, j * 44)
    load_cols(k, d["ffn_conv_b"], d["ffn_conv_b"].ap()[l].rearrange("(t p) -> t p", p=128), 44, fpar, 132)
    load_cols(k, d["state_ffn_conv"], d["state_ffn_conv"].ap()[l].rearrange("s j (t p) -> (s j t) p", p=128), 352, hist, 0)

    wdma = [0]

    def load_wup(jb, buf):
        for which in range(2):
            c0 = which * DFF + jb * 256
            P.dma("pool", buf[:, :, which * 256:(which + 1) * 256],
                  d["ffn_w_up"].ap()[l, :, c0:c0 + 256].rearrange("(kt p) c -> p kt c", p=128),
                  reads=[d["ffn_w_up"]], writes=[buf])

    def load_wdn(dt, buf):
        P.dma("pool", buf[:], d["ffn_w_down"].ap()[l, :, dt * 128:(dt + 1) * 128].rearrange("(j p) c -> p j c", p=128),
              reads=[d["ffn_w_down"]], writes=[buf])

    psi = [0]
    for half in range(2):
        if half == 0:
            t0, n = 0, 1024
            npr = 1024
            ucol0 = 2
        else:
            t0, n = 1024, 1040
            npr = 1024
            ucol0 = 2
        rmsnorm(k, 32 + l * 8, t0, n, hT, 0, scratch)
        nprm = n - (16 if half == 1 else 0)
        if half == 1:
            for kt in range(8):
                P.i("pool", "tensor_copy", hsel[:, kt, 0:2], hT[kt][:, 1022:1024], reads=[hT[kt]], writes=[hsel])
                P.i("pool", "tensor_copy",
                    hsel[:, kt, 2:10].rearrange("p (s c) -> p s c", c=2),
                    hT[kt][:, 1024:1040].rearrange("p (s c) -> p s c", c=4)[:, :, 2:4], reads=[hT[kt]], writes=[hsel])
        pend = []
        load_wup(0, wup[0])
        for jb in range(11):
            wb = wup[jb % 2]
            if jb + 1 < 11:
                load_wup(jb + 1, wup[(jb + 1) % 2])
            if half == 1:
                pst = k.PS[7]
                for which in range(2):
                    for kt in range(8):
                        P.i("pe", "matmul",
                            pst[0:10, which * 256:(which + 1) * 256], hsel[:, kt, 0:10], wb[:, kt, which * 256:(which + 1) * 256],
                            start=(kt == 0), stop=(kt == 7), reads=[hsel, wb], writes=[pst])
                P.i("act", "copy", st_sb[0:10, :], pst[0:10, :], reads=[pst], writes=[st_sb])
                for which in range(2):
                    c0 = which * DFF + jb * 256
                    P.dma("sp", d["ffn_p"].ap()[l, :, c0:c0 + 256], st_sb[0:2, which * 256:(which + 1) * 256], reads=[st_sb], writes=[d["ffn_p"]])
                    P.dma("sp", d["ffn_s"].ap()[l, :, :, c0:c0 + 256].rearrange("s j c -> (s j) c"), st_sb[2:10, which * 256:(which + 1) * 256], reads=[st_sb], writes=[d["ffn_s"]])
            for jj in range(2):
                j = jb * 2 + jj
                par = j % 2
                for which in range(2):
                    u = ub[par][which]
                    tix = j + 22 * which
                    if half == 0:
                        P.i("pool", "memset", u[:, 0:2], 0.0, writes=[u])
                    else:
                        P.i("pool", "tensor_copy", u[:, 0:2], halo[:, tix * 2:tix * 2 + 2], reads=[halo], writes=[u])
                        P.i("pool", "tensor_copy",
                            u[:, 1026:1050].rearrange("p (s c) -> p s c", c=6)[:, :, 0:2],
                            hist[:, :].rearrange("p (s j t) -> p s j t", j=2, t=44)[:, :, :, tix], reads=[hist], writes=[u])
                    for (c0, cn) in chunks(0, nprm):
                        ps = k.PS[psi[0] % 4]
                        psi[0] += 1
                        for kt in range(8):
                            P.i("pe", "matmul",
                                ps[:, 0:cn], wb[:, kt, which * 256 + jj * 128:which * 256 + (jj + 1) * 128], hT[kt][:, c0:c0 + cn],
                                start=(kt == 0), stop=(kt == 7), reads=[wb, hT[kt]], writes=[ps])
                        P.i("act", "copy", u[:, ucol0 + c0:ucol0 + c0 + cn], ps[:, 0:cn], reads=[ps], writes=[u])
                    if half == 1:
                        ps = k.PS[psi[0] % 4]
                        psi[0] += 1
                        for kt in range(8):
                            P.i("pe", "matmul",
                                ps[:, 0:16], wb[:, kt, which * 256 + jj * 128:which * 256 + (jj + 1) * 128], hT[kt][:, 1024:1040],
                                start=(kt == 0), stop=(kt == 7), reads=[wb, hT[kt]], writes=[ps])
                        P.i("act", "copy",
                            u[:, 1026:1050].rearrange("p (s c) -> p s c", c=6)[:, :, 2:6],
                            ps[:, 0:16].rearrange("p (s c) -> p s c", c=4), reads=[ps], writes=[u])
                    if half == 0:
                        P.i("pool", "tensor_copy", halo[:, tix * 2:tix * 2 + 2], u[:, 1024:1026], reads=[u], writes=[halo])
                    c = cb2[par][which]
                    w0 = fpar[:, tix:tix + 1]
                    w1 = fpar[:, 44 + tix:44 + tix + 1]
                    w2 = fpar[:, 88 + tix:88 + tix + 1]
                    bb = fpar[:, 132 + tix:132 + tix + 1]
                    eng = "dve"
                    regions = [(lambda a, off: a[:, off:off + npr], lambda a: a[:, 0:npr])]
                    if half == 1:
                        regions.append((lambda a, off: a[:, 1026:1050].rearrange("p (s c) -> p s c", c=6)[:, :, off:off + 4],
                                        lambda a: a[:, 1024:1040].rearrange("p (s c) -> p s c", c=4)))
                    for (uin, cout) in regions:
                        P.i(eng, "tensor_scalar", cout(c), uin(u, 2), w2, bb, ALU.mult, ALU.add,
                             reads=[u, fpar], writes=[c])
                        P.i(eng, "scalar_tensor_tensor", cout(c), uin(u, 1), w1, cout(c), ALU.mult, ALU.add,
                             reads=[u, fpar, c], writes=[c])
                        P.i(eng, "scalar_tensor_tensor", cout(c), uin(u, 0), w0, cout(c), ALU.mult, ALU.add,
                             reads=[u, fpar, c], writes=[c])
                ntok = npr + (16 if half == 1 else 0)

                def fin(j=j, par=par, ntok=ntok):
                    cb = cb2[par]
                    P.i("act", "activation", cb[0][:, 0:ntok], cb[0][:, 0:ntok], AF.Silu, reads=[cb[0]], writes=[cb[0]])
                    P.i("dve", "tensor_tensor", act[j][:, 0:ntok], cb[0][:, 0:ntok], cb[1][:, 0:ntok], ALU.mult,
                         reads=[cb[0], cb[1]], writes=[act[j]])
                if pend:
                    pend.pop()()
                pend.append(fin)
        while pend:
            pend.pop()()
        ntok = npr + (16 if half == 1 else 0)
        tok0 = 0 if half == 0 else 1024
        load_wdn(0, wdn[0])
        for dt in range(8):
            wd = wdn[dt % 2]
            if dt + 1 < 8:
                load_wdn(dt + 1, wdn[(dt + 1) % 2])
            for (c0, cn) in chunks(0, ntok):
                ps = k.PS[4 + (psi[0] % 2)]
                psi[0] += 1
                for j in range(22):
                    P.i("pe", "matmul", ps[:, 0:cn], wd[:, j, :], act[j][:, c0:c0 + cn], start=(j == 0), stop=(j == 21),
                         reads=[wd, act[j]], writes=[ps])
                P.i("dve", "tensor_tensor",
                    k.R[dt][:, tok0 + c0:tok0 + c0 + cn], ps[:, 0:cn], k.R[dt][:, tok0 + c0:tok0 + c0 + cn], ALU.add,
                    reads=[ps, k.R[dt]], writes=[k.R[dt]])
    P.release(m)


def final_out(k):
    P = k.P
    d = k.d
    m = P.mark()
    yt = [P.sb([128, 4, D], F32, "yt%d" % i) for i in range(2)]
    hn = [P.sb([128, 528], F32, "hn%d" % i) for i in range(8)]
    scratch = {"sq": [P.sb([128, 512], F32, "sq%d" % i) for i in range(2)], "rstd": P.sb([128, 512], F32, "rstd")}
    for g, (c0, cn) in enumerate(chunks(0, NT)):
        rmsnorm(k, 64, c0, cn, hn, 0, scratch)
        b = yt[g % 2]
        na = (cn + 127) // 128
        for a in range(na):
            tn = min(128, cn - a * 128)
            for dt in range(8):
                ps = k.PS[(a * 8 + dt) // 4 % 4]
                q = dt % 4
                P.i("pe", "transpose", ps[0:tn, q * 128:(q + 1) * 128], hn[dt][:, a * 128:a * 128 + tn], k.ident[:],
                     reads=[hn[dt], k.ident], writes=[ps])
                if q == 3:
                    h4 = dt // 4
                    eng = "dve" if h4 == 0 else "act"
                    if eng == "dve":
                        P.i("dve", "tensor_copy", b[0:tn, a, h4 * 512:(h4 + 1) * 512], ps[0:tn, :], reads=[ps], writes=[b])
                    else:
                        P.i("act", "copy", b[0:tn, a, h4 * 512:(h4 + 1) * 512], ps[0:tn, :], reads=[ps], writes=[b])
        if c0 < LP:
            P.dma("sp", d["y_p"].ap()[c0:c0 + cn, :].rearrange("(a p) c -> p a c", p=128), b[:], reads=[b], writes=[d["y_p"]])
        else:
            P.dma("sp", d["y_s"].ap(), b[0:16, 0, :], reads=[b], writes=[d["y_s"]])
    P.release(m)


def gdn_layer(k, l):
    P = k.P
    d = k.d
    j = l // 2
    m = P.mark()
    TG = 2304
    hT = [P.sb([128, NT], BF16, "gh%d" % i) for i in range(8)]
    scratch = {"sq": [P.sb([128, 512], F32, "sq%d" % i) for i in range(2)], "rstd": P.sb([128, 512], F32, "rstd")}
    sq, rstd = scratch["sq"], scratch["rstd"]
    rmsnorm(k, l * 8, 0, NT, hT, 0, scratch)
    PS = k.PS
    gcw = P.sb([128, 96], F32, "gcw")
    load_cols(k, d["gdn_conv_w"], d["gdn_conv_w"].ap()[j].rearrange("r (t p) -> (r t) p", p=128), 96, gcw, 0)
    gnw = P.sb([128, 1], F32, "gnw")
    load_cols(k, d["gdn_norm_w"], d["gdn_norm_w"].ap()[j:j + 1, :], 1, gnw, 0)
    hsel = P.sb([128, 8, 16], BF16, "ghsel")
    beta = P.sb([64, 36, 8], F32, "beta")
    Gc = P.sb([64, 36, 8], F32, "Gc")
    egl = P.sb([128, 288], F32, "egl")
    edec = P.sb([64, 36, 8], F32, "edec")
    ebg = P.sb([64, 36, 8], F32, "ebg")
    nbeta = P.sb([64, 36, 8], F32, "nbeta")
    mpre = P.mark()
    wab = P.sb([128, 8, 16], BF16, "wab")
    P.dma("pool", wab[:], d["gdn_w_in"].ap()[j, :, 4096:4112].rearrange("(kt p) c -> p kt c", p=128), reads=[d["gdn_w_in"]], writes=[wab])
    hsp = P.sb([128, 8, 4, 64], BF16, "hsp")
    P.i("pool", "memset", hsp[:], 0.0, writes=[hsp])
    for kt in range(8):
        P.i("pool", "tensor_copy", hsp[:, kt, :, 0:4], hT[kt][:, LP:NT].rearrange("p (s c) -> p s c", c=4), reads=[hT[kt]], writes=[hsp])
        P.i("dve", "tensor_copy", hsel[:, kt, 0:3], hT[kt][:, LP - 3:LP], reads=[hT[kt]], writes=[hsel])
        P.i("dve", "tensor_copy", hsel[:, kt, 3:15].rearrange("p (s c) -> p s c", c=3),
            hT[kt][:, LP:NT].rearrange("p (s c) -> p s c", c=4)[:, :, 1:4], reads=[hT[kt]], writes=[hsel])
    ab_all = P.sb([64, 36, 16], F32, "ab_all")
    for n in range(36):
        ps = PS[0] if n < 32 else PS[1]
        c0 = (n % 32) * 16
        for kt in range(8):
            lhsT = hT[kt][:, n * 64:(n + 1) * 64] if n < 32 else hsp[:, kt, n - 32, :]
            P.i("pe", "matmul", ps[0:64, c0:c0 + 16], lhsT, wab[:, kt, :], start=(kt == 0), stop=(kt == 7),
                reads=[hT[kt], hsp, wab], writes=[ps])
    P.i("dve", "tensor_copy", ab_all[:, 0:32, :], PS[0][0:64, 0:512].rearrange("p (n c) -> p n c", c=16), reads=[PS[0]], writes=[ab_all])
    P.i("dve", "tensor_copy", ab_all[:, 32:36, :], PS[1][0:64, 0:64].rearrange("p (n c) -> p n c", c=16), reads=[PS[1]], writes=[ab_all])
    alog = P.sb([64, 8], F32, "alog")
    dtb = P.sb([64, 8], F32, "dtb")
    P.dma("sp", alog[:], d["gdn_a_log"].ap()[j].partition_broadcast(64), reads=[d["gdn_a_log"]], writes=[alog])
    P.dma("sp", dtb[:], d["gdn_dt_bias"].ap()[j].partition_broadcast(64), reads=[d["gdn_dt_bias"]], writes=[dtb])
    P.i("act", "activation", alog[:], alog[:], AF.Exp, reads=[alog], writes=[alog])
    P.i("dve", "tensor_scalar_mul", alog[:], alog[:], -1.0, reads=[alog], writes=[alog])
    g_all = P.sb([64, 36, 8], F32, "g_all")
    bc8 = lambda t: t[:, :].unsqueeze(1).to_broadcast([64, 36, 8])
    P.i("dve", "tensor_tensor", g_all[:], ab_all[:, :, 0:8], bc8(dtb), ALU.add, reads=[ab_all, dtb], writes=[g_all])
    P.i("act", "activation", g_all[:], g_all[:], AF.Exp, reads=[g_all], writes=[g_all])
    P.i("dve", "tensor_scalar_add", g_all[:], g_all[:], 1.0, reads=[g_all], writes=[g_all])
    P.i("act", "activation", g_all[:], g_all[:], AF.Ln, reads=[g_all], writes=[g_all])
    P.i("dve", "tensor_tensor", g_all[:], g_all[:], bc8(alog), ALU.mult, reads=[g_all, alog], writes=[g_all])
    P.i("act", "activation", beta[:], ab_all[:, :, 8:16], AF.Sigmoid, reads=[ab_all], writes=[beta])
    P.i("dve", "tensor_scalar_mul", g_all[:, 32:36, :], g_all[:, 32:36, :], k.pm4[:, 0:1], reads=[g_all, k.pm4], writes=[g_all])
    P.i("dve", "tensor_scalar_mul", beta[:, 32:36, :], beta[:, 32:36, :], k.pm4[:, 0:1], reads=[beta, k.pm4], writes=[beta])
    fl = lambda t: t[:].rearrange("p n h -> p (n h)")
    P.i("pe", "matmul", PS[0][0:64, 0:288], k.ltri[:], fl(g_all), start=True, stop=True, reads=[k.ltri, g_all], writes=[PS[0]])
    P.i("dve", "tensor_copy", fl(Gc), PS[0][0:64, 0:288], reads=[PS[0]], writes=[Gc])
    P.i("pe", "matmul", PS[1][:, 0:288], k.e63[:], fl(Gc), start=True, stop=True, reads=[k.e63, Gc], writes=[PS[1]])
    P.i("act", "activation", egl[:], PS[1][:, 0:288], AF.Exp, reads=[PS[1]], writes=[egl])
    P.i("dve", "tensor_tensor", fl(edec), PS[1][0:64, 0:288], fl(Gc), ALU.subtract, reads=[PS[1], Gc], writes=[edec])
    P.i("act", "activation", edec[:], edec[:], AF.Exp, reads=[edec], writes=[edec])
    P.i("act", "activation", ebg[:], Gc[:], AF.Exp, reads=[Gc], writes=[ebg])
    P.i("dve", "tensor_tensor", ebg[:], ebg[:], beta[:], ALU.mult, reads=[ebg, beta], writes=[ebg])
    P.i("dve", "tensor_scalar_mul", nbeta[:], beta[:], -1.0, reads=[beta], writes=[nbeta])
    P.release(mpre)

    wqkvz = [P.sb([128, 8, 128], BF16, "gw%d" % i) for i in range(4)]
    wout = P.sb([128, D], BF16, "gwout")
    xb = P.sb([128, 2080], F32, "gxb")
    X = [P.sb([128, TG], BF16, "gX%d" % i) for i in range(3)]
    for t in X:
        P.i("pool", "memset", t[:, LP:TG], 0.0, writes=[t])
    qgb = [P.sb([128, 512], BF16, "qgb%d" % i) for i in range(5)]
    zsb = P.sb([128, NT], BF16, "zsb")
    ogb = P.sb([128, NT], BF16, "ogb")
    cf = P.sb([128, 512], F32, "gcf")
    hist12 = P.sb([128, 12], F32, "hist12")
    st_sb = P.sb([16, 128], F32, "gst")
    G = {}
    for nm in ["negD", "decL", "decU", "tmp", "Q0", "Q1", "R0", "R1", "Tt", "egrow"]:
        G[nm] = P.sb([64 if nm != "egrow" else 128, 512], BF16 if nm[0] in "QR" else F32, "g" + nm)
    Ttq = P.sb([64, 512], BF16, "gTtq")
    i64bb = k.identb[0:64, 0:64].unsqueeze(1).to_broadcast([64, 8, 64])
    DG = P.sb([64, 512], F32, "gDG")
    Ttb2 = [P.sb([64, 512], BF16, "gTtb%d" % i) for i in range(2)]
    kb = P.sb([64, 8, 128], BF16, "gkb")
    kdec2 = [P.sb([64, 8, 128], BF16, "gkdec%d" % i) for i in range(2)]
    vb2 = [P.sb([64, 8, 128], BF16, "gvb%d" % i) for i in range(2)]
    nwk2 = [P.sb([128, 512], BF16, "gnwk%d" % i) for i in range(2)]
    qkm2 = [P.sb([64, 512], BF16, "gqkm%d" % i) for i in range(2)]
    S = P.sb([128, 128], F32, "gS")
    Sb = P.sb([128, 128], BF16, "gSb")
    ub = P.sb([64, 128], BF16, "gub")
    on = P.sb([128, 512], F32, "gon")
    i64b = k.ident[0:64, 0:64].unsqueeze(1).to_broadcast([64, 8, 64])
    v3 = lambda ap: ap.rearrange("p (n c) -> p n c", c=64)

    for h in range(8):
        for part in range(4):
            c0 = part * 1024 + h * 128
            P.dma("pool", wqkvz[part][:], d["gdn_w_in"].ap()[j, :, c0:c0 + 128].rearrange("(kt p) c -> p kt c", p=128),
                  reads=[d["gdn_w_in"]], writes=[wqkvz[part]])
        P.dma("pool", wout[:], d["gdn_w_out"].ap()[j, h * 128:(h + 1) * 128, :], reads=[d["gdn_w_out"]], writes=[wout])
        for part in range(4):
            w = wqkvz[part]
            col0 = part * 1024 + h * 128
            tix = part * 8 + h
            if part < 3:
                pst = PS[7]
                for kt in range(8):
                    P.i("pe", "matmul", pst[0:15, 256:384], hsel[:, kt, 0:15], w[:, kt, :], start=(kt == 0), stop=(kt == 7),
                        reads=[hsel, w], writes=[pst])
                P.i("act", "copy", st_sb[0:15, :], pst[0:15, 256:384], reads=[pst], writes=[st_sb])
                P.dma("sp", d["gconv_p"].ap()[j, :, col0:col0 + 128], st_sb[0:3, :], reads=[st_sb], writes=[d["gconv_p"]])
                P.dma("sp", d["gconv_s"].ap()[j, :, :, col0:col0 + 128].rearrange("s r c -> (s r) c"), st_sb[3:15, :],
                      reads=[st_sb], writes=[d["gconv_s"]])
                load_cols(k, d["state_gdn_conv"], d["state_gdn_conv"].ap()[j, :, :, col0:col0 + 128].rearrange("s r c -> (s r) c"),
                          12, hist12, 0)
                P.i("pool", "memset", xb[:, 0:3], 0.0, writes=[xb])
                P.i("pool", "tensor_copy", xb[:, 2051:2079].rearrange("p (s c) -> p s c", c=7)[:, :, 0:3],
                    hist12[:, :].rearrange("p (s c) -> p s c", c=3), reads=[hist12], writes=[xb])
            for ci, (c0, cn) in enumerate(chunks(0, NT)):
                ps = PS[ci % 2]
                for kt in range(8):
                    P.i("pe", "matmul", ps[:, 0:cn], w[:, kt, :], hT[kt][:, c0:c0 + cn], start=(kt == 0), stop=(kt == 7),
                        reads=[w, hT[kt]], writes=[ps])
                if part == 3:
                    P.i("act", "activation", zsb[:, c0:c0 + cn], ps[:, 0:cn], AF.Silu, reads=[ps], writes=[zsb])
                elif c0 < LP:
                    P.i("act", "copy", xb[:, 3 + c0:3 + c0 + cn], ps[:, 0:cn], reads=[ps], writes=[xb])
                else:
                    P.i("act", "copy", xb[:, 2051:2079].rearrange("p (s c) -> p s c", c=7)[:, :, 3:7],
                        ps[:, 0:16].rearrange("p (s c) -> p s c", c=4), reads=[ps], writes=[xb])
            if part == 3:
                continue
            wc = [gcw[:, r * 24 + tix:r * 24 + tix + 1] for r in range(4)]
            for (c0, cn) in chunks(0, NT):
                if c0 < LP:
                    src = lambda off: xb[:, c0 + off:c0 + off + cn]
                    cfv = cf[:, 0:cn]
                    dst = X[part][:, c0:c0 + cn]
                else:
                    src = lambda off: xb[:, 2051:2079].rearrange("p (s c) -> p s c", c=7)[:, :, off:off + 4]
                    cfv = cf[:, 0:16].rearrange("p (s c) -> p s c", c=4)
                    dst = X[part][:, LP:TG].rearrange("p (s c) -> p s c", c=64)[:, :, 0:4]
                P.i("dve", "tensor_scalar_mul", cfv, src(3), wc[3], reads=[xb, gcw], writes=[cf])
                for r in range(3):
                    P.i("dve", "scalar_tensor_tensor", cfv, src(r), wc[r], cfv, ALU.mult, ALU.add, reads=[xb, gcw, cf], writes=[cf])
                P.i("act", "activation", cf[:, 0:cn], cf[:, 0:cn], AF.Silu, reads=[cf], writes=[cf])
                if part == 2:
                    P.i("act", "copy", dst, cfv, reads=[cf], writes=[X[part]])
                else:
                    sqb = sq[0]
                    P.i("act", "activation", sqb[:, 0:cn], cf[:, 0:cn], AF.Square, reads=[cf], writes=[sqb])
                    P.i("pe", "matmul", PS[6][:, 0:cn], k.ones[:], sqb[:, 0:cn], start=True, stop=True, reads=[sqb, k.ones], writes=[PS[6]])
                    P.i("dve", "tensor_scalar_add", rstd[:, 0:cn], PS[6][:, 0:cn], EPS, reads=[PS[6]], writes=[rstd])
                    P.i("act", "activation", rstd[:, 0:cn], rstd[:, 0:cn], AF.Ln, reads=[rstd], writes=[rstd])
                    P.i("act", "activation", rstd[:, 0:cn], rstd[:, 0:cn], AF.Exp, scale=-0.5, reads=[rstd], writes=[rstd])
                    rv = rstd[:, 0:cn] if c0 < LP else rstd[:, 0:16].rearrange("p (s c) -> p s c", c=4)
                    P.i("dve", "scalar_tensor_tensor", dst, cfv, (128.0 ** -0.5) if part == 0 else 1.0, rv, ALU.mult, ALU.mult,
                        reads=[cf, rstd], writes=[X[part]])
        qn, kn, vn = X
        def pre(g):
            nch = 8 if g < 4 else 4
            W = nch * 64
            n0 = g * 8
            Ttb, kdec, vb, nwk, qkm = Ttb2[g % 2], kdec2[g % 2], vb2[g % 2], nwk2[g % 2], qkm2[g % 2]
            gcol = lambda t: t[:, n0:n0 + nch, h].unsqueeze(2)
            cs = lambda c: slice(g * 512 + c * 64, g * 512 + (c + 1) * 64)
            P.i("dve", "tensor_tensor", v3(DG[:, 0:W]), i64b[:, 0:nch, :], gcol(Gc).to_broadcast([64, nch, 64]), ALU.mult,
                reads=[k.ident, Gc], writes=[DG])
            P.i("pe", "matmul", PS[6][:, 0:W], k.ones[0:64, :], DG[:, 0:W], start=True, stop=True, reads=[k.ones, DG], writes=[PS[6]])
            P.i("act", "activation", G["egrow"][:, 0:W], PS[6][:, 0:W], AF.Exp, reads=[PS[6]], writes=[G["egrow"]])
            yield
            P.i("dve", "tensor_tensor", qgb[g][:, 0:W], qn[:, g * 512:g * 512 + W], G["egrow"][:, 0:W], ALU.mult,
                reads=[qn, G["egrow"]], writes=[qgb[g]])
            P.i("dve", "tensor_tensor", v3(G["negD"][:, 0:W]), v3(PS[6][0:64, 0:W]), gcol(Gc).to_broadcast([64, nch, 64]), ALU.subtract,
                reads=[PS[6], Gc], writes=[G["negD"]])
            P.i("dve", "tensor_scalar_max", G["decL"][:, 0:W], G["negD"][:, 0:W], 0.0, reads=[G["negD"]], writes=[G["decL"]])
            P.i("act", "activation", G["decL"][:, 0:W], G["decL"][:, 0:W], AF.Exp, scale=-1.0, reads=[G["decL"]], writes=[G["decL"]])
            P.i("pool", "tensor_tensor", v3(G["decL"][:, 0:W]), v3(G["decL"][:, 0:W]), k.msl[:, :].unsqueeze(1).to_broadcast([64, nch, 64]), ALU.mult,
                reads=[G["decL"], k.msl], writes=[G["decL"]])
            P.i("dve", "tensor_scalar_min", G["decU"][:, 0:W], G["negD"][:, 0:W], 0.0, reads=[G["negD"]], writes=[G["decU"]])
            P.i("act", "activation", G["decU"][:, 0:W], G["decU"][:, 0:W], AF.Exp, reads=[G["decU"]], writes=[G["decU"]])
            P.i("pool", "tensor_tensor", v3(G["decU"][:, 0:W]), v3(G["decU"][:, 0:W]), k.ltri[:, :].unsqueeze(1).to_broadcast([64, nch, 64]), ALU.mult,
                reads=[G["decU"], k.ltri], writes=[G["decU"]])
            yield
            for c in range(nch):
                P.i("pe", "matmul", PS[0][0:64, c * 64:(c + 1) * 64], kn[:, cs(c)], kn[:, cs(c)], start=True, stop=True, reads=[kn], writes=[PS[0]])
            P.i("dve", "tensor_tensor", G["tmp"][:, 0:W], PS[0][0:64, 0:W], G["decL"][:, 0:W], ALU.mult, reads=[PS[0], G["decL"]], writes=[G["tmp"]])
            P.i("pool", "tensor_tensor", v3(G["R0"][:, 0:W]), v3(G["tmp"][:, 0:W]), gcol(nbeta).to_broadcast([64, nch, 64]), ALU.mult,
                reads=[G["tmp"], nbeta], writes=[G["R0"]])
            yield
            p1b = PS[1][0:64, :].bitcast(BF16)
            for c in range(nch):
                P.i("pe", "transpose", p1b[:, c * 64:(c + 1) * 64], G["R0"][:, c * 64:(c + 1) * 64], k.identb[0:64, 0:64],
                    reads=[G["R0"], k.identb], writes=[PS[1]])
            P.i("act", "copy", G["Q0"][:, 0:W], p1b[:, 0:W], reads=[PS[1]], writes=[G["Q0"]])
            P.i("dve", "tensor_tensor", v3(Ttq[:, 0:W]), v3(p1b[:, 0:W]), i64bb[:, 0:nch, :], ALU.add, reads=[PS[1], k.identb], writes=[Ttq])
            P.i("dve", "tensor_tensor", v3(G["Tt"][:, 0:W]), v3(p1b[:, 0:W]), i64b[:, 0:nch, :], ALU.add, reads=[PS[1], k.ident], writes=[G["Tt"]])
            yield
            for step in range(1, 6):
                cur, nxt = str((step - 1) % 2), str(step % 2)
                Qc, Rc, Qn, Rn = G["Q" + cur], G["R" + cur], G["Q" + nxt], G["R" + nxt]
                for c in range(nch):
                    sl = slice(c * 64, (c + 1) * 64)
                    if step < 5:
                        P.i("pe", "matmul", PS[1][0:64, sl], Rc[:, sl], Qc[:, sl], start=True, stop=True, reads=[Rc, Qc], writes=[PS[1]])
                    P.i("pe", "matmul", PS[2][0:64, sl], Qc[:, sl], Rc[:, sl], start=True, stop=True, reads=[Rc, Qc], writes=[PS[2]])
                if step < 5:
                    P.i("act", "copy", Qn[:, 0:W], PS[1][0:64, 0:W], reads=[PS[1]], writes=[Qn])
                P.i("dve", "tensor_copy", Rn[:, 0:W], PS[2][0:64, 0:W], reads=[PS[2]], writes=[Rn])
                yield
                for c in range(nch):
                    sl = slice(c * 64, (c + 1) * 64)
                    P.i("pe", "matmul", PS[0][0:64, sl], Rn[:, sl], Ttq[:, sl], start=True, stop=True, reads=[Rn, Ttq], writes=[PS[0]])
                Tnext = Ttq if step < 5 else Ttb
                P.i("dve", "tensor_tensor", Tnext[:, 0:W], G["Tt"][:, 0:W], PS[0][0:64, 0:W], ALU.add, reads=[PS[0], G["Tt"]], writes=[Tnext])
                if step < 5:
                    P.i("dve", "tensor_tensor", G["Tt"][:, 0:W], G["Tt"][:, 0:W], PS[0][0:64, 0:W], ALU.add, reads=[PS[0], G["Tt"]], writes=[G["Tt"]])
                yield
            pk = PS[3][0:64, :].bitcast(BF16)
            pv = PS[0][0:64, :].bitcast(BF16)
            for c in range(nch):
                P.i("pe", "transpose", pk[:, c * 128:(c + 1) * 128], kn[:, cs(c)], k.identb[:], reads=[kn, k.identb], writes=[PS[3]])
                P.i("pe", "transpose", pv[:, c * 128:(c + 1) * 128], vn[:, cs(c)], k.identb[:], reads=[vn, k.identb], writes=[PS[0]])
            p3 = lambda ap: ap.rearrange("p (n c) -> p n c", c=128)
            bc = lambda t: gcol(t).to_broadcast([64, nch, 128])
            P.i("dve", "tensor_tensor", kb[:, 0:nch, :], p3(pk[:, 0:nch * 128]), bc(ebg), ALU.mult, reads=[PS[3], ebg], writes=[kb])
            P.i("dve", "tensor_tensor", kdec[:, 0:nch, :], p3(pk[:, 0:nch * 128]), bc(edec), ALU.mult, reads=[PS[3], edec], writes=[kdec])
            P.i("dve", "tensor_tensor", vb[:, 0:nch, :], p3(pv[:, 0:nch * 128]), bc(beta), ALU.mult, reads=[PS[0], beta], writes=[vb])
            yield
            for c in range(nch):
                sl = slice(c * 64, (c + 1) * 64)
                P.i("pe", "matmul", PS[2][:, sl], kb[:, c, :], Ttb[:, sl], start=True, stop=True, reads=[kb, Ttb], writes=[PS[2]])
                P.i("pe", "matmul", PS[1][0:64, sl], kn[:, cs(c)], qn[:, cs(c)], start=True, stop=True, reads=[kn, qn], writes=[PS[1]])
            P.i("act", "activation", nwk[:, 0:W], PS[2][:, 0:W], AF.Copy, scale=-1.0, reads=[PS[2]], writes=[nwk])
            P.i("dve", "tensor_tensor", qkm[:, 0:W], PS[1][0:64, 0:W], G["decU"][:, 0:W], ALU.mult, reads=[PS[1], G["decU"]], writes=[qkm])
            yield

        def rec(g):
            nch = 8 if g < 4 else 4
            W = nch * 64
            n0 = g * 8
            Ttb, kdec, vb, nwk, qkm = Ttb2[g % 2], kdec2[g % 2], vb2[g % 2], nwk2[g % 2], qkm2[g % 2]
            gcol = lambda t: t[:, n0:n0 + nch, h].unsqueeze(2)
            cs = lambda c: slice(g * 512 + c * 64, g * 512 + (c + 1) * 64)
            for c in range(nch):
                n = n0 + c
                sl = slice(c * 64, (c + 1) * 64)
                if n == 0:
                    P.i("pool", "memset", S[:], 0.0, writes=[S])
                    P.i("pool", "memset", Sb[:], 0.0, writes=[Sb])
                elif n >= 32:
                    P.dma("sp", S[:], d["state_gdn"].ap()[j, n - 32, h], reads=[d["state_gdn"]], writes=[S])
                    P.i("act", "copy", Sb[:], S[:], reads=[S], writes=[Sb])
                P.i("pe", "matmul", PS[7][0:64, 0:128], Ttb[:, sl], vb[:, c, :], start=True, stop=False, reads=[Ttb, vb], writes=[PS[7]])
                P.i("pe", "matmul", PS[7][0:64, 0:128], nwk[:, sl], Sb[:], start=False, stop=True, reads=[nwk, Sb], writes=[PS[7]])
                P.i("act", "copy", ub[:], PS[7][0:64, 0:128], reads=[PS[7]], writes=[ub])
                yield
                P.i("pe", "matmul", PS[5][:, sl], Sb[:], qgb[g][:, sl], start=True, stop=False, reads=[Sb, qgb[g]], writes=[PS[5]])
                P.i("pe", "matmul", PS[5][:, sl], ub[:], qkm[:, sl], start=False, stop=True, reads=[ub, qkm], writes=[PS[5]])
                P.i("pe", "matmul", PS[4][:, 0:128], kdec[:, c, :], ub[:], start=True, stop=True, reads=[kdec, ub], writes=[PS[4]])
                P.i("dve", "scalar_tensor_tensor", S[:], S[:], egl[:, n * 8 + h:n * 8 + h + 1], PS[4][:, 0:128], ALU.mult, ALU.add,
                    reads=[S, egl, PS[4]], writes=[S])
                P.i("act", "copy", Sb[:], S[:], reads=[S], writes=[Sb])
                if n == 31:
                    P.dma("sp", d["gdn_p"].ap()[j, h], S[:], reads=[S], writes=[d["gdn_p"]])
                elif n >= 32:
                    P.dma("sp", d["gdn_s"].ap()[j, n - 32, h], S[:], reads=[S], writes=[d["gdn_s"]])
                yield
            P.i("act", "activation", sq[1][:, 0:W], PS[5][:, 0:W], AF.Square, reads=[PS[5]], writes=[sq[1]])
            P.i("pe", "matmul", PS[6][:, 0:W], k.ones[:], sq[1][:, 0:W], start=True, stop=True, reads=[sq[1], k.ones], writes=[PS[6]])
            P.i("dve", "tensor_scalar", rstd[:, 0:W], PS[6][:, 0:W], 1.0 / 128, EPS, ALU.mult, ALU.add, reads=[PS[6]], writes=[rstd])
            P.i("act", "activation", rstd[:, 0:W], rstd[:, 0:W], AF.Ln, reads=[rstd], writes=[rstd])
            P.i("act", "activation", rstd[:, 0:W], rstd[:, 0:W], AF.Exp, scale=-0.5, reads=[rstd], writes=[rstd])
            yield
            P.i("dve", "scalar_tensor_tensor", on[:, 0:W], PS[5][:, 0:W], gnw[:, 0:1], rstd[:, 0:W], ALU.mult, ALU.mult,
                reads=[PS[5], gnw, rstd], writes=[on])
            if g < 4:
                P.i("dve", "tensor_tensor", ogb[:, g * 512:(g + 1) * 512], on[:, 0:512], zsb[:, g * 512:(g + 1) * 512], ALU.mult,
                    reads=[on, zsb], writes=[ogb])
            else:
                P.i("dve", "tensor_tensor", ogb[:, LP:NT].rearrange("p (s c) -> p s c", c=4), v3(on[:, 0:256])[:, :, 0:4],
                    zsb[:, LP:NT].rearrange("p (s c) -> p s c", c=4), ALU.mult, reads=[on, zsb], writes=[ogb])
            yield

        run_il([pre(0)])
        for g in range(5):
            run_il([rec(g), pre(g + 1) if g < 4 else None])
        for dt in range(8):
            for ci, (c0, cn) in enumerate(chunks(0, NT)):
                ps = PS[(dt * 5 + ci) % 2]
                P.i("pe", "matmul", ps[:, 0:cn], wout[:, dt * 128:(dt + 1) * 128], ogb[:, c0:c0 + cn], start=True, stop=True,
                    reads=[wout, ogb], writes=[ps])
                P.i("dve", "tensor_tensor", k.R[dt][:, c0:c0 + cn], ps[:, 0:cn], k.R[dt][:, c0:c0 + cn], ALU.add,
                    reads=[ps, k.R[dt]], writes=[k.R[dt]])
    P.release(m)


SCALE = 64.0 ** -0.5


def rope_apply(P, out, x, tab, nh, np_, tmps, tabbuf):
    t1, t2 = tmps
    cosb = tab[:, 0:32].unsqueeze(1).to_broadcast([np_, nh, 32])
    sinb = tab[:, 32:64].unsqueeze(1).to_broadcast([np_, nh, 32])
    x1, x2 = x[:, :, 0:32], x[:, :, 32:64]
    a, b = t1[0:np_, 0:nh, :], t2[0:np_, 0:nh, :]
    P.i("dve", "tensor_tensor", a, x1, cosb, ALU.mult, reads=x.bufs + [tabbuf], writes=[t1])
    P.i("dve", "tensor_tensor", b, x2, sinb, ALU.mult, reads=x.bufs, writes=[t2])
    P.i("dve", "tensor_tensor", out[:, :, 0:32], a, b, ALU.subtract, reads=[t1, t2], writes=out.bufs)
    P.i("dve", "tensor_tensor", a, x2, cosb, ALU.mult, reads=x.bufs + [t2], writes=[t1])
    P.i("dve", "tensor_tensor", b, x1, sinb, ALU.mult, reads=x.bufs + [t1], writes=[t2])
    P.i("dve", "tensor_tensor", out[:, :, 32:64], a, b, ALU.add, reads=[t1, t2], writes=out.bufs)


class V:
    def __init__(self, ap, bufs):
        self.ap = ap
        self.bufs = bufs

    def __getitem__(self, idx):
        return self.ap[idx]


def nsa_layer(k, l):
    mm = k.P.mark()
    try:
        _nsa_layer(k, l)
    except StopNSA:
        k.P.release(mm)


def _nsa_layer(k, l):
    P = k.P
    d = k.d
    j = l // 2
    PS = k.PS
    m0 = P.mark()
    hTs = P.sb([128, 8, 16], BF16, "nhTs")
    m1 = P.mark()
    hT = [P.sb([128, NT], BF16, "nh%d" % i) for i in range(8)]
    m2 = P.mark()
    scratch = {"sq": [P.sb([128, 512], F32, "sq%d" % i) for i in range(2)], "rstd": P.sb([128, 512], F32, "rstd")}
    rmsnorm(k, l * 8, 0, NT, hT, 0, scratch)
    P.release(m2)
    for kt in range(8):
        P.i("pool", "tensor_copy", hTs[:, kt, :], hT[kt][:, LP:NT], reads=[hT[kt]], writes=[hTs])
    ropeT = P.sb([128, 16, 64], F32, "ropeT")
    P.dma("sp", ropeT[:], d["rope_tab"].ap()[0:16].rearrange("t p c -> p t c"), reads=[d["rope_tab"]], writes=[ropeT])
    eexp = P.sb([32, LP], BF16, "eexp")
    P.i("pool", "memset", eexp[:], 1.0, writes=[eexp])
    P.i("pool", "affine_select", out=eexp[:], in_=eexp[:], pattern=[[1, LP]], compare_op=ALU.is_ge, fill=0.0, base=0,
        channel_multiplier=-64, reads=[eexp], writes=[eexp])
    P.i("pool", "affine_select", out=eexp[:], in_=eexp[:], pattern=[[-1, LP]], compare_op=ALU.is_ge, fill=0.0, base=63,
        channel_multiplier=64, reads=[eexp], writes=[eexp])
    niota = P.sb([128, 32], F32, "niota")
    curcol = P.sb([128, 16], F32, "curcol")
    hfcol = P.sb([128, 1], F32, "hfcol")
    tiota = P.sb([32, 128], F32, "tiota")
    thr = P.sb([32, 16], F32, "thr")
    P.i("pool", "iota", niota[:], pattern=[[1, 32]], base=0, channel_multiplier=0, allow_small_or_imprecise_dtypes=True, writes=[niota])
    P.i("pool", "iota", curcol[:], pattern=[[2, 16]], base=0, channel_multiplier=0, allow_small_or_imprecise_dtypes=True, writes=[curcol])
    P.i("pool", "memset", hfcol[0:64, :], 0.0, writes=[hfcol])
    P.i("pool", "memset", hfcol[64:128, :], 1.0, writes=[hfcol])
    P.i("dve", "tensor_scalar_add", curcol[:], curcol[:], hfcol[:, 0:1], reads=[curcol, hfcol], writes=[curcol])
    P.i("pool", "iota", tiota[:], pattern=[[1, 128]], base=0, channel_multiplier=0, allow_small_or_imprecise_dtypes=True, writes=[tiota])
    P.i("pool", "iota", thr[:], pattern=[[-128, 16]], base=63, channel_multiplier=64, allow_small_or_imprecise_dtypes=True, writes=[thr])
    cm_ = P.sb([32, 128], F32, "cm_")
    cand = P.sb([128, 32], F32, "cand")
    wc = P.sb([64, 64, 2, 64], BF16, "wc")
    for lh in range(4):
        P.dma("pool", wc[:, lh * 16:(lh + 1) * 16], d["nsa_cmp_w"].ap()[j, lh * 16:(lh + 1) * 16].rearrange("l c d e -> d l c e"),
              reads=[d["nsa_cmp_w"]], writes=[wc])
    peT = P.sb([64, 2, 64], F32, "peT")
    stg64 = P.sb([128, 64], F32, "stg64")
    P.dma("sp", stg64[:], d["nsa_cmp_pe"].ap()[j].rearrange("l c d -> (l c) d"), reads=[d["nsa_cmp_pe"]], writes=[stg64])
    P.i("pe", "transpose", PS[6][0:64, 0:128], stg64[:], k.ident[:], reads=[stg64, k.ident], writes=[PS[6]])
    P.i("dve", "tensor_copy", peT[:], PS[6][0:64, 0:128].rearrange("p (l c) -> p c l", c=2), reads=[PS[6]], writes=[peT])

    wg = P.sb([128, 8, 652], BF16, "wg")
    wog = P.sb([128, 2, D], BF16, "wog")
    tm = P.sb([128, 652], F32, "tm")
    tr = P.sb([128, 384], F32, "trr")
    tmb = P.sb([128, 640], BF16, "tmb")
    trb = P.sb([128, 384], BF16, "trb")
    rt = [P.sb([128, 6, 32], F32, "rt%d" % i) for i in range(2)]
    qT2 = [P.sb([64, 8, 128], BF16, "qT%d" % i) for i in range(2)]
    KselT = [P.sb([64, 128], BF16, "KselT%d" % i) for i in range(16)]
    KwinT = [P.sb([64, 128], BF16, "KwinT%d" % i) for i in range(16)]
    XkT = P.sb([64, LP], BF16, "XkT")
    XvT = P.sb([64, LP], BF16, "XvT")
    Vsel = [P.sb([128, 66], BF16, "Vsel%d" % i) for i in range(16)]
    Vwin = [P.sb([128, 66], BF16, "Vwin%d" % i) for i in range(16)]
    for t in Vsel + Vwin:
        P.i("pool", "memset", t[:], 1.0, writes=[t])
    gt2 = [P.sb([128, 12], F32, "gt%d" % i) for i in range(2)]
    CkT = P.sb([64, 32], BF16, "CkT")
    CvA = P.sb([32, 64], BF16, "CvA")
    Ec = P.sb([32, 512], F32, "Ec")
    rs = P.sb([32, 512], F32, "rs")
    pb = P.sb([32, 512], BF16, "pb")
    oc2 = [P.sb([128, 256], F32, "oc%d" % i) for i in range(2)]
    impT = P.sb([32, 128], F32, "impT")
    impc = P.sb([128, 32], F32, "impc")
    tmp32 = P.sb([128, 32], F32, "tmp32")
    sel = P.sb([128, 32], F32, "sel")
    m8 = P.sb([128, 16], F32, "m8")
    smT2 = [P.sb([32, 128], BF16, "smT%d" % i) for i in range(2)]
    Eb = [P.sb([128, 512], BF16, "Eb%d" % i) for i in range(2)]
    Mb = [P.sb([128, 128], BF16, "Mb%d" % i) for i in range(2)]
    cf = P.sb([128, 8], F32, "ncf")
    om = P.sb([128, 256], F32, "om")
    om2 = P.sb([128, 256], F32, "om2")
    omb = P.sb([128, 256], BF16, "omb")
    omT = P.sb([128, 2, 128], BF16, "omT")
    cnt = [0]
    ck("n_setup")

    for g in range(0 if not SKIP_PROMPT[0] else 4, 4):
        srcs = [(g * 256, 256, 0), (1024 + 2 * 256 + g * 64, 64, 256), (1024 + 4 * 256 + g * 64, 64, 320),
                (1024 + 0 * 256 + g * 64, 64, 384), (1024 + 1 * 256 + g * 64, 64, 448), (1024 + 3 * 256 + g * 64, 64, 512),
                (1024 + 5 * 256 + g * 64, 64, 576), (2560 + g * 12, 12, 640)]
        for (c0, w_, o0) in srcs:
            P.dma("pool", wg[:, :, o0:o0 + w_], d["nsa_w_in"].ap()[j, :, c0:c0 + w_].rearrange("(kt p) c -> p kt c", p=128),
                  reads=[d["nsa_w_in"]], writes=[wg])
        P.dma("pool", wog[:], d["nsa_w_out"].ap()[j, g * 256:(g + 1) * 256, :].rearrange("(a p) c -> p a c", p=128),
              reads=[d["nsa_w_out"]], writes=[wog])
        for i in range(16):
            tok = slice(i * 128, (i + 1) * 128)
            ps = PS[i % 2]
            for kt in range(8):
                P.i("pe", "matmul", ps[:, 0:128], hT[kt][:, tok], wg[:, kt, 384:512], start=(kt == 0), stop=(kt == 7),
                    reads=[hT[kt], wg], writes=[ps])
            P.i("act", "copy", tm[:, 384:512], ps[:, 0:128], reads=[ps], writes=[tm])
            P.i("dve", "tensor_copy", tmb[:, 384:512], ps[:, 0:128], reads=[ps], writes=[tmb])
            for c in range(2):
                P.dma("sp", d["cmp_p"].ap()[j, tok, c * 256 + g * 64:c * 256 + (g + 1) * 64], tm[:, 384 + c * 64:448 + c * 64],
                      reads=[tm], writes=[d["cmp_p"]])
            pt = PS[2][0:64, :].bitcast(BF16)
            for c in range(2):
                P.i("pe", "transpose", pt[:, c * 128:(c + 1) * 128], tmb[:, 384 + c * 64:448 + c * 64], k.identb[:],
                    reads=[tmb, k.identb], writes=[PS[2]])
            for c, XT in enumerate((XkT, XvT)):
                P.i("dve", "tensor_tensor", XT[:, tok].rearrange("p (n l) -> p n l", l=64),
                    pt[:, c * 128:(c + 1) * 128].rearrange("p (n l) -> p n l", l=64),
                    peT[:, c, :].unsqueeze(1).to_broadcast([64, 2, 64]), ALU.add, reads=[PS[2], peT], writes=[XT])
        for ll in range(64):
            P.i("pe", "matmul", PS[3][0:64, 0:32], wc[:, ll, 0, :], XkT[:, :].rearrange("p (n l) -> p n l", l=64)[:, :, ll],
                start=(ll == 0), stop=(ll == 63), reads=[wc, XkT], writes=[PS[3]])
        P.i("act", "copy", CkT[:], PS[3][0:64, 0:32], reads=[PS[3]], writes=[CkT])
        for ll in range(64):
            P.i("pe", "matmul", PS[3][0:32, 64:128], XvT[:, :].rearrange("p (n l) -> p n l", l=64)[:, :, ll], wc[:, ll, 1, :],
                start=(ll == 0), stop=(ll == 63), reads=[wc, XvT], writes=[PS[3]])
        P.i("act", "copy", CvA[:], PS[3][0:32, 64:128], reads=[PS[3]], writes=[CvA])
        ck("n_pre%d" % g)

        def chain(i):
            slot = i % 2
            qT, gt, oc, smT = qT2[slot], gt2[slot], oc2[slot], smT2[slot]
            tok = slice(i * 128, (i + 1) * 128)
            for (c0, cn, ps) in ((0, 384, PS[0]), (512, 140, PS[1])):
                for kt in range(8):
                    P.i("pe", "matmul", ps[:, 0:cn], hT[kt][:, tok], wg[:, kt, c0:c0 + cn], start=(kt == 0), stop=(kt == 7),
                        reads=[hT[kt], wg], writes=[ps])
                P.i("act", "copy", tm[:, c0:c0 + cn], ps[:, 0:cn], reads=[ps], writes=[tm])
            yield
            rope_apply(P, V(tr[:, :].rearrange("p (h c) -> p h c", c=64), [tr]), V(tm[:, 0:384].rearrange("p (h c) -> p h c", c=64), [tm]),
                       ropeT[:, i, :], 6, 128, rt, ropeT)
            P.i("act", "copy", tmb[:, 0:256], tm[:, 0:256], reads=[tm], writes=[tmb])
            P.i("act", "copy", trb[:], tr[:], reads=[tr], writes=[trb])
            P.i("act", "activation", gt[:], tm[:, 640:652], AF.Sigmoid, reads=[tm], writes=[gt])
            P.i("pool", "tensor_copy", Vsel[i][:, 0:64], tm[:, 512:576], reads=[tm], writes=[Vsel[i]])
            P.i("pool", "tensor_copy", Vwin[i][:, 0:64], tm[:, 576:640], reads=[tm], writes=[Vwin[i]])
            P.dma("sp", d["sel_p"].ap()[j, tok, g * 64:(g + 1) * 64], tr[:, 256:320], reads=[tr], writes=[d["sel_p"]])
            P.dma("sp", d["sel_p"].ap()[j, tok, 256 + g * 64:256 + (g + 1) * 64], tm[:, 512:576], reads=[tm], writes=[d["sel_p"]])
            if i >= 12:
                wt = slice((i - 12) * 128, (i - 11) * 128)
                P.dma("sp", d["win_p"].ap()[j, wt, g * 64:(g + 1) * 64], tr[:, 320:384], reads=[tr], writes=[d["win_p"]])
                P.dma("sp", d["win_p"].ap()[j, wt, 256 + g * 64:256 + (g + 1) * 64], tm[:, 576:640], reads=[tm], writes=[d["win_p"]])
            yield
            pq = PS[2][0:64, :].bitcast(BF16)
            pk = PS[1][0:64, :].bitcast(BF16)[:, 512:768]
            for h in range(4):
                P.i("pe", "transpose", pq[:, h * 128:(h + 1) * 128], tmb[:, h * 64:(h + 1) * 64], k.identb[:], reads=[tmb, k.identb], writes=[PS[2]])
                P.i("pe", "transpose", pq[:, (4 + h) * 128:(5 + h) * 128], trb[:, h * 64:(h + 1) * 64], k.identb[:], reads=[trb, k.identb], writes=[PS[2]])
            P.i("pe", "transpose", pk[:, 0:128], trb[:, 256:320], k.identb[:], reads=[trb, k.identb], writes=[PS[1]])
            P.i("pe", "transpose", pk[:, 128:256], trb[:, 320:384], k.identb[:], reads=[trb, k.identb], writes=[PS[1]])
            P.i("dve", "tensor_copy", qT[:].rearrange("p a t -> p (a t)"), pq[:, 0:1024], reads=[PS[2]], writes=[qT])
            P.i("act", "copy", KselT[i][:], pk[:, 0:128], reads=[PS[1]], writes=[KselT[i]])
            P.i("act", "copy", KwinT[i][:], pk[:, 128:256], reads=[PS[1]], writes=[KwinT[i]])
            yield
            qraw = qT[:, 0:4, :]
            P.i("pe", "matmul", PS[5][0:32, :], CkT[:], qraw, start=True, stop=True, reads=[CkT, qT], writes=[PS[5]])
            P.i("act", "activation", Ec[:], PS[5][0:32, :], AF.Exp, scale=SCALE, reads=[PS[5]], writes=[Ec])
            P.i("dve", "tensor_scalar", cm_[:], tiota[:], thr[:, i:i + 1], None, ALU.is_ge, reads=[tiota, thr], writes=[cm_])
            P.i("dve", "tensor_tensor", Ec[:].rearrange("p (a t) -> p a t", a=4), Ec[:].rearrange("p (a t) -> p a t", a=4),
                cm_[:, :].unsqueeze(1).to_broadcast([32, 4, 128]), ALU.mult, reads=[Ec, cm_], writes=[Ec])
            yield
            P.i("pe", "matmul", PS[5][0:32, :], k.ones[0:32, 0:32], Ec[:], start=True, stop=True, reads=[k.ones, Ec], writes=[PS[5]])
            P.i("dve", "tensor_scalar_max", rs[:], PS[5][0:32, :], 1e-30, reads=[PS[5]], writes=[rs])
            P.i("dve", "reciprocal", rs[:], rs[:], reads=[rs], writes=[rs])
            P.i("dve", "tensor_tensor", Ec[:], Ec[:], rs[:], ALU.mult, reads=[Ec, rs], writes=[Ec])
            P.i("act", "copy", pb[:], Ec[:], reads=[Ec], writes=[pb])
            yield
            for h in range(4):
                P.i("pe", "matmul", PS[5][:, h * 64:(h + 1) * 64], pb[:, h * 128:(h + 1) * 128], CvA[:], start=True, stop=True,
                    reads=[pb, CvA], writes=[PS[5]])
            P.i("act", "copy", oc[:], PS[5][:, 0:256], reads=[PS[5]], writes=[oc])
            P.i("dve", "tensor_reduce", impT[:], Ec[:].rearrange("p (a t) -> p t a", a=4), AX.X, ALU.add, reads=[Ec], writes=[impT])
            yield
            P.i("pe", "transpose", PS[5][:, 256:288], impT[:], k.ident[0:32, 0:32], reads=[impT, k.ident], writes=[PS[5]])
            P.i("dve", "tensor_scalar", cand[:], niota[:], curcol[:, i:i + 1], None, ALU.is_lt, reads=[niota, curcol], writes=[cand])
            P.i("dve", "tensor_scalar_add", impc[:], PS[5][:, 256:288], 1.0, reads=[PS[5]], writes=[impc])
            P.i("dve", "tensor_tensor", impc[:], impc[:], cand[:], ALU.mult, reads=[impc, cand], writes=[impc])
            P.i("dve", "tensor_scalar_add", impc[:], impc[:], -1.0, reads=[impc], writes=[impc])
            P.i("dve", "max", m8[:, 0:8], impc[:], reads=[impc], writes=[m8])
            P.i("dve", "match_replace", tmp32[:], m8[:, 0:8], impc[:], -2.0, reads=[m8, impc], writes=[tmp32])
            P.i("dve", "max", m8[:, 8:16], tmp32[:], reads=[tmp32], writes=[m8])
            yield
            P.i("dve", "tensor_scalar", sel[:], impc[:], m8[:, 14:15], None, ALU.is_ge, reads=[impc, m8], writes=[sel])
            P.i("dve", "tensor_single_scalar", tmp32[:], impc[:], -0.5, ALU.is_gt, reads=[impc], writes=[tmp32])
            P.i("dve", "tensor_tensor", sel[:], sel[:], tmp32[:], ALU.mult, reads=[sel, tmp32], writes=[sel])
            P.i("dve", "tensor_scalar", cand[:], niota[:], curcol[:, i:i + 1], None, ALU.is_equal, reads=[niota, curcol], writes=[cand])
            P.i("dve", "tensor_tensor", sel[:], sel[:], cand[:], ALU.max, reads=[sel, cand], writes=[sel])
            P.i("pe", "transpose", PS[5][0:32, 384:512], sel[:], k.ident[:], reads=[sel, k.ident], writes=[PS[5]])
            P.i("act", "copy", smT[:], PS[5][0:32, 384:512], reads=[PS[5]], writes=[smT])
            yield

        def attn(i):
            slot = i % 2
            qT, gt, oc, smT = qT2[slot], gt2[slot], oc2[slot], smT2[slot]
            tok = slice(i * 128, (i + 1) * 128)
            qrot = qT[:, 4:8, :]
            for br in range(2):
                KT, Vv = (KselT, Vsel) if br == 0 else (KwinT, Vwin)
                acc = PS[6 + br]
                j0 = 0 if br == 0 else max(0, i - 4)
                for jt in range(j0, i + 1):
                    kk = slice(jt * 128, (jt + 1) * 128)
                    E = Eb[cnt[0] % 2]
                    M = Mb[cnt[0] % 2]
                    cnt[0] += 1
                    msk = None
                    if br == 0:
                        P.i("pe", "matmul", PS[4][:, 0:128], eexp[:, kk], smT[:], start=True, stop=True, reads=[eexp, smT], writes=[PS[4]])
                        if jt == i:
                            P.i("dve", "tensor_tensor", M[:], PS[4][:, 0:128], k.caus[:], ALU.mult, reads=[PS[4], k.caus], writes=[M])
                        else:
                            P.i("dve", "tensor_copy", M[:], PS[4][:, 0:128], reads=[PS[4]], writes=[M])
                        msk = M
                    elif jt == i:
                        msk = k.caus
                    elif jt == i - 4:
                        msk = k.wmask
                    P.i("pe", "matmul", PS[3][:, :], KT[jt][:], qrot, start=True, stop=True, reads=[KT[jt], qT], writes=[PS[3]])
                    P.i("act", "activation", E[:], PS[3][:, :], AF.Exp, scale=SCALE, reads=[PS[3]], writes=[E])
                    if msk is not None:
                        P.i("dve", "tensor_tensor", E[:].rearrange("p (a t) -> p a t", a=4), E[:].rearrange("p (a t) -> p a t", a=4),
                            msk[:, :].unsqueeze(1).to_broadcast([128, 4, 128]), ALU.mult, reads=[E, msk], writes=[E])
                    for h in range(4):
                        P.i("pe", "matmul", acc[:, h * 65:(h + 1) * 65], E[:, h * 128:(h + 1) * 128], Vv[jt][:, 0:65],
                            start=(jt == j0 and h == 0), stop=(jt == i), skip_group_check=True, reads=[E, Vv[jt]], writes=[acc])
                    yield
            g3 = gt[:, :].rearrange("p (h c) -> p h c", c=3)
            a3 = lambda ps_: ps_[:, 0:260].rearrange("p (h c) -> p h c", c=65)
            P.i("dve", "reciprocal", cf[:, 0:4], a3(PS[6])[:, :, 64], reads=[PS[6]], writes=[cf])
            P.i("dve", "reciprocal", cf[:, 4:8], a3(PS[7])[:, :, 64], reads=[PS[7]], writes=[cf])
            P.i("dve", "tensor_tensor", cf[:, 0:4], cf[:, 0:4], g3[:, :, 1], ALU.mult, reads=[cf, gt], writes=[cf])
            P.i("dve", "tensor_tensor", cf[:, 4:8], cf[:, 4:8], g3[:, :, 2], ALU.mult, reads=[cf, gt], writes=[cf])
            o3 = lambda t: t[:, :].rearrange("p (h c) -> p h c", c=64)
            bc = lambda ap: ap.unsqueeze(2).to_broadcast([128, 4, 64])
            P.i("dve", "tensor_tensor", o3(om), o3(oc), bc(g3[:, :, 0]), ALU.mult, reads=[oc, gt], writes=[om])
            P.i("dve", "tensor_tensor", o3(om2), a3(PS[6])[:, :, 0:64], bc(cf[:, 0:4]), ALU.mult, reads=[PS[6], cf], writes=[om2])
            P.i("pool", "tensor_tensor", om[:], om[:], om2[:], ALU.add, reads=[om, om2], writes=[om])
            P.i("dve", "tensor_tensor", o3(om2), a3(PS[7])[:, :, 0:64], bc(cf[:, 4:8]), ALU.mult, reads=[PS[7], cf], writes=[om2])
            P.i("pool", "tensor_tensor", omb[:], om[:], om2[:], ALU.add, reads=[om, om2], writes=[omb])
            yield
            po = PS[4][:, :].bitcast(BF16)[:, 256:512]
            for a in range(2):
                P.i("pe", "transpose", po[:, a * 128:(a + 1) * 128], omb[:, a * 128:(a + 1) * 128], k.identb[:], reads=[omb, k.identb], writes=[PS[4]])
            P.i("act", "copy", omT[:].rearrange("p a t -> p (a t)"), po[:, 0:256], reads=[PS[4]], writes=[omT])
            yield
            for hb in range(2):
                ps = PS[6 + hb]
                for q in range(4):
                    dt = hb * 4 + q
                    for a in range(2):
                        P.i("pe", "matmul", ps[:, q * 128:(q + 1) * 128], wog[:, a, dt * 128:(dt + 1) * 128], omT[:, a, :],
                            start=(a == 0), stop=(a == 1), reads=[wog, omT], writes=[ps])
                for q in range(4):
                    dt = hb * 4 + q
                    P.i("dve", "tensor_tensor", k.R[dt][:, tok], ps[:, q * 128:(q + 1) * 128], k.R[dt][:, tok], ALU.add,
                        reads=[ps, k.R[dt]], writes=[k.R[dt]])
                yield
            ck("n_main%d_%d" % (g, i))

        run_il([chain(0)])
        for i in range(16):
            run_il([attn(i), chain(i + 1) if i < 15 else None])
        ck("n_main%d" % g)
    ck("n_prompt")
    P.release(m1)
    nsa_sample(k, l, hTs)
    P.release(m0)

def nsa_sample(k, l, hTs):
    P = k.P
    d = k.d
    j = l // 2
    PS = k.PS
    m = P.mark()
    SQ2 = P.sb([128, 4, 4, 8, 4], BF16, "SQ2")
    SKs = P.sb([64, 4, 4, 4], BF16, "SKs")
    SKw = P.sb([64, 4, 4, 4], BF16, "SKw")
    SVs = P.sb([4, 4, 4, 65], BF16, "SVs")
    SVw = P.sb([4, 4, 4, 65], BF16, "SVw")
    SG = P.sb([4, 4, 4, 12], F32, "SG")
    oTs = P.sb([64, 4, 16, 4], BF16, "oTs")
    ropeS = P.sb([4, 64], F32, "ropeS")
    P.dma("sp", ropeS[:], d["rope_tab"].ap()[16, 0:4, :], reads=[d["rope_tab"]], writes=[ropeS])
    P.i("pool", "memset", SVs[:], 1.0, writes=[SVs])
    P.i("pool", "memset", SVw[:], 1.0, writes=[SVw])
    i4 = P.sb([4, 4], F32, "i4")
    P.i("dve", "tensor_copy", i4[:], k.ident[0:4, 0:4], reads=[k.ident], writes=[i4])
    pti = P.sb([128, NS * NPG], I32, "pti")
    ptf = P.sb([128, NS * NPG], F32, "ptf")
    iot = P.sb([128, 1], F32, "iot")
    idx = P.sb([128, NS * NPG], I32, "idx")
    P.dma("sp", pti[:], d["page_table"].ap().rearrange("s g -> (s g)").partition_broadcast(128), reads=[d["page_table"]], writes=[pti])
    P.i("dve", "tensor_copy", ptf[:], pti[:], reads=[pti], writes=[ptf])
    P.i("pool", "iota", iot[:], pattern=[[0, 1]], base=j * k.n_phys * 128, channel_multiplier=1, allow_small_or_imprecise_dtypes=True, writes=[iot])
    P.i("dve", "tensor_scalar", ptf[:], ptf[:], 128.0, iot[:, 0:1], ALU.mult, ALU.add, reads=[ptf, iot], writes=[ptf])
    P.i("dve", "tensor_copy", idx[:], ptf[:], reads=[ptf], writes=[idx])
    ck("s_setup")

    m0 = P.mark()
    wg = P.sb([128, 8, 652], BF16, "swg")
    tmS = P.sb([4, 652], F32, "tmS")
    trS = P.sb([4, 384], F32, "trS")
    qd = P.sb([4, 8, 2, 64], BF16, "qd")
    kb2 = P.sb([4, 128], BF16, "kb2")
    rt = [P.sb([4, 6, 32], F32, "srt%d" % i) for i in range(2)]
    for g in range(4):
        srcs = [(g * 256, 256, 0), (1024 + 2 * 256 + g * 64, 64, 256), (1024 + 4 * 256 + g * 64, 64, 320),
                (1024 + 0 * 256 + g * 64, 64, 384), (1024 + 1 * 256 + g * 64, 64, 448), (1024 + 3 * 256 + g * 64, 64, 512),
                (1024 + 5 * 256 + g * 64, 64, 576), (2560 + g * 12, 12, 640)]
        for (c0, w_, o0) in srcs:
            P.dma("pool", wg[:, :, o0:o0 + w_], d["nsa_w_in"].ap()[j, :, c0:c0 + w_].rearrange("(kt p) c -> p kt c", p=128),
                  reads=[d["nsa_w_in"]], writes=[wg])
        for s_ in range(4):
            rows = slice(s_ * 4, (s_ + 1) * 4)
            for (c0, cn, ps) in ((0, 512, PS[0]), (512, 140, PS[1])):
                for kt in range(8):
                    P.i("pe", "matmul", ps[0:4, 0:cn], hTs[:, kt, rows], wg[:, kt, c0:c0 + cn], start=(kt == 0), stop=(kt == 7),
                        reads=[hTs, wg], writes=[ps])
                P.i("act", "copy", tmS[:, c0:c0 + cn], ps[0:4, 0:cn], reads=[ps], writes=[tmS])
            rope_apply(P, V(trS[:, :].rearrange("p (h c) -> p h c", c=64), [trS]), V(tmS[:, 0:384].rearrange("p (h c) -> p h c", c=64), [tmS]),
                       ropeS[:, :], 6, 4, rt, ropeS)
            for c in range(2):
                P.dma("sp", d["cmp_s"].ap()[j, rows, c * 256 + g * 64:c * 256 + (g + 1) * 64], tmS[:, 384 + c * 64:448 + c * 64],
                      reads=[tmS], writes=[d["cmp_s"]])
            P.dma("sp", d["sel_s"].ap()[j, rows, g * 64:(g + 1) * 64], trS[:, 256:320], reads=[trS], writes=[d["sel_s"]])
            P.dma("sp", d["sel_s"].ap()[j, rows, 256 + g * 64:256 + (g + 1) * 64], tmS[:, 512:576], reads=[tmS], writes=[d["sel_s"]])
            P.dma("sp", d["win_s"].ap()[j, s_, 508:512, g * 64:(g + 1) * 64], trS[:, 320:384], reads=[trS], writes=[d["win_s"]])
            P.dma("sp", d["win_s"].ap()[j, s_, 508:512, 256 + g * 64:256 + (g + 1) * 64], tmS[:, 576:640], reads=[tmS], writes=[d["win_s"]])
            for r in range(2):
                P.i("dve", "tensor_copy", qd[:, 0:4, r, :], tmS[:, 0:256].rearrange("p (h c) -> p h c", c=64), reads=[tmS], writes=[qd])
                P.i("dve", "tensor_copy", qd[:, 4:8, r, :], trS[:, 0:256].rearrange("p (h c) -> p h c", c=64), reads=[trS], writes=[qd])
            P.i("dve", "tensor_copy", kb2[:], trS[:, 256:384], reads=[trS], writes=[kb2])
            P.i("act", "copy", SVs[:, s_, g, 0:64], tmS[:, 512:576], reads=[tmS], writes=[SVs])
            P.i("act", "copy", SVw[:, s_, g, 0:64], tmS[:, 576:640], reads=[tmS], writes=[SVw])
            P.i("act", "activation", SG[:, s_, g, :], tmS[:, 640:652], AF.Sigmoid, reads=[tmS], writes=[SG])
            pq = PS[2][:, :].bitcast(BF16)
            for h in range(8):
                P.i("pe", "transpose", pq[:, h * 4:(h + 1) * 4], qd[:, h, :, :].rearrange("p r c -> p (r c)"), k.identb[0:4, 0:4],
                    reads=[qd, k.identb], writes=[PS[2]])
            P.i("dve", "tensor_copy", SQ2[:, s_, g, :, :].rearrange("p h t -> p (h t)"), pq[:, 0:32], reads=[PS[2]], writes=[SQ2])
            pk = PS[3][0:64, :].bitcast(BF16)
            P.i("pe", "transpose", pk[:, 0:4], kb2[:, 0:64], k.identb[0:4, 0:4], reads=[kb2, k.identb], writes=[PS[3]])
            P.i("pe", "transpose", pk[:, 4:8], kb2[:, 64:128], k.identb[0:4, 0:4], reads=[kb2, k.identb], writes=[PS[3]])
            P.i("act", "copy", SKs[:, s_, g, :], pk[:, 0:4], reads=[PS[3]], writes=[SKs])
            P.i("act", "copy", SKw[:, s_, g, :], pk[:, 4:8], reads=[PS[3]], writes=[SKw])
    P.release(m0)
    ck("s_S0")

    m1 = P.mark()
    wc2 = P.sb([128, 64, 2, 64], BF16, "wc2")
    for hf in range(2):
        for lh in range(4):
            P.dma("pool", wc2[hf * 64:(hf + 1) * 64, lh * 16:(lh + 1) * 16], d["nsa_cmp_w"].ap()[j, lh * 16:(lh + 1) * 16].rearrange("l c d e -> d l c e"),
                  reads=[d["nsa_cmp_w"]], writes=[wc2])
    pe4 = P.sb([128, 4, 64], F32, "pe4")
    stg64 = P.sb([128, 128], F32, "sstg")
    for r in range(2):
        P.dma("sp", stg64[:, r * 64:(r + 1) * 64], d["nsa_cmp_pe"].ap()[j].rearrange("l c d -> (l c) d"), reads=[d["nsa_cmp_pe"]], writes=[stg64])
    P.i("pe", "transpose", PS[6][:, 0:128], stg64[:], k.ident[:], reads=[stg64, k.ident], writes=[PS[6]])
    for b in range(4):
        P.i("dve", "tensor_copy", pe4[:, b, :], PS[6][:, 0:128].rearrange("p (l c) -> p c l", c=2)[:, b // 2, :], reads=[PS[6]], writes=[pe4])
    XT = P.sb([128, 4, NPG * 128], BF16, "XT")
    pgf = [P.sb([128, 512], F32, "pgf%d" % i) for i in range(2)]
    pgb = [P.sb([128, 512], BF16, "pgb%d" % i) for i in range(2)]
    pgfA = [P.sb([128, 512], F32, "pgfA%d" % i) for i in range(2)]
    pgbA = [P.sb([128, 512], BF16, "pgbA%d" % i) for i in range(2)]
    CkS = P.sb([64, 4, 128], BF16, "CkS")
    CvS = P.sb([128, 4, 64], BF16, "CvS")
    Ecs = P.sb([128, 64], F32, "Ecs")
    rss = P.sb([128, 64], F32, "rss")
    pbs = P.sb([128, 64], BF16, "pbs")
    impTs = P.sb([128, 16], F32, "impTs")
    imp16 = P.sb([16, 128], F32, "imp16")
    tmp16 = P.sb([16, 128], F32, "tmp16")
    sel16 = P.sb([16, 128], BF16, "sel16")
    m8s = P.sb([16, 16], F32, "m8s")
    selX = [P.sb([16, 128], BF16, "selX%d" % i) for i in range(2)]
    Mbs = [P.sb([128, 16], BF16, "Mbs%d" % i) for i in range(2)]
    KT4 = [P.sb([64, 4, 128], BF16, "KT4%d" % i) for i in range(2)]
    onesb = P.sb([128, 64], BF16, "onesb")
    P.i("pool", "memset", onesb[:], 1.0, writes=[onesb])
    Es = [P.sb([128, 64], BF16, "Es%d" % i) for i in range(2)]
    occ = P.sb([64, 64], F32, "occ")
    bcs = P.sb([64, 64], F32, "bcs")
    gd = P.sb([4, 16, 4], F32, "gd")
    om_ = P.sb([64, 64], F32, "som")
    om2_ = P.sb([64, 64], F32, "som2")
    cnt = [0]

    def gather(cache, s_, pg, buf):
        c = s_ * NPG + pg
        P.dma("pool", buf[:], d[cache].ap().rearrange("l r c -> (l r) c"), indirect=idx[:, c:c + 1].bitcast(U32),
              reads=[d[cache], idx], writes=[buf])

    def attend_tile(kt4, pb_, mask_fn, first, qsl):
        E = Es[cnt[0] % 2]
        cnt[0] += 1
        for g in range(4):
            P.i("pe", "matmul", PS[5][:, g * 16:(g + 1) * 16], kt4[:, g, :], SQ2[0:64, qsl[0], g, 4:8, :], start=True, stop=True,
                reads=[kt4, SQ2], writes=[PS[5]])
        P.i("act", "activation", E[:], PS[5][:, 0:64], AF.Exp, scale=SCALE, reads=[PS[5]], writes=[E])
        if mask_fn is not None:
            mask_fn(E)
        for g in range(4):
            P.i("pe", "matmul", PS[7][0:64, g * 16:(g + 1) * 16], pb_[:, 256 + g * 64:256 + (g + 1) * 64], E[:, g * 16:(g + 1) * 16],
                start=(first and g == 0), stop=False, skip_group_check=True, reads=[pb_, E], writes=[PS[7]])
        P.i("pe", "matmul", PS[6][0:64, 0:64], onesb[:, :], E[:, :], start=first, stop=False, skip_group_check=True,
            reads=[onesb, E], writes=[PS[6]])

    def finish_branch(dst):
        P.i("dve", "reciprocal", bcs[:], PS[6][0:64, 0:64], reads=[PS[6]], writes=[bcs])
        P.i("dve", "tensor_tensor", dst[:], PS[7][0:64, 0:64], bcs[:], ALU.mult, reads=[PS[7], bcs], writes=[dst])

    def gate_bc(s_, which):
        P.i("dve", "tensor_tensor", gd[:], SG[:, s_, :, :].rearrange("p g (h c) -> p (g h) c", c=3)[:, :, which].unsqueeze(2).to_broadcast([4, 16, 4]),
            i4[:, :].unsqueeze(1).to_broadcast([4, 16, 4]), ALU.mult, reads=[SG, i4], writes=[gd])
        P.i("pe", "matmul", PS[5][0:64, 128:192], k.ones[0:4, 0:64], gd[:].rearrange("p a t -> p (a t)"), start=True, stop=True,
            reads=[k.ones, gd], writes=[PS[5]])

    def stage_a(s_):
        for pg in range(NPG):
            pf, pb_ = pgfA[pg % 2], pgbA[pg % 2]
            psa = PS[2 + pg % 2]
            gather("cache_cmp", s_, pg, pf)
            if pg % 2 == 0:
                P.i("act", "copy", pb_[:], pf[:], reads=[pf], writes=[pb_])
            else:
                P.i("dve", "tensor_copy", pb_[:], pf[:], reads=[pf], writes=[pb_])
            pt = psa[:, :].bitcast(BF16)
            for b in range(4):
                P.i("pe", "transpose", pt[:, b * 128:(b + 1) * 128], pb_[:, b * 128:(b + 1) * 128], k.identb[:], reads=[pb_, k.identb], writes=[psa])
            P.i("dve", "tensor_tensor", XT[:, :, pg * 128:(pg + 1) * 128].rearrange("p b (n l) -> p b n l", l=64),
                pt[:, 0:512].rearrange("p (b n l) -> p b n l", b=4, l=64), pe4[:, :, :].unsqueeze(2).to_broadcast([128, 4, 2, 64]), ALU.add,
                reads=[psa, pe4], writes=[XT])
            yield

    def stage_rest(s_):
        qs = (s_,)
        ck("s_a%d" % s_)
        for g in range(4):
            gp, gl = g // 2, g % 2
            pr = slice(gl * 64, (gl + 1) * 64)
            xk = XT[pr, 0 * 2 + gp, :].rearrange("p (n l) -> p n l", l=64)
            xv = XT[pr, 1 * 2 + gp, :].rearrange("p (n l) -> p n l", l=64)
            for ll in range(64):
                P.i("pe", "matmul", PS[2][0:64, 0:128], wc2[pr, ll, 0, :], xk[:, :, ll], start=(ll == 0), stop=(ll == 63), reads=[wc2, XT], writes=[PS[2]])
            P.i("act", "copy", CkS[:, g, :], PS[2][0:64, 0:128], reads=[PS[2]], writes=[CkS])
            for ll in range(64):
                P.i("pe", "matmul", PS[3][:, 0:64], xv[:, :, ll], wc2[pr, ll, 1, :], start=(ll == 0), stop=(ll == 63), reads=[wc2, XT], writes=[PS[3]])
            P.i("act", "copy", CvS[:, g, :], PS[3][:, 0:64], reads=[PS[3]], writes=[CvS])
        ck("s_b%d" % s_)
        for g in range(4):
            P.i("pe", "matmul", PS[5][:, g * 16:(g + 1) * 16], CkS[:, g, :], SQ2[0:64, s_, g, 0:4, :], start=True, stop=True,
                reads=[CkS, SQ2], writes=[PS[5]])
        P.i("act", "activation", Ecs[:], PS[5][:, 0:64], AF.Exp, scale=SCALE, reads=[PS[5]], writes=[Ecs])
        P.i("pe", "matmul", PS[5][:, 0:64], k.ones[:], Ecs[:], start=True, stop=True, reads=[k.ones, Ecs], writes=[PS[5]])
        P.i("dve", "reciprocal", rss[:], PS[5][:, 0:64], reads=[PS[5]], writes=[rss])
        P.i("dve", "tensor_tensor", Ecs[:], Ecs[:], rss[:], ALU.mult, reads=[Ecs, rss], writes=[Ecs])
        P.i("act", "copy", pbs[:], Ecs[:], reads=[Ecs], writes=[pbs])
        for g in range(4):
            P.i("pe", "matmul", PS[6][0:64, g * 16:(g + 1) * 16], CvS[:, g, :], pbs[:, g * 16:(g + 1) * 16], start=True, stop=True,
                reads=[CvS, pbs], writes=[PS[6]])
        P.i("act", "copy", occ[:], PS[6][0:64, 0:64], reads=[PS[6]], writes=[occ])
        P.i("dve", "tensor_reduce", impTs[:].rearrange("p (g t) -> p g t", g=4), Ecs[:].rearrange("p (g a t) -> p g t a", g=4, a=4), AX.X, ALU.add,
            reads=[Ecs], writes=[impTs])
        P.i("pe", "transpose", PS[4][0:16, 0:128], impTs[:], k.ident[:], reads=[impTs, k.ident], writes=[PS[4]])
        P.i("dve", "tensor_copy", imp16[:], PS[4][0:16, 0:128], reads=[PS[4]], writes=[imp16])
        P.i("dve", "max", m8s[:, 0:8], imp16[:], reads=[imp16], writes=[m8s])
        P.i("dve", "match_replace", tmp16[:], m8s[:, 0:8], imp16[:], -2.0, reads=[m8s, imp16], writes=[tmp16])
        P.i("dve", "max", m8s[:, 8:16], tmp16[:], reads=[tmp16], writes=[m8s])
        P.i("dve", "tensor_scalar", sel16[:], imp16[:], m8s[:, 14:15], None, ALU.is_ge, reads=[imp16, m8s], writes=[sel16])
        ck("s_c%d" % s_)
        for pg in range(NPG):
            pf, pb_ = pgf[pg % 2], pgb[pg % 2]
            gather("cache_sel", s_, pg, pf)
            if pg % 2 == 0:
                P.i("act", "copy", pb_[:], pf[:], reads=[pf], writes=[pb_])
            else:
                P.i("dve", "tensor_copy", pb_[:], pf[:], reads=[pf], writes=[pb_])
            kt4, sx, mb = KT4[pg % 2], selX[pg % 2], Mbs[pg % 2]
            pt = PS[pg % 2][0:64, :].bitcast(BF16)
            for g in range(4):
                P.i("pe", "transpose", pt[:, g * 128:(g + 1) * 128], pb_[:, g * 64:(g + 1) * 64], k.identb[:], reads=[pb_, k.identb], writes=[PS[pg % 2]])
            P.i("act", "copy", kt4[:].rearrange("p a t -> p (a t)"), pt[:, 0:512], reads=[PS[pg % 2]], writes=[kt4])
            P.i("dve", "tensor_copy", sx[:].rearrange("p (n l) -> p n l", l=64), sel16[:, 2 * pg:2 * pg + 2].unsqueeze(2).to_broadcast([16, 2, 64]),
                reads=[sel16], writes=[sx])
            P.i("pe", "matmul", PS[4][:, 128:144], sx[:], k.identb[0:16, 0:16], start=True, stop=True, reads=[sx, k.identb], writes=[PS[4]])
            P.i("dve", "tensor_copy", mb[:], PS[4][:, 128:144], reads=[PS[4]], writes=[mb])

            def mfn(E, mb=mb):
                P.i("dve", "tensor_tensor", E[:].rearrange("p (g a t) -> p g a t", g=4, a=4), E[:].rearrange("p (g a t) -> p g a t", g=4, a=4),
                    mb[:].rearrange("p (g t) -> p g t", g=4).unsqueeze(2).to_broadcast([128, 4, 4, 4]), ALU.mult, reads=[E, mb], writes=[E])
            attend_tile(kt4, pb_, mfn, pg == 0, qs)
            yield
        ck("s_dpre%d" % s_)
        new_tile(k, P, PS, SKs, SVs, SQ2, s_, Es, cnt, onesb)
        ck("s_dnew%d" % s_)
        finish_branch(om2_)
        ck("s_dfin%d" % s_)
        gate_bc(s_, 1)
        P.i("dve", "tensor_tensor", om_[:], om2_[:], PS[5][0:64, 128:192], ALU.mult, reads=[om2_, PS[5]], writes=[om_])
        gate_bc(s_, 0)
        P.i("dve", "tensor_tensor", om2_[:], occ[:], PS[5][0:64, 128:192], ALU.mult, reads=[occ, PS[5]], writes=[om2_])
        P.i("pool", "tensor_tensor", om_[:], om_[:], om2_[:], ALU.add, reads=[om_, om2_], writes=[om_])
        ck("s_d%d" % s_)
        for a in range(4):
            pf, pb_ = pgf[a % 2], pgb[a % 2]
            P.dma("sp", pf[:], d["state_win"].ap()[j, s_, a * 128:(a + 1) * 128, :], reads=[d["state_win"]], writes=[pf])
            if a == 0:
                P.dma("sp", d["win_s"].ap()[j, s_, 0:124, :], pf[4:128, :], reads=[pf], writes=[d["win_s"]])
            else:
                P.dma("sp", d["win_s"].ap()[j, s_, a * 128 - 4:a * 128 + 124, :], pf[:, :], reads=[pf], writes=[d["win_s"]])
            P.i("act", "copy", pb_[:], pf[:], reads=[pf], writes=[pb_])
            kt4 = KT4[a % 2]
            pt = PS[a % 2][0:64, :].bitcast(BF16)
            for g in range(4):
                P.i("pe", "transpose", pt[:, g * 128:(g + 1) * 128], pb_[:, g * 64:(g + 1) * 64], k.identb[:], reads=[pb_, k.identb], writes=[PS[a % 2]])
            P.i("act", "copy", kt4[:].rearrange("p a t -> p (a t)"), pt[:, 0:512], reads=[PS[a % 2]], writes=[kt4])
            mfn = None
            if a == 0:
                def mfn(E):
                    P.i("dve", "tensor_tensor", E[:].rearrange("p (a t) -> p a t", t=4), E[:].rearrange("p (a t) -> p a t", t=4),
                        k.wmask[:, 0:4].unsqueeze(1).to_broadcast([128, 16, 4]), ALU.mult, reads=[E, k.wmask], writes=[E])
            attend_tile(kt4, pb_, mfn, a == 0, qs)
            yield
        new_tile(k, P, PS, SKw, SVw, SQ2, s_, Es, cnt, onesb)
        finish_branch(om2_)
        gate_bc(s_, 2)
        P.i("dve", "tensor_tensor", om2_[:], om2_[:], PS[5][0:64, 128:192], ALU.mult, reads=[om2_, PS[5]], writes=[om2_])
        P.i("pool", "tensor_tensor", oTs[:, s_, :, :].rearrange("p a t -> p (a t)"), om_[:], om2_[:], ALU.add, reads=[om_, om2_], writes=[oTs])
        ck("s_e%d" % s_)
        yield

    run_il([stage_a(0)])
    for s_ in range(4):
        r_ = stage_rest(s_)
        next(r_)
        run_il([r_, stage_a(s_ + 1) if s_ < 3 else None])
    P.release(m1)

    m2 = P.mark()
    wo = P.sb([64, 16, D], BF16, "swo")
    for q4 in range(4):
        P.dma("pool", wo[:, q4 * 4:(q4 + 1) * 4, :], d["nsa_w_out"].ap()[j, q4 * 256:(q4 + 1) * 256, :].rearrange("(h p) c -> p h c", p=64),
              reads=[d["nsa_w_out"]], writes=[wo])
    for dt in range(8):
        ps = PS[dt % 2]
        for h in range(16):
            P.i("pe", "matmul", ps[:, 0:16], wo[:, h, dt * 128:(dt + 1) * 128], oTs[:, :, h, :], start=(h == 0), stop=(h == 15),
                reads=[wo, oTs], writes=[ps])
        P.i("dve", "tensor_tensor", k.R[dt][:, LP:NT], ps[:, 0:16], k.R[dt][:, LP:NT], ALU.add, reads=[ps, k.R[dt]], writes=[k.R[dt]])
    P.release(m2)
    P.release(m)


def new_tile(k, P, PS, SK, SV, SQ2, s_, Es, cnt, onesb):
    E = Es[cnt[0] % 2]
    cnt[0] += 1
    for g in range(4):
        P.i("pe", "matmul", PS[5][0:4, g * 16:(g + 1) * 16], SK[:, s_, g, :], SQ2[0:64, s_, g, 4:8, :], start=True, stop=True,
            reads=[SK, SQ2], writes=[PS[5]])
    P.i("act", "activation", E[0:4, :], PS[5][0:4, 0:64], AF.Exp, scale=SCALE, reads=[PS[5]], writes=[E])
    P.i("dve", "tensor_tensor", E[0:4, :].rearrange("p (a t) -> p a t", t=4), E[0:4, :].rearrange("p (a t) -> p a t", t=4),
        k.caus[0:4, 0:4].unsqueeze(1).to_broadcast([4, 16, 4]), ALU.mult, reads=[E, k.caus], writes=[E])
    for g in range(4):
        P.i("pe", "matmul", PS[7][0:64, g * 16:(g + 1) * 16], SV[:, s_, g, 0:64], E[0:4, g * 16:(g + 1) * 16],
            start=False, stop=(g == 3), skip_group_check=True, reads=[SV, E], writes=[PS[7]])
    P.i("pe", "matmul", PS[6][0:64, 0:64], onesb[0:4, :], E[0:4, :], start=False, stop=True, skip_group_check=True,
        reads=[onesb, E], writes=[PS[6]])


_OUT_ORDER = ["y_p", "y_s", "gdn_p", "gdn_s", "gconv_p", "gconv_s", "cmp_p", "cmp_s", "sel_p", "sel_s",
              "win_p", "win_s", "ffn_p", "ffn_s"]


def _rope_table():
    inv = (np.float32(10000.0) ** (-np.arange(32, dtype=np.float32) / np.float32(32))).astype(np.float32)
    pos = np.zeros((17, 128), np.float32)
    pos[:16] = np.arange(LP, dtype=np.float32).reshape(16, 128)
    pos[16] = 8192 + (np.arange(128) % 4)
    ang = (pos[:, :, None] * inv[None, None, :]).astype(np.float32)
    return np.concatenate([np.cos(ang), np.sin(ang)], axis=-1).astype(np.float32)


_ROPE_TAB = _rope_table()


def make_in_maps(inp, compact=False):
    maps = []
    nphys = inp["cache_cmp"].shape[1]
    cc = np.ascontiguousarray(inp["cache_cmp"]).reshape(2, nphys * 128, 512)
    cs = np.ascontiguousarray(inp["cache_sel"]).reshape(2, nphys * 128, 512)
    for c in range(8):
        s0, s1 = c * NS, (c + 1) * NS
        m = {}
        m["xp"] = np.ascontiguousarray(inp["x_prompt"][c])
        m["xs"] = np.ascontiguousarray(inp["x_sample"][s0:s1]).reshape(NS * DS, D)
        m["state_gdn"] = np.ascontiguousarray(inp["state_gdn"][:, s0:s1])
        m["state_gdn_conv"] = np.ascontiguousarray(inp["state_gdn_conv"][:, s0:s1])
        m["state_win"] = np.ascontiguousarray(inp["state_win"][:, s0:s1]).reshape(2, NS, 512, 512)
        m["state_ffn_conv"] = np.ascontiguousarray(inp["state_ffn_conv"][:, s0:s1])
        pt = np.ascontiguousarray(inp["page_table"][s0:s1]).astype(np.int32)
        if compact:
            flat = pt.reshape(-1)
            m["cache_cmp"] = np.ascontiguousarray(inp["cache_cmp"][:, flat]).reshape(2, flat.size * 128, 512)
            m["cache_sel"] = np.ascontiguousarray(inp["cache_sel"][:, flat]).reshape(2, flat.size * 128, 512)
            pt = np.arange(flat.size, dtype=np.int32).reshape(NS, NPG)
        else:
            m["cache_cmp"] = cc
            m["cache_sel"] = cs
        m["page_table"] = pt
        m["rope_tab"] = _ROPE_TAB
        for nm in ["norm_mix", "norm_ffn", "norm_final", "gdn_w_in", "gdn_conv_w", "gdn_a_log", "gdn_dt_bias", "gdn_norm_w",
                   "gdn_w_out", "nsa_w_in", "nsa_cmp_pe", "nsa_cmp_w", "nsa_w_out", "ffn_w_up", "ffn_conv_w", "ffn_conv_b", "ffn_w_down"]:
            m[nm] = np.ascontiguousarray(inp[nm])
        maps.append(m)
    return maps


def assemble(results):
    def cat(nm, axis):
        return np.concatenate([r[nm] for r in results], axis=axis)

    def stack(nm, axis):
        return np.stack([r[nm] for r in results], axis=axis)
    y_p = stack("y_p", 0)
    y_s = cat("y_s", 0).reshape(32, DS, D)
    gdn_p = stack("gdn_p", 1)
    gdn_s = cat("gdn_s", 1)
    gconv_p = stack("gconv_p", 1)
    gconv_s = cat("gconv_s", 1)
    cmp_p = stack("cmp_p", 1).reshape(2, 8, LP, 2, 4, 64)
    cmp_s = cat("cmp_s", 1).reshape(2, 32, DS, 2, 4, 64)
    sel_p = stack("sel_p", 1).reshape(2, 8, LP, 2, 4, 64)
    sel_s = cat("sel_s", 1).reshape(2, 32, DS, 2, 4, 64)
    win_p = stack("win_p", 1).reshape(2, 8, 512, 2, 4, 64)
    win_s = cat("win_s", 1).reshape(2, 32, 512, 2, 4, 64)
    ffn_p = stack("ffn_p", 1)
    ffn_s = cat("ffn_s", 1)
    return (y_p, y_s, gdn_p, gdn_s, gconv_p, gconv_s, cmp_p, cmp_s, sel_p, sel_s, win_p, win_s, ffn_p, ffn_s)


def kernel(**inputs):
    inp = {k_: np.asarray(v) for k_, v in inputs.items()}
    n_phys = inp["cache_cmp"].shape[1]
    nc = build(n_phys)
    maps = make_in_maps(inp)
    res = run_bass_kernel_spmd(nc, maps, core_ids=list(range(8)))
    return assemble(res.results)
```

```python
import numpy as np
import concourse.bass as bass
import concourse.mybir as mybir
from concourse.bass_utils import run_bass_kernel_spmd

F32 = mybir.dt.float32
BF16 = mybir.dt.bfloat16
I32 = mybir.dt.int32
U32 = mybir.dt.uint32
AF = mybir.ActivationFunctionType
ALU = mybir.AluOpType
AX = mybir.AxisListType

D = 1024
LP = 2048
NS = 4
DS = 4
NT = LP + NS * DS
DFF = 2816
NPG = 64
EPS = 1e-6
GW = 4112
NW = 2608


class Buf:
    __slots__ = ("t", "name", "lw", "rd", "psum")

    def __init__(self, t, name, psum=False):
        self.t = t
        self.name = name
        self.lw = None
        self.rd = []
        self.psum = psum

    def __getitem__(self, idx):
        return self.t[idx]

    def ap(self):
        return self.t.ap()


class Prog:
    ENGS = ("pe", "act", "dve", "pool", "sp")

    def __init__(self, nc, ndma=8):
        self.nc = nc
        self.stack = []
        self.ops = {e: [] for e in self.ENGS}
        self.sems = {}
        self.cnt = {}
        self.waited = {e: {} for e in self.ENGS}
        self.semguards = []
        self.ekey = {}
        self.epoch = {}
        self.LIMIT = 3000
        for e in self.ENGS:
            self._mksem("E_" + e)
            self.ekey[e] = "E_" + e
            self.epoch[e] = 0
        self.dkey = {}
        self.ndma = ndma
        self.dma_rr = {"sp": 0, "pool": 0, "act": 0}
        for q in ("sp", "pool", "act"):
            for i in range(ndma):
                self._mksem("D_%s%d" % (q, i))
                self.dkey[(q, i)] = "D_%s%d" % (q, i)
        self.nrot = 0
        self.nbuf = 0
        self.nops = 0

    def _mksem(self, key):
        g = self.nc.semaphore(key)
        s = g.__enter__()
        self.semguards.append(g)
        self.sems[key] = s
        self.cnt[key] = 0

    def sb(self, shape, dt=F32, name=None):
        self.nbuf += 1
        name = (name or "sb") + "_%d" % self.nbuf
        g = self.nc.sbuf_tensor(name, list(shape), dt)
        t = g.__enter__()
        self.stack.append(g)
        return Buf(t, name)

    def ps(self, shape, dt=F32, name=None):
        self.nbuf += 1
        name = (name or "ps") + "_%d" % self.nbuf
        g = self.nc.psum_tensor(name, list(shape), dt)
        t = g.__enter__()
        self.stack.append(g)
        return Buf(t, name, psum=True)

    def dram(self, name, shape, dt=F32, kind="Internal"):
        t = self.nc.dram_tensor(name, list(shape), dt, kind=kind)
        return Buf(t, name)

    def mark(self):
        return len(self.stack)

    def release(self, mark):
        self.barrier()
        while len(self.stack) > mark:
            g = self.stack.pop()
            g.__exit__(None, None, None)

    def _need(self, eng, ev, waits):
        if ev is None:
            return
        key, val = ev
        if eng == "pe" and key.startswith("E_pe"):
            return
        if self.waited[eng].get(key, 0) >= val:
            return
        self.waited[eng][key] = val
        waits.append((key, val))

    def _deps(self, eng, reads, writes):
        waits = []
        for b in reads:
            self._need(eng, b.lw, waits)
            if b.psum:
                for ev in b.rd:
                    if not ev[0].startswith("E_" + eng):
                        self._need(eng, ev, waits)
        for b in writes:
            self._need(eng, b.lw, waits)
            for ev in b.rd:
                self._need(eng, ev, waits)
        return waits

    def _commit(self, ev, reads, writes):
        for b in reads:
            b.rd.append(ev)
            if len(b.rd) > 48:
                m = {}
                for k, v in b.rd:
                    if m.get(k, 0) < v:
                        m[k] = v
                b.rd = list(m.items())
        for b in writes:
            b.lw = ev
            b.rd = []

    def op(self, eng, fn, reads=(), writes=()):
        waits = self._deps(eng, reads, writes)
        key = self.ekey[eng]
        self.cnt[key] += 1
        ev = (key, self.cnt[key])
        self.ops[eng].append((waits, fn, (key, 1)))
        self._commit(ev, reads, writes)
        self.nops += 1
        if self.cnt[key] >= self.LIMIT:
            self.epoch[eng] += 1
            self.ekey[eng] = self._rotate(key, "E_%s#%d" % (eng, self.epoch[eng]))
        return ev

    def _rotate(self, old_key, new_key):
        final = self.cnt[old_key]
        for e in self.ENGS:
            waits = []
            self._need(e, (old_key, final), waits)
            if waits:
                self.ops[e].append((waits, None, None))
        self._mksem(new_key)
        return new_key

    def i(self, eng, name, *args, reads=(), writes=(), **kw):
        def fn(e, name=name, args=args, kw=kw, eng=eng):
            try:
                return getattr(e, name)(*args, **kw)
            except Exception as ex:
                raise RuntimeError("instr %s.%s failed: %s | args=%s kw=%s" % (eng, name, ex, [str(a)[:160] for a in args], kw)) from ex
        return self.op(eng, fn, reads, writes)

    def dma(self, q, out_ap, in_ap, reads=(), writes=(), indirect=None, **kw):
        i = self.dma_rr[q]
        self.dma_rr[q] = (i + 1) % self.ndma
        key = self.dkey[(q, i)]
        if self.cnt[key] >= self.LIMIT:
            self.nrot += 1
            key = self._rotate(key, "D_%s%d#%d" % (q, i, self.nrot))
            self.dkey[(q, i)] = key
        waits = self._deps(q, reads, writes)
        if self.cnt[key] > 0:
            self._need(q, (key, self.cnt[key]), waits)
        self.cnt[key] += 16
        ev = (key, self.cnt[key])
        if indirect is None:
            def fn(e, out_ap=out_ap, in_ap=in_ap, kw=kw):
                return e.dma_start(out=out_ap, in_=in_ap, **kw)
        else:
            def fn(e, out_ap=out_ap, in_ap=in_ap, idx=indirect):
                return e.indirect_dma_start(out=out_ap, out_offset=None, in_=in_ap,
                                            in_offset=bass.IndirectOffsetOnAxis(ap=idx, axis=0))
        self.ops[q].append((waits, fn, (key, 16)))
        self._commit(ev, reads, writes)
        self.nops += 1
        return ev

    def barrier(self):
        for e in self.ENGS:
            waits = []
            for key, c in self.cnt.items():
                if c > 0:
                    self._need(e, (key, c), waits)
            if waits:
                self.ops[e].append((waits, None, None))

    def finish(self):
        self.barrier()
        nc = self.nc
        hmap = {"pe": "tensor", "act": "scalar", "dve": "vector", "pool": "gpsimd", "sp": "sync"}
        with nc.Block() as block:
            for e in self.ENGS:
                oplist = self.ops[e]

                def body(h, oplist=oplist):
                    for waits, fn, inc in oplist:
                        for key, val in waits:
                            h.wait_ge(self.sems[key], val)
                        if fn is not None:
                            ins = fn(h)
                            ins.then_inc(self.sems[inc[0]], inc[1])
                getattr(block, hmap[e])(body)
        while self.stack:
            self.stack.pop().__exit__(None, None, None)
        for g in reversed(self.semguards):
            g.__exit__(None, None, None)


def run_il(gens):
    gens = [g for g in gens if g is not None]
    while gens:
        for g in list(gens):
            try:
                next(g)
            except StopIteration:
                gens.remove(g)


def chunks(t0, t1, sz=512):
    out = []
    t = t0
    while t < t1:
        n = min(sz, t1 - t)
        out.append((t, n))
        t += n
    return out


class K:
    pass


class StopNSA(Exception):
    pass


STOP = [None]
SKIP_PROMPT = [False]


def ck(name):
    if STOP[0] == name:
        raise StopNSA(name)


def build(n_phys, n_layers=4, mixers=True):
    nc = bass.Bass("TRN2", target_bir_lowering=False)
    P = Prog(nc)
    k = K()
    k.P = P
    k.nc = nc
    k.n_phys = n_phys
    EI, EO = "ExternalInput", "ExternalOutput"
    d = {}
    k.d = d
    d["xp"] = P.dram("xp", [LP, D], F32, EI)
    d["xs"] = P.dram("xs", [NS * DS, D], F32, EI)
    d["state_gdn"] = P.dram("state_gdn", [2, NS, 8, 128, 128], F32, EI)
    d["state_gdn_conv"] = P.dram("state_gdn_conv", [2, NS, 3, 3072], F32, EI)
    d["cache_cmp"] = P.dram("cache_cmp", [2, n_phys * 128, 512], F32, EI)
    d["cache_sel"] = P.dram("cache_sel", [2, n_phys * 128, 512], F32, EI)
    d["state_win"] = P.dram("state_win", [2, NS, 512, 512], F32, EI)
    d["state_ffn_conv"] = P.dram("state_ffn_conv", [4, NS, 2, 2 * DFF], F32, EI)
    d["page_table"] = P.dram("page_table", [NS, NPG], I32, EI)
    d["norm_mix"] = P.dram("norm_mix", [4, D], F32, EI)
    d["norm_ffn"] = P.dram("norm_ffn", [4, D], F32, EI)
    d["norm_final"] = P.dram("norm_final", [D], F32, EI)
    d["gdn_w_in"] = P.dram("gdn_w_in", [2, D, GW], F32, EI)
    d["gdn_conv_w"] = P.dram("gdn_conv_w", [2, 4, 3072], F32, EI)
    d["gdn_a_log"] = P.dram("gdn_a_log", [2, 8], F32, EI)
    d["gdn_dt_bias"] = P.dram("gdn_dt_bias", [2, 8], F32, EI)
    d["gdn_norm_w"] = P.dram("gdn_norm_w", [2, 128], F32, EI)
    d["gdn_w_out"] = P.dram("gdn_w_out", [2, D, D], F32, EI)
    d["nsa_w_in"] = P.dram("nsa_w_in", [2, D, NW], F32, EI)
    d["nsa_cmp_pe"] = P.dram("nsa_cmp_pe", [2, 64, 2, 64], F32, EI)
    d["nsa_cmp_w"] = P.dram("nsa_cmp_w", [2, 64, 2, 64, 64], F32, EI)
    d["nsa_w_out"] = P.dram("nsa_w_out", [2, D, D], F32, EI)
    d["ffn_w_up"] = P.dram("ffn_w_up", [4, D, 2 * DFF], F32, EI)
    d["ffn_conv_w"] = P.dram("ffn_conv_w", [4, 3, 2 * DFF], F32, EI)
    d["ffn_conv_b"] = P.dram("ffn_conv_b", [4, 2 * DFF], F32, EI)
    d["ffn_w_down"] = P.dram("ffn_w_down", [4, DFF, D], F32, EI)
    d["rope_tab"] = P.dram("rope_tab", [17, 128, 64], F32, EI)
    d["y_p"] = P.dram("y_p", [LP, D], F32, EO)
    d["y_s"] = P.dram("y_s", [NS * DS, D], F32, EO)
    d["gdn_p"] = P.dram("gdn_p", [2, 8, 128, 128], F32, EO)
    d["gdn_s"] = P.dram("gdn_s", [2, NS, 8, 128, 128], F32, EO)
    d["gconv_p"] = P.dram("gconv_p", [2, 3, 3072], F32, EO)
    d["gconv_s"] = P.dram("gconv_s", [2, NS, 3, 3072], F32, EO)
    d["cmp_p"] = P.dram("cmp_p", [2, LP, 512], F32, EO)
    d["cmp_s"] = P.dram("cmp_s", [2, NS * DS, 512], F32, EO)
    d["sel_p"] = P.dram("sel_p", [2, LP, 512], F32, EO)
    d["sel_s"] = P.dram("sel_s", [2, NS * DS, 512], F32, EO)
    d["win_p"] = P.dram("win_p", [2, 512, 512], F32, EO)
    d["win_s"] = P.dram("win_s", [2, NS, 512, 512], F32, EO)
    d["ffn_p"] = P.dram("ffn_p", [4, 2, 2 * DFF], F32, EO)
    d["ffn_s"] = P.dram("ffn_s", [4, NS, 2, 2 * DFF], F32, EO)

    k.R = [P.sb([128, NT], F32, "R%d" % i) for i in range(8)]
    k.ident = P.sb([128, 128], F32, "ident")
    k.identb = P.sb([128, 128], BF16, "identb")
    k.ones = P.sb([128, 128], F32, "ones")
    k.ncol = P.sb([128, 72], F32, "ncol")
    k.PS = [P.ps([128, 512], F32, "psb%d" % i) for i in range(8)]
    k.stg = P.sb([128, 128], F32, "stg")

    P.i("pool", "memset", k.ident[:], 0.0, writes=[k.ident])
    P.i("pool", "affine_select", out=k.ident[:], in_=k.ident[:], pattern=[[-1, 128]],
                                            compare_op=ALU.not_equal, fill=1.0, base=0, channel_multiplier=1,
         reads=[k.ident], writes=[k.ident])
    P.i("dve", "tensor_copy", k.identb[:], k.ident[:], reads=[k.ident], writes=[k.identb])
    P.i("pool", "memset", k.ones[:], 1.0, writes=[k.ones])


    k.ltri = P.sb([64, 64], F32, "ltri")
    k.msl = P.sb([64, 64], F32, "msl")
    k.e63 = P.sb([64, 128], F32, "e63")
    k.pm4 = P.sb([64, 1], F32, "pm4")
    P.i("pool", "memset", k.ltri[:], 1.0, writes=[k.ltri])
    P.i("pool", "affine_select", out=k.ltri[:], in_=k.ltri[:], pattern=[[1, 64]], compare_op=ALU.is_ge, fill=0.0, base=0,
        channel_multiplier=-1, reads=[k.ltri], writes=[k.ltri])
    P.i("pool", "memset", k.msl[:], 1.0, writes=[k.msl])
    P.i("pool", "affine_select", out=k.msl[:], in_=k.msl[:], pattern=[[-1, 64]], compare_op=ALU.is_ge, fill=0.0, base=-1,
        channel_multiplier=1, reads=[k.msl], writes=[k.msl])
    P.i("pool", "memset", k.e63[:], 0.0, writes=[k.e63])
    P.i("pool", "affine_select", out=k.e63[:], in_=k.e63[:], pattern=[[0, 128]], compare_op=ALU.not_equal, fill=1.0, base=-63,
        channel_multiplier=1, reads=[k.e63], writes=[k.e63])
    P.i("pool", "memset", k.pm4[:], 1.0, writes=[k.pm4])
    P.i("pool", "affine_select", out=k.pm4[:], in_=k.pm4[:], pattern=[[0, 1]], compare_op=ALU.is_ge, fill=0.0, base=3,
        channel_multiplier=-1, reads=[k.pm4], writes=[k.pm4])


    k.caus = P.sb([128, 128], BF16, "caus")
    k.wmask = P.sb([128, 128], BF16, "wmask")
    P.i("pool", "memset", k.caus[:], 1.0, writes=[k.caus])
    P.i("pool", "affine_select", out=k.caus[:], in_=k.caus[:], pattern=[[1, 128]], compare_op=ALU.is_ge, fill=0.0, base=0,
        channel_multiplier=-1, reads=[k.caus], writes=[k.caus])
    P.i("pool", "memset", k.wmask[:], 1.0, writes=[k.wmask])
    P.i("pool", "affine_select", out=k.wmask[:], in_=k.wmask[:], pattern=[[-1, 128]], compare_op=ALU.is_gt, fill=0.0, base=0,
        channel_multiplier=1, reads=[k.wmask], writes=[k.wmask])

    load_cols(k, d["norm_mix"], d["norm_mix"].ap().rearrange("l (t p) -> (l t) p", p=128), 32, k.ncol, 0)
    load_cols(k, d["norm_ffn"], d["norm_ffn"].ap().rearrange("l (t p) -> (l t) p", p=128), 32, k.ncol, 32)
    load_cols(k, d["norm_final"], d["norm_final"].ap().rearrange("(t p) -> t p", p=128), 8, k.ncol, 64)

    load_x(k)
    for l in range(n_layers):
        if mixers:
            if l % 2 == 0:
                gdn_layer(k, l)
            else:
                nsa_layer(k, l)
        ffn_layer(k, l)
    final_out(k)
    P.finish()
    return nc


def load_cols(k, src_buf, src2d, nrows, dst, col0, ps=None):
    P = k.P
    ps = ps or k.PS[6]
    r0 = 0
    while r0 < nrows:
        n = min(128, nrows - r0)
        P.dma("sp", k.stg[0:n, :], src2d[r0:r0 + n, :], reads=[src_buf], writes=[k.stg])
        P.i("pe", "transpose", ps[:, 0:n], k.stg[0:n, :], k.ident[0:n, 0:n],
             reads=[k.stg, k.ident], writes=[ps])
        P.i("dve", "tensor_copy", dst[:, col0 + r0:col0 + r0 + n], ps[:, 0:n], reads=[ps], writes=[dst])
        r0 += n


def load_x(k):
    P = k.P
    d = k.d
    m = P.mark()
    xt = [P.sb([128, 4, D], F32, "xt%d" % i) for i in range(2)]
    for g in range(4):
        b = xt[g % 2]
        P.dma("sp", b[:], d["xp"].ap()[g * 512:(g + 1) * 512, :].rearrange("(a p) c -> p a c", p=128),
              reads=[d["xp"]], writes=[b])
        for dt in range(8):
            ps = k.PS[dt % 4]
            for a in range(4):
                P.i("pe", "transpose", ps[:, a * 128:(a + 1) * 128], b[:, a, dt * 128:(dt + 1) * 128], k.ident[:],
                     reads=[b, k.ident], writes=[ps])
            eng = "dve" if dt % 2 == 0 else "act"
            if eng == "dve":
                P.i("dve", "tensor_copy", k.R[dt][:, g * 512:(g + 1) * 512], ps[:], reads=[ps], writes=[k.R[dt]])
            else:
                P.i("act", "copy", k.R[dt][:, g * 512:(g + 1) * 512], ps[:], reads=[ps], writes=[k.R[dt]])
    b = xt[0]
    P.dma("sp", b[0:16, 0, :], d["xs"].ap(), reads=[d["xs"]], writes=[b])
    ps = k.PS[0]
    for dt in range(8):
        P.i("pe", "transpose", ps[:, dt * 16:(dt + 1) * 16], b[0:16, 0, dt * 128:(dt + 1) * 128], k.ident[0:16, 0:16],
             reads=[b, k.ident], writes=[ps])
    for dt in range(8):
        P.i("dve", "tensor_copy", k.R[dt][:, LP:NT], ps[:, dt * 16:(dt + 1) * 16], reads=[ps], writes=[k.R[dt]])
    P.release(m)


def rmsnorm(k, wcol0, t0, n, out_tiles, o0, scratch):
    P = k.P
    for (c0, cn) in chunks(t0, t0 + n):
        ps = k.PS[6]
        for dt in range(8):
            sq = scratch["sq"][dt % 2]
            P.i("act", "activation", sq[:, 0:cn], k.R[dt][:, c0:c0 + cn], AF.Square,
                 reads=[k.R[dt]], writes=[sq])
            P.i("pe", "matmul", ps[:, 0:cn], k.ones[:], sq[:, 0:cn], start=(dt == 0), stop=(dt == 7),
                 reads=[sq, k.ones], writes=[ps])
        rstd = scratch["rstd"]
        P.i("dve", "tensor_scalar", rstd[:, 0:cn], ps[:, 0:cn], 1.0 / D, EPS, ALU.mult, ALU.add, reads=[ps], writes=[rstd])
        P.i("act", "activation", rstd[:, 0:cn], rstd[:, 0:cn], AF.Ln, reads=[rstd], writes=[rstd])
        P.i("act", "activation", rstd[:, 0:cn], rstd[:, 0:cn], AF.Exp, scale=-0.5, reads=[rstd], writes=[rstd])
        for dt in range(8):
            eng = "dve"
            P.i(eng, "scalar_tensor_tensor",
                out_tiles[dt][:, o0 + c0 - t0:o0 + c0 - t0 + cn], k.R[dt][:, c0:c0 + cn],
                k.ncol[:, wcol0 + dt:wcol0 + dt + 1], rstd[:, 0:cn], ALU.mult, ALU.mult,
                reads=[k.R[dt], k.ncol, rstd], writes=[out_tiles[dt]])


def ffn_layer(k, l):
    P = k.P
    d = k.d
    m = P.mark()
    NH = 1042
    hT = [P.sb([128, NH], BF16, "fh%d" % i) for i in range(8)]
    act = [P.sb([128, 1040], BF16, "fa%d" % i) for i in range(22)]
    wup = [P.sb([128, 8, 512], BF16, "wup%d" % i) for i in range(2)]
    wdn = [P.sb([128, 22, 128], BF16, "wdn%d" % i) for i in range(2)]
    ub = [[P.sb([128, 1056], F32, "ub%d%d" % (i, j)) for j in range(2)] for i in range(2)]
    cb2 = [[P.sb([128, 1040], F32, "cb%d%d" % (p_, i)) for i in range(2)] for p_ in range(2)]
    scratch = {"sq": [P.sb([128, 512], F32, "sq%d" % i) for i in range(2)], "rstd": P.sb([128, 512], F32, "rstd")}
    fpar = P.sb([128, 176], F32, "fpar")
    hist = P.sb([128, 352], F32, "hist")
    hsel = P.sb([128, 8, 16], BF16, "hsel")
    halo = P.sb([128, 88], F32, "halo")
    strow = P.sb([16, 2 * DFF], F32, "strow") if False else None
    st_sb = P.sb([16, 512], F32, "stsb")

    for j in range(3):
        load_cols(k, d["ffn_conv_w"], d["ffn_conv_w"].ap()[l, j].rearrange("(t p) -> t p", p=128), 44, fpar, j * 44)
    load_cols(k, d["ffn_conv_b"], d["ffn_conv_b"].ap()[l].rearrange("(t p) -> t p", p=128), 44, fpar, 132)
    load_cols(k, d["state_ffn_conv"], d["state_ffn_conv"].ap()[l].rearrange("s j (t p) -> (s j t) p", p=128), 352, hist, 0)

    wdma = [0]

    def load_wup(jb, buf):
        for which in range(2):
            c0 = which * DFF + jb * 256
            P.dma("pool", buf[:, :, which * 256:(which + 1) * 256],
                  d["ffn_w_up"].ap()[l, :, c0:c0 + 256].rearrange("(kt p) c -> p kt c", p=128),
                  reads=[d["ffn_w_up"]], writes=[buf])

    def load_wdn(dt, buf):
        P.dma("pool", buf[:], d["ffn_w_down"].ap()[l, :, dt * 128:(dt + 1) * 128].rearrange("(j p) c -> p j c", p=128),
              reads=[d["ffn_w_down"]], writes=[buf])

    psi = [0]
    for half in range(2):
        if half == 0:
            t0, n = 0, 1024
            npr = 1024
            ucol0 = 2
        else:
            t0, n = 1024, 1040
            npr = 1024
            ucol0 = 2
        rmsnorm(k, 32 + l * 8, t0, n, hT, 0, scratch)
        nprm = n - (16 if half == 1 else 0)
        if half == 1:
            for kt in range(8):
                P.i("pool", "tensor_copy", hsel[:, kt, 0:2], hT[kt][:, 1022:1024], reads=[hT[kt]], writes=[hsel])
                P.i("pool", "tensor_copy",
                    hsel[:, kt, 2:10].rearrange("p (s c) -> p s c", c=2),
                    hT[kt][:, 1024:1040].rearrange("p (s c) -> p s c", c=4)[:, :, 2:4], reads=[hT[kt]], writes=[hsel])
        pend = []
        load_wup(0, wup[0])
        for jb in range(11):
            wb = wup[jb % 2]
            if jb + 1 < 11:
                load_wup(jb + 1, wup[(jb + 1) % 2])
            if half == 1:
                pst = k.PS[7]
                for which in range(2):
                    for kt in range(8):
                        P.i("pe", "matmul",
                            pst[0:10, which * 256:(which + 1) * 256], hsel[:, kt, 0:10], wb[:, kt, which * 256:(which + 1) * 256],
                            start=(kt == 0), stop=(kt == 7), reads=[hsel, wb], writes=[pst])
                P.i("act", "copy", st_sb[0:10, :], pst[0:10, :], reads=[pst], writes=[st_sb])
                for which in range(2):
                    c0 = which * DFF + jb * 256
                    P.dma("sp", d["ffn_p"].ap()[l, :, c0:c0 + 256], st_sb[0:2, which * 256:(which + 1) * 256], reads=[st_sb], writes=[d["ffn_p"]])
                    P.dma("sp", d["ffn_s"].ap()[l, :, :, c0:c0 + 256].rearrange("s j c -> (s j) c"), st_sb[2:10, which * 256:(which + 1) * 256], reads=[st_sb], writes=[d["ffn_s"]])
            for jj in range(2):
                j = jb * 2 + jj
                par = j % 2
                for which in range(2):
                    u = ub[par][which]
                    tix = j + 22 * which
                    if half == 0:
                        P.i("pool", "memset", u[:, 0:2], 0.0, writes=[u])
                    else:
                        P.i("pool", "tensor_copy", u[:, 0:2], halo[:, tix * 2:tix * 2 + 2], reads=[halo], writes=[u])
                        P.i("pool", "tensor_copy",
                            u[:, 1026:1050].rearrange("p (s c) -> p s c", c=6)[:, :, 0:2],
                            hist[:, :].rearrange("p (s j t) -> p s j t", j=2, t=44)[:, :, :, tix], reads=[hist], writes=[u])
                    for (c0, cn) in chunks(0, nprm):
                        ps = k.PS[psi[0] % 4]
                        psi[0] += 1
                        for kt in range(8):
                            P.i("pe", "matmul",
                                ps[:, 0:cn], wb[:, kt, which * 256 + jj * 128:which * 256 + (jj + 1) * 128], hT[kt][:, c0:c0 + cn],
                                start=(kt == 0), stop=(kt == 7), reads=[wb, hT[kt]], writes=[ps])
                        P.i("act", "copy", u[:, ucol0 + c0:ucol0 + c0 + cn], ps[:, 0:cn], reads=[ps], writes=[u])
                    if half == 1:
                        ps = k.PS[psi[0] % 4]
                        psi[0] += 1
                        for kt in range(8):
                            P.i("pe", "matmul",
                                ps[:, 0:16], wb[:, kt, which * 256 + jj * 128:which * 256 + (jj + 1) * 128], hT[kt][:, 1024:1040],
                                start=(kt == 0), stop=(kt == 7), reads=[wb, hT[kt]], writes=[ps])
                        P.i("act", "copy",
                            u[:, 1026:1050].rearrange("p (s c) -> p s c", c=6)[:, :, 2:6],
                            ps[:, 0:16].rearrange("p (s c) -> p s c", c=4), reads=[ps], writes=[u])
                    if half == 0:
                        P.i("pool", "tensor_copy", halo[:, tix * 2:tix * 2 + 2], u[:, 1024:1026], reads=[u], writes=[halo])
                    c = cb2[par][which]
                    w0 = fpar[:, tix:tix + 1]
                    w1 = fpar[:, 44 + tix:44 + tix + 1]
                    w2 = fpar[:, 88 + tix:88 + tix + 1]
                    bb = fpar[:, 132 + tix:132 + tix + 1]
                    eng = "dve"
                    regions = [(lambda a, off: a[:, off:off + npr], lambda a: a[:, 0:npr])]
                    if half == 1:
                        regions.append((lambda a, off: a[:, 1026:1050].rearrange("p (s c) -> p s c", c=6)[:, :, off:off + 4],
                                        lambda a: a[:, 1024:1040].rearrange("p (s c) -> p s c", c=4)))
                    for (uin, cout) in regions:
                        P.i(eng, "tensor_scalar", cout(c), uin(u, 2), w2, bb, ALU.mult, ALU.add,
                             reads=[u, fpar], writes=[c])
                        P.i(eng, "scalar_tensor_tensor", cout(c), uin(u, 1), w1, cout(c), ALU.mult, ALU.add,
                             reads=[u, fpar, c], writes=[c])
                        P.i(eng, "scalar_tensor_tensor", cout(c), uin(u, 0), w0, cout(c), ALU.mult, ALU.add,
                             reads=[u, fpar, c], writes=[c])
                ntok = npr + (16 if half == 1 else 0)

                def fin(j=j, par=par, ntok=ntok):
                    cb = cb2[par]
                    P.i("act", "activation", cb[0][:, 0:ntok], cb[0][:, 0:ntok], AF.Silu, reads=[cb[0]], writes=[cb[0]])
                    P.i("dve", "tensor_tensor", act[j][:, 0:ntok], cb[0][:, 0:ntok], cb[1][:, 0:ntok], ALU.mult,
                         reads=[cb[0], cb[1]], writes=[act[j]])
                if pend:
                    pend.pop()()
                pend.append(fin)
        while pend:
            pend.pop()()
        ntok = npr + (16 if half == 1 else 0)
        tok0 = 0 if half == 0 else 1024
        load_wdn(0, wdn[0])
        for dt in range(8):
            wd = wdn[dt % 2]
            if dt + 1 < 8:
                load_wdn(dt + 1, wdn[(dt + 1) % 2])
            for (c0, cn) in chunks(0, ntok):
                ps = k.PS[4 + (psi[0] % 2)]
                psi[0] += 1
                for j in range(22):
                    P.i("pe", "matmul", ps[:, 0:cn], wd[:, j, :], act[j][:, c0:c0 + cn], start=(j == 0), stop=(j == 21),
                         reads=[wd, act[j]], writes=[ps])
                P.i("dve", "tensor_tensor",
                    k.R[dt][:, tok0 + c0:tok0 + c0 + cn], ps[:, 0:cn], k.R[dt][:, tok0 + c0:tok0 + c0 + cn], ALU.add,
                    reads=[ps, k.R[dt]], writes=[k.R[dt]])
    P.release(m)


def final_out(k):
    P = k.P
    d = k.d
    m = P.mark()
    yt = [P.sb([128, 4, D], F32, "yt%d" % i) for i in range(2)]
    hn = [P.sb([128, 528], F32, "hn%d" % i) for i in range(8)]
    scratch = {"sq": [P.sb([128, 512], F32, "sq%d" % i) for i in range(2)], "rstd": P.sb([128, 512], F32, "rstd")}
    for g, (c0, cn) in enumerate(chunks(0, NT)):
        rmsnorm(k, 64, c0, cn, hn, 0, scratch)
        b = yt[g % 2]
        na = (cn + 127) // 128
        for a in range(na):
            tn = min(128, cn - a * 128)
            for dt in range(8):
                ps = k.PS[(a * 8 + dt) // 4 % 4]
                q = dt % 4
                P.i("pe", "transpose", ps[0:tn, q * 128:(q + 1) * 128], hn[dt][:, a * 128:a * 128 + tn], k.ident[:],
                     reads=[hn[dt], k.ident], writes=[ps])
                if q == 3:
                    h4 = dt // 4
                    eng = "dve" if h4 == 0 else "act"
                    if eng == "dve":
                        P.i("dve", "tensor_copy", b[0:tn, a, h4 * 512:(h4 + 1) * 512], ps[0:tn, :], reads=[ps], writes=[b])
                    else:
                        P.i("act", "copy", b[0:tn, a, h4 * 512:(h4 + 1) * 512], ps[0:tn, :], reads=[ps], writes=[b])
        if c0 < LP:
            P.dma("sp", d["y_p"].ap()[c0:c0 + cn, :].rearrange("(a p) c -> p a c", p=128), b[:], reads=[b], writes=[d["y_p"]])
        else:
            P.dma("sp", d["y_s"].ap(), b[0:16, 0, :], reads=[b], writes=[d["y_s"]])
    P.release(m)


def gdn_layer(k, l):
    P = k.P
    d = k.d
    j = l // 2
    m = P.mark()
    TG = 2304
    hT = [P.sb([128, NT], BF16, "gh%d" % i) for i in range(8)]
    scratch = {"sq": [P.sb([128, 512], F32, "sq%d" % i) for i in range(2)], "rstd": P.sb([128, 512], F32, "rstd")}
    sq, rstd = scratch["sq"], scratch["rstd"]
    rmsnorm(k, l * 8, 0, NT, hT, 0, scratch)
    PS = k.PS
    gcw = P.sb([128, 96], F32, "gcw")
    load_cols(k, d["gdn_conv_w"], d["gdn_conv_w"].ap()[j].rearrange("r (t p) -> (r t) p", p=128), 96, gcw, 0)
    gnw = P.sb([128, 1], F32, "gnw")
    load_cols(k, d["gdn_norm_w"], d["gdn_norm_w"].ap()[j:j + 1, :], 1, gnw, 0)
    hsel = P.sb([128, 8, 16], BF16, "ghsel")
    beta = P.sb([64, 36, 8], F32, "beta")
    Gc = P.sb([64, 36, 8], F32, "Gc")
    egl = P.sb([128, 288], F32, "egl")
    edec = P.sb([64, 36, 8], F32, "edec")
    ebg = P.sb([64, 36, 8], F32, "ebg")
    nbeta = P.sb([64, 36, 8], F32, "nbeta")
    mpre = P.mark()
    wab = P.sb([128, 8, 16], BF16, "wab")
    P.dma("pool", wab[:], d["gdn_w_in"].ap()[j, :, 4096:4112].rearrange("(kt p) c -> p kt c", p=128), reads=[d["gdn_w_in"]], writes=[wab])
    hsp = P.sb([128, 8, 4, 64], BF16, "hsp")
    P.i("pool", "memset", hsp[:], 0.0, writes=[hsp])
    for kt in range(8):
        P.i("pool", "tensor_copy", hsp[:, kt, :, 0:4], hT[kt][:, LP:NT].rearrange("p (s c) -> p s c", c=4), reads=[hT[kt]], writes=[hsp])
        P.i("dve", "tensor_copy", hsel[:, kt, 0:3], hT[kt][:, LP - 3:LP], reads=[hT[kt]], writes=[hsel])
        P.i("dve", "tensor_copy", hsel[:, kt, 3:15].rearrange("p (s c) -> p s c", c=3),
            hT[kt][:, LP:NT].rearrange("p (s c) -> p s c", c=4)[:, :, 1:4], reads=[hT[kt]], writes=[hsel])
    ab_all = P.sb([64, 36, 16], F32, "ab_all")
    for n in range(36):
        ps = PS[0] if n < 32 else PS[1]
        c0 = (n % 32) * 16
        for kt in range(8):
            lhsT = hT[kt][:, n * 64:(n + 1) * 64] if n < 32 else hsp[:, kt, n - 32, :]
            P.i("pe", "matmul", ps[0:64, c0:c0 + 16], lhsT, wab[:, kt, :], start=(kt == 0), stop=(kt == 7),
                reads=[hT[kt], hsp, wab], writes=[ps])
    P.i("dve", "tensor_copy", ab_all[:, 0:32, :], PS[0][0:64, 0:512].rearrange("p (n c) -> p n c", c=16), reads=[PS[0]], writes=[ab_all])
    P.i("dve", "tensor_copy", ab_all[:, 32:36, :], PS[1][0:64, 0:64].rearrange("p (n c) -> p n c", c=16), reads=[PS[1]], writes=[ab_all])
    alog = P.sb([64, 8], F32, "alog")
    dtb = P.sb([64, 8], F32, "dtb")
    P.dma("sp", alog[:], d["gdn_a_log"].ap()[j].partition_broadcast(64), reads=[d["gdn_a_log"]], writes=[alog])
    P.dma("sp", dtb[:], d["gdn_dt_bias"].ap()[j].partition_broadcast(64), reads=[d["gdn_dt_bias"]], writes=[dtb])
    P.i("act", "activation", alog[:], alog[:], AF.Exp, reads=[alog], writes=[alog])
    P.i("dve", "tensor_scalar_mul", alog[:], alog[:], -1.0, reads=[alog], writes=[alog])
    g_all = P.sb([64, 36, 8], F32, "g_all")
    bc8 = lambda t: t[:, :].unsqueeze(1).to_broadcast([64, 36, 8])
    P.i("dve", "tensor_tensor", g_all[:], ab_all[:, :, 0:8], bc8(dtb), ALU.add, reads=[ab_all, dtb], writes=[g_all])
    P.i("act", "activation", g_all[:], g_all[:], AF.Exp, reads=[g_all], writes=[g_all])
    P.i("dve", "tensor_scalar_add", g_all[:], g_all[:], 1.0, reads=[g_all], writes=[g_all])
    P.i("act", "activation", g_all[:], g_all[:], AF.Ln, reads=[g_all], writes=[g_all])
    P.i("dve", "tensor_tensor", g_all[:], g_all[:], bc8(alog), ALU.mult, reads=[g_all, alog], writes=[g_all])
    P.i("act", "activation", beta[:], ab_all[:, :, 8:16], AF.Sigmoid, reads=[ab_all], writes=[beta])
    P.i("dve", "tensor_scalar_mul", g_all[:, 32:36, :], g_all[:, 32:36, :], k.pm4[:, 0:1], reads=[g_all, k.pm4], writes=[g_all])
    P.i("dve", "tensor_scalar_mul", beta[:, 32:36, :], beta[:, 32:36, :], k.pm4[:, 0:1], reads=[beta, k.pm4], writes=[beta])
    fl = lambda t: t[:].rearrange("p n h -> p (n h)")
    P.i("pe", "matmul", PS[0][0:64, 0:288], k.ltri[:], fl(g_all), start=True, stop=True, reads=[k.ltri, g_all], writes=[PS[0]])
    P.i("dve", "tensor_copy", fl(Gc), PS[0][0:64, 0:288], reads=[PS[0]], writes=[Gc])
    P.i("pe", "matmul", PS[1][:, 0:288], k.e63[:], fl(Gc), start=True, stop=True, reads=[k.e63, Gc], writes=[PS[1]])
    P.i("act", "activation", egl[:], PS[1][:, 0:288], AF.Exp, reads=[PS[1]], writes=[egl])
    P.i("dve", "tensor_tensor", fl(edec), PS[1][0:64, 0:288], fl(Gc), ALU.subtract, reads=[PS[1], Gc], writes=[edec])
    P.i("act", "activation", edec[:], edec[:], AF.Exp, reads=[edec], writes=[edec])
    P.i("act", "activation", ebg[:], Gc[:], AF.Exp, reads=[Gc], writes=[ebg])
    P.i("dve", "tensor_tensor", ebg[:], ebg[:], beta[:], ALU.mult, reads=[ebg, beta], writes=[ebg])
    P.i("dve", "tensor_scalar_mul", nbeta[:], beta[:], -1.0, reads=[beta], writes=[nbeta])
    P.release(mpre)

    wqkvz = [P.sb([128, 8, 128], BF16, "gw%d" % i) for i in range(4)]
    wout = P.sb([128, D], BF16, "gwout")
    xb = P.sb([128, 2080], F32, "gxb")
    X = [P.sb([128, TG], BF16, "gX%d" % i) for i in range(3)]
    for t in X:
        P.i("pool", "memset", t[:, LP:TG], 0.0, writes=[t])
    qgb = [P.sb([128, 512], BF16, "qgb%d" % i) for i in range(5)]
    zsb = P.sb([128, NT], BF16, "zsb")
    ogb = P.sb([128, NT], BF16, "ogb")
    cf = P.sb([128, 512], F32, "gcf")
    hist12 = P.sb([128, 12], F32, "hist12")
    st_sb = P.sb([16, 128], F32, "gst")
    G = {}
    for nm in ["negD", "decL", "decU", "tmp", "Q0", "Q1", "R0", "R1", "Tt", "egrow"]:
        G[nm] = P.sb([64 if nm != "egrow" else 128, 512], BF16 if nm[0] in "QR" else F32, "g" + nm)
    Ttq = P.sb([64, 512], BF16, "gTtq")
    i64bb = k.identb[0:64, 0:64].unsqueeze(1).to_broadcast([64, 8, 64])
    DG = P.sb([64, 512], F32, "gDG")
    Ttb2 = [P.sb([64, 512], BF16, "gTtb%d" % i) for i in range(2)]
    kb = P.sb([64, 8, 128], BF16, "gkb")
    kdec2 = [P.sb([64, 8, 128], BF16, "gkdec%d" % i) for i in range(2)]
    vb2 = [P.sb([64, 8, 128], BF16, "gvb%d" % i) for i in range(2)]
    nwk2 = [P.sb([128, 512], BF16, "gnwk%d" % i) for i in range(2)]
    qkm2 = [P.sb([64, 512], BF16, "gqkm%d" % i) for i in range(2)]
    S2 = [P.sb([128, 128], F32, "gS%d" % i) for i in range(2)]
    Sin4 = P.sb([128, NS, 128], F32, "gSin4")
    Sb = P.sb([128, 128], BF16, "gSb")
    ub = P.sb([64, 128], BF16, "gub")
    on = P.sb([128, 512], F32, "gon")
    i64b = k.ident[0:64, 0:64].unsqueeze(1).to_broadcast([64, 8, 64])
    v3 = lambda ap: ap.rearrange("p (n c) -> p n c", c=64)

    for h in range(8):
        P.dma("sp", Sin4[:], d["state_gdn"].ap()[j, :, h].rearrange("s p c -> p s c"), reads=[d["state_gdn"]], writes=[Sin4])
        for part in range(4):
            c0 = part * 1024 + h * 128
            P.dma("pool", wqkvz[part][:], d["gdn_w_in"].ap()[j, :, c0:c0 + 128].rearrange("(kt p) c -> p kt c", p=128),
                  reads=[d["gdn_w_in"]], writes=[wqkvz[part]])
        P.dma("pool", wout[:], d["gdn_w_out"].ap()[j, h * 128:(h + 1) * 128, :], reads=[d["gdn_w_out"]], writes=[wout])
        for part in range(4):
            w = wqkvz[part]
            col0 = part * 1024 + h * 128
            tix = part * 8 + h
            if part < 3:
                pst = PS[7]
                for kt in range(8):
                    P.i("pe", "matmul", pst[0:15, 256:384], hsel[:, kt, 0:15], w[:, kt, :], start=(kt == 0), stop=(kt == 7),
                        reads=[hsel, w], writes=[pst])
                P.i("act", "copy", st_sb[0:15, :], pst[0:15, 256:384], reads=[pst], writes=[st_sb])
                P.dma("sp", d["gconv_p"].ap()[j, :, col0:col0 + 128], st_sb[0:3, :], reads=[st_sb], writes=[d["gconv_p"]])
                P.dma("sp", d["gconv_s"].ap()[j, :, :, col0:col0 + 128].rearrange("s r c -> (s r) c"), st_sb[3:15, :],
                      reads=[st_sb], writes=[d["gconv_s"]])
                load_cols(k, d["state_gdn_conv"], d["state_gdn_conv"].ap()[j, :, :, col0:col0 + 128].rearrange("s r c -> (s r) c"),
                          12, hist12, 0)
                P.i("pool", "memset", xb[:, 0:3], 0.0, writes=[xb])
                P.i("pool", "tensor_copy", xb[:, 2051:2079].rearrange("p (s c) -> p s c", c=7)[:, :, 0:3],
                    hist12[:, :].rearrange("p (s c) -> p s c", c=3), reads=[hist12], writes=[xb])
            for ci, (c0, cn) in enumerate(chunks(0, NT)):
                ps = PS[ci % 2]
                for kt in range(8):
                    P.i("pe", "matmul", ps[:, 0:cn], w[:, kt, :], hT[kt][:, c0:c0 + cn], start=(kt == 0), stop=(kt == 7),
                        reads=[w, hT[kt]], writes=[ps])
                if part == 3:
                    P.i("act", "activation", zsb[:, c0:c0 + cn], ps[:, 0:cn], AF.Silu, reads=[ps], writes=[zsb])
                elif c0 < LP:
                    P.i("act", "copy", xb[:, 3 + c0:3 + c0 + cn], ps[:, 0:cn], reads=[ps], writes=[xb])
                else:
                    P.i("act", "copy", xb[:, 2051:2079].rearrange("p (s c) -> p s c", c=7)[:, :, 3:7],
                        ps[:, 0:16].rearrange("p (s c) -> p s c", c=4), reads=[ps], writes=[xb])
            if part == 3:
                continue
            wc = [gcw[:, r * 24 + tix:r * 24 + tix + 1] for r in range(4)]
            for (c0, cn) in chunks(0, NT):
                if c0 < LP:
                    src = lambda off: xb[:, c0 + off:c0 + off + cn]
                    cfv = cf[:, 0:cn]
                    dst = X[part][:, c0:c0 + cn]
                else:
                    src = lambda off: xb[:, 2051:2079].rearrange("p (s c) -> p s c", c=7)[:, :, off:off + 4]
                    cfv = cf[:, 0:16].rearrange("p (s c) -> p s c", c=4)
                    dst = X[part][:, LP:TG].rearrange("p (s c) -> p s c", c=64)[:, :, 0:4]
                P.i("dve", "tensor_scalar_mul", cfv, src(3), wc[3], reads=[xb, gcw], writes=[cf])
                for r in range(3):
                    P.i("dve", "scalar_tensor_tensor", cfv, src(r), wc[r], cfv, ALU.mult, ALU.add, reads=[xb, gcw, cf], writes=[cf])
                P.i("act", "activation", cf[:, 0:cn], cf[:, 0:cn], AF.Silu, reads=[cf], writes=[cf])
                if part == 2:
                    P.i("act", "copy", dst, cfv, reads=[cf], writes=[X[part]])
                else:
                    sqb = sq[0]
                    P.i("act", "activation", sqb[:, 0:cn], cf[:, 0:cn], AF.Square, reads=[cf], writes=[sqb])
                    P.i("pe", "matmul", PS[6][:, 0:cn], k.ones[:], sqb[:, 0:cn], start=True, stop=True, reads=[sqb, k.ones], writes=[PS[6]])
                    P.i("dve", "tensor_scalar_add", rstd[:, 0:cn], PS[6][:, 0:cn], EPS, reads=[PS[6]], writes=[rstd])
                    P.i("act", "activation", rstd[:, 0:cn], rstd[:, 0:cn], AF.Ln, reads=[rstd], writes=[rstd])
                    P.i("act", "activation", rstd[:, 0:cn], rstd[:, 0:cn], AF.Exp, scale=-0.5, reads=[rstd], writes=[rstd])
                    rv = rstd[:, 0:cn] if c0 < LP else rstd[:, 0:16].rearrange("p (s c) -> p s c", c=4)
                    P.i("dve", "scalar_tensor_tensor", dst, cfv, (128.0 ** -0.5) if part == 0 else 1.0, rv, ALU.mult, ALU.mult,
                        reads=[cf, rstd], writes=[X[part]])
        qn, kn, vn = X
        def pre(g):
            nch = 8 if g < 4 else 4
            W = nch * 64
            n0 = g * 8
            Ttb, kdec, vb, nwk, qkm = Ttb2[g % 2], kdec2[g % 2], vb2[g % 2], nwk2[g % 2], qkm2[g % 2]
            gcol = lambda t: t[:, n0:n0 + nch, h].unsqueeze(2)
            cs = lambda c: slice(g * 512 + c * 64, g * 512 + (c + 1) * 64)
            P.i("dve", "tensor_tensor", v3(DG[:, 0:W]), i64b[:, 0:nch, :], gcol(Gc).to_broadcast([64, nch, 64]), ALU.mult,
                reads=[k.ident, Gc], writes=[DG])
            P.i("pe", "matmul", PS[6][:, 0:W], k.ones[0:64, :], DG[:, 0:W], start=True, stop=True, reads=[k.ones, DG], writes=[PS[6]])
            P.i("act", "activation", G["egrow"][:, 0:W], PS[6][:, 0:W], AF.Exp, reads=[PS[6]], writes=[G["egrow"]])
            yield
            P.i("dve", "tensor_tensor", qgb[g][:, 0:W], qn[:, g * 512:g * 512 + W], G["egrow"][:, 0:W], ALU.mult,
                reads=[qn, G["egrow"]], writes=[qgb[g]])
            P.i("dve", "tensor_tensor", v3(G["negD"][:, 0:W]), v3(PS[6][0:64, 0:W]), gcol(Gc).to_broadcast([64, nch, 64]), ALU.subtract,
                reads=[PS[6], Gc], writes=[G["negD"]])
            P.i("dve", "tensor_scalar_max", G["decL"][:, 0:W], G["negD"][:, 0:W], 0.0, reads=[G["negD"]], writes=[G["decL"]])
            P.i("act", "activation", G["decL"][:, 0:W], G["decL"][:, 0:W], AF.Exp, scale=-1.0, reads=[G["decL"]], writes=[G["decL"]])
            P.i("pool", "tensor_tensor", v3(G["decL"][:, 0:W]), v3(G["decL"][:, 0:W]), k.msl[:, :].unsqueeze(1).to_broadcast([64, nch, 64]), ALU.mult,
                reads=[G["decL"], k.msl], writes=[G["decL"]])
            P.i("dve", "tensor_scalar_min", G["decU"][:, 0:W], G["negD"][:, 0:W], 0.0, reads=[G["negD"]], writes=[G["decU"]])
            P.i("act", "activation", G["decU"][:, 0:W], G["decU"][:, 0:W], AF.Exp, reads=[G["decU"]], writes=[G["decU"]])
            P.i("pool", "tensor_tensor", v3(G["decU"][:, 0:W]), v3(G["decU"][:, 0:W]), k.ltri[:, :].unsqueeze(1).to_broadcast([64, nch, 64]), ALU.mult,
                reads=[G["decU"], k.ltri], writes=[G["decU"]])
            yield
            for c in range(nch):
                P.i("pe", "matmul", PS[0][0:64, c * 64:(c + 1) * 64], kn[:, cs(c)], kn[:, cs(c)], start=True, stop=True, reads=[kn], writes=[PS[0]])
            P.i("dve", "tensor_tensor", G["tmp"][:, 0:W], PS[0][0:64, 0:W], G["decL"][:, 0:W], ALU.mult, reads=[PS[0], G["decL"]], writes=[G["tmp"]])
            P.i("pool", "tensor_tensor", v3(G["R0"][:, 0:W]), v3(G["tmp"][:, 0:W]), gcol(nbeta).to_broadcast([64, nch, 64]), ALU.mult,
                reads=[G["tmp"], nbeta], writes=[G["R0"]])
            yield
            p1b = PS[1][0:64, :].bitcast(BF16)
            for c in range(nch):
                P.i("pe", "transpose", p1b[:, c * 64:(c + 1) * 64], G["R0"][:, c * 64:(c + 1) * 64], k.identb[0:64, 0:64],
                    reads=[G["R0"], k.identb], writes=[PS[1]])
            P.i("act", "copy", G["Q0"][:, 0:W], p1b[:, 0:W], reads=[PS[1]], writes=[G["Q0"]])
            P.i("dve", "tensor_tensor", v3(Ttq[:, 0:W]), v3(p1b[:, 0:W]), i64bb[:, 0:nch, :], ALU.add, reads=[PS[1], k.identb], writes=[Ttq])
            P.i("dve", "tensor_tensor", v3(G["Tt"][:, 0:W]), v3(p1b[:, 0:W]), i64b[:, 0:nch, :], ALU.add, reads=[PS[1], k.ident], writes=[G["Tt"]])
            yield
            for step in range(1, 6):
                cur, nxt = str((step - 1) % 2), str(step % 2)
                Qc, Rc, Qn, Rn = G["Q" + cur], G["R" + cur], G["Q" + nxt], G["R" + nxt]
                for c in range(nch):
                    sl = slice(c * 64, (c + 1) * 64)
                    if step < 5:
                        P.i("pe", "matmul", PS[1][0:64, sl], Rc[:, sl], Qc[:, sl], start=True, stop=True, reads=[Rc, Qc], writes=[PS[1]])
                    P.i("pe", "matmul", PS[2][0:64, sl], Qc[:, sl], Rc[:, sl], start=True, stop=True, reads=[Rc, Qc], writes=[PS[2]])
                if step < 5:
                    P.i("act", "copy", Qn[:, 0:W], PS[1][0:64, 0:W], reads=[PS[1]], writes=[Qn])
                P.i("dve", "tensor_copy", Rn[:, 0:W], PS[2][0:64, 0:W], reads=[PS[2]], writes=[Rn])
                yield
                for c in range(nch):
                    sl = slice(c * 64, (c + 1) * 64)
                    P.i("pe", "matmul", PS[0][0:64, sl], Rn[:, sl], Ttq[:, sl], start=True, stop=True, reads=[Rn, Ttq], writes=[PS[0]])
                Tnext = Ttq if step < 5 else Ttb
                P.i("dve", "tensor_tensor", Tnext[:, 0:W], G["Tt"][:, 0:W], PS[0][0:64, 0:W], ALU.add, reads=[PS[0], G["Tt"]], writes=[Tnext])
                if step < 5:
                    P.i("dve", "tensor_tensor", G["Tt"][:, 0:W], G["Tt"][:, 0:W], PS[0][0:64, 0:W], ALU.add, reads=[PS[0], G["Tt"]], writes=[G["Tt"]])
                yield
            pk = PS[3][0:64, :].bitcast(BF16)
            pv = PS[0][0:64, :].bitcast(BF16)
            for c in range(nch):
                P.i("pe", "transpose", pk[:, c * 128:(c + 1) * 128], kn[:, cs(c)], k.identb[:], reads=[kn, k.identb], writes=[PS[3]])
                P.i("pe", "transpose", pv[:, c * 128:(c + 1) * 128], vn[:, cs(c)], k.identb[:], reads=[vn, k.identb], writes=[PS[0]])
            p3 = lambda ap: ap.rearrange("p (n c) -> p n c", c=128)
            bc = lambda t: gcol(t).to_broadcast([64, nch, 128])
            P.i("dve", "tensor_tensor", kb[:, 0:nch, :], p3(pk[:, 0:nch * 128]), bc(ebg), ALU.mult, reads=[PS[3], ebg], writes=[kb])
            P.i("dve", "tensor_tensor", kdec[:, 0:nch, :], p3(pk[:, 0:nch * 128]), bc(edec), ALU.mult, reads=[PS[3], edec], writes=[kdec])
            P.i("dve", "tensor_tensor", vb[:, 0:nch, :], p3(pv[:, 0:nch * 128]), bc(beta), ALU.mult, reads=[PS[0], beta], writes=[vb])
            yield
            for c in range(nch):
                sl = slice(c * 64, (c + 1) * 64)
                P.i("pe", "matmul", PS[2][:, sl], kb[:, c, :], Ttb[:, sl], start=True, stop=True, reads=[kb, Ttb], writes=[PS[2]])
                P.i("pe", "matmul", PS[1][0:64, sl], kn[:, cs(c)], qn[:, cs(c)], start=True, stop=True, reads=[kn, qn], writes=[PS[1]])
            P.i("act", "activation", nwk[:, 0:W], PS[2][:, 0:W], AF.Copy, scale=-1.0, reads=[PS[2]], writes=[nwk])
            P.i("dve", "tensor_tensor", qkm[:, 0:W], PS[1][0:64, 0:W], G["decU"][:, 0:W], ALU.mult, reads=[PS[1], G["decU"]], writes=[qkm])
            yield

        def rec(g):
            nch = 8 if g < 4 else 4
            W = nch * 64
            n0 = g * 8
            Ttb, kdec, vb, nwk, qkm = Ttb2[g % 2], kdec2[g % 2], vb2[g % 2], nwk2[g % 2], qkm2[g % 2]
            gcol = lambda t: t[:, n0:n0 + nch, h].unsqueeze(2)
            cs = lambda c: slice(g * 512 + c * 64, g * 512 + (c + 1) * 64)
            for c in range(nch):
                n = n0 + c
                sl = slice(c * 64, (c + 1) * 64)
                S = S2[0] if n < 32 else S2[(n + 1) % 2]
                if n == 0:
                    P.i("pool", "memset", S[:], 0.0, writes=[S])
                    P.i("pool", "memset", Sb[:], 0.0, writes=[Sb])
                elif n >= 32:
                    P.i("pool", "tensor_copy", S[:], Sin4[:, n - 32, :], reads=[Sin4], writes=[S])
                    P.i("act", "copy", Sb[:], Sin4[:, n - 32, :], reads=[Sin4], writes=[Sb])
                P.i("pe", "matmul", PS[7][0:64, 0:128], Ttb[:, sl], vb[:, c, :], start=True, stop=False, reads=[Ttb, vb], writes=[PS[7]])
                P.i("pe", "matmul", PS[7][0:64, 0:128], nwk[:, sl], Sb[:], start=False, stop=True, reads=[nwk, Sb], writes=[PS[7]])
                P.i("act", "copy", ub[:], PS[7][0:64, 0:128], reads=[PS[7]], writes=[ub])
                yield
                P.i("pe", "matmul", PS[5][:, sl], Sb[:], qgb[g][:, sl], start=True, stop=False, reads=[Sb, qgb[g]], writes=[PS[5]])
                P.i("pe", "matmul", PS[5][:, sl], ub[:], qkm[:, sl], start=False, stop=True, reads=[ub, qkm], writes=[PS[5]])
                P.i("pe", "matmul", PS[4][:, 0:128], kdec[:, c, :], ub[:], start=True, stop=True, reads=[kdec, ub], writes=[PS[4]])
                P.i("dve", "scalar_tensor_tensor", S[:], S[:], egl[:, n * 8 + h:n * 8 + h + 1], PS[4][:, 0:128], ALU.mult, ALU.add,
                    reads=[S, egl, PS[4]], writes=[S])
                P.i("act", "copy", Sb[:], S[:], reads=[S], writes=[Sb])
                if n == 31:
                    P.dma("sp", d["gdn_p"].ap()[j, h], S[:], reads=[S], writes=[d["gdn_p"]])
                elif n >= 32:
                    P.dma("sp", d["gdn_s"].ap()[j, n - 32, h], S[:], reads=[S], writes=[d["gdn_s"]])
                yield
            P.i("act", "activation", sq[1][:, 0:W], PS[5][:, 0:W], AF.Square, reads=[PS[5]], writes=[sq[1]])
            P.i("pe", "matmul", PS[6][:, 0:W], k.ones[:], sq[1][:, 0:W], start=True, stop=True, reads=[sq[1], k.ones], writes=[PS[6]])
            P.i("dve", "tensor_scalar", rstd[:, 0:W], PS[6][:, 0:W], 1.0 / 128, EPS, ALU.mult, ALU.add, reads=[PS[6]], writes=[rstd])
            P.i("act", "activation", rstd[:, 0:W], rstd[:, 0:W], AF.Ln, reads=[rstd], writes=[rstd])
            P.i("act", "activation", rstd[:, 0:W], rstd[:, 0:W], AF.Exp, scale=-0.5, reads=[rstd], writes=[rstd])
            yield
            P.i("dve", "scalar_tensor_tensor", on[:, 0:W], PS[5][:, 0:W], gnw[:, 0:1], rstd[:, 0:W], ALU.mult, ALU.mult,
                reads=[PS[5], gnw, rstd], writes=[on])
            if g < 4:
                P.i("dve", "tensor_tensor", ogb[:, g * 512:(g + 1) * 512], on[:, 0:512], zsb[:, g * 512:(g + 1) * 512], ALU.mult,
                    reads=[on, zsb], writes=[ogb])
            else:
                P.i("dve", "tensor_tensor", ogb[:, LP:NT].rearrange("p (s c) -> p s c", c=4), v3(on[:, 0:256])[:, :, 0:4],
                    zsb[:, LP:NT].rearrange("p (s c) -> p s c", c=4), ALU.mult, reads=[on, zsb], writes=[ogb])
            yield

        run_il([pre(0)])
        for g in range(5):
            run_il([rec(g), pre(g + 1) if g < 4 else None])
        for dt in range(8):
            for ci, (c0, cn) in enumerate(chunks(0, NT)):
                ps = PS[(dt * 5 + ci) % 2]
                P.i("pe", "matmul", ps[:, 0:cn], wout[:, dt * 128:(dt + 1) * 128], ogb[:, c0:c0 + cn], start=True, stop=True,
                    reads=[wout, ogb], writes=[ps])
                P.i("dve", "tensor_tensor", k.R[dt][:, c0:c0 + cn], ps[:, 0:cn], k.R[dt][:, c0:c0 + cn], ALU.add,
                    reads=[ps, k.R[dt]], writes=[k.R[dt]])
    P.release(m)


SCALE = 64.0 ** -0.5


def rope_apply(P, out, x, tab, nh, np_, tmps, tabbuf):
    t1, t2 = tmps
    cosb = tab[:, 0:32].unsqueeze(1).to_broadcast([np_, nh, 32])
    sinb = tab[:, 32:64].unsqueeze(1).to_broadcast([np_, nh, 32])
    x1, x2 = x[:, :, 0:32], x[:, :, 32:64]
    a, b = t1[0:np_, 0:nh, :], t2[0:np_, 0:nh, :]
    P.i("dve", "tensor_tensor", a, x1, cosb, ALU.mult, reads=x.bufs + [tabbuf], writes=[t1])
    P.i("dve", "tensor_tensor", b, x2, sinb, ALU.mult, reads=x.bufs, writes=[t2])
    P.i("dve", "tensor_tensor", out[:, :, 0:32], a, b, ALU.subtract, reads=[t1, t2], writes=out.bufs)
    P.i("dve", "tensor_tensor", a, x2, cosb, ALU.mult, reads=x.bufs + [t2], writes=[t1])
    P.i("dve", "tensor_tensor", b, x1, sinb, ALU.mult, reads=x.bufs + [t1], writes=[t2])
    P.i("dve", "tensor_tensor", out[:, :, 32:64], a, b, ALU.add, reads=[t1, t2], writes=out.bufs)


class V:
    def __init__(self, ap, bufs):
        self.ap = ap
        self.bufs = bufs

    def __getitem__(self, idx):
        return self.ap[idx]


def nsa_layer(k, l):
    mm = k.P.mark()
    try:
        _nsa_layer(k, l)
    except StopNSA:
        k.P.release(mm)


def _nsa_layer(k, l):
    P = k.P
    d = k.d
    j = l // 2
    PS = k.PS
    m0 = P.mark()
    hTs = P.sb([128, 8, 16], BF16, "nhTs")
    m1 = P.mark()
    hT = [P.sb([128, NT], BF16, "nh%d" % i) for i in range(8)]
    m2 = P.mark()
    scratch = {"sq": [P.sb([128, 512], F32, "sq%d" % i) for i in range(2)], "rstd": P.sb([128, 512], F32, "rstd")}
    rmsnorm(k, l * 8, 0, NT, hT, 0, scratch)
    P.release(m2)
    for kt in range(8):
        P.i("pool", "tensor_copy", hTs[:, kt, :], hT[kt][:, LP:NT], reads=[hT[kt]], writes=[hTs])
    ropeT = P.sb([128, 16, 64], F32, "ropeT")
    P.dma("sp", ropeT[:], d["rope_tab"].ap()[0:16].rearrange("t p c -> p t c"), reads=[d["rope_tab"]], writes=[ropeT])
    eexp = P.sb([32, LP], BF16, "eexp")
    P.i("pool", "memset", eexp[:], 1.0, writes=[eexp])
    P.i("pool", "affine_select", out=eexp[:], in_=eexp[:], pattern=[[1, LP]], compare_op=ALU.is_ge, fill=0.0, base=0,
        channel_multiplier=-64, reads=[eexp], writes=[eexp])
    P.i("pool", "affine_select", out=eexp[:], in_=eexp[:], pattern=[[-1, LP]], compare_op=ALU.is_ge, fill=0.0, base=63,
        channel_multiplier=64, reads=[eexp], writes=[eexp])
    niota = P.sb([128, 32], F32, "niota")
    curcol = P.sb([128, 16], F32, "curcol")
    hfcol = P.sb([128, 1], F32, "hfcol")
    tiota = P.sb([32, 128], F32, "tiota")
    thr = P.sb([32, 16], F32, "thr")
    P.i("pool", "iota", niota[:], pattern=[[1, 32]], base=0, channel_multiplier=0, allow_small_or_imprecise_dtypes=True, writes=[niota])
    P.i("pool", "iota", curcol[:], pattern=[[2, 16]], base=0, channel_multiplier=0, allow_small_or_imprecise_dtypes=True, writes=[curcol])
    P.i("pool", "memset", hfcol[0:64, :], 0.0, writes=[hfcol])
    P.i("pool", "memset", hfcol[64:128, :], 1.0, writes=[hfcol])
    P.i("dve", "tensor_scalar_add", curcol[:], curcol[:], hfcol[:, 0:1], reads=[curcol, hfcol], writes=[curcol])
    P.i("pool", "iota", tiota[:], pattern=[[1, 128]], base=0, channel_multiplier=0, allow_small_or_imprecise_dtypes=True, writes=[tiota])
    P.i("pool", "iota", thr[:], pattern=[[-128, 16]], base=63, channel_multiplier=64, allow_small_or_imprecise_dtypes=True, writes=[thr])
    cm_ = P.sb([32, 128], F32, "cm_")
    cand = P.sb([128, 32], F32, "cand")
    wc = P.sb([64, 64, 2, 64], BF16, "wc")
    for lh in range(4):
        P.dma("pool", wc[:, lh * 16:(lh + 1) * 16], d["nsa_cmp_w"].ap()[j, lh * 16:(lh + 1) * 16].rearrange("l c d e -> d l c e"),
              reads=[d["nsa_cmp_w"]], writes=[wc])
    peT = P.sb([64, 2, 64], F32, "peT")
    stg64 = P.sb([128, 64], F32, "stg64")
    P.dma("sp", stg64[:], d["nsa_cmp_pe"].ap()[j].rearrange("l c d -> (l c) d"), reads=[d["nsa_cmp_pe"]], writes=[stg64])
    P.i("pe", "transpose", PS[6][0:64, 0:128], stg64[:], k.ident[:], reads=[stg64, k.ident], writes=[PS[6]])
    P.i("dve", "tensor_copy", peT[:], PS[6][0:64, 0:128].rearrange("p (l c) -> p c l", c=2), reads=[PS[6]], writes=[peT])

    wg = P.sb([128, 8, 652], BF16, "wg")
    wog = P.sb([128, 2, D], BF16, "wog")
    tm = P.sb([128, 652], F32, "tm")
    tr = P.sb([128, 384], F32, "trr")
    tmb = P.sb([128, 640], BF16, "tmb")
    trb = P.sb([128, 384], BF16, "trb")
    rt = [P.sb([128, 6, 32], F32, "rt%d" % i) for i in range(2)]
    qT2 = [P.sb([64, 8, 128], BF16, "qT%d" % i) for i in range(2)]
    KselT = [P.sb([64, 128], BF16, "KselT%d" % i) for i in range(16)]
    KwinT = [P.sb([64, 128], BF16, "KwinT%d" % i) for i in range(16)]
    XkT = P.sb([64, LP], BF16, "XkT")
    XvT = P.sb([64, LP], BF16, "XvT")
    Vsel = [P.sb([128, 66], BF16, "Vsel%d" % i) for i in range(16)]
    Vwin = [P.sb([128, 66], BF16, "Vwin%d" % i) for i in range(16)]
    for t in Vsel + Vwin:
        P.i("pool", "memset", t[:], 1.0, writes=[t])
    gt2 = [P.sb([128, 12], F32, "gt%d" % i) for i in range(2)]
    CkT = P.sb([64, 32], BF16, "CkT")
    CvA = P.sb([32, 64], BF16, "CvA")
    Ec = P.sb([32, 512], F32, "Ec")
    rs = P.sb([32, 512], F32, "rs")
    pb = P.sb([32, 512], BF16, "pb")
    oc2 = [P.sb([128, 256], F32, "oc%d" % i) for i in range(2)]
    impT = P.sb([32, 128], F32, "impT")
    impc = P.sb([128, 32], F32, "impc")
    tmp32 = P.sb([128, 32], F32, "tmp32")
    sel = P.sb([128, 32], F32, "sel")
    m8 = P.sb([128, 16], F32, "m8")
    smT2 = [P.sb([32, 128], BF16, "smT%d" % i) for i in range(2)]
    Eb = [P.sb([128, 512], BF16, "Eb%d" % i) for i in range(2)]
    Mb = [P.sb([128, 128], BF16, "Mb%d" % i) for i in range(2)]
    cf = P.sb([128, 8], F32, "ncf")
    om = P.sb([128, 256], F32, "om")
    om2 = P.sb([128, 256], F32, "om2")
    omb = P.sb([128, 256], BF16, "omb")
    omT = P.sb([128, 2, 128], BF16, "omT")
    cnt = [0]
    ck("n_setup")

    for g in range(0 if not SKIP_PROMPT[0] else 4, 4):
        srcs = [(g * 256, 256, 0), (1024 + 2 * 256 + g * 64, 64, 256), (1024 + 4 * 256 + g * 64, 64, 320),
                (1024 + 0 * 256 + g * 64, 64, 384), (1024 + 1 * 256 + g * 64, 64, 448), (1024 + 3 * 256 + g * 64, 64, 512),
                (1024 + 5 * 256 + g * 64, 64, 576), (2560 + g * 12, 12, 640)]
        for (c0, w_, o0) in srcs:
            P.dma("pool", wg[:, :, o0:o0 + w_], d["nsa_w_in"].ap()[j, :, c0:c0 + w_].rearrange("(kt p) c -> p kt c", p=128),
                  reads=[d["nsa_w_in"]], writes=[wg])
        P.dma("pool", wog[:], d["nsa_w_out"].ap()[j, g * 256:(g + 1) * 256, :].rearrange("(a p) c -> p a c", p=128),
              reads=[d["nsa_w_out"]], writes=[wog])
        for i in range(16):
            tok = slice(i * 128, (i + 1) * 128)
            ps = PS[i % 2]
            for kt in range(8):
                P.i("pe", "matmul", ps[:, 0:128], hT[kt][:, tok], wg[:, kt, 384:512], start=(kt == 0), stop=(kt == 7),
                    reads=[hT[kt], wg], writes=[ps])
            P.i("act", "copy", tm[:, 384:512], ps[:, 0:128], reads=[ps], writes=[tm])
            P.i("dve", "tensor_copy", tmb[:, 384:512], ps[:, 0:128], reads=[ps], writes=[tmb])
            for c in range(2):
                P.dma("sp", d["cmp_p"].ap()[j, tok, c * 256 + g * 64:c * 256 + (g + 1) * 64], tm[:, 384 + c * 64:448 + c * 64],
                      reads=[tm], writes=[d["cmp_p"]])
            pt = PS[2][0:64, :].bitcast(BF16)
            for c in range(2):
                P.i("pe", "transpose", pt[:, c * 128:(c + 1) * 128], tmb[:, 384 + c * 64:448 + c * 64], k.identb[:],
                    reads=[tmb, k.identb], writes=[PS[2]])
            for c, XT in enumerate((XkT, XvT)):
                P.i("dve", "tensor_tensor", XT[:, tok].rearrange("p (n l) -> p n l", l=64),
                    pt[:, c * 128:(c + 1) * 128].rearrange("p (n l) -> p n l", l=64),
                    peT[:, c, :].unsqueeze(1).to_broadcast([64, 2, 64]), ALU.add, reads=[PS[2], peT], writes=[XT])
        for ll in range(64):
            P.i("pe", "matmul", PS[3][0:64, 0:32], wc[:, ll, 0, :], XkT[:, :].rearrange("p (n l) -> p n l", l=64)[:, :, ll],
                start=(ll == 0), stop=(ll == 63), reads=[wc, XkT], writes=[PS[3]])
        P.i("act", "copy", CkT[:], PS[3][0:64, 0:32], reads=[PS[3]], writes=[CkT])
        for ll in range(64):
            P.i("pe", "matmul", PS[3][0:32, 64:128], XvT[:, :].rearrange("p (n l) -> p n l", l=64)[:, :, ll], wc[:, ll, 1, :],
                start=(ll == 0), stop=(ll == 63), reads=[wc, XvT], writes=[PS[3]])
        P.i("act", "copy", CvA[:], PS[3][0:32, 64:128], reads=[PS[3]], writes=[CvA])
        ck("n_pre%d" % g)

        def chain(i):
            slot = i % 2
            qT, gt, oc, smT = qT2[slot], gt2[slot], oc2[slot], smT2[slot]
            tok = slice(i * 128, (i + 1) * 128)
            for (c0, cn, ps) in ((0, 384, PS[0]), (512, 140, PS[1])):
                for kt in range(8):
                    P.i("pe", "matmul", ps[:, 0:cn], hT[kt][:, tok], wg[:, kt, c0:c0 + cn], start=(kt == 0), stop=(kt == 7),
                        reads=[hT[kt], wg], writes=[ps])
                P.i("act", "copy", tm[:, c0:c0 + cn], ps[:, 0:cn], reads=[ps], writes=[tm])
            yield
            rope_apply(P, V(tr[:, :].rearrange("p (h c) -> p h c", c=64), [tr]), V(tm[:, 0:384].rearrange("p (h c) -> p h c", c=64), [tm]),
                       ropeT[:, i, :], 6, 128, rt, ropeT)
            P.i("act", "copy", tmb[:, 0:256], tm[:, 0:256], reads=[tm], writes=[tmb])
            P.i("act", "copy", trb[:], tr[:], reads=[tr], writes=[trb])
            P.i("act", "activation", gt[:], tm[:, 640:652], AF.Exp, scale=-1.0, reads=[tm], writes=[gt])
            P.i("dve", "tensor_scalar_add", gt[:], gt[:], 1.0, reads=[gt], writes=[gt])
            P.i("dve", "reciprocal", gt[:], gt[:], reads=[gt], writes=[gt])
            P.i("pool", "tensor_copy", Vsel[i][:, 0:64], tm[:, 512:576], reads=[tm], writes=[Vsel[i]])
            P.i("pool", "tensor_copy", Vwin[i][:, 0:64], tm[:, 576:640], reads=[tm], writes=[Vwin[i]])
            P.dma("sp", d["sel_p"].ap()[j, tok, g * 64:(g + 1) * 64], tr[:, 256:320], reads=[tr], writes=[d["sel_p"]])
            P.dma("sp", d["sel_p"].ap()[j, tok, 256 + g * 64:256 + (g + 1) * 64], tm[:, 512:576], reads=[tm], writes=[d["sel_p"]])
            if i >= 12:
                wt = slice((i - 12) * 128, (i - 11) * 128)
                P.dma("sp", d["win_p"].ap()[j, wt, g * 64:(g + 1) * 64], tr[:, 320:384], reads=[tr], writes=[d["win_p"]])
                P.dma("sp", d["win_p"].ap()[j, wt, 256 + g * 64:256 + (g + 1) * 64], tm[:, 576:640], reads=[tm], writes=[d["win_p"]])
            yield
            pq = PS[2][0:64, :].bitcast(BF16)
            pk = PS[1][0:64, :].bitcast(BF16)[:, 512:768]
            for h in range(4):
                P.i("pe", "transpose", pq[:, h * 128:(h + 1) * 128], tmb[:, h * 64:(h + 1) * 64], k.identb[:], reads=[tmb, k.identb], writes=[PS[2]])
                P.i("pe", "transpose", pq[:, (4 + h) * 128:(5 + h) * 128], trb[:, h * 64:(h + 1) * 64], k.identb[:], reads=[trb, k.identb], writes=[PS[2]])
            P.i("pe", "transpose", pk[:, 0:128], trb[:, 256:320], k.identb[:], reads=[trb, k.identb], writes=[PS[1]])
            P.i("pe", "transpose", pk[:, 128:256], trb[:, 320:384], k.identb[:], reads=[trb, k.identb], writes=[PS[1]])
            P.i("dve", "tensor_copy", qT[:].rearrange("p a t -> p (a t)"), pq[:, 0:1024], reads=[PS[2]], writes=[qT])
            P.i("act", "copy", KselT[i][:], pk[:, 0:128], reads=[PS[1]], writes=[KselT[i]])
            P.i("act", "copy", KwinT[i][:], pk[:, 128:256], reads=[PS[1]], writes=[KwinT[i]])
            yield
            qraw = qT[:, 0:4, :]
            P.i("pe", "matmul", PS[5][0:32, :], CkT[:], qraw, start=True, stop=True, reads=[CkT, qT], writes=[PS[5]])
            P.i("act", "activation", Ec[:], PS[5][0:32, :], AF.Exp, scale=SCALE, reads=[PS[5]], writes=[Ec])
            P.i("dve", "tensor_scalar", cm_[:], tiota[:], thr[:, i:i + 1], None, ALU.is_ge, reads=[tiota, thr], writes=[cm_])
            P.i("dve", "tensor_tensor", Ec[:].rearrange("p (a t) -> p a t", a=4), Ec[:].rearrange("p (a t) -> p a t", a=4),
                cm_[:, :].unsqueeze(1).to_broadcast([32, 4, 128]), ALU.mult, reads=[Ec, cm_], writes=[Ec])
            yield
            P.i("pe", "matmul", PS[5][0:32, :], k.ones[0:32, 0:32], Ec[:], start=True, stop=True, reads=[k.ones, Ec], writes=[PS[5]])
            P.i("dve", "tensor_scalar_max", rs[:], PS[5][0:32, :], 1e-30, reads=[PS[5]], writes=[rs])
            P.i("dve", "reciprocal", rs[:], rs[:], reads=[rs], writes=[rs])
            P.i("dve", "tensor_tensor", Ec[:], Ec[:], rs[:], ALU.mult, reads=[Ec, rs], writes=[Ec])
            P.i("act", "copy", pb[:], Ec[:], reads=[Ec], writes=[pb])
            yield
            for h in range(4):
                P.i("pe", "matmul", PS[5][:, h * 64:(h + 1) * 64], pb[:, h * 128:(h + 1) * 128], CvA[:], start=True, stop=True,
                    reads=[pb, CvA], writes=[PS[5]])
            P.i("act", "copy", oc[:], PS[5][:, 0:256], reads=[PS[5]], writes=[oc])
            P.i("dve", "tensor_reduce", impT[:], Ec[:].rearrange("p (a t) -> p t a", a=4), AX.X, ALU.add, reads=[Ec], writes=[impT])
            yield
            P.i("pe", "transpose", PS[5][:, 256:288], impT[:], k.ident[0:32, 0:32], reads=[impT, k.ident], writes=[PS[5]])
            P.i("dve", "tensor_scalar", cand[:], niota[:], curcol[:, i:i + 1], None, ALU.is_lt, reads=[niota, curcol], writes=[cand])
            P.i("dve", "tensor_scalar_add", impc[:], PS[5][:, 256:288], 1.0, reads=[PS[5]], writes=[impc])
            P.i("dve", "tensor_tensor", impc[:], impc[:], cand[:], ALU.mult, reads=[impc, cand], writes=[impc])
            P.i("dve", "tensor_scalar_add", impc[:], impc[:], -1.0, reads=[impc], writes=[impc])
            P.i("dve", "max", m8[:, 0:8], impc[:], reads=[impc], writes=[m8])
            P.i("dve", "match_replace", tmp32[:], m8[:, 0:8], impc[:], -2.0, reads=[m8, impc], writes=[tmp32])
            P.i("dve", "max", m8[:, 8:16], tmp32[:], reads=[tmp32], writes=[m8])
            yield
            P.i("dve", "tensor_scalar", sel[:], impc[:], m8[:, 14:15], None, ALU.is_ge, reads=[impc, m8], writes=[sel])
            P.i("dve", "tensor_single_scalar", tmp32[:], impc[:], -0.5, ALU.is_gt, reads=[impc], writes=[tmp32])
            P.i("dve", "tensor_tensor", sel[:], sel[:], tmp32[:], ALU.mult, reads=[sel, tmp32], writes=[sel])
            P.i("dve", "tensor_scalar", cand[:], niota[:], curcol[:, i:i + 1], None, ALU.is_equal, reads=[niota, curcol], writes=[cand])
            P.i("dve", "tensor_tensor", sel[:], sel[:], cand[:], ALU.max, reads=[sel, cand], writes=[sel])
            P.i("pe", "transpose", PS[5][0:32, 384:512], sel[:], k.ident[:], reads=[sel, k.ident], writes=[PS[5]])
            P.i("act", "copy", smT[:], PS[5][0:32, 384:512], reads=[PS[5]], writes=[smT])
            yield

        def attn(i):
            slot = i % 2
            qT, gt, oc, smT = qT2[slot], gt2[slot], oc2[slot], smT2[slot]
            tok = slice(i * 128, (i + 1) * 128)
            qrot = qT[:, 4:8, :]
            for br in range(2):
                KT, Vv = (KselT, Vsel) if br == 0 else (KwinT, Vwin)
                acc = PS[6 + br]
                j0 = 0 if br == 0 else max(0, i - 4)
                for jt in range(j0, i + 1):
                    kk = slice(jt * 128, (jt + 1) * 128)
                    E = Eb[cnt[0] % 2]
                    M = Mb[cnt[0] % 2]
                    cnt[0] += 1
                    msk = None
                    if br == 0:
                        P.i("pe", "matmul", PS[4][:, 0:128], eexp[:, kk], smT[:], start=True, stop=True, reads=[eexp, smT], writes=[PS[4]])
                        if jt == i:
                            P.i("dve", "tensor_tensor", M[:], PS[4][:, 0:128], k.caus[:], ALU.mult, reads=[PS[4], k.caus], writes=[M])
                        else:
                            P.i("dve", "tensor_copy", M[:], PS[4][:, 0:128], reads=[PS[4]], writes=[M])
                        msk = M
                    elif jt == i:
                        msk = k.caus
                    elif jt == i - 4:
                        msk = k.wmask
                    P.i("pe", "matmul", PS[3][:, :], KT[jt][:], qrot, start=True, stop=True, reads=[KT[jt], qT], writes=[PS[3]])
                    P.i("act", "activation", E[:], PS[3][:, :], AF.Exp, scale=SCALE, reads=[PS[3]], writes=[E])
                    if msk is not None:
                        P.i("dve", "tensor_tensor", E[:].rearrange("p (a t) -> p a t", a=4), E[:].rearrange("p (a t) -> p a t", a=4),
                            msk[:, :].unsqueeze(1).to_broadcast([128, 4, 128]), ALU.mult, reads=[E, msk], writes=[E])
                    for h in range(4):
                        P.i("pe", "matmul", acc[:, h * 65:(h + 1) * 65], E[:, h * 128:(h + 1) * 128], Vv[jt][:, 0:65],
                            start=(jt == j0 and h == 0), stop=(jt == i), skip_group_check=True, reads=[E, Vv[jt]], writes=[acc])
                    yield
            g3 = gt[:, :].rearrange("p (h c) -> p h c", c=3)
            a3 = lambda ps_: ps_[:, 0:260].rearrange("p (h c) -> p h c", c=65)
            P.i("dve", "reciprocal", cf[:, 0:4], a3(PS[6])[:, :, 64], reads=[PS[6]], writes=[cf])
            P.i("dve", "reciprocal", cf[:, 4:8], a3(PS[7])[:, :, 64], reads=[PS[7]], writes=[cf])
            P.i("dve", "tensor_tensor", cf[:, 0:4], cf[:, 0:4], g3[:, :, 1], ALU.mult, reads=[cf, gt], writes=[cf])
            P.i("dve", "tensor_tensor", cf[:, 4:8], cf[:, 4:8], g3[:, :, 2], ALU.mult, reads=[cf, gt], writes=[cf])
            o3 = lambda t: t[:, :].rearrange("p (h c) -> p h c", c=64)
            bc = lambda ap: ap.unsqueeze(2).to_broadcast([128, 4, 64])
            P.i("dve", "tensor_tensor", o3(om), o3(oc), bc(g3[:, :, 0]), ALU.mult, reads=[oc, gt], writes=[om])
            P.i("dve", "tensor_tensor", o3(om2), a3(PS[6])[:, :, 0:64], bc(cf[:, 0:4]), ALU.mult, reads=[PS[6], cf], writes=[om2])
            P.i("pool", "tensor_tensor", om[:], om[:], om2[:], ALU.add, reads=[om, om2], writes=[om])
            P.i("dve", "tensor_tensor", o3(om2), a3(PS[7])[:, :, 0:64], bc(cf[:, 4:8]), ALU.mult, reads=[PS[7], cf], writes=[om2])
            P.i("pool", "tensor_tensor", omb[:], om[:], om2[:], ALU.add, reads=[om, om2], writes=[omb])
            yield
            po = PS[4][:, :].bitcast(BF16)[:, 256:512]
            for a in range(2):
                P.i("pe", "transpose", po[:, a * 128:(a + 1) * 128], omb[:, a * 128:(a + 1) * 128], k.identb[:], reads=[omb, k.identb], writes=[PS[4]])
            P.i("act", "copy", omT[:].rearrange("p a t -> p (a t)"), po[:, 0:256], reads=[PS[4]], writes=[omT])
            yield
            for hb in range(2):
                ps = PS[6 + hb]
                for q in range(4):
                    dt = hb * 4 + q
                    for a in range(2):
                        P.i("pe", "matmul", ps[:, q * 128:(q + 1) * 128], wog[:, a, dt * 128:(dt + 1) * 128], omT[:, a, :],
                            start=(a == 0), stop=(a == 1), reads=[wog, omT], writes=[ps])
                for q in range(4):
                    dt = hb * 4 + q
                    P.i("dve", "tensor_tensor", k.R[dt][:, tok], ps[:, q * 128:(q + 1) * 128], k.R[dt][:, tok], ALU.add,
                        reads=[ps, k.R[dt]], writes=[k.R[dt]])
                yield
            ck("n_main%d_%d" % (g, i))

        run_il([chain(0)])
        for i in range(16):
            run_il([attn(i), chain(i + 1) if i < 15 else None])
        ck("n_main%d" % g)
    ck("n_prompt")
    P.release(m1)
    nsa_sample(k, l, hTs)
    P.release(m0)

def nsa_sample(k, l, hTs):
    P = k.P
    d = k.d
    j = l // 2
    PS = k.PS
    m = P.mark()
    SQ2 = P.sb([128, 4, 4, 8, 4], BF16, "SQ2")
    SKs = P.sb([64, 4, 4, 4], BF16, "SKs")
    SKw = P.sb([64, 4, 4, 4], BF16, "SKw")
    SVs = P.sb([4, 4, 4, 65], BF16, "SVs")
    SVw = P.sb([4, 4, 4, 65], BF16, "SVw")
    SG = P.sb([4, 4, 4, 12], F32, "SG")
    oTs = P.sb([64, 4, 16, 4], BF16, "oTs")
    ropeS = P.sb([4, 64], F32, "ropeS")
    P.dma("sp", ropeS[:], d["rope_tab"].ap()[16, 0:4, :], reads=[d["rope_tab"]], writes=[ropeS])
    P.i("pool", "memset", SVs[:], 1.0, writes=[SVs])
    P.i("pool", "memset", SVw[:], 1.0, writes=[SVw])
    i4 = P.sb([4, 4], F32, "i4")
    P.i("dve", "tensor_copy", i4[:], k.ident[0:4, 0:4], reads=[k.ident], writes=[i4])
    pti = P.sb([128, NS * NPG], I32, "pti")
    ptf = P.sb([128, NS * NPG], F32, "ptf")
    iot = P.sb([128, 1], F32, "iot")
    idx = P.sb([128, NS * NPG], I32, "idx")
    P.dma("sp", pti[:], d["page_table"].ap().rearrange("s g -> (s g)").partition_broadcast(128), reads=[d["page_table"]], writes=[pti])
    P.i("dve", "tensor_copy", ptf[:], pti[:], reads=[pti], writes=[ptf])
    P.i("pool", "iota", iot[:], pattern=[[0, 1]], base=j * k.n_phys * 128, channel_multiplier=1, allow_small_or_imprecise_dtypes=True, writes=[iot])
    P.i("dve", "tensor_scalar", ptf[:], ptf[:], 128.0, iot[:, 0:1], ALU.mult, ALU.add, reads=[ptf, iot], writes=[ptf])
    P.i("dve", "tensor_copy", idx[:], ptf[:], reads=[ptf], writes=[idx])
    ck("s_setup")

    m0 = P.mark()
    wg = P.sb([128, 8, 652], BF16, "swg")
    tmS = P.sb([4, 652], F32, "tmS")
    trS = P.sb([4, 384], F32, "trS")
    qd = P.sb([4, 8, 2, 64], BF16, "qd")
    kb2 = P.sb([4, 128], BF16, "kb2")
    rt = [P.sb([4, 6, 32], F32, "srt%d" % i) for i in range(2)]
    for g in range(4):
        srcs = [(g * 256, 256, 0), (1024 + 2 * 256 + g * 64, 64, 256), (1024 + 4 * 256 + g * 64, 64, 320),
                (1024 + 0 * 256 + g * 64, 64, 384), (1024 + 1 * 256 + g * 64, 64, 448), (1024 + 3 * 256 + g * 64, 64, 512),
                (1024 + 5 * 256 + g * 64, 64, 576), (2560 + g * 12, 12, 640)]
        for (c0, w_, o0) in srcs:
            P.dma("pool", wg[:, :, o0:o0 + w_], d["nsa_w_in"].ap()[j, :, c0:c0 + w_].rearrange("(kt p) c -> p kt c", p=128),
                  reads=[d["nsa_w_in"]], writes=[wg])
        for s_ in range(4):
            rows = slice(s_ * 4, (s_ + 1) * 4)
            for (c0, cn, ps) in ((0, 512, PS[0]), (512, 140, PS[1])):
                for kt in range(8):
                    P.i("pe", "matmul", ps[0:4, 0:cn], hTs[:, kt, rows], wg[:, kt, c0:c0 + cn], start=(kt == 0), stop=(kt == 7),
                        reads=[hTs, wg], writes=[ps])
                P.i("act", "copy", tmS[:, c0:c0 + cn], ps[0:4, 0:cn], reads=[ps], writes=[tmS])
            rope_apply(P, V(trS[:, :].rearrange("p (h c) -> p h c", c=64), [trS]), V(tmS[:, 0:384].rearrange("p (h c) -> p h c", c=64), [tmS]),
                       ropeS[:, :], 6, 4, rt, ropeS)
            for c in range(2):
                P.dma("sp", d["cmp_s"].ap()[j, rows, c * 256 + g * 64:c * 256 + (g + 1) * 64], tmS[:, 384 + c * 64:448 + c * 64],
                      reads=[tmS], writes=[d["cmp_s"]])
            P.dma("sp", d["sel_s"].ap()[j, rows, g * 64:(g + 1) * 64], trS[:, 256:320], reads=[trS], writes=[d["sel_s"]])
            P.dma("sp", d["sel_s"].ap()[j, rows, 256 + g * 64:256 + (g + 1) * 64], tmS[:, 512:576], reads=[tmS], writes=[d["sel_s"]])
            P.dma("sp", d["win_s"].ap()[j, s_, 508:512, g * 64:(g + 1) * 64], trS[:, 320:384], reads=[trS], writes=[d["win_s"]])
            P.dma("sp", d["win_s"].ap()[j, s_, 508:512, 256 + g * 64:256 + (g + 1) * 64], tmS[:, 576:640], reads=[tmS], writes=[d["win_s"]])
            for r in range(2):
                P.i("dve", "tensor_copy", qd[:, 0:4, r, :], tmS[:, 0:256].rearrange("p (h c) -> p h c", c=64), reads=[tmS], writes=[qd])
                P.i("dve", "tensor_copy", qd[:, 4:8, r, :], trS[:, 0:256].rearrange("p (h c) -> p h c", c=64), reads=[trS], writes=[qd])
            P.i("dve", "tensor_copy", kb2[:], trS[:, 256:384], reads=[trS], writes=[kb2])
            P.i("act", "copy", SVs[:, s_, g, 0:64], tmS[:, 512:576], reads=[tmS], writes=[SVs])
            P.i("act", "copy", SVw[:, s_, g, 0:64], tmS[:, 576:640], reads=[tmS], writes=[SVw])
            P.i("act", "activation", SG[:, s_, g, :], tmS[:, 640:652], AF.Sigmoid, reads=[tmS], writes=[SG])
            pq = PS[2][:, :].bitcast(BF16)
            for h in range(8):
                P.i("pe", "transpose", pq[:, h * 4:(h + 1) * 4], qd[:, h, :, :].rearrange("p r c -> p (r c)"), k.identb[0:4, 0:4],
                    reads=[qd, k.identb], writes=[PS[2]])
            P.i("dve", "tensor_copy", SQ2[:, s_, g, :, :].rearrange("p h t -> p (h t)"), pq[:, 0:32], reads=[PS[2]], writes=[SQ2])
            pk = PS[3][0:64, :].bitcast(BF16)
            P.i("pe", "transpose", pk[:, 0:4], kb2[:, 0:64], k.identb[0:4, 0:4], reads=[kb2, k.identb], writes=[PS[3]])
            P.i("pe", "transpose", pk[:, 4:8], kb2[:, 64:128], k.identb[0:4, 0:4], reads=[kb2, k.identb], writes=[PS[3]])
            P.i("act", "copy", SKs[:, s_, g, :], pk[:, 0:4], reads=[PS[3]], writes=[SKs])
            P.i("act", "copy", SKw[:, s_, g, :], pk[:, 4:8], reads=[PS[3]], writes=[SKw])
    P.release(m0)
    ck("s_S0")

    m1 = P.mark()
    wc2 = P.sb([128, 64, 2, 64], BF16, "wc2")
    for hf in range(2):
        for lh in range(4):
            P.dma("pool", wc2[hf * 64:(hf + 1) * 64, lh * 16:(lh + 1) * 16], d["nsa_cmp_w"].ap()[j, lh * 16:(lh + 1) * 16].rearrange("l c d e -> d l c e"),
                  reads=[d["nsa_cmp_w"]], writes=[wc2])
    pe4 = P.sb([128, 4, 64], F32, "pe4")
    stg64 = P.sb([128, 128], F32, "sstg")
    for r in range(2):
        P.dma("sp", stg64[:, r * 64:(r + 1) * 64], d["nsa_cmp_pe"].ap()[j].rearrange("l c d -> (l c) d"), reads=[d["nsa_cmp_pe"]], writes=[stg64])
    P.i("pe", "transpose", PS[6][:, 0:128], stg64[:], k.ident[:], reads=[stg64, k.ident], writes=[PS[6]])
    for b in range(4):
        P.i("dve", "tensor_copy", pe4[:, b, :], PS[6][:, 0:128].rearrange("p (l c) -> p c l", c=2)[:, b // 2, :], reads=[PS[6]], writes=[pe4])
    XT = P.sb([128, 4, NPG * 128], BF16, "XT")
    pgf = [P.sb([128, 512], F32, "pgf%d" % i) for i in range(2)]
    pgb = [P.sb([128, 512], BF16, "pgb%d" % i) for i in range(2)]
    pgfA = [P.sb([128, 512], F32, "pgfA%d" % i) for i in range(2)]
    pgbA = [P.sb([128, 512], BF16, "pgbA%d" % i) for i in range(2)]
    CkS = P.sb([64, 4, 128], BF16, "CkS")
    CvS = P.sb([128, 4, 64], BF16, "CvS")
    Ecs = P.sb([128, 64], F32, "Ecs")
    rss = P.sb([128, 64], F32, "rss")
    pbs = P.sb([128, 64], BF16, "pbs")
    impTs = P.sb([128, 16], F32, "impTs")
    imp16 = P.sb([16, 128], F32, "imp16")
    tmp16 = P.sb([16, 128], F32, "tmp16")
    sel16 = P.sb([16, 128], BF16, "sel16")
    m8s = P.sb([16, 16], F32, "m8s")
    selX = [P.sb([16, 128], BF16, "selX%d" % i) for i in range(2)]
    Mbs = [P.sb([128, 16], BF16, "Mbs%d" % i) for i in range(2)]
    KT4 = [P.sb([64, 4, 128], BF16, "KT4%d" % i) for i in range(2)]
    onesb = P.sb([128, 64], BF16, "onesb")
    P.i("pool", "memset", onesb[:], 1.0, writes=[onesb])
    Es = [P.sb([128, 64], BF16, "Es%d" % i) for i in range(2)]
    occ = P.sb([64, 64], F32, "occ")
    bcs = P.sb([64, 64], F32, "bcs")
    gd = P.sb([4, 16, 4], F32, "gd")
    om_ = P.sb([64, 64], F32, "som")
    om2_ = P.sb([64, 64], F32, "som2")
    cnt = [0]

    def gather(cache, s_, pg, buf):
        c = s_ * NPG + pg
        P.dma("pool", buf[:], d[cache].ap().rearrange("l r c -> (l r) c"), indirect=idx[:, c:c + 1].bitcast(U32),
              reads=[d[cache], idx], writes=[buf])

    def attend_tile(kt4, pb_, mask_fn, first, qsl):
        E = Es[cnt[0] % 2]
        cnt[0] += 1
        for g in range(4):
            P.i("pe", "matmul", PS[5][:, g * 16:(g + 1) * 16], kt4[:, g, :], SQ2[0:64, qsl[0], g, 4:8, :], start=True, stop=True,
                reads=[kt4, SQ2], writes=[PS[5]])
        P.i("act", "activation", E[:], PS[5][:, 0:64], AF.Exp, scale=SCALE, reads=[PS[5]], writes=[E])
        if mask_fn is not None:
            mask_fn(E)
        for g in range(4):
            P.i("pe", "matmul", PS[7][0:64, g * 16:(g + 1) * 16], pb_[:, 256 + g * 64:256 + (g + 1) * 64], E[:, g * 16:(g + 1) * 16],
                start=(first and g == 0), stop=False, skip_group_check=True, reads=[pb_, E], writes=[PS[7]])
        P.i("pe", "matmul", PS[6][0:64, 0:64], onesb[:, :], E[:, :], start=first, stop=False, skip_group_check=True,
            reads=[onesb, E], writes=[PS[6]])

    def finish_branch(dst):
        P.i("dve", "reciprocal", bcs[:], PS[6][0:64, 0:64], reads=[PS[6]], writes=[bcs])
        P.i("dve", "tensor_tensor", dst[:], PS[7][0:64, 0:64], bcs[:], ALU.mult, reads=[PS[7], bcs], writes=[dst])

    def gate_bc(s_, which):
        P.i("dve", "tensor_tensor", gd[:], SG[:, s_, :, :].rearrange("p g (h c) -> p (g h) c", c=3)[:, :, which].unsqueeze(2).to_broadcast([4, 16, 4]),
            i4[:, :].unsqueeze(1).to_broadcast([4, 16, 4]), ALU.mult, reads=[SG, i4], writes=[gd])
        P.i("pe", "matmul", PS[5][0:64, 128:192], k.ones[0:4, 0:64], gd[:].rearrange("p a t -> p (a t)"), start=True, stop=True,
            reads=[k.ones, gd], writes=[PS[5]])

    def stage_a(s_):
        for pg in range(NPG):
            pf, pb_ = pgfA[pg % 2], pgbA[pg % 2]
            psa = PS[2 + pg % 2]
            gather("cache_cmp", s_, pg, pf)
            if pg % 2 == 0:
                P.i("act", "copy", pb_[:], pf[:], reads=[pf], writes=[pb_])
            else:
                P.i("dve", "tensor_copy", pb_[:], pf[:], reads=[pf], writes=[pb_])
            pt = psa[:, :].bitcast(BF16)
            for b in range(4):
                P.i("pe", "transpose", pt[:, b * 128:(b + 1) * 128], pb_[:, b * 128:(b + 1) * 128], k.identb[:], reads=[pb_, k.identb], writes=[psa])
            P.i("dve", "tensor_tensor", XT[:, :, pg * 128:(pg + 1) * 128].rearrange("p b (n l) -> p b n l", l=64),
                pt[:, 0:512].rearrange("p (b n l) -> p b n l", b=4, l=64), pe4[:, :, :].unsqueeze(2).to_broadcast([128, 4, 2, 64]), ALU.add,
                reads=[psa, pe4], writes=[XT])
            yield

    def stage_rest(s_):
        qs = (s_,)
        ck("s_a%d" % s_)
        for g in range(4):
            gp, gl = g // 2, g % 2
            pr = slice(gl * 64, (gl + 1) * 64)
            xk = XT[pr, 0 * 2 + gp, :].rearrange("p (n l) -> p n l", l=64)
            xv = XT[pr, 1 * 2 + gp, :].rearrange("p (n l) -> p n l", l=64)
            for ll in range(64):
                P.i("pe", "matmul", PS[2][0:64, 0:128], wc2[pr, ll, 0, :], xk[:, :, ll], start=(ll == 0), stop=(ll == 63), reads=[wc2, XT], writes=[PS[2]])
            P.i("act", "copy", CkS[:, g, :], PS[2][0:64, 0:128], reads=[PS[2]], writes=[CkS])
            for ll in range(64):
                P.i("pe", "matmul", PS[3][:, 0:64], xv[:, :, ll], wc2[pr, ll, 1, :], start=(ll == 0), stop=(ll == 63), reads=[wc2, XT], writes=[PS[3]])
            P.i("act", "copy", CvS[:, g, :], PS[3][:, 0:64], reads=[PS[3]], writes=[CvS])
        ck("s_b%d" % s_)
        for g in range(4):
            P.i("pe", "matmul", PS[5][:, g * 16:(g + 1) * 16], CkS[:, g, :], SQ2[0:64, s_, g, 0:4, :], start=True, stop=True,
                reads=[CkS, SQ2], writes=[PS[5]])
        P.i("act", "activation", Ecs[:], PS[5][:, 0:64], AF.Exp, scale=SCALE, reads=[PS[5]], writes=[Ecs])
        P.i("pe", "matmul", PS[5][:, 0:64], k.ones[:], Ecs[:], start=True, stop=True, reads=[k.ones, Ecs], writes=[PS[5]])
        P.i("dve", "reciprocal", rss[:], PS[5][:, 0:64], reads=[PS[5]], writes=[rss])
        P.i("dve", "tensor_tensor", Ecs[:], Ecs[:], rss[:], ALU.mult, reads=[Ecs, rss], writes=[Ecs])
        P.i("act", "copy", pbs[:], Ecs[:], reads=[Ecs], writes=[pbs])
        for g in range(4):
            P.i("pe", "matmul", PS[6][0:64, g * 16:(g + 1) * 16], CvS[:, g, :], pbs[:, g * 16:(g + 1) * 16], start=True, stop=True,
                reads=[CvS, pbs], writes=[PS[6]])
        P.i("act", "copy", occ[:], PS[6][0:64, 0:64], reads=[PS[6]], writes=[occ])
        P.i("dve", "tensor_reduce", impTs[:].rearrange("p (g t) -> p g t", g=4), Ecs[:].rearrange("p (g a t) -> p g t a", g=4, a=4), AX.X, ALU.add,
            reads=[Ecs], writes=[impTs])
        P.i("pe", "transpose", PS[4][0:16, 0:128], impTs[:], k.ident[:], reads=[impTs, k.ident], writes=[PS[4]])
        P.i("dve", "tensor_copy", imp16[:], PS[4][0:16, 0:128], reads=[PS[4]], writes=[imp16])
        P.i("dve", "max", m8s[:, 0:8], imp16[:], reads=[imp16], writes=[m8s])
        P.i("dve", "match_replace", tmp16[:], m8s[:, 0:8], imp16[:], -2.0, reads=[m8s, imp16], writes=[tmp16])
        P.i("dve", "max", m8s[:, 8:16], tmp16[:], reads=[tmp16], writes=[m8s])
        P.i("dve", "tensor_scalar", sel16[:], imp16[:], m8s[:, 14:15], None, ALU.is_ge, reads=[imp16, m8s], writes=[sel16])
        ck("s_c%d" % s_)
        for pg in range(NPG):
            pf, pb_ = pgf[pg % 2], pgb[pg % 2]
            gather("cache_sel", s_, pg, pf)
            if pg % 2 == 0:
                P.i("act", "copy", pb_[:], pf[:], reads=[pf], writes=[pb_])
            else:
                P.i("dve", "tensor_copy", pb_[:], pf[:], reads=[pf], writes=[pb_])
            kt4, sx, mb = KT4[pg % 2], selX[pg % 2], Mbs[pg % 2]
            pt = PS[pg % 2][0:64, :].bitcast(BF16)
            for g in range(4):
                P.i("pe", "transpose", pt[:, g * 128:(g + 1) * 128], pb_[:, g * 64:(g + 1) * 64], k.identb[:], reads=[pb_, k.identb], writes=[PS[pg % 2]])
            P.i("act", "copy", kt4[:].rearrange("p a t -> p (a t)"), pt[:, 0:512], reads=[PS[pg % 2]], writes=[kt4])
            P.i("dve", "tensor_copy", sx[:].rearrange("p (n l) -> p n l", l=64), sel16[:, 2 * pg:2 * pg + 2].unsqueeze(2).to_broadcast([16, 2, 64]),
                reads=[sel16], writes=[sx])
            P.i("pe", "matmul", PS[4][:, 128:144], sx[:], k.identb[0:16, 0:16], start=True, stop=True, reads=[sx, k.identb], writes=[PS[4]])
            P.i("dve", "tensor_copy", mb[:], PS[4][:, 128:144], reads=[PS[4]], writes=[mb])

            def mfn(E, mb=mb):
                P.i("dve", "tensor_tensor", E[:].rearrange("p (g a t) -> p g a t", g=4, a=4), E[:].rearrange("p (g a t) -> p g a t", g=4, a=4),
                    mb[:].rearrange("p (g t) -> p g t", g=4).unsqueeze(2).to_broadcast([128, 4, 4, 4]), ALU.mult, reads=[E, mb], writes=[E])
            attend_tile(kt4, pb_, mfn, pg == 0, qs)
            yield
        ck("s_dpre%d" % s_)
        new_tile(k, P, PS, SKs, SVs, SQ2, s_, Es, cnt, onesb)
        ck("s_dnew%d" % s_)
        finish_branch(om2_)
        ck("s_dfin%d" % s_)
        gate_bc(s_, 1)
        P.i("dve", "tensor_tensor", om_[:], om2_[:], PS[5][0:64, 128:192], ALU.mult, reads=[om2_, PS[5]], writes=[om_])
        gate_bc(s_, 0)
        P.i("dve", "tensor_tensor", om2_[:], occ[:], PS[5][0:64, 128:192], ALU.mult, reads=[occ, PS[5]], writes=[om2_])
        P.i("pool", "tensor_tensor", om_[:], om_[:], om2_[:], ALU.add, reads=[om_, om2_], writes=[om_])
        ck("s_d%d" % s_)
        for a in range(4):
            pf, pb_ = pgf[a % 2], pgb[a % 2]
            P.dma("sp", pf[:], d["state_win"].ap()[j, s_, a * 128:(a + 1) * 128, :], reads=[d["state_win"]], writes=[pf])
            if a == 0:
                P.dma("sp", d["win_s"].ap()[j, s_, 0:124, :], pf[4:128, :], reads=[pf], writes=[d["win_s"]])
            else:
                P.dma("sp", d["win_s"].ap()[j, s_, a * 128 - 4:a * 128 + 124, :], pf[:, :], reads=[pf], writes=[d["win_s"]])
            P.i("act", "copy", pb_[:], pf[:], reads=[pf], writes=[pb_])
            kt4 = KT4[a % 2]
            pt = PS[a % 2][0:64, :].bitcast(BF16)
            for g in range(4):
                P.i("pe", "transpose", pt[:, g * 128:(g + 1) * 128], pb_[:, g * 64:(g + 1) * 64], k.identb[:], reads=[pb_, k.identb], writes=[PS[a % 2]])
            P.i("act", "copy", kt4[:].rearrange("p a t -> p (a t)"), pt[:, 0:512], reads=[PS[a % 2]], writes=[kt4])
            mfn = None
            if a == 0:
                def mfn(E):
                    P.i("dve", "tensor_tensor", E[:].rearrange("p (a t) -> p a t", t=4), E[:].rearrange("p (a t) -> p a t", t=4),
                        k.wmask[:, 0:4].unsqueeze(1).to_broadcast([128, 16, 4]), ALU.mult, reads=[E, k.wmask], writes=[E])
            attend_tile(kt4, pb_, mfn, a == 0, qs)
            yield
        new_tile(k, P, PS, SKw, SVw, SQ2, s_, Es, cnt, onesb)
        finish_branch(om2_)
        gate_bc(s_, 2)
        P.i("dve", "tensor_tensor", om2_[:], om2_[:], PS[5][0:64, 128:192], ALU.mult, reads=[om2_, PS[5]], writes=[om2_])
        P.i("pool", "tensor_tensor", oTs[:, s_, :, :].rearrange("p a t -> p (a t)"), om_[:], om2_[:], ALU.add, reads=[om_, om2_], writes=[oTs])
        ck("s_e%d" % s_)
        yield

    run_il([stage_a(0)])
    for s_ in range(4):
        r_ = stage_rest(s_)
        next(r_)
        run_il([r_, stage_a(s_ + 1) if s_ < 3 else None])
    P.release(m1)

    m2 = P.mark()
    wo = P.sb([64, 16, D], BF16, "swo")
    for q4 in range(4):
        P.dma("pool", wo[:, q4 * 4:(q4 + 1) * 4, :], d["nsa_w_out"].ap()[j, q4 * 256:(q4 + 1) * 256, :].rearrange("(h p) c -> p h c", p=64),
              reads=[d["nsa_w_out"]], writes=[wo])
    for dt in range(8):
        ps = PS[dt % 2]
        for h in range(16):
            P.i("pe", "matmul", ps[:, 0:16], wo[:, h, dt * 128:(dt + 1) * 128], oTs[:, :, h, :], start=(h == 0), stop=(h == 15),
                reads=[wo, oTs], writes=[ps])
        P.i("dve", "tensor_tensor", k.R[dt][:, LP:NT], ps[:, 0:16], k.R[dt][:, LP:NT], ALU.add, reads=[ps, k.R[dt]], writes=[k.R[dt]])
    P.release(m2)
    P.release(m)


def new_tile(k, P, PS, SK, SV, SQ2, s_, Es, cnt, onesb):
    E = Es[cnt[0] % 2]
    cnt[0] += 1
    for g in range(4):
        P.i("pe", "matmul", PS[5][0:4, g * 16:(g + 1) * 16], SK[:, s_, g, :], SQ2[0:64, s_, g, 4:8, :], start=True, stop=True,
            reads=[SK, SQ2], writes=[PS[5]])
    P.i("act", "activation", E[0:4, :], PS[5][0:4, 0:64], AF.Exp, scale=SCALE, reads=[PS[5]], writes=[E])
    P.i("dve", "tensor_tensor", E[0:4, :].rearrange("p (a t) -> p a t", t=4), E[0:4, :].rearrange("p (a t) -> p a t", t=4),
        k.caus[0:4, 0:4].unsqueeze(1).to_broadcast([4, 16, 4]), ALU.mult, reads=[E, k.caus], writes=[E])
    for g in range(4):
        P.i("pe", "matmul", PS[7][0:64, g * 16:(g + 1) * 16], SV[:, s_, g, 0:64], E[0:4, g * 16:(g + 1) * 16],
            start=False, stop=(g == 3), skip_group_check=True, reads=[SV, E], writes=[PS[7]])
    P.i("pe", "matmul", PS[6][0:64, 0:64], onesb[0:4, :], E[0:4, :], start=False, stop=True, skip_group_check=True,
        reads=[onesb, E], writes=[PS[6]])


_OUT_ORDER = ["y_p", "y_s", "gdn_p", "gdn_s", "gconv_p", "gconv_s", "cmp_p", "cmp_s", "sel_p", "sel_s",
              "win_p", "win_s", "ffn_p", "ffn_s"]


def _rope_table():
    inv = (np.float32(10000.0) ** (-np.arange(32, dtype=np.float32) / np.float32(32))).astype(np.float32)
    pos = np.zeros((17, 128), np.float32)
    pos[:16] = np.arange(LP, dtype=np.float32).reshape(16, 128)
    pos[16] = 8192 + (np.arange(128) % 4)
    ang = (pos[:, :, None] * inv[None, None, :]).astype(np.float32)
    return np.concatenate([np.cos(ang), np.sin(ang)], axis=-1).astype(np.float32)


_ROPE_TAB = _rope_table()


def make_in_maps(inp, compact=False):
    maps = []
    nphys = inp["cache_cmp"].shape[1]
    cc = np.ascontiguousarray(inp["cache_cmp"]).reshape(2, nphys * 128, 512)
    cs = np.ascontiguousarray(inp["cache_sel"]).reshape(2, nphys * 128, 512)
    for c in range(8):
        s0, s1 = c * NS, (c + 1) * NS
        m = {}
        m["xp"] = np.ascontiguousarray(inp["x_prompt"][c])
        m["xs"] = np.ascontiguousarray(inp["x_sample"][s0:s1]).reshape(NS * DS, D)
        m["state_gdn"] = np.ascontiguousarray(inp["state_gdn"][:, s0:s1])
        m["state_gdn_conv"] = np.ascontiguousarray(inp["state_gdn_conv"][:, s0:s1])
        m["state_win"] = np.ascontiguousarray(inp["state_win"][:, s0:s1]).reshape(2, NS, 512, 512)
        m["state_ffn_conv"] = np.ascontiguousarray(inp["state_ffn_conv"][:, s0:s1])
        pt = np.ascontiguousarray(inp["page_table"][s0:s1]).astype(np.int32)
        if compact:
            flat = pt.reshape(-1)
            m["cache_cmp"] = np.ascontiguousarray(inp["cache_cmp"][:, flat]).reshape(2, flat.size * 128, 512)
            m["cache_sel"] = np.ascontiguousarray(inp["cache_sel"][:, flat]).reshape(2, flat.size * 128, 512)
            pt = np.arange(flat.size, dtype=np.int32).reshape(NS, NPG)
        else:
            m["cache_cmp"] = cc
            m["cache_sel"] = cs
        m["page_table"] = pt
        m["rope_tab"] = _ROPE_TAB
        for nm in ["norm_mix", "norm_ffn", "norm_final", "gdn_w_in", "gdn_conv_w", "gdn_a_log", "gdn_dt_bias", "gdn_norm_w",
                   "gdn_w_out", "nsa_w_in", "nsa_cmp_pe", "nsa_cmp_w", "nsa_w_out", "ffn_w_up", "ffn_conv_w", "ffn_conv_b", "ffn_w_down"]:
            m[nm] = np.ascontiguousarray(inp[nm])
        maps.append(m)
    return maps


def assemble(results):
    def cat(nm, axis):
        return np.concatenate([r[nm] for r in results], axis=axis)

    def stack(nm, axis):
        return np.stack([r[nm] for r in results], axis=axis)
    y_p = stack("y_p", 0)
    y_s = cat("y_s", 0).reshape(32, DS, D)
    gdn_p = stack("gdn_p", 1)
    gdn_s = cat("gdn_s", 1)
    gconv_p = stack("gconv_p", 1)
    gconv_s = cat("gconv_s", 1)
    cmp_p = stack("cmp_p", 1).reshape(2, 8, LP, 2, 4, 64)
    cmp_s = cat("cmp_s", 1).reshape(2, 32, DS, 2, 4, 64)
    sel_p = stack("sel_p", 1).reshape(2, 8, LP, 2, 4, 64)
    sel_s = cat("sel_s", 1).reshape(2, 32, DS, 2, 4, 64)
    win_p = stack("win_p", 1).reshape(2, 8, 512, 2, 4, 64)
    win_s = cat("win_s", 1).reshape(2, 32, 512, 2, 4, 64)
    ffn_p = stack("ffn_p", 1)
    ffn_s = cat("ffn_s", 1)
    return (y_p, y_s, gdn_p, gdn_s, gconv_p, gconv_s, cmp_p, cmp_s, sel_p, sel_s, win_p, win_s, ffn_p, ffn_s)


def kernel(**inputs):
    inp = {k_: np.asarray(v) for k_, v in inputs.items()}
    n_phys = inp["cache_cmp"].shape[1]
    nc = build(n_phys)
    maps = make_in_maps(inp)
    res = run_bass_kernel_spmd(nc, maps, core_ids=list(range(8)))
    return assemble(res.results)
```
